# Optimizing a Trainium2 kernel written in Bass

```python
import jax, jax.numpy as jnp
from jax import lax
import numpy as np

D_MODEL = 2048
BATCH = 4
SEQ = 4096
DEPTH = 1

GRID_W = 64
CTX_LEN = 256
RWKV_WIDTH = 1024
RWKV_HEAD = 64
RWKV_HEADS = RWKV_WIDTH // RWKV_HEAD
LORA = 64
GN_EPS = 64e-5
ATT_WIDTH = D_MODEL - RWKV_WIDTH
HEAD_DIM = 64
N_HEADS = ATT_WIDTH // HEAD_DIM
KV_HEADS = 4
GROUP = N_HEADS // KV_HEADS
KV_WIDTH = KV_HEADS * HEAD_DIM
WINDOW = 128
BLOCK = 128
ROPE_THETA = 10000.0
NORM_EPS = 1e-6
SHIFT_COLS = 3 * RWKV_WIDTH + 4 * LORA
IN_COLS = SHIFT_COLS + RWKV_WIDTH + ATT_WIDTH + 2 * KV_WIDTH + ATT_WIDTH

kernel_name = "hymba_rwkv7_swa_prefix_dit_block"


def _rms(t, g):
    tf = t.astype(jnp.float32)
    return tf * lax.rsqrt(jnp.mean(tf * tf, axis=-1, keepdims=True) + NORM_EPS) * g


def _adaln(cvec, w_ada, b_ada):
    return jax.nn.silu(cvec) @ w_ada + b_ada


def _modulate(xs, mod, norm_g):
    shift, scale, gate = jnp.split(mod, 3, axis=-1)
    xn = _rms(xs, norm_g) * (1.0 + scale[..., None, :]) + shift[..., None, :]
    return xn, gate[..., None, :]


def _centred_shift(p, mu):
    zeros = jnp.zeros_like(p[:, :1])
    prev = jnp.concatenate([zeros, p[:, :-1]], axis=1)
    nxt = jnp.concatenate([p[:, 1:], zeros], axis=1)
    return p + mu * (0.5 * (prev + nxt) - p)


def _project(xn, w_in, mu_shift):
    p = jnp.einsum('bld,de->ble', xn, w_in)
    cuts = [int(i) for i in np.cumsum([SHIFT_COLS, RWKV_WIDTH, ATT_WIDTH, KV_WIDTH, KV_WIDTH])]
    sh, z_r, q, k, v, z_a = jnp.split(p, cuts, axis=-1)
    return _centred_shift(sh, mu_shift), z_r, q, k, v, z_a


def _rwkv_prep(sh, w0, w2, a0, a2, k_k, k_a):
    B, L, _ = sh.shape
    sh = sh.astype(jnp.float32)
    r, k, v, lw, la = jnp.split(sh, [RWKV_WIDTH, 2 * RWKV_WIDTH, 3 * RWKV_WIDTH, 3 * RWKV_WIDTH + 2 * LORA], axis=-1)
    heads = lambda t: t.reshape(B, L, RWKV_HEADS, RWKV_HEAD)
    lw = lw.reshape(B, L, 2, LORA)
    la = la.reshape(B, L, 2, LORA)
    w_log = -jax.nn.softplus(-(w0 + jnp.einsum('bldr,drc->bldc', jnp.tanh(lw), w2))) - 0.5
    decay = jnp.exp(-jnp.exp(w_log)).reshape(B, L, 2, RWKV_HEADS, RWKV_HEAD)
    a = jax.nn.sigmoid(a0 + jnp.einsum('bldr,drc->bldc', la, a2)).reshape(B, L, 2, RWKV_HEADS, RWKV_HEAD)
    r, k, v = heads(r), heads(k), heads(v)
    kk = k * k_k.reshape(RWKV_HEADS, RWKV_HEAD)
    kk = kk / jnp.maximum(jnp.linalg.norm(kk, axis=-1, keepdims=True), 1e-12)
    kd = k[:, :, None] * (1.0 + (a - 1.0) * k_a.reshape(RWKV_HEADS, RWKV_HEAD))
    return r, v, kk, decay, a, kd


def _rwkv_scan(S0, r, w, k, v, kk, a, reverse):
    xs = tuple(jnp.moveaxis(t, 1, 0) for t in (r, w, k, v, kk, a))

    def step(S, inp):
        r_t, w_t, k_t, v_t, kk_t, a_t = inp
        s_kk = jnp.einsum('bhvk,bhk->bhv', S, kk_t)
        S = (S * w_t[:, :, None, :] - s_kk[..., None] * (kk_t * a_t)[:, :, None, :]
             + v_t[..., None] * k_t[:, :, None, :])
        return S, jnp.einsum('bhvk,bhk->bhv', S, r_t)

    S, ys = lax.scan(step, S0, xs, reverse=reverse)
    return S, jnp.moveaxis(ys, 0, 1)


def _bidirectional_rwkv(lat, ctx):
    r, v, kk, decay, a, kd = lat
    rc, vc, kkc, decayc, ac, kdc = ctx
    S0 = jnp.zeros((r.shape[0], RWKV_HEADS, RWKV_HEAD, RWKV_HEAD), jnp.float32)
    S_cf, yc_f = _rwkv_scan(S0, rc, decayc[:, :, 0], kdc[:, :, 0], vc, kkc, ac[:, :, 0], False)
    _, yl_f = _rwkv_scan(S_cf, r, decay[:, :, 0], kd[:, :, 0], v, kk, a[:, :, 0], False)
    S_cb, yc_b = _rwkv_scan(S0, rc, decayc[:, :, 1], kdc[:, :, 1], vc, kkc, ac[:, :, 1], True)
    _, yl_b = _rwkv_scan(S_cb, r, decay[:, :, 1], kd[:, :, 1], v, kk, a[:, :, 1], True)
    return yl_f + yl_b, yc_f + yc_b


def _rwkv_output(y, prep, r_k, ln_g, ln_b):
    r, v, _, _, _, kd = prep
    B, L = y.shape[:2]
    mu = jnp.mean(y, axis=-1, keepdims=True)
    var = jnp.mean(jnp.square(y - mu), axis=-1, keepdims=True)
    yn = ((y - mu) * lax.rsqrt(var + GN_EPS)).reshape(B, L, RWKV_WIDTH) * ln_g + ln_b
    bonus = jnp.einsum('blhn,bldhn,hn->blh', r, kd, r_k)[..., None] * v
    return yn + bonus.reshape(B, L, RWKV_WIDTH)


def _axial_rope_tables(L):
    rows = L // GRID_W
    row_ids = jnp.repeat(jnp.arange(rows), GRID_W).astype(jnp.float32)
    col_ids = jnp.tile(jnp.arange(GRID_W), rows).astype(jnp.float32)
    half = HEAD_DIM // 4
    inv = ROPE_THETA ** (-jnp.arange(half, dtype=jnp.float32) / half)
    ang_r = row_ids[:, None] * inv
    ang_c = col_ids[:, None] * inv
    ang = jnp.concatenate([ang_r, ang_r, ang_c, ang_c], axis=-1)
    return jnp.cos(ang), jnp.sin(ang)


def _apply_rope(t, cos, sin):
    parts = t.reshape(*t.shape[:-1], 2, 2, HEAD_DIM // 4)
    rot = jnp.stack([-parts[..., 1, :], parts[..., 0, :]], axis=-2).reshape(t.shape)
    return t * cos[None, :, None, :] + rot * sin[None, :, None, :]


def _sink_softmax(scores, sink):
    m = sink
    for s in scores:
        m = jnp.maximum(m, jnp.max(s, axis=-1, keepdims=True))
    ps = [jnp.exp(s - m) for s in scores]
    denom = jnp.exp(sink - m)
    for p in ps:
        denom = denom + jnp.sum(p, axis=-1, keepdims=True)
    return [p / denom for p in ps]


def _window_attention(q, k, v, kc, vc, sink):
    B, L = q.shape[:2]
    nb = L // BLOCK
    qb = q.reshape(B, nb, BLOCK, KV_HEADS, GROUP, HEAD_DIM) * (HEAD_DIM ** -0.5)
    pad = ((0, 0), (BLOCK, BLOCK), (0, 0), (0, 0))
    kp = jnp.pad(k, pad).reshape(B, nb + 2, BLOCK, KV_HEADS, HEAD_DIM)
    vp = jnp.pad(v, pad).reshape(B, nb + 2, BLOCK, KV_HEADS, HEAD_DIM)
    band = lambda t: jnp.concatenate([t[:, :-2], t[:, 1:-1], t[:, 2:]], axis=2)
    kb, vb = band(kp), band(vp)
    s_lat = jnp.einsum('bnqhgd,bnshd->bhgnqs', qb, kb)
    s_ctx = jnp.einsum('bnqhgd,bshd->bhgnqs', qb, kc)
    qpos = jnp.arange(nb)[:, None, None] * BLOCK + jnp.arange(BLOCK)[None, :, None]
    kpos = (jnp.arange(nb)[:, None, None] - 1) * BLOCK + jnp.arange(3 * BLOCK)[None, None, :]
    valid = (jnp.abs(qpos - kpos) <= WINDOW) & (kpos >= 0) & (kpos < L)
    s_lat = jnp.where(valid, s_lat, -jnp.inf)
    p_lat, p_ctx = _sink_softmax([s_lat, s_ctx], sink.astype(jnp.float32).reshape(1, KV_HEADS, GROUP, 1, 1, 1))
    o = jnp.einsum('bhgnqs,bnshd->bnqhgd', p_lat, vb) + jnp.einsum('bhgnqs,bshd->bnqhgd', p_ctx, vc)
    return o.reshape(B, L, ATT_WIDTH)


def _context_attention(qc, kc, vc, sink):
    B, C = qc.shape[:2]
    qg = qc.reshape(B, C, KV_HEADS, GROUP, HEAD_DIM) * (HEAD_DIM ** -0.5)
    s = jnp.einsum('bqhgd,bshd->bhgqs', qg, kc)
    (p,) = _sink_softmax([s], sink.astype(jnp.float32).reshape(1, KV_HEADS, GROUP, 1, 1))
    return jnp.einsum('bhgqs,bshd->bqhgd', p, vc).reshape(B, C, ATT_WIDTH)


def setup_inputs(seed: int = 0) -> dict:
    key = jax.random.key(seed)
    ks = jax.random.split(key, 24)
    f32 = jnp.float32
    nrm = lambda k, shape, s: jax.random.normal(k, shape, f32) * s
    D = D_MODEL
    ramp = -7.0 + 5.0 * (jnp.arange(RWKV_WIDTH, dtype=f32) / (RWKV_WIDTH - 1)) ** 0.85 + 0.5
    return {
        "x": nrm(ks[0], (BATCH, SEQ, D), 1.0),
        "c": nrm(ks[1], (BATCH, D), 1.0),
        "ctx": nrm(ks[2], (BATCH, CTX_LEN, D), 1.0),
        "c_ctx": nrm(ks[3], (D,), 1.0),
        "w_ada": nrm(ks[4], (DEPTH, D, 3 * D), 0.5 * D ** -0.5),
        "b_ada": nrm(ks[5], (DEPTH, 3 * D), 0.01),
        "norm_g": 1.0 + nrm(ks[6], (DEPTH, D), 0.02),
        "w_in": nrm(ks[7], (DEPTH, D, IN_COLS), D ** -0.5),
        "mu_shift": jax.random.uniform(ks[8], (DEPTH, SHIFT_COLS), f32),
        "w0": ramp + nrm(ks[9], (DEPTH, 2, RWKV_WIDTH), 0.1),
        "w2": nrm(ks[10], (DEPTH, 2, LORA, RWKV_WIDTH), 0.5 * LORA ** -0.5),
        "a0": nrm(ks[11], (DEPTH, 2, RWKV_WIDTH), 0.5),
        "a2": nrm(ks[12], (DEPTH, 2, LORA, RWKV_WIDTH), 0.5 * LORA ** -0.5),
        "k_k": 0.85 + nrm(ks[13], (DEPTH, RWKV_WIDTH), 0.05),
        "k_a": 1.0 + nrm(ks[14], (DEPTH, RWKV_WIDTH), 0.05),
        "r_k": nrm(ks[15], (DEPTH, RWKV_HEADS, RWKV_HEAD), 0.1),
        "ln_x_g": 1.0 + nrm(ks[16], (DEPTH, RWKV_WIDTH), 0.02),
        "ln_x_b": nrm(ks[17], (DEPTH, RWKV_WIDTH), 0.01),
        "q_norm_g": 1.0 + nrm(ks[18], (DEPTH, HEAD_DIM), 0.02),
        "k_norm_g": 1.0 + nrm(ks[19], (DEPTH, HEAD_DIM), 0.02),
        "sink": nrm(ks[20], (DEPTH, N_HEADS), 0.5),
        "w_out": nrm(ks[21], (DEPTH, D, D), D ** -0.5),
    }


def reference(x, c, ctx, c_ctx, w_ada, b_ada, norm_g, w_in, mu_shift, w0, w2, a0, a2, k_k, k_a, r_k,
              ln_x_g, ln_x_b, q_norm_g, k_norm_g, sink, w_out):
    B, L, _ = x.shape
    C = ctx.shape[1]
    cos, sin = _axial_rope_tables(L)
    for l in range(DEPTH):
        mod = _adaln(c, w_ada[l], b_ada[l])
        mod_c = _adaln(c_ctx, w_ada[l], b_ada[l])
        xn, gate = _modulate(x, mod, norm_g[l])
        xcn, gate_c = _modulate(ctx, mod_c, norm_g[l])
        sh, z_r, q, k, v, z_a = _project(xn, w_in[l], mu_shift[l])
        sh_c, z_rc, q_c, k_c, v_c, z_ac = _project(xcn, w_in[l], mu_shift[l])

        prep = _rwkv_prep(sh, w0[l], w2[l], a0[l], a2[l], k_k[l], k_a[l])
        prep_c = _rwkv_prep(sh_c, w0[l], w2[l], a0[l], a2[l], k_k[l], k_a[l])
        y_lat, y_ctx = _bidirectional_rwkv(prep, prep_c)
        rw_out = _rwkv_output(y_lat, prep, r_k[l], ln_x_g[l], ln_x_b[l])

        qh = _apply_rope(_rms(q.reshape(B, L, N_HEADS, HEAD_DIM), q_norm_g[l]), cos, sin)
        kh = _apply_rope(_rms(k.reshape(B, L, KV_HEADS, HEAD_DIM), k_norm_g[l]), cos, sin)
        vh = v.reshape(B, L, KV_HEADS, HEAD_DIM).astype(jnp.float32)
        kch = _rms(k_c.reshape(B, C, KV_HEADS, HEAD_DIM), k_norm_g[l])
        vch = v_c.reshape(B, C, KV_HEADS, HEAD_DIM).astype(jnp.float32)
        att_out = _window_attention(qh, kh, vh, kch, vch, sink[l])

        mixed = jnp.concatenate([rw_out * jax.nn.silu(z_r), att_out * jax.nn.silu(z_a)], axis=-1)
        x_new = x + gate * (mixed @ w_out[l])

        if l + 1 < DEPTH:
            rw_c = _rwkv_output(y_ctx, prep_c, r_k[l], ln_x_g[l], ln_x_b[l])
            qch = _rms(q_c.reshape(B, C, N_HEADS, HEAD_DIM), q_norm_g[l])
            att_c = _context_attention(qch, kch, vch, sink[l])
            mixed_c = jnp.concatenate([rw_c * jax.nn.silu(z_rc), att_c * jax.nn.silu(z_ac)], axis=-1)
            ctx = (ctx + gate_c * (mixed_c @ w_out[l])).astype(ctx.dtype)
        x = x_new.astype(x.dtype)
    return x
```

```python
import contextlib
import numpy as np
import concourse.bass as bass
import concourse.mybir as mybir
from concourse.bass_utils import run_bass_kernel_spmd

F32 = mybir.dt.float32
BF16 = mybir.dt.bfloat16
AF = mybir.ActivationFunctionType
ALU = mybir.AluOpType
AX = mybir.AxisListType

ENGS = ["pe", "act", "dve", "pool", "sp"]
CDEC = 0.6065306597126334

D = 2048
NT = 34
OWN0, OWN1 = 2, 18
TL = NT * 128
PC = 6912
C_R, C_K, C_V, C_LO, C_ZR, C_Q, C_ZA, C_KA, C_VA = 0, 1024, 2048, 3072, 3328, 4352, 5376, 6400, 6656
SHC = 3328
DEBUG = {}


class Buf:
    __slots__ = ("last_w", "readers")

    def __init__(self):
        self.last_w = None
        self.readers = []


class T:
    def __init__(self, t):
        self.t = t
        self.b = Buf()

    def __getitem__(self, k):
        return self.t[k]


class Op:
    __slots__ = ("eng", "fn", "deps", "is_dma", "has_dep", "mile", "dsem", "dval")


class Sched:
    def __init__(self, nc, n_dma_sems=32):
        self.nc = nc
        self.ops = []
        self.n_dma_sems = n_dma_sems
        self.last_on = {e: None for e in ENGS}
        self.dmas_since_barrier = []

    def op(self, eng, fn, reads=(), writes=(), dma=False, extra_deps=()):
        o = Op()
        o.eng = eng; o.fn = fn; o.is_dma = dma; o.has_dep = False; o.mile = None; o.dsem = None; o.dval = None
        o.deps = set(extra_deps)
        oid = len(self.ops)
        for t in reads:
            b = t.b
            if b.last_w is not None:
                o.deps.add(b.last_w)
        for t in writes:
            b = t.b
            if b.last_w is not None:
                o.deps.add(b.last_w)
            o.deps.update(b.readers)
        for t in reads:
            t.b.readers.append(oid)
        for t in writes:
            t.b.last_w = oid
            t.b.readers = []
        o.deps.discard(oid)
        self.ops.append(o)
        self.last_on[eng] = oid
        if dma:
            self.dmas_since_barrier.append(oid)
        return oid

    def barrier(self):
        deps = [v for v in self.last_on.values() if v is not None] + list(self.dmas_since_barrier)
        self.dmas_since_barrier = []
        for e in ENGS:
            self.op(e, None, extra_deps=deps)

    def emit(self, final_wait_ops=()):
        nc = self.nc
        ops = self.ops
        for o in ops:
            nd = set()
            for d in o.deps:
                p = ops[d]
                if p.fn is None:
                    if p.eng == o.eng:
                        continue
                    nd.update(p.deps)
                    continue
                if p.eng == o.eng and not p.is_dma and o.eng == "pe" and not o.is_dma:
                    continue
                nd.add(d)
            o.deps = nd
        for o in ops:
            for d in o.deps:
                ops[d].has_dep = True
        for d in final_wait_ops:
            ops[d].has_dep = True
        cnt = {e: 0 for e in ENGS}
        dma_i = 0
        dma_cnt = [0] * self.n_dma_sems
        dma_prev = [None] * self.n_dma_sems
        for i, o in enumerate(ops):
            if o.fn is None:
                continue
            if o.is_dma:
                s = dma_i % self.n_dma_sems
                dma_i += 1
                if dma_prev[s] is not None:
                    o.deps.add(dma_prev[s])
                dma_prev[s] = i
                dma_cnt[s] += 16
                o.dsem = s
                o.dval = dma_cnt[s]
            elif o.has_dep:
                cnt[o.eng] += 1
                o.mile = cnt[o.eng]
        streams = {e: [] for e in ENGS}
        for i, o in enumerate(ops):
            streams[o.eng].append(i)
        with contextlib.ExitStack() as st:
            esem = {e: st.enter_context(nc.semaphore("s_" + e)) for e in ENGS}
            dsem = [st.enter_context(nc.semaphore("d_%d" % k)) for k in range(self.n_dma_sems)]
            block = st.enter_context(nc.Block())

            def run(eng_name, engine):
                seen = {}
                for i in streams[eng_name]:
                    o = ops[i]
                    need = {}
                    for d in o.deps:
                        p = ops[d]
                        if p.is_dma:
                            key = ("d", p.dsem); val = p.dval
                        else:
                            key = ("e", p.eng); val = p.mile
                        if need.get(key, 0) < val:
                            need[key] = val
                    for key, val in need.items():
                        if seen.get(key, 0) >= val:
                            continue
                        seen[key] = val
                        sem = dsem[key[1]] if key[0] == "d" else esem[key[1]]
                        engine.wait_ge(sem, val)
                    if o.fn is None:
                        continue
                    ins = o.fn(engine)
                    if o.is_dma:
                        ins.then_inc(dsem[o.dsem], 16)
                    elif o.mile is not None:
                        ins.then_inc(esem[o.eng], 1)
                if eng_name == "sp":
                    fin = {}
                    for d in final_wait_ops:
                        p = ops[d]
                        fin[p.dsem] = max(fin.get(p.dsem, 0), p.dval)
                    for k_, v_ in fin.items():
                        engine.wait_ge(dsem[k_], v_)

            block.tensor(lambda e: run("pe", e))
            block.scalar(lambda e: run("act", e))
            block.vector(lambda e: run("dve", e))
            block.gpsimd(lambda e: run("pool", e))
            block.sync(lambda e: run("sp", e))


def build(debug_names=(), stop_after=99):
    nc = bass.Bass("TRN2", target_bir_lowering=False)

    def din(name, shape, dt=F32):
        return nc.dram_tensor(name, list(shape), dt, kind="ExternalInput").ap()

    xin = din("xin", [TL, D])
    cc = din("cc", [128, 32])
    w_ada = din("w_ada", [D, 3 * D])
    bcol = din("bcol", [128, 96])
    ngcol = din("ngcol", [128, 16])
    w_in = din("w_in", [D, PC])
    mu = din("mu", [1, SHC])
    w0 = din("w0", [2, 1024]); w2 = din("w2", [2, 64, 1024]); a0 = din("a0", [2, 1024]); a2 = din("a2", [2, 64, 1024])
    k_k = din("k_k", [1, 1024]); k_a = din("k_a", [1, 1024]); r_k = din("r_k", [1, 1024])
    ln_g = din("ln_g", [1, 1024]); ln_b = din("ln_b", [1, 1024])
    qg = din("qg", [1, 64]); kg = din("kg", [1, 64]); sink = din("sink", [1, 16])
    w_out = din("w_out", [D, D])
    rope = din("rope", [17 * 128, 128])
    c_ident = din("c_ident", [128, 128])
    c_tri = din("c_tri", [4, 128, 128])
    c_maskx = din("c_maskx", [2, 128, 448])
    c_masky = din("c_masky", [2, 128, 256])
    c_ms = din("c_ms", [2, 7, 128, 256])
    c_sh = din("c_sh", [128, 128])
    c_e = din("c_e", [2, 128])
    c_i64 = din("c_i64", [128, 64])
    c_amask = din("c_amask", [2, 128, 128])
    out = nc.dram_tensor("out", [2048, D], F32, kind="ExternalOutput").ap()

    def scratch(name, shape, dt):
        kind = "ExternalOutput" if name in debug_names else None
        if kind:
            return nc.dram_tensor(name, list(shape), dt, kind=kind).ap()
        return nc.dram_tensor(name, list(shape), dt).ap()

    P = scratch("P", [TL, PC], F32)
    YA = scratch("YA", [2048, 1024], F32)
    MT = scratch("MT", [16, 128, 2048], BF16)
    dbg = {n: scratch(n, shp, F32) for n, shp in [("dbg_bc", [5, 128, D]), ("dbg_sh", [TL, SHC]), ("dbg_y", [2048, 1024]),
                                                  ("dbg_rw", [2048, 1024]), ("dbg_att", [2048, 1024])] if n in debug_names}
    Pb = T(None); YAb = T(None); MTb = T(None)
    Ptile = [T(None) for _ in range(NT)]

    S = Sched(nc)
    top = contextlib.ExitStack()

    uid = [0]

    def sb(st, name, shape, dt=F32):
        uid[0] += 1
        return T(st.enter_context(nc.sbuf_tensor("%s_%d" % (name, uid[0]), list(shape), dt)))

    def ps(st, name, shape, dt=F32):
        uid[0] += 1
        return T(st.enter_context(nc.psum_tensor("%s_%d" % (name, uid[0]), list(shape), dt)))

    def dma(eng, out_ap, in_ap, reads=(), writes=()):
        return S.op(eng, lambda e: e.dma_start(out=out_ap, in_=in_ap), reads=reads, writes=writes, dma=True)

    def mm(o, lhsT, rhs, start, stop, reads, writes):
        return S.op("pe", lambda e: e.matmul(o, lhsT=lhsT, rhs=rhs, start=start, stop=stop), reads=reads, writes=writes)

    def tr(o, in_, ident, reads, writes):
        return S.op("pe", lambda e: e.transpose(out=o, in_=in_, identity=ident), reads=reads, writes=writes)

    def act(o, in_, func, reads, writes, scale=1.0, bias=0.0, accum=None, eng="act"):
        if accum is None:
            return S.op(eng, lambda e: e.activation(out=o, in_=in_, func=func, scale=scale, bias=bias), reads=reads, writes=writes)
        return S.op(eng, lambda e: e.activation(out=o, in_=in_, func=func, scale=scale, bias=bias, accum_out=accum),
                    reads=reads, writes=writes)

    def tt(eng, o, a, b, op, reads, writes):
        return S.op(eng, lambda e: e.tensor_tensor(out=o, in0=a, in1=b, op=op), reads=reads, writes=writes)

    def ts(eng, o, a, s1, s2, op0, op1, reads, writes):
        if s2 is None:
            return S.op(eng, lambda e: e.tensor_scalar(out=o, in0=a, scalar1=s1, scalar2=None, op0=op0), reads=reads, writes=writes)
        return S.op(eng, lambda e: e.tensor_scalar(out=o, in0=a, scalar1=s1, scalar2=s2, op0=op0, op1=op1), reads=reads, writes=writes)

    def stt(eng, o, a, s, b, op0, op1, reads, writes):
        return S.op(eng, lambda e: e.scalar_tensor_tensor(out=o, in0=a, scalar=s, in1=b, op0=op0, op1=op1), reads=reads, writes=writes)

    def cp(eng, o, a, reads, writes):
        if eng == "act":
            return S.op("act", lambda e: e.activation(out=o, in_=a, func=AF.Copy), reads=reads, writes=writes)
        return S.op(eng, lambda e: e.tensor_copy(out=o, in_=a), reads=reads, writes=writes)

    def red(eng, o, a, reads, writes, op=ALU.add):
        return S.op(eng, lambda e: e.tensor_reduce(out=o, in_=a, axis=AX.X, op=op), reads=reads, writes=writes)

    def recip(o, a, reads, writes):
        return S.op("dve", lambda e: e.reciprocal(out=o, in_=a), reads=reads, writes=writes)

    def memset(eng, o, val, writes):
        return S.op(eng, lambda e: e.memset(o, val), writes=writes)

    ddn = [0]

    def dd(name, T_, ap, shape):
        if "dd" not in debug_names:
            return
        t = nc.dram_tensor("dd_" + name, list(shape), F32, kind="ExternalOutput").ap()
        dma("pool", t, ap, reads=[T_])

    def bc_row(ap_row, n):
        return ap_row.partition_broadcast(128) if hasattr(ap_row, "partition_broadcast") else ap_row

    identf = sb(top, "identf", [128, 128]); identb = sb(top, "identb", [128, 128], BF16)
    onesf = sb(top, "onesf", [128, 128])
    bonA = sb(top, "bonA", [128, 16, 16])
    st01 = contextlib.ExitStack()
    Abc = [sb(st01, "Abc%d" % v, [128, D]) for v in range(2)]
    Bbc = [sb(st01, "Bbc%d" % v, [128, D]) for v in range(2)]
    gatebc = sb(st01, "gatebc", [128, D])
    GATE = scratch("GATE", [128, D], F32); GATEb = T(None)
    dma("sp", identf[:], c_ident[:, :], writes=[identf])
    cp("dve", identb[:], identf[:], [identf], [identb])
    memset("dve", onesf[:], 1.0, [onesf])

    with contextlib.ExitStack() as st:
        cct = sb(st, "cct", [128, 32]); sc = sb(st, "sc", [128, 32])
        bct = sb(st, "bct", [128, 96]); ngt = sb(st, "ngt", [128, 16])
        modc = sb(st, "modc", [128, 96]); acol = sb(st, "acol", [128, 2, 16])
        wa = [sb(st, "wa%d" % i, [128, 16, 512]) for i in range(2)]
        dg = [sb(st, "dg%d" % i, [128, 512]) for i in range(2)]
        psA = ps(st, "psA", [128, 96])
        psB = [ps(st, "psB%d" % i, [128, 512]) for i in range(2)]
        dma("sp", cct[:], cc[:, :], writes=[cct]); dma("sp", bct[:], bcol[:, :], writes=[bct]); dma("sp", ngt[:], ngcol[:, :], writes=[ngt])
        act(sc[:], cct[:], AF.Silu, [cct], [sc])
        sc3 = sc[:].rearrange("p (v j) -> p v j", j=16)
        wav = w_ada.rearrange("(j p) n -> p j n", p=128)
        for g in range(12):
            w = wa[g % 2]
            dma("sp" if g % 2 == 0 else "pool", w[:], wav[:, :, g * 512:(g + 1) * 512], writes=[w])
            for m4 in range(4):
                m = g * 4 + m4
                for j in range(16):
                    mm(psA[:, 2 * m:2 * m + 2], w[:, j, m4 * 128:(m4 + 1) * 128], sc3[:, :, j], j == 0, j == 15, [w, sc], [psA])
        tt("dve", modc[:], psA[:], bct[:], ALU.add, [psA, bct], [modc])
        mc3 = modc[:].rearrange("p (m v) -> p v m", v=2)
        for v in range(2):
            stt("dve", acol[:, v, :], mc3[:, v, 16:32], 1.0, ngt[:], ALU.add, ALU.mult, [modc, ngt], [acol])
        jobs = [(acol, lambda v, m: acol[:, v, m:m + 1], Abc[0], 0), (acol, lambda v, m: acol[:, v, m:m + 1], Abc[1], 1),
                (modc, lambda v, m: mc3[:, v, m:m + 1], Bbc[0], 0), (modc, lambda v, m: mc3[:, v, m:m + 1], Bbc[1], 1),
                (modc, lambda v, m: mc3[:, v, 32 + m:33 + m], gatebc, 0)]
        k = 0
        for src, colf, dst, v in jobs:
            for m4 in range(4):
                d_ = dg[k % 2]; p_ = psB[k % 2]; k += 1
                for mi in range(4):
                    m = m4 * 4 + mi
                    ts("dve", d_[:, mi * 128:(mi + 1) * 128], identf[:], colf(v, m), None, ALU.mult, ALU.bypass, [identf, src], [d_])
                mm(p_[:], onesf[:], d_[:], True, True, [onesf, d_], [p_])
                cp("act", dst[:, m4 * 512:(m4 + 1) * 512], p_[:], [p_], [dst])
        dma("sp", GATE[:, :], gatebc[:], reads=[gatebc], writes=[GATEb])
        if "dbg_bc" in dbg:
            for i, t_ in enumerate([Abc[0], Abc[1], Bbc[0], Bbc[1], gatebc]):
                dma("sp", dbg["dbg_bc"][i], t_[:], reads=[t_])
    S.barrier()

    blocks = [(0, 512, 'r'), (512, 512, 'r'), (1024, 512, 'k'), (1536, 512, 'k'), (2048, 512, 'v'), (2560, 512, 'v'),
              (3072, 256, 'lo'), (3328, 512, 'zr'), (3840, 512, 'zr'), (4352, 512, 'q'), (4864, 512, 'q'),
              (5376, 512, 'za'), (5888, 512, 'za'), (6400, 512, 'kv')]
    tblocks = [([0, 1], {'k', 'v', 'lo', 'kv'})] + [(list(range(s, s + 4)), None) for s in (2, 6, 10, 14)] + \
              [([18, 19, 20, 21], {'r', 'k', 'v', 'lo', 'kv'})] + [(list(range(s, s + 4)), {'k', 'v', 'lo'}) for s in (22, 26, 30)]
    if stop_after >= 1:
        with contextlib.ExitStack() as st:
            xt = [sb(st, "xt%d" % i, [128, D]) for i in range(2)]
            junk = sb(st, "junk", [128, D], BF16)
            ssq = [sb(st, "ssq%d" % i, [128, 1]) for i in range(2)]
            xs = [sb(st, "xs%d" % i, [128, D]) for i in range(2)]
            xn = [sb(st, "xn%d" % i, [128, D], BF16) for i in range(2)]
            xnT = [sb(st, "xnT%d" % i, [128, 16, 512], BF16) for i in range(2)]
            wb = [sb(st, "wb%d" % i, [128, 16, 512], BF16) for i in range(3)]
            stg = [sb(st, "stg%d" % i, [128, 512]) for i in range(4)]
            pT = [ps(st, "pT%d" % i, [128, 1024], BF16) for i in range(2)]
            pp = [ps(st, "pp%d" % i, [128, 512]) for i in range(4)]
            winv = w_in.rearrange("(j p) n -> p j n", p=128)
            wi = 0; si = 0; ti = 0
            for bi, (tiles, need) in enumerate(tblocks):
                xT = xnT[bi % 2]
                for tl, i in enumerate(tiles):
                    v = 1 if i < 2 else 0
                    x_ = xt[ti % 2]; sq_ = ssq[ti % 2]; xs_ = xs[ti % 2]; xn_ = xn[ti % 2]; ti += 1
                    dma("act", x_[:], xin[i * 128:(i + 1) * 128, :], writes=[x_])
                    memset("dve", sq_[:], 0.0, [sq_])
                    act(junk[:], x_[:], AF.Square, [x_, sq_], [junk, sq_], accum=sq_[:, 0:1])
                    act(sq_[:], sq_[:], AF.Sqrt, [sq_], [sq_], scale=1.0 / D, bias=1e-6)
                    recip(sq_[:], sq_[:], [sq_], [sq_])
                    stt("dve", xs_[:], x_[:], sq_[:, 0:1], Abc[v][:], ALU.mult, ALU.mult, [x_, sq_, Abc[v]], [xs_])
                    tt("pool", xn_[:], xs_[:], Bbc[v][:], ALU.add, [xs_, Bbc[v]], [xn_])
                    for half in range(2):
                        p_ = pT[half]
                        for jj in range(8):
                            j = half * 8 + jj
                            tr(p_[:, jj * 128:(jj + 1) * 128], xn_[:, j * 128:(j + 1) * 128], identb[:], [xn_, identb], [p_])
                        cp("act" if half == 0 else "dve", xT[:, half * 8:(half + 1) * 8, tl * 128:(tl + 1) * 128],
                           p_[:].rearrange("p (j t) -> p j t", t=128), [p_], [xT])
                for (c0, cw, grp) in blocks:
                    if need is not None and grp not in need:
                        continue
                    w = wb[wi % 3]; wi += 1
                    dma("pool", w[:, :, 0:cw], winv[:, :, c0:c0 + cw], writes=[w])
                    for tl, i in enumerate(tiles):
                        p_ = pp[si % 4]; s_ = stg[si % 4]; si += 1
                        for j in range(16):
                            mm(p_[:, 0:cw], xT[:, j, tl * 128:(tl + 1) * 128], w[:, j, 0:cw], j == 0, j == 15, [xT, w], [p_])
                        cp("act" if si % 2 == 0 else "dve", s_[:, 0:cw], p_[:, 0:cw], [p_], [s_])
                        dma("sp", P[i * 128:(i + 1) * 128, c0:c0 + cw], s_[:, 0:cw], reads=[s_], writes=[Ptile[i]])
        S.barrier()

    st01.close()
    GN_EPS = 64e-5

    def bcast_load(eng, dst, src_row):
        return dma(eng, dst[:].rearrange("p (o n) -> p o n", o=1), src_row.partition_broadcast(128), writes=[dst])

    def rwkv_sweep(d):
        tiles = list(range(0, 18)) if d == 0 else [1, 0] + list(range(33, 1, -1))
        import os
        KT_ = int(os.environ.get("KTILES", "99")); KS_ = int(os.environ.get("KSTAGE", "99"))
        tiles = tiles[:KT_]
        with contextlib.ExitStack() as st:
            mubc = sb(st, "mubc", [128, SHC]); bcast_load("sp", mubc, mu[0:1, :])
            w0bc = sb(st, "w0bc", [128, 1024]); bcast_load("sp", w0bc, w0[d:d + 1, :])
            a0bc = sb(st, "a0bc", [128, 1024]); bcast_load("sp", a0bc, a0[d:d + 1, :])
            kkbc = sb(st, "kkbc", [128, 1024]); bcast_load("sp", kkbc, k_k[0:1, :])
            kabc = sb(st, "kabc", [128, 1024]); bcast_load("sp", kabc, k_a[0:1, :])
            omka = sb(st, "omka", [128, 1024])
            ts("dve", omka[:], kabc[:], -1.0, 1.0, ALU.mult, ALU.add, [kabc], [omka])
            rkbc = sb(st, "rkbc", [128, 1024]); bcast_load("sp", rkbc, r_k[0:1, :])
            if d == 1:
                lngbc = sb(st, "lngbc", [128, 1024]); bcast_load("sp", lngbc, ln_g[0:1, :])
                lnbbc = sb(st, "lnbbc", [128, 1024]); bcast_load("sp", lnbbc, ln_b[0:1, :])
            WAb = sb(st, "WAb", [128, 1024], BF16)
            dma("pool", WAb[0:64, :], w2[d], writes=[WAb]); dma("pool", WAb[64:128, :], a2[d], writes=[WAb])
            triI = sb(st, "triI", [128, 128]); dma("sp", triI[:], c_tri[2 * d], writes=[triI])
            triS = sb(st, "triS", [128, 128]); dma("sp", triS[:], c_tri[2 * d + 1], writes=[triS])
            negc = sb(st, "negc", [128, 1]); memset("dve", negc[:], -CDEC, [negc])
            maskx = sb(st, "maskx", [128, 448]); dma("sp", maskx[:], c_maskx[d], writes=[maskx])
            masky = sb(st, "masky", [128, 256]); dma("sp", masky[:], c_masky[d], writes=[masky])
            shm = sb(st, "shm", [128, 128], BF16); dma("pool", shm[:], c_sh[:, :], writes=[shm])
            e2 = sb(st, "e2", [2, 128], BF16); dma("pool", e2[:], c_e[:, :], writes=[e2])
            CM = sb(st, "CM", [128, 8, 576], BF16)
            for pr in range(8):
                dma("pool", CM[:, pr, 128:192], c_i64[:, :], writes=[CM])
            ST = [sb(st, "ST%d" % h, [64, 2, 64]) for h in range(16)]
            for h in range(16):
                memset("pool", ST[h][:], 0.0, [ST[h]])
            cur = [0] * 16
            sh = sb(st, "sh", [128, SHC]); pc16 = sb(st, "pc16", [128, SHC], BF16); nb16 = sb(st, "nb16", [2, SHC], BF16)
            tmp = [sb(st, "tmp%d" % k, [128, 512]) for k in range(2)]
            lo16 = sb(st, "lo16", [128, 128], BF16); loT = sb(st, "loT", [128, 128], BF16)
            sg = sb(st, "sg", [128, 1024]); al = sb(st, "al", [128, 1024]); kk = sb(st, "kk", [128, 1024])
            bb = sb(st, "bb", [128, 1024]); kd = sb(st, "kd", [128, 1024]); t1 = sb(st, "t1", [128, 1024])
            E = [sb(st, "E%d" % k, [128, 1024]) for k in range(2)]
            ss = sb(st, "ss", [128, 16]); bon = sb(st, "bon", [128, 16]); WC = sb(st, "WC", [64, 16])
            rt, kt, bt, at, ktp, btp, vb = [sb(st, n, [128, 1024], BF16) for n in ("rt", "kt", "bt", "at", "ktp", "btp", "vb")]
            X0 = [sb(st, "X0%d" % p_, [128, 448], BF16) for p_ in range(2)]
            TT = [sb(st, "TT%d" % p_, [128, 256], BF16) for p_ in range(2)]
            CC = [sb(st, "CC%d" % p_, [128, 256], BF16) for p_ in range(2)]
            tmpT = [sb(st, "tmpT%d" % p_, [128, 256], BF16) for p_ in range(2)]
            Zf = [sb(st, "Zf%d" % p_, [128, 192], BF16) for p_ in range(2)]
            msk = sb(st, "msk", [128, 7, 256]); dma("sp", msk[:], c_ms[d].rearrange("s p c -> p s c"), writes=[msk])
            II = sb(st, "II", [128, 256], BF16)
            cp("pool", II[:, 0:128], identb[:], [identb], [II]); cp("pool", II[:, 128:256], identb[:], [identb], [II])
            ARBK = [sb(st, "ARBK%d" % p_, [128, 256], BF16) for p_ in range(2)]
            QG = [sb(st, "QG%d" % p_, [64, 192], BF16) for p_ in range(2)]
            STb = [sb(st, "STb%d" % h, [64, 64], BF16) for h in range(16)]
            tmpS = [sb(st, "tmpS%d" % h, [64, 64]) for h in range(2)]
            tmpP = [sb(st, "tmpP%d" % h, [64, 64]) for h in range(2)]
            for h in range(16):
                memset("pool", STb[h][:], 0.0, [STb[h]])
            MYH = [sb(st, "MYH%d" % p_, [128, 192], BF16) for p_ in range(2)]
            ysb = sb(st, "ysb", [128, 1024])
            if d == 1:
                ya = sb(st, "ya", [128, 1024]); zr = sb(st, "zr", [128, 1024])
                mixb = sb(st, "mixb", [128, 1024], BF16); mtT = sb(st, "mtT", [128, 8, 128], BF16)
                s1 = sb(st, "s1", [128, 16]); s2 = sb(st, "s2", [128, 16]); mean = sb(st, "mean", [128, 16]); m2 = sb(st, "m2", [128, 16])
            Wd = [ps(st, "Wd%d" % k, [128, 512]) for k in range(2)]
            psT = ps(st, "psT", [128, 1024], BF16)
            Xp = [ps(st, "Xp%d" % k, [128, 512]) for k in range(2)]
            b5 = ps(st, "b5", [128, 512]); B5y0 = b5; B5f2 = T(b5.t); B5wc = T(b5.t)
            B6 = ps(st, "B6", [128, 512])
            b7 = ps(st, "b7", [128, 512])
            B7 = [T(b7.t) for k in range(2)]
            B7s = [T(b7.t) for k in range(2)]
            wk = 0
            for i in tiles:
                own = OWN0 <= i < OWN1
                c0 = 0 if own else 1024
                r0 = i * 128
                has_prev = i not in (0, 2); has_next = i not in (1, 33)
                dma("sp", sh[:, c0:SHC], P[r0:r0 + 128, c0:SHC], reads=[Ptile[i]], writes=[sh])
                dma("pool", pc16[:, c0:SHC], P[r0:r0 + 128, c0:SHC], reads=[Ptile[i]], writes=[pc16])
                memset("pool", nb16[:], 0.0, [nb16])
                if has_prev:
                    dma("pool", nb16[0:1, c0:SHC], P[r0 - 1:r0, c0:SHC], reads=[Ptile[i - 1]], writes=[nb16])
                if has_next:
                    dma("pool", nb16[1:2, c0:SHC], P[r0 + 128:r0 + 129, c0:SHC], reads=[Ptile[i + 1]], writes=[nb16])
                if d == 1 and own:
                    dma("sp", ya[:], YA[(i - 2) * 128:(i - 1) * 128, :], reads=[YAb], writes=[ya])
                    dma("sp", zr[:], P[r0:r0 + 128, C_ZR:C_ZR + 1024], reads=[Ptile[i]], writes=[zr])
                cs = c0
                while cs < SHC:
                    cw = min(512, SHC - cs)
                    W = Wd[wk % 2]; tm_ = tmp[wk % 2]; wk += 1
                    mm(W[:, 0:cw], shm[:], pc16[:, cs:cs + cw], True, False, [shm, pc16], [W])
                    mm(W[:, 0:cw], e2[0:2, :], nb16[0:2, cs:cs + cw], False, True, [e2, nb16], [W])
                    tt("dve", tm_[:, 0:cw], W[:, 0:cw], mubc[:, cs:cs + cw], ALU.mult, [W, mubc], [tm_])
                    tt("pool", sh[:, cs:cs + cw], tm_[:, 0:cw], sh[:, cs:cs + cw], ALU.add, [tm_, sh], [sh])
                    cs += cw
                r_ = sh[:, 0:1024]; k_ = sh[:, 1024:2048]; v_ = sh[:, 2048:3072]
                if KS_ <= 1:
                    continue
                act(lo16[:, 0:64], sh[:, 3072 + 64 * d:3136 + 64 * d], AF.Tanh, [sh], [lo16])
                cp("dve", lo16[:, 64:128], sh[:, 3200 + 64 * d:3264 + 64 * d], [sh], [lo16])
                tr(psT[:, 0:128], lo16[:, :], identb[:], [lo16, identb], [psT])
                cp("dve", loT[:], psT[:, 0:128], [psT], [loT])
                for (lo_, bias_, dst_) in ((0, w0bc, sg), (64, a0bc, al)):
                    for hf in range(2):
                        W = Wd[wk % 2]; wk += 1
                        mm(W[:], loT[lo_:lo_ + 64, :], WAb[lo_:lo_ + 64, hf * 512:(hf + 1) * 512], True, True, [loT, WAb], [W])
                        tt("dve", t1[:, hf * 512:(hf + 1) * 512], W[:], bias_[:, hf * 512:(hf + 1) * 512], ALU.add, [W, bias_], [t1])
                    act(dst_[:], t1[:], AF.Sigmoid, [t1], [dst_])
                tt("pool", kk[:], k_, kkbc[:], ALU.mult, [sh, kkbc], [kk])
                tt("dve", t1[:], kk[:], kk[:], ALU.mult, [kk], [t1])
                red("dve", ss[:], t1[:].rearrange("p (a b) -> p a b", b=64), [t1], [ss])
                act(ss[:], ss[:], AF.Sqrt, [ss], [ss])
                ts("dve", ss[:], ss[:], 1e-12, None, ALU.max, None, [ss], [ss])
                recip(ss[:], ss[:], [ss], [ss])
                tt("dve", kk[:].rearrange("p (a b) -> p a b", b=64), kk[:].rearrange("p (a b) -> p a b", b=64),
                   ss[:].unsqueeze(2).to_broadcast([128, 16, 64]), ALU.mult, [kk, ss], [kk])
                tt("pool", bb[:], kk[:], al[:], ALU.mult, [kk, al], [bb])
                tt("dve", t1[:], al[:], kabc[:], ALU.mult, [al, kabc], [t1])
                tt("dve", t1[:], t1[:], omka[:], ALU.add, [t1, omka], [t1])
                tt("pool", kd[:], k_, t1[:], ALU.mult, [sh, t1], [kd])
                if own:
                    tt("dve", t1[:], r_, rkbc[:], ALU.mult, [sh, rkbc], [t1])
                    tt("dve", t1[:], t1[:], kd[:], ALU.mult, [t1, kd], [t1])
                    if d == 0:
                        red("dve", bonA[:, i - 2, :], t1[:].rearrange("p (a b) -> p a b", b=64), [t1], [bonA])
                    else:
                        red("dve", bon[:], t1[:].rearrange("p (a b) -> p a b", b=64), [t1], [bon])
                        tt("dve", bon[:], bon[:], bonA[:, i - 2, :], ALU.add, [bon, bonA], [bon])
                if "dbg_sh" in dbg and d == 0 and i == 2:
                    dma("sp", dbg["dbg_sh"][0:128, :], sh[:], reads=[sh])
                    for k_i, t_ in enumerate([sg, al, kk, kd]):
                        dma("sp", dbg["dbg_sh"][128 * (k_i + 1):128 * (k_i + 2), 0:1024], t_[:], reads=[t_])
                for hf in range(2):
                    hs = slice(hf * 512, (hf + 1) * 512)
                    W = Wd[wk % 2]; wk += 1
                    mm(W[:], triI[:], sg[:, hs], True, True, [triI, sg], [W])
                    if own:
                        act(E[0][:, hs], W[:], AF.Exp, [W], [E[0]])
                    act(E[1][:, hs], W[:], AF.Exp, [W], [E[1]], scale=-1.0)
                    stt("dve", t1[:, hs], sg[:, hs], CDEC, W[:], ALU.mult, ALU.add, [sg, W], [t1])
                if own:
                    tt("pool", rt[:], r_, E[0][:], ALU.mult, [sh, E[0]], [rt])
                tt("dve", kt[:], kd[:], E[1][:], ALU.mult, [kd, E[1]], [kt])
                tt("pool", bt[:], bb[:], E[1][:], ALU.mult, [bb, E[1]], [bt])
                act(E[0][:], t1[:], AF.Exp, [t1], [E[0]])
                stt("dve", at[:], kk[:], -1.0, E[0][:], ALU.mult, ALU.mult, [kk, E[0]], [at])
                for hf in range(2):
                    hs = slice(hf * 512, (hf + 1) * 512)
                    W = Wd[wk % 2]; wk += 1
                    mm(W[:], triS[:], sg[:, hs], True, True, [triS, sg], [W])
                    act(E[1][:, hs], W[:], AF.Exp, [W], [E[1]])
                tt("pool", ktp[:], kd[:], E[1][:], ALU.mult, [kd, E[1]], [ktp])
                tt("dve", btp[:], bb[:], E[1][:], ALU.mult, [bb, E[1]], [btp])
                cp("pool", vb[:], v_, [sh], [vb])
                for h in range(16):
                    mm(b5[0:64, 448 + h:449 + h], sg[:, h * 64:(h + 1) * 64], negc[:, 0:1], True, True, [sg, negc], [B5wc])
                act(WC[:], b5[0:64, 448:464], AF.Exp, [B5wc], [WC])
                if KS_ <= 2:
                    continue
                DD = (d == 0 and i in (0, 2))
                if DD:
                    for nm_, t_ in (("bt", bt), ("kt", kt), ("at", at), ("rt", rt), ("btp", btp), ("ktp", ktp), ("vb", vb)):
                        dd("%s_%d" % (nm_, i), t_, t_[:], [128, 1024])
                    dd("WC_%d" % i, WC, WC[:], [64, 16])
                srcs = [(bt, 0), (kt, 192), (at, 320)] + ([(rt, 448)] if own else [])
                for si_, (src, off) in enumerate(srcs):
                    for pr in range(8):
                        tr(psT[:, pr * 128:(pr + 1) * 128], src[:, pr * 128:(pr + 1) * 128], identb[:], [src, identb], [psT])
                    cp("act" if si_ % 2 == 0 else "dve", CM[:, :, off:off + 128], psT[:].rearrange("p (a b) -> p a b", b=128), [psT], [CM])
                if DD:
                    dd("CM_%d" % i, CM, CM[:, 0, :], [128, 576])
                if KS_ <= 3:
                    continue
                for pr in range(8):
                    hh = [(2 * pr + par, par, 64 * par) for par in range(2)]
                    for (h, par, pb) in hh:
                        mm(Xp[par][:, 128:448], CM[pb:pb + 64, pr, 320:448], CM[pb:pb + 64, pr, 0:320], True, True, [CM], [Xp[par]])
                        mm(Xp[par][:, 0:128], CM[pb:pb + 64, pr, 0:128], CM[pb:pb + 64, pr, 320:448], True, True, [CM], [Xp[par]])
                        tt("dve", X0[par][:], Xp[par][:, 0:448], maskx[:], ALU.mult, [Xp[par], maskx], [X0[par]])
                        tt("dve", tmpT[par][:], Xp[par][:, 0:256], msk[:, 0, :], ALU.mult, [Xp[par], msk], [tmpT[par]])
                        tt("pool", TT[par][:], tmpT[par][:], II[:], ALU.add, [tmpT[par], II], [TT[par]])
                        if DD and pr == 0:
                            dd("X0_%d_%d" % (i, h), X0[par], X0[par][:], [128, 448])
                    for lev in range(1, 7):
                        for (h, par, pb) in hh:
                            xp = Xp[par]
                            mm(xp[:, 0:128], X0[par][:, 128:256], TT[par][:, 0:128], True, True, [X0[par], TT[par]], [xp])
                            mm(xp[:, 128:256], X0[par][:, 0:128], TT[par][:, 128:256], True, True, [X0[par], TT[par]], [xp])
                            cp("act", CC[par][:], xp[:, 0:256], [xp], [CC[par]])
                            mm(xp[:, 256:384], TT[par][:, 128:256], CC[par][:, 0:128], True, True, [TT[par], CC[par]], [xp])
                            mm(xp[:, 384:512], TT[par][:, 0:128], CC[par][:, 128:256], True, True, [TT[par], CC[par]], [xp])
                            tt("dve", tmpT[par][:], xp[:, 256:512], msk[:, lev, :], ALU.mult, [xp, msk], [tmpT[par]])
                            tt("pool", TT[par][:], TT[par][:], tmpT[par][:], ALU.add, [TT[par], tmpT[par]], [TT[par]])
                    for (h, par, pb) in hh:
                        xp = Xp[par]
                        mm(xp[:, 0:192], TT[par][:, 0:128], X0[par][:, 256:448], True, True, [TT[par], X0[par]], [xp])
                        cp("dve" if par == 0 else "act", Zf[par][:], xp[:, 0:192], [xp], [Zf[par]])
                    for (h, par, pb) in hh:
                        hc = slice(h * 64, (h + 1) * 64)
                        Xf = Zf[par]
                        AbT = Xf[:, 0:64]; PTt = Xf[:, 64:192]
                        if DD and pr == 0:
                            dd("Zf_%d_%d" % (i, h), Xf, Xf[:], [128, 192])
                            dd("TT_%d_%d" % (i, h), TT[par], TT[par][:], [128, 256])
                        if own:
                            mm(b5[:, 0:128], CM[pb:pb + 64, pr, 0:128], CM[pb:pb + 64, pr, 448:576], True, True, [CM], [B5y0])
                            mm(b5[:, 128:256], CM[pb:pb + 64, pr, 192:320], CM[pb:pb + 64, pr, 448:576], True, True, [CM], [B5y0])
                            tt("dve", ARBK[par][:], b5[:, 0:256], masky[:], ALU.mult, [B5y0, masky], [ARBK[par]])
                            mm(b7[0:64, par * 256:par * 256 + 128], AbT, ARBK[par][:, 0:128], True, False, [Xf, ARBK[par]], [B7[par]])
                            mm(b7[0:64, par * 256:par * 256 + 128], rt[:, hc], identb[:], False, True, [rt, identb], [B7[par]])
                        mm(b7[0:64, par * 256 + 128:par * 256 + 192], AbT, btp[:, hc], True, True, [Xf, btp], [B7[par]])
                        lo_c = 0 if own else 128
                        cp("act", QG[par][:, lo_c:192], b7[0:64, par * 256 + lo_c:par * 256 + 192], [B7[par]], [QG[par]])
                        if KS_ <= 6:
                            continue
                        if own:
                            mm(b5[:, 256:384], PTt, ARBK[par][:, 0:128], True, False, [Xf, ARBK[par]], [B5f2])
                            mm(b5[:, 256:384], identb[:], ARBK[par][:, 128:256], False, True, [identb, ARBK[par]], [B5f2])
                        mm(b5[:, 384:448], PTt, btp[:, hc], True, False, [Xf, btp], [B5f2])
                        mm(b5[:, 384:448], identb[:], ktp[:, hc], False, True, [identb, ktp], [B5f2])
                        cp("dve", MYH[par][:, lo_c:192], b5[:, 256 + lo_c:448], [B5f2], [MYH[par]])
                        if DD and pr == 0:
                            dd("QG_%d_%d" % (i, h), QG[par], QG[par][:], [64, 192])
                            dd("MYH_%d_%d" % (i, h), MYH[par], MYH[par][:], [128, 192])
                            if own:
                                dd("ARBK_%d_%d" % (i, h), ARBK[par], ARBK[par][:], [128, 256])
                        if KS_ <= 7:
                            continue
                        STc = ST[h][:, cur[h], :]; STn = ST[h][:, 1 - cur[h], :]
                        if own:
                            yo = B6[:, (h % 8) * 64:(h % 8) * 64 + 64]
                            mm(yo, QG[par][:, 0:128], STb[h][:], True, False, [QG[par], STb[h]], [B6])
                            mm(yo, MYH[par][:, 0:128], vb[:, hc], False, True, [MYH[par], vb], [B6])
                        sreg = b7[0:64, par * 256 + 192:par * 256 + 256]
                        KSUB = int(os.environ.get("KSUB", "99"))
                        mm(sreg, QG[par][:, 128:192], STb[h][:], True, KSUB == 1, [QG[par], STb[h]], [B7s[par]])
                        if KSUB >= 2:
                            mm(sreg, MYH[par][:, 128:192], vb[:, hc], False, True, [MYH[par], vb], [B7s[par]])
                        if KSUB >= 3:
                            act(tmpS[par][:], STc, AF.Copy, [ST[h], WC], [tmpS[par]], scale=WC[:, h:h + 1])
                            act(tmpP[par][:], sreg, AF.Copy, [B7s[par]], [tmpP[par]])
                            tt("pool", STn, tmpP[par][:], tmpS[par][:], ALU.add, [tmpP[par], tmpS[par]], [ST[h]])
                        if KSUB >= 4:
                            cp("pool", STb[h][:], STn, [ST[h]], [STb[h]])
                        if DD and pr == 0:
                            dd("ST_%d_%d" % (i, h), ST[h], STn, [64, 64])
                        cur[h] ^= 1
                    if own and pr in (3, 7):
                        hs = slice((pr // 4) * 512, (pr // 4) * 512 + 512)
                        if d == 0:
                            cp("act", ysb[:, hs], B6[:], [B6], [ysb])
                        else:
                            tt("dve", ysb[:, hs], B6[:], ya[:, hs], ALU.add, [B6, ya], [ysb])
                if not own:
                    continue
                if d == 0:
                    if DD:
                        dd("ysb_%d" % i, ysb, ysb[:], [128, 1024])
                    dma("sp", YA[(i - 2) * 128:(i - 1) * 128, :], ysb[:], reads=[ysb], writes=[YAb])
                    continue
                if "dbg_y" in dbg:
                    dma("sp", dbg["dbg_y"][(i - 2) * 128:(i - 1) * 128, :], ysb[:], reads=[ysb])
                y3 = ysb[:].rearrange("p (a b) -> p a b", b=64)
                red("dve", s1[:], y3, [ysb], [s1])
                tt("pool", t1[:], ysb[:], ysb[:], ALU.mult, [ysb], [t1])
                red("dve", s2[:], t1[:].rearrange("p (a b) -> p a b", b=64), [t1], [s2])
                ts("dve", mean[:], s1[:], 1.0 / 64, None, ALU.mult, None, [s1], [mean])
                tt("dve", m2[:], mean[:], mean[:], ALU.mult, [mean], [m2])
                stt("dve", s2[:], s2[:], 1.0 / 64, m2[:], ALU.mult, ALU.subtract, [s2, m2], [s2])
                act(s2[:], s2[:], AF.Sqrt, [s2], [s2], bias=GN_EPS)
                recip(s2[:], s2[:], [s2], [s2])
                tt("dve", y3, y3, mean[:].unsqueeze(2).to_broadcast([128, 16, 64]), ALU.subtract, [ysb, mean], [ysb])
                tt("dve", y3, y3, s2[:].unsqueeze(2).to_broadcast([128, 16, 64]), ALU.mult, [ysb, s2], [ysb])
                tt("pool", ysb[:], ysb[:], lngbc[:], ALU.mult, [ysb, lngbc], [ysb])
                tt("pool", ysb[:], ysb[:], lnbbc[:], ALU.add, [ysb, lnbbc], [ysb])
                tt("dve", t1[:].rearrange("p (a b) -> p a b", b=64), sh[:, 2048:3072].rearrange("p (a b) -> p a b", b=64),
                   bon[:].unsqueeze(2).to_broadcast([128, 16, 64]), ALU.mult, [sh, bon], [t1])
                tt("dve", ysb[:], ysb[:], t1[:], ALU.add, [ysb, t1], [ysb])
                if "dbg_rw" in dbg:
                    dma("sp", dbg["dbg_rw"][(i - 2) * 128:(i - 1) * 128, :], ysb[:], reads=[ysb])
                act(zr[:], zr[:], AF.Silu, [zr], [zr])
                tt("pool", mixb[:], ysb[:], zr[:], ALU.mult, [ysb, zr], [mixb])
                for cch in range(8):
                    tr(psT[:, cch * 128:(cch + 1) * 128], mixb[:, cch * 128:(cch + 1) * 128], identb[:], [mixb, identb], [psT])
                cp("act", mtT[:], psT[:].rearrange("p (a b) -> p a b", b=128), [psT], [mtT])
                dma("sp", MT[0:8, :, (i - 2) * 128:(i - 1) * 128].rearrange("c p t -> p c t"), mtT[:], reads=[mtT], writes=[MTb])
        S.barrier()

    if stop_after >= 2:
        rwkv_sweep(0)
    if stop_after >= 3:
        rwkv_sweep(1)

    def rope_apply(eng2, t3, rot3, rp, nh, reads_t, T_t, T_rot, T_rp):
        t5 = t3.rearrange("p h (a b c) -> p h a b c", a=2, b=2)
        r5 = rot3.rearrange("p h (a b c) -> p h a b c", a=2, b=2)
        cp("pool", r5[:, :, :, 0, :], t5[:, :, :, 1, :], [T_t], [T_rot])
        cp("pool", r5[:, :, :, 1, :], t5[:, :, :, 0, :], [T_t], [T_rot])
        cosb = rp[:, 0:64].unsqueeze(1).to_broadcast([128, nh, 64])
        sinb = rp[:, 64:128].unsqueeze(1).to_broadcast([128, nh, 64])
        tt("dve", t3, t3, cosb, ALU.mult, [T_t, T_rp], [T_t])
        tt("dve", rot3, rot3, sinb, ALU.mult, [T_rot, T_rp], [T_rot])
        tt("dve", t3, t3, rot3, ALU.add, [T_t, T_rot], [T_t])

    def attention():
        with contextlib.ExitStack() as st:
            kgbc = sb(st, "kgbc", [128, 64]); bcast_load("sp", kgbc, kg[0:1, :])
            qgbc = sb(st, "qgbc", [128, 64]); bcast_load("sp", qgbc, qg[0:1, :])
            ts("dve", qgbc[:], qgbc[:], 0.125, None, ALU.mult, None, [qgbc], [qgbc])
            esk = sb(st, "esk", [128, 16]); bcast_load("sp", esk, sink[0:1, :])
            act(esk[:], esk[:], AF.Exp, [esk], [esk])
            amask = sb(st, "amask", [128, 2, 128], BF16)
            dma("pool", amask[:], c_amask.rearrange("a k q -> k a q"), writes=[amask])
            KT = sb(st, "KT", [64, 19, 4, 128], BF16)
            V1 = sb(st, "V1", [128, 19, 4, 65], BF16)
            memset("pool", V1[:], 1.0, [V1])
            kv = sb(st, "kv", [128, 512]); rot = sb(st, "rot", [128, 1024]); rp = sb(st, "rp", [128, 128])
            sq = sb(st, "sq", [128, 1024]); ss = sb(st, "ssq_a", [128, 16])
            kb = sb(st, "kb", [128, 256], BF16)
            q = sb(st, "q", [128, 1024]); za = sb(st, "za", [128, 1024]); qb = sb(st, "qb", [128, 1024], BF16)
            QT = sb(st, "QT", [64, 16, 128], BF16)
            PTs = [sb(st, "PTs%d" % k, [128, 512], BF16) for k in range(5)]
            att = sb(st, "att", [128, 1024]); den = sb(st, "den", [128, 16])
            mixb = sb(st, "mixb_a", [128, 1024], BF16); mtT = sb(st, "mtT_a", [128, 8, 128], BF16)
            psT = ps(st, "psT_a", [128, 1024], BF16)
            Sps = [ps(st, "Sps%d" % k, [128, 512]) for k in range(2)]
            Og = [ps(st, "Og%d" % k, [128, 512]) for k in range(4)]
            for n in range(19):
                i = n
                dma("sp", kv[:], P[i * 128:(i + 1) * 128, C_KA:C_KA + 512], reads=[Ptile[i]], writes=[kv])
                k3 = kv[:, 0:256].rearrange("p (h c) -> p h c", c=64)
                tt("dve", sq[:, 0:256], kv[:, 0:256], kv[:, 0:256], ALU.mult, [kv], [sq])
                red("dve", ss[:, 0:4], sq[:, 0:256].rearrange("p (h c) -> p h c", c=64), [sq], [ss])
                act(ss[:, 0:4], ss[:, 0:4], AF.Sqrt, [ss], [ss], scale=1.0 / 64, bias=1e-6)
                recip(ss[:, 0:4], ss[:, 0:4], [ss], [ss])
                tt("dve", k3, k3, ss[:, 0:4].unsqueeze(2).to_broadcast([128, 4, 64]), ALU.mult, [kv, ss], [kv])
                tt("dve", k3, k3, kgbc[:].unsqueeze(1).to_broadcast([128, 4, 64]), ALU.mult, [kv, kgbc], [kv])
                if i >= 2:
                    dma("sp", rp[:], rope[(i - 2) * 128:(i - 1) * 128, :], writes=[rp])
                    rope_apply(None, k3, rot[:, 0:256].rearrange("p (h c) -> p h c", c=64), rp, 4, None, kv, rot, rp)
                cp("act", kb[:], kv[:, 0:256], [kv], [kb])
                for g in range(4):
                    tr(psT[0:64, g * 128:(g + 1) * 128], kb[:, g * 64:(g + 1) * 64], identb[:], [kb, identb], [psT])
                cp("dve", KT[:, n, :, :], psT[0:64, 0:512].rearrange("p (g t) -> p g t", t=128), [psT], [KT])
                cp("act", V1[:, n, :, 0:64], kv[:, 256:512].rearrange("p (h c) -> p h c", c=64), [kv], [V1])
            sk = 0
            for i in range(OWN0, OWN1):
                r0 = i * 128
                dma("sp", q[:], P[r0:r0 + 128, C_Q:C_Q + 1024], reads=[Ptile[i]], writes=[q])
                dma("sp", za[:], P[r0:r0 + 128, C_ZA:C_ZA + 1024], reads=[Ptile[i]], writes=[za])
                dma("sp", rp[:], rope[(i - 2) * 128:(i - 1) * 128, :], writes=[rp])
                q3 = q[:].rearrange("p (h c) -> p h c", c=64)
                tt("pool", sq[:], q[:], q[:], ALU.mult, [q], [sq])
                red("dve", ss[:], sq[:].rearrange("p (h c) -> p h c", c=64), [sq], [ss])
                act(ss[:], ss[:], AF.Sqrt, [ss], [ss], scale=1.0 / 64, bias=1e-6)
                recip(ss[:], ss[:], [ss], [ss])
                tt("dve", q3, q3, ss[:].unsqueeze(2).to_broadcast([128, 16, 64]), ALU.mult, [q, ss], [q])
                tt("dve", q3, q3, qgbc[:].unsqueeze(1).to_broadcast([128, 16, 64]), ALU.mult, [q, qgbc], [q])
                rope_apply(None, q3, rot[:].rearrange("p (h c) -> p h c", c=64), rp, 16, None, q, rot, rp)
                cp("act", qb[:], q[:], [q], [qb])
                for half in range(2):
                    for hx in range(8):
                        hd = half * 8 + hx
                        tr(psT[0:64, hx * 128:(hx + 1) * 128], qb[:, hd * 64:(hd + 1) * 64], identb[:], [qb, identb], [psT])
                    cp("dve", QT[:, half * 8:(half + 1) * 8, :], psT[0:64, :].rearrange("p (g t) -> p g t", t=128), [psT], [QT])
                for g in range(4):
                    keyt = ([i - 1] if i > OWN0 else []) + [i, i + 1, 0, 1]
                    for ki, kt_ in enumerate(keyt):
                        sp_ = Sps[sk % 2]; sk += 1
                        pt_ = PTs[ki]
                        mm(sp_[:], KT[:, kt_, g, :], QT[:, 4 * g:4 * g + 4, :].rearrange("p a b -> p (a b)"), True, True, [KT, QT], [sp_])
                        act(pt_[:], sp_[:], AF.Exp, [sp_], [pt_])
                        is_prev = (i > OWN0 and ki == 0); is_next = (ki == (2 if i > OWN0 else 1))
                        if is_prev or is_next:
                            mi = 0 if is_prev else 1
                            for hx in range(4):
                                tt("pool", pt_[:, hx * 128:(hx + 1) * 128], pt_[:, hx * 128:(hx + 1) * 128], amask[:, mi, :], ALU.mult, [pt_, amask], [pt_])
                    for hx in range(4):
                        for ki, kt_ in enumerate(keyt):
                            mm(Og[g][:, hx * 65:(hx + 1) * 65], PTs[ki][:, hx * 128:(hx + 1) * 128], V1[:, kt_, g, :],
                               ki == 0, ki == len(keyt) - 1, [PTs[ki], V1], [Og[g]])
                for g in range(4):
                    o3 = Og[g][:, 0:260].rearrange("p (h c) -> p h c", c=65)
                    tt("dve", den[:, 4 * g:4 * g + 4], o3[:, :, 64], esk[:, 4 * g:4 * g + 4], ALU.add, [Og[g], esk], [den])
                    recip(den[:, 4 * g:4 * g + 4], den[:, 4 * g:4 * g + 4], [den], [den])
                    tt("dve", att[:, g * 256:(g + 1) * 256].rearrange("p (h c) -> p h c", c=64), o3[:, :, 0:64],
                       den[:, 4 * g:4 * g + 4].unsqueeze(2).to_broadcast([128, 4, 64]), ALU.mult, [Og[g], den], [att])
                if "dbg_att" in dbg:
                    dma("sp", dbg["dbg_att"][(i - 2) * 128:(i - 1) * 128, :], att[:], reads=[att])
                act(za[:], za[:], AF.Silu, [za], [za])
                tt("pool", mixb[:], att[:], za[:], ALU.mult, [att, za], [mixb])
                for cch in range(8):
                    tr(psT[:, cch * 128:(cch + 1) * 128], mixb[:, cch * 128:(cch + 1) * 128], identb[:], [mixb, identb], [psT])
                cp("act", mtT[:], psT[:].rearrange("p (a b) -> p a b", b=128), [psT], [mtT])
                dma("sp", MT[8:16, :, (i - 2) * 128:(i - 1) * 128].rearrange("c p t -> p c t"), mtT[:], reads=[mtT], writes=[MTb])
        S.barrier()

    def outproj():
        with contextlib.ExitStack() as st:
            wo = sb(st, "wo", [128, 16, D], BF16)
            wov = w_out.rearrange("(j p) n -> p j n", p=128)
            for k4 in range(4):
                dma("pool", wo[:, k4 * 4:(k4 + 1) * 4, :], wov[:, k4 * 4:(k4 + 1) * 4, :], writes=[wo])
            gate = sb(st, "gate", [128, D]); dma("sp", gate[:], GATE[:, :], reads=[GATEb], writes=[gate])
            mt = [sb(st, "mt%d" % k, [128, 16, 128], BF16) for k in range(2)]
            xo = [sb(st, "xo%d" % k, [128, D]) for k in range(2)]
            ot = [sb(st, "ot%d" % k, [128, D]) for k in range(2)]
            pp = [ps(st, "ppo%d" % k, [128, 512]) for k in range(4)]
            for tl in range(16):
                i = tl + 2
                m_ = mt[tl % 2]; x_ = xo[tl % 2]; o_ = ot[tl % 2]
                dma("sp", m_[:], MT[:, :, tl * 128:(tl + 1) * 128].rearrange("c p t -> p c t"), reads=[MTb], writes=[m_])
                dma("act", x_[:], xin[i * 128:(i + 1) * 128, :], writes=[x_])
                for nb in range(4):
                    ns = slice(nb * 512, (nb + 1) * 512)
                    p_ = pp[nb]
                    for cc_ in range(16):
                        mm(p_[:], m_[:, cc_, :], wo[:, cc_, ns], cc_ == 0, cc_ == 15, [m_, wo], [p_])
                    tt("dve", o_[:, ns], p_[:], gate[:, ns], ALU.mult, [p_, gate], [o_])
                    tt("pool", o_[:, ns], o_[:, ns], x_[:, ns], ALU.add, [o_, x_], [o_])
                dma("sp", out[tl * 128:(tl + 1) * 128, :], o_[:], reads=[o_])

    if stop_after >= 4:
        attention()
    if stop_after >= 5:
        outproj()

    S.emit(final_wait_ops=[i for i, o in enumerate(S.ops) if o.is_dma])
    top.close()
    return nc


def host_consts():
    c = {}
    c["c_ident"] = np.eye(128, dtype=np.float32)
    i = np.arange(128)
    triA_incl = (i[:, None] <= i[None, :]).astype(np.float32)
    triA_suf = (i[:, None] > i[None, :]).astype(np.float32)
    triB_incl = (i[:, None] >= i[None, :]).astype(np.float32)
    triB_suf = (i[:, None] < i[None, :]).astype(np.float32)
    c["c_tri"] = (-CDEC * np.stack([triA_incl, triA_suf, triB_incl, triB_suf])).astype(np.float32)
    mx = np.zeros((2, 128, 448), np.float32); my = np.zeros((2, 128, 256), np.float32)
    for d in range(2):
        prec = (i[:, None] < i[None, :]) if d == 0 else (i[:, None] > i[None, :])
        preceq = prec | np.eye(128, dtype=bool)
        mx[d, :, 0:128] = prec; mx[d, :, 128:256] = prec.T; mx[d, :, 256:320] = 1.0; mx[d, :, 320:448] = prec.T
        my[d, :, 0:128] = preceq; my[d, :, 128:256] = preceq
    c["c_maskx"] = mx; c["c_masky"] = my
    cms = np.zeros((2, 7, 128, 256), np.float32)
    for d in range(2):
        prec = (i[:, None] < i[None, :]) if d == 0 else (i[:, None] > i[None, :])
        for s_ in range(7):
            m = prec & ((i[:, None] >> (s_ + 1)) == (i[None, :] >> (s_ + 1))) & ((i[:, None] >> s_) != (i[None, :] >> s_))
            cms[d, s_, :, 0:128] = m; cms[d, s_, :, 128:256] = m.T
    c["c_ms"] = cms
    sh = np.zeros((128, 128), np.float32)
    sh[i, i] = -1.0; sh[i[:-1], i[:-1] + 1] = 0.5; sh[i[1:], i[1:] - 1] = 0.5
    c["c_sh"] = sh
    e = np.zeros((2, 128), np.float32); e[0, 0] = 0.5; e[1, 127] = 0.5
    c["c_e"] = e
    c["c_i64"] = np.concatenate([np.eye(64, dtype=np.float32)] * 2, axis=0)
    am = np.zeros((2, 128, 128), np.float32)
    am[0] = (i[:, None] >= i[None, :]); am[1] = (i[:, None] <= i[None, :])
    c["c_amask"] = am
    return c


def rope_tables(pos):
    pos = np.asarray(pos)
    row = (pos // 64).astype(np.float32); col = (pos % 64).astype(np.float32)
    half = 16
    inv = (10000.0 ** (-np.arange(half, dtype=np.float32) / half)).astype(np.float32)
    ar = row[:, None] * inv; ac = col[:, None] * inv
    ang = np.concatenate([ar, ar, ac, ac], axis=-1)
    cos = np.cos(ang); sin = np.sin(ang)
    sgn = np.concatenate([-np.ones(16), np.ones(16), -np.ones(16), np.ones(16)]).astype(np.float32)
    return np.concatenate([cos, sin * sgn], axis=-1).astype(np.float32)


def core_inputs(inp, b, h, consts):
    f = np.ascontiguousarray
    x = inp["x"][b]; ctx = inp["ctx"][b]
    if h == 1:
        x = x[::-1]; ctx = ctx[::-1]
    m = {}
    m["xin"] = f(np.concatenate([ctx, x], axis=0))
    colf = lambda v: v.reshape(-1, 128).T
    m["cc"] = f(np.concatenate([colf(inp["c"][b]), colf(inp["c_ctx"])], axis=1))
    m["w_ada"] = f(inp["w_ada"][0])
    m["bcol"] = f(np.repeat(colf(inp["b_ada"][0]), 2, axis=1))
    m["ngcol"] = f(colf(inp["norm_g"][0]))
    dA, dB = (0, 1) if h == 0 else (1, 0)
    lw = lambda d: np.arange(3072 + 64 * d, 3072 + 64 * d + 64)
    la = lambda d: np.arange(3200 + 64 * d, 3200 + 64 * d + 64)
    perm = np.concatenate([np.arange(0, 3072), lw(dA), lw(dB), la(dA), la(dB), np.arange(3328, 4352), np.arange(4352, 5376),
                           np.arange(5888, 6912), np.arange(5376, 5632), np.arange(5632, 5888)])
    m["w_in"] = f(inp["w_in"][0][:, perm])
    m["mu"] = f(inp["mu_shift"][0][perm[:SHC]][None, :])
    sel = [dA, dB]
    m["w0"] = f(inp["w0"][0][sel]); m["w2"] = f(inp["w2"][0][sel]); m["a0"] = f(inp["a0"][0][sel]); m["a2"] = f(inp["a2"][0][sel])
    for k in ("k_k", "k_a"):
        m[k] = f(inp[k][0][None, :])
    m["r_k"] = f(inp["r_k"][0].reshape(1, 1024))
    m["ln_g"] = f(inp["ln_x_g"][0][None, :]); m["ln_b"] = f(inp["ln_x_b"][0][None, :])
    m["qg"] = f(inp["q_norm_g"][0][None, :]); m["kg"] = f(inp["k_norm_g"][0][None, :]); m["sink"] = f(inp["sink"][0][None, :])
    m["w_out"] = f(inp["w_out"][0])
    loc = np.arange(17 * 128)
    pos = loc if h == 0 else 4095 - loc
    m["rope"] = rope_tables(pos)
    m.update(consts)
    return m


def kernel(**inputs):
    inp = {k: np.asarray(v) for k, v in inputs.items()}
    consts = host_consts()
    nc = build()
    in_maps = [core_inputs(inp, c // 2, c % 2, consts) for c in range(8)]
    res = run_bass_kernel_spmd(nc, in_maps, core_ids=list(range(8)))
    out = np.empty((4, 4096, D), np.float32)
    for c in range(8):
        b, h = c // 2, c % 2
        o = res.results[c]["out"]
        if h == 0:
            out[b, :2048] = o
        else:
            out[b, 2048:] = o[::-1]
    return out
```

```python
import contextlib
import numpy as np
import concourse.bass as bass
import concourse.mybir as mybir
from concourse.bass_utils import run_bass_kernel_spmd

F32 = mybir.dt.float32
BF16 = mybir.dt.bfloat16
AF = mybir.ActivationFunctionType
ALU = mybir.AluOpType
AX = mybir.AxisListType

ENGS = ["pe", "act", "dve", "pool", "sp"]
CDEC = 0.6065306597126334

D = 2048
NT = 34
OWN0, OWN1 = 2, 18
TL = NT * 128
PC = 6912
C_R, C_K, C_V, C_LO, C_ZR, C_Q, C_ZA, C_KA, C_VA = 0, 1024, 2048, 3072, 3328, 4352, 5376, 6400, 6656
SHC = 3328
DEBUG = {}


class Buf:
    __slots__ = ("last_w", "readers")

    def __init__(self):
        self.last_w = None
        self.readers = []


class T:
    def __init__(self, t):
        self.t = t
        self.b = Buf()

    def __getitem__(self, k):
        return self.t[k]


class Op:
    __slots__ = ("eng", "fn", "deps", "is_dma", "has_dep", "mile", "dsem", "dval")


class Sched:
    def __init__(self, nc, n_dma_sems=32):
        self.nc = nc
        self.ops = []
        self.n_dma_sems = n_dma_sems
        self.last_on = {e: None for e in ENGS}
        self.dmas_since_barrier = []

    def op(self, eng, fn, reads=(), writes=(), dma=False, extra_deps=()):
        o = Op()
        o.eng = eng; o.fn = fn; o.is_dma = dma; o.has_dep = False; o.mile = None; o.dsem = None; o.dval = None
        o.deps = set(extra_deps)
        oid = len(self.ops)
        for t in reads:
            b = t.b
            if b.last_w is not None:
                o.deps.add(b.last_w)
        for t in writes:
            b = t.b
            if b.last_w is not None:
                o.deps.add(b.last_w)
            o.deps.update(b.readers)
        for t in reads:
            t.b.readers.append(oid)
        for t in writes:
            t.b.last_w = oid
            t.b.readers = []
        o.deps.discard(oid)
        self.ops.append(o)
        self.last_on[eng] = oid
        if dma:
            self.dmas_since_barrier.append(oid)
        return oid

    def barrier(self):
        deps = [v for v in self.last_on.values() if v is not None] + list(self.dmas_since_barrier)
        self.dmas_since_barrier = []
        for e in ENGS:
            self.op(e, None, extra_deps=deps)

    def emit(self, final_wait_ops=()):
        nc = self.nc
        ops = self.ops
        for o in ops:
            nd = set()
            for d in o.deps:
                p = ops[d]
                if p.fn is None:
                    if p.eng == o.eng:
                        continue
                    nd.update(p.deps)
                    continue
                if p.eng == o.eng and not p.is_dma and o.eng == "pe" and not o.is_dma:
                    continue
                nd.add(d)
            o.deps = nd
        for o in ops:
            for d in o.deps:
                ops[d].has_dep = True
        for d in final_wait_ops:
            ops[d].has_dep = True
        cnt = {e: 0 for e in ENGS}
        dma_i = 0
        dma_cnt = [0] * self.n_dma_sems
        dma_prev = [None] * self.n_dma_sems
        for i, o in enumerate(ops):
            if o.fn is None:
                continue
            if o.is_dma:
                s = dma_i % self.n_dma_sems
                dma_i += 1
                if dma_prev[s] is not None:
                    o.deps.add(dma_prev[s])
                dma_prev[s] = i
                dma_cnt[s] += 16
                o.dsem = s
                o.dval = dma_cnt[s]
            elif o.has_dep:
                cnt[o.eng] += 1
                o.mile = cnt[o.eng]
        streams = {e: [] for e in ENGS}
        for i, o in enumerate(ops):
            streams[o.eng].append(i)
        with contextlib.ExitStack() as st:
            esem = {e: st.enter_context(nc.semaphore("s_" + e)) for e in ENGS}
            dsem = [st.enter_context(nc.semaphore("d_%d" % k)) for k in range(self.n_dma_sems)]
            block = st.enter_context(nc.Block())

            def run(eng_name, engine):
                seen = {}
                for i in streams[eng_name]:
                    o = ops[i]
                    need = {}
                    for d in o.deps:
                        p = ops[d]
                        if p.is_dma:
                            key = ("d", p.dsem); val = p.dval
                        else:
                            key = ("e", p.eng); val = p.mile
                        if need.get(key, 0) < val:
                            need[key] = val
                    for key, val in need.items():
                        if seen.get(key, 0) >= val:
                            continue
                        seen[key] = val
                        sem = dsem[key[1]] if key[0] == "d" else esem[key[1]]
                        engine.wait_ge(sem, val)
                    if o.fn is None:
                        continue
                    ins = o.fn(engine)
                    if o.is_dma:
                        ins.then_inc(dsem[o.dsem], 16)
                    elif o.mile is not None:
                        ins.then_inc(esem[o.eng], 1)
                if eng_name == "sp":
                    fin = {}
                    for d in final_wait_ops:
                        p = ops[d]
                        fin[p.dsem] = max(fin.get(p.dsem, 0), p.dval)
                    for k_, v_ in fin.items():
                        engine.wait_ge(dsem[k_], v_)

            block.tensor(lambda e: run("pe", e))
            block.scalar(lambda e: run("act", e))
            block.vector(lambda e: run("dve", e))
            block.gpsimd(lambda e: run("pool", e))
            block.sync(lambda e: run("sp", e))


def build(debug_names=(), stop_after=99):
    nc = bass.Bass("TRN2", target_bir_lowering=False)

    def din(name, shape, dt=F32):
        return nc.dram_tensor(name, list(shape), dt, kind="ExternalInput").ap()

    xin = din("xin", [TL, D])
    cc = din("cc", [128, 32])
    w_ada = din("w_ada", [D, 3 * D])
    bcol = din("bcol", [128, 96])
    ngcol = din("ngcol", [128, 16])
    w_in = din("w_in", [D, PC])
    mu = din("mu", [1, SHC])
    w0 = din("w0", [2, 1024]); w2 = din("w2", [2, 64, 1024]); a0 = din("a0", [2, 1024]); a2 = din("a2", [2, 64, 1024])
    k_k = din("k_k", [1, 1024]); k_a = din("k_a", [1, 1024]); r_k = din("r_k", [1, 1024])
    ln_g = din("ln_g", [1, 1024]); ln_b = din("ln_b", [1, 1024])
    qg = din("qg", [1, 64]); kg = din("kg", [1, 64]); sink = din("sink", [1, 16])
    w_out = din("w_out", [D, D])
    rope = din("rope", [17 * 128, 128])
    c_ident = din("c_ident", [128, 128])
    c_tri = din("c_tri", [4, 128, 128])
    c_maskx = din("c_maskx", [2, 128, 448])
    c_masky = din("c_masky", [2, 128, 256])
    c_ms = din("c_ms", [2, 7, 128, 256])
    c_sh = din("c_sh", [128, 128])
    c_e = din("c_e", [2, 128])
    c_i64 = din("c_i64", [128, 64])
    c_amask = din("c_amask", [2, 128, 128])
    out = nc.dram_tensor("out", [2048, D], F32, kind="ExternalOutput").ap()

    def scratch(name, shape, dt):
        kind = "ExternalOutput" if name in debug_names else None
        if kind:
            return nc.dram_tensor(name, list(shape), dt, kind=kind).ap()
        return nc.dram_tensor(name, list(shape), dt).ap()

    P = scratch("P", [TL, PC], F32)
    YA = scratch("YA", [2048, 1024], F32)
    MT = scratch("MT", [16, 128, 2048], BF16)
    dbg = {n: scratch(n, shp, F32) for n, shp in [("dbg_bc", [5, 128, D]), ("dbg_sh", [TL, SHC]), ("dbg_y", [2048, 1024]),
                                                  ("dbg_rw", [2048, 1024]), ("dbg_att", [2048, 1024])] if n in debug_names}
    Pb = T(None); YAb = T(None); MTb = T(None)
    Ptile = [T(None) for _ in range(NT)]

    S = Sched(nc)
    top = contextlib.ExitStack()

    uid = [0]

    def sb(st, name, shape, dt=F32):
        uid[0] += 1
        return T(st.enter_context(nc.sbuf_tensor("%s_%d" % (name, uid[0]), list(shape), dt)))

    def ps(st, name, shape, dt=F32):
        uid[0] += 1
        return T(st.enter_context(nc.psum_tensor("%s_%d" % (name, uid[0]), list(shape), dt)))

    def dma(eng, out_ap, in_ap, reads=(), writes=()):
        return S.op(eng, lambda e: e.dma_start(out=out_ap, in_=in_ap), reads=reads, writes=writes, dma=True)

    def mm(o, lhsT, rhs, start, stop, reads, writes):
        return S.op("pe", lambda e: e.matmul(o, lhsT=lhsT, rhs=rhs, start=start, stop=stop), reads=reads, writes=writes)

    def tr(o, in_, ident, reads, writes):
        return S.op("pe", lambda e: e.transpose(out=o, in_=in_, identity=ident), reads=reads, writes=writes)

    def act(o, in_, func, reads, writes, scale=1.0, bias=0.0, accum=None, eng="act"):
        if accum is None:
            return S.op(eng, lambda e: e.activation(out=o, in_=in_, func=func, scale=scale, bias=bias), reads=reads, writes=writes)
        return S.op(eng, lambda e: e.activation(out=o, in_=in_, func=func, scale=scale, bias=bias, accum_out=accum),
                    reads=reads, writes=writes)

    def tt(eng, o, a, b, op, reads, writes):
        return S.op(eng, lambda e: e.tensor_tensor(out=o, in0=a, in1=b, op=op), reads=reads, writes=writes)

    def ts(eng, o, a, s1, s2, op0, op1, reads, writes):
        if s2 is None:
            return S.op(eng, lambda e: e.tensor_scalar(out=o, in0=a, scalar1=s1, scalar2=None, op0=op0), reads=reads, writes=writes)
        return S.op(eng, lambda e: e.tensor_scalar(out=o, in0=a, scalar1=s1, scalar2=s2, op0=op0, op1=op1), reads=reads, writes=writes)

    def stt(eng, o, a, s, b, op0, op1, reads, writes):
        return S.op(eng, lambda e: e.scalar_tensor_tensor(out=o, in0=a, scalar=s, in1=b, op0=op0, op1=op1), reads=reads, writes=writes)

    def cp(eng, o, a, reads, writes):
        if eng == "act":
            return S.op("act", lambda e: e.activation(out=o, in_=a, func=AF.Copy), reads=reads, writes=writes)
        return S.op(eng, lambda e: e.tensor_copy(out=o, in_=a), reads=reads, writes=writes)

    def red(eng, o, a, reads, writes, op=ALU.add):
        return S.op(eng, lambda e: e.tensor_reduce(out=o, in_=a, axis=AX.X, op=op), reads=reads, writes=writes)

    def recip(o, a, reads, writes):
        return S.op("dve", lambda e: e.reciprocal(out=o, in_=a), reads=reads, writes=writes)

    def memset(eng, o, val, writes):
        return S.op(eng, lambda e: e.memset(o, val), writes=writes)

    ddn = [0]

    def dd(name, T_, ap, shape):
        if "dd" not in debug_names:
            return
        t = nc.dram_tensor("dd_" + name, list(shape), F32, kind="ExternalOutput").ap()
        dma("pool", t, ap, reads=[T_])

    def bc_row(ap_row, n):
        return ap_row.partition_broadcast(128) if hasattr(ap_row, "partition_broadcast") else ap_row

    identf = sb(top, "identf", [128, 128]); identb = sb(top, "identb", [128, 128], BF16)
    onesf = sb(top, "onesf", [128, 128])
    bonA = sb(top, "bonA", [128, 16, 16])
    st01 = contextlib.ExitStack()
    Abc = [sb(st01, "Abc%d" % v, [128, D]) for v in range(2)]
    Bbc = [sb(st01, "Bbc%d" % v, [128, D]) for v in range(2)]
    gatebc = sb(st01, "gatebc", [128, D])
    GATE = scratch("GATE", [128, D], F32); GATEb = T(None)
    dma("sp", identf[:], c_ident[:, :], writes=[identf])
    cp("dve", identb[:], identf[:], [identf], [identb])
    memset("dve", onesf[:], 1.0, [onesf])

    with contextlib.ExitStack() as st:
        cct = sb(st, "cct", [128, 32]); sc = sb(st, "sc", [128, 32])
        bct = sb(st, "bct", [128, 96]); ngt = sb(st, "ngt", [128, 16])
        modc = sb(st, "modc", [128, 96]); acol = sb(st, "acol", [128, 2, 16])
        wa = [sb(st, "wa%d" % i, [128, 16, 512]) for i in range(2)]
        dg = [sb(st, "dg%d" % i, [128, 512]) for i in range(2)]
        psA = ps(st, "psA", [128, 96])
        psB = [ps(st, "psB%d" % i, [128, 512]) for i in range(2)]
        dma("sp", cct[:], cc[:, :], writes=[cct]); dma("sp", bct[:], bcol[:, :], writes=[bct]); dma("sp", ngt[:], ngcol[:, :], writes=[ngt])
        act(sc[:], cct[:], AF.Silu, [cct], [sc])
        sc3 = sc[:].rearrange("p (v j) -> p v j", j=16)
        wav = w_ada.rearrange("(j p) n -> p j n", p=128)
        for g in range(12):
            w = wa[g % 2]
            dma("sp" if g % 2 == 0 else "pool", w[:], wav[:, :, g * 512:(g + 1) * 512], writes=[w])
            for m4 in range(4):
                m = g * 4 + m4
                for j in range(16):
                    mm(psA[:, 2 * m:2 * m + 2], w[:, j, m4 * 128:(m4 + 1) * 128], sc3[:, :, j], j == 0, j == 15, [w, sc], [psA])
        tt("dve", modc[:], psA[:], bct[:], ALU.add, [psA, bct], [modc])
        mc3 = modc[:].rearrange("p (m v) -> p v m", v=2)
        for v in range(2):
            stt("dve", acol[:, v, :], mc3[:, v, 16:32], 1.0, ngt[:], ALU.add, ALU.mult, [modc, ngt], [acol])
        jobs = [(acol, lambda v, m: acol[:, v, m:m + 1], Abc[0], 0), (acol, lambda v, m: acol[:, v, m:m + 1], Abc[1], 1),
                (modc, lambda v, m: mc3[:, v, m:m + 1], Bbc[0], 0), (modc, lambda v, m: mc3[:, v, m:m + 1], Bbc[1], 1),
                (modc, lambda v, m: mc3[:, v, 32 + m:33 + m], gatebc, 0)]
        k = 0
        for src, colf, dst, v in jobs:
            for m4 in range(4):
                d_ = dg[k % 2]; p_ = psB[k % 2]; k += 1
                for mi in range(4):
                    m = m4 * 4 + mi
                    ts("dve", d_[:, mi * 128:(mi + 1) * 128], identf[:], colf(v, m), None, ALU.mult, ALU.bypass, [identf, src], [d_])
                mm(p_[:], onesf[:], d_[:], True, True, [onesf, d_], [p_])
                cp("act", dst[:, m4 * 512:(m4 + 1) * 512], p_[:], [p_], [dst])
        dma("sp", GATE[:, :], gatebc[:], reads=[gatebc], writes=[GATEb])
        if "dbg_bc" in dbg:
            for i, t_ in enumerate([Abc[0], Abc[1], Bbc[0], Bbc[1], gatebc]):
                dma("sp", dbg["dbg_bc"][i], t_[:], reads=[t_])
    S.barrier()

    blocks = [(0, 512, 'r'), (512, 512, 'r'), (1024, 512, 'k'), (1536, 512, 'k'), (2048, 512, 'v'), (2560, 512, 'v'),
              (3072, 256, 'lo'), (3328, 512, 'zr'), (3840, 512, 'zr'), (4352, 512, 'q'), (4864, 512, 'q'),
              (5376, 512, 'za'), (5888, 512, 'za'), (6400, 512, 'kv')]
    tblocks = [([0, 1], {'k', 'v', 'lo', 'kv'})] + [(list(range(s, s + 4)), None) for s in (2, 6, 10, 14)] + \
              [([18, 19, 20, 21], {'r', 'k', 'v', 'lo', 'kv'})] + [(list(range(s, s + 4)), {'k', 'v', 'lo'}) for s in (22, 26, 30)]
    if stop_after >= 1:
        with contextlib.ExitStack() as st:
            xt = [sb(st, "xt%d" % i, [128, D]) for i in range(2)]
            junk = sb(st, "junk", [128, D], BF16)
            ssq = [sb(st, "ssq%d" % i, [128, 1]) for i in range(2)]
            xs = [sb(st, "xs%d" % i, [128, D]) for i in range(2)]
            xn = [sb(st, "xn%d" % i, [128, D], BF16) for i in range(2)]
            xnT = [sb(st, "xnT%d" % i, [128, 16, 512], BF16) for i in range(2)]
            wb = [sb(st, "wb%d" % i, [128, 16, 512], BF16) for i in range(3)]
            stg = [sb(st, "stg%d" % i, [128, 512]) for i in range(4)]
            pT = [ps(st, "pT%d" % i, [128, 1024], BF16) for i in range(2)]
            pp = [ps(st, "pp%d" % i, [128, 512]) for i in range(4)]
            winv = w_in.rearrange("(j p) n -> p j n", p=128)
            wi = 0; si = 0; ti = 0
            for bi, (tiles, need) in enumerate(tblocks):
                xT = xnT[bi % 2]
                for tl, i in enumerate(tiles):
                    v = 1 if i < 2 else 0
                    x_ = xt[ti % 2]; sq_ = ssq[ti % 2]; xs_ = xs[ti % 2]; xn_ = xn[ti % 2]; ti += 1
                    dma("act", x_[:], xin[i * 128:(i + 1) * 128, :], writes=[x_])
                    memset("dve", sq_[:], 0.0, [sq_])
                    act(junk[:], x_[:], AF.Square, [x_, sq_], [junk, sq_], accum=sq_[:, 0:1])
                    act(sq_[:], sq_[:], AF.Sqrt, [sq_], [sq_], scale=1.0 / D, bias=1e-6)
                    recip(sq_[:], sq_[:], [sq_], [sq_])
                    stt("dve", xs_[:], x_[:], sq_[:, 0:1], Abc[v][:], ALU.mult, ALU.mult, [x_, sq_, Abc[v]], [xs_])
                    tt("pool", xn_[:], xs_[:], Bbc[v][:], ALU.add, [xs_, Bbc[v]], [xn_])
                    for half in range(2):
                        p_ = pT[half]
                        for jj in range(8):
                            j = half * 8 + jj
                            tr(p_[:, jj * 128:(jj + 1) * 128], xn_[:, j * 128:(j + 1) * 128], identb[:], [xn_, identb], [p_])
                        cp("act" if half == 0 else "dve", xT[:, half * 8:(half + 1) * 8, tl * 128:(tl + 1) * 128],
                           p_[:].rearrange("p (j t) -> p j t", t=128), [p_], [xT])
                for (c0, cw, grp) in blocks:
                    if need is not None and grp not in need:
                        continue
                    w = wb[wi % 3]; wi += 1
                    dma("pool", w[:, :, 0:cw], winv[:, :, c0:c0 + cw], writes=[w])
                    for tl, i in enumerate(tiles):
                        p_ = pp[si % 4]; s_ = stg[si % 4]; si += 1
                        for j in range(16):
                            mm(p_[:, 0:cw], xT[:, j, tl * 128:(tl + 1) * 128], w[:, j, 0:cw], j == 0, j == 15, [xT, w], [p_])
                        cp("act" if si % 2 == 0 else "dve", s_[:, 0:cw], p_[:, 0:cw], [p_], [s_])
                        dma("sp", P[i * 128:(i + 1) * 128, c0:c0 + cw], s_[:, 0:cw], reads=[s_], writes=[Ptile[i]])
        S.barrier()

    st01.close()
    GN_EPS = 64e-5

    def bcast_load(eng, dst, src_row):
        return dma(eng, dst[:].rearrange("p (o n) -> p o n", o=1), src_row.partition_broadcast(128), writes=[dst])

    def rwkv_sweep(d):
        tiles = list(range(0, 18)) if d == 0 else [1, 0] + list(range(33, 1, -1))
        import os
        KT_ = int(os.environ.get("KTILES", "99")); KS_ = int(os.environ.get("KSTAGE", "99"))
        tiles = tiles[:KT_]
        with contextlib.ExitStack() as st:
            mubc = sb(st, "mubc", [128, SHC]); bcast_load("sp", mubc, mu[0:1, :])
            w0bc = sb(st, "w0bc", [128, 1024]); bcast_load("sp", w0bc, w0[d:d + 1, :])
            a0bc = sb(st, "a0bc", [128, 1024]); bcast_load("sp", a0bc, a0[d:d + 1, :])
            kkbc = sb(st, "kkbc", [128, 1024]); bcast_load("sp", kkbc, k_k[0:1, :])
            kabc = sb(st, "kabc", [128, 1024]); bcast_load("sp", kabc, k_a[0:1, :])
            omka = sb(st, "omka", [128, 1024])
            ts("dve", omka[:], kabc[:], -1.0, 1.0, ALU.mult, ALU.add, [kabc], [omka])
            rkbc = sb(st, "rkbc", [128, 1024]); bcast_load("sp", rkbc, r_k[0:1, :])
            if d == 1:
                lngbc = sb(st, "lngbc", [128, 1024]); bcast_load("sp", lngbc, ln_g[0:1, :])
                lnbbc = sb(st, "lnbbc", [128, 1024]); bcast_load("sp", lnbbc, ln_b[0:1, :])
            WAb = sb(st, "WAb", [128, 1024], BF16)
            dma("pool", WAb[0:64, :], w2[d], writes=[WAb]); dma("pool", WAb[64:128, :], a2[d], writes=[WAb])
            triI = sb(st, "triI", [128, 128]); dma("sp", triI[:], c_tri[2 * d], writes=[triI])
            triS = sb(st, "triS", [128, 128]); dma("sp", triS[:], c_tri[2 * d + 1], writes=[triS])
            negc = sb(st, "negc", [128, 1]); memset("dve", negc[:], -CDEC, [negc])
            maskx = sb(st, "maskx", [128, 448]); dma("sp", maskx[:], c_maskx[d], writes=[maskx])
            masky = sb(st, "masky", [128, 256]); dma("sp", masky[:], c_masky[d], writes=[masky])
            shm = sb(st, "shm", [128, 128], BF16); dma("pool", shm[:], c_sh[:, :], writes=[shm])
            e2 = sb(st, "e2", [2, 128], BF16); dma("pool", e2[:], c_e[:, :], writes=[e2])
            CM = sb(st, "CM", [128, 8, 576], BF16)
            for pr in range(8):
                dma("pool", CM[:, pr, 128:192], c_i64[:, :], writes=[CM])
            ST = [sb(st, "ST%d" % h, [64, 2, 64]) for h in range(16)]
            for h in range(16):
                memset("pool", ST[h][:], 0.0, [ST[h]])
            cur = [0] * 16
            sh = sb(st, "sh", [128, SHC]); pc16 = sb(st, "pc16", [128, SHC], BF16); nb16 = sb(st, "nb16", [2, SHC], BF16)
            tmp = [sb(st, "tmp%d" % k, [128, 512]) for k in range(2)]
            lo16 = sb(st, "lo16", [128, 128], BF16); loT = sb(st, "loT", [128, 128], BF16)
            sg = sb(st, "sg", [128, 1024]); al = sb(st, "al", [128, 1024]); kk = sb(st, "kk", [128, 1024])
            bb = sb(st, "bb", [128, 1024]); kd = sb(st, "kd", [128, 1024]); t1 = sb(st, "t1", [128, 1024])
            E = [sb(st, "E%d" % k, [128, 1024]) for k in range(2)]
            ss = sb(st, "ss", [128, 16]); bon = sb(st, "bon", [128, 16]); WC = sb(st, "WC", [64, 16])
            rt, kt, bt, at, ktp, btp, vb = [sb(st, n, [128, 1024], BF16) for n in ("rt", "kt", "bt", "at", "ktp", "btp", "vb")]
            X0 = [sb(st, "X0%d" % p_, [128, 448], BF16) for p_ in range(4)]
            TT = [sb(st, "TT%d" % p_, [128, 256], BF16) for p_ in range(4)]
            CC = [sb(st, "CC%d" % p_, [128, 256], BF16) for p_ in range(4)]
            tmpT = [sb(st, "tmpT%d" % p_, [128, 256], BF16) for p_ in range(4)]
            Zf = [sb(st, "Zf%d" % p_, [128, 192], BF16) for p_ in range(4)]
            msk = sb(st, "msk", [128, 7, 256]); dma("sp", msk[:], c_ms[d].rearrange("s p c -> p s c"), writes=[msk])
            II = sb(st, "II", [128, 256], BF16)
            cp("pool", II[:, 0:128], identb[:], [identb], [II]); cp("pool", II[:, 128:256], identb[:], [identb], [II])
            ARBK = [sb(st, "ARBK%d" % p_, [128, 256], BF16) for p_ in range(4)]
            QG = [sb(st, "QG%d" % p_, [64, 192], BF16) for p_ in range(4)]
            STb = [sb(st, "STb%d" % h, [64, 64], BF16) for h in range(16)]
            tmpS = [sb(st, "tmpS%d" % h, [64, 64]) for h in range(4)]
            tmpP = [sb(st, "tmpP%d" % h, [64, 64]) for h in range(4)]
            for h in range(16):
                memset("pool", STb[h][:], 0.0, [STb[h]])
            MYH = [sb(st, "MYH%d" % p_, [128, 192], BF16) for p_ in range(4)]
            ysb = sb(st, "ysb", [128, 1024])
            if d == 1:
                ya = sb(st, "ya", [128, 1024]); zr = sb(st, "zr", [128, 1024])
                mixb = sb(st, "mixb", [128, 1024], BF16); mtT = sb(st, "mtT", [128, 8, 128], BF16)
                s1 = sb(st, "s1", [128, 16]); s2 = sb(st, "s2", [128, 16]); mean = sb(st, "mean", [128, 16]); m2 = sb(st, "m2", [128, 16])
            Wd = [ps(st, "Wd%d" % k, [128, 512]) for k in range(2)]
            psT = ps(st, "psT", [128, 1024], BF16)
            Xp = [ps(st, "Xp%d" % k, [128, 512]) for k in range(2)]
            LB = [Xp[0], Xp[1], Wd[0], Wd[1]]
            b5 = ps(st, "b5", [128, 512])
            B6 = ps(st, "B6", [128, 512])
            b7 = ps(st, "b7", [128, 512])
            FB = [b5, b5, b7, b7]
            _ft5 = T(b5.t); _ft7 = T(b7.t)
            FT = [_ft5, _ft5, _ft7, _ft7]
            wk = 0
            for i in tiles:
                own = OWN0 <= i < OWN1
                c0 = 0 if own else 1024
                r0 = i * 128
                has_prev = i not in (0, 2); has_next = i not in (1, 33)
                dma("sp", sh[:, c0:SHC], P[r0:r0 + 128, c0:SHC], reads=[Ptile[i]], writes=[sh])
                dma("pool", pc16[:, c0:SHC], P[r0:r0 + 128, c0:SHC], reads=[Ptile[i]], writes=[pc16])
                memset("pool", nb16[:], 0.0, [nb16])
                if has_prev:
                    dma("pool", nb16[0:1, c0:SHC], P[r0 - 1:r0, c0:SHC], reads=[Ptile[i - 1]], writes=[nb16])
                if has_next:
                    dma("pool", nb16[1:2, c0:SHC], P[r0 + 128:r0 + 129, c0:SHC], reads=[Ptile[i + 1]], writes=[nb16])
                if d == 1 and own:
                    dma("sp", ya[:], YA[(i - 2) * 128:(i - 1) * 128, :], reads=[YAb], writes=[ya])
                    dma("sp", zr[:], P[r0:r0 + 128, C_ZR:C_ZR + 1024], reads=[Ptile[i]], writes=[zr])
                cs = c0
                while cs < SHC:
                    cw = min(512, SHC - cs)
                    W = Wd[wk % 2]; tm_ = tmp[wk % 2]; wk += 1
                    mm(W[:, 0:cw], shm[:], pc16[:, cs:cs + cw], True, False, [shm, pc16], [W])
                    mm(W[:, 0:cw], e2[0:2, :], nb16[0:2, cs:cs + cw], False, True, [e2, nb16], [W])
                    tt("dve", tm_[:, 0:cw], W[:, 0:cw], mubc[:, cs:cs + cw], ALU.mult, [W, mubc], [tm_])
                    tt("pool", sh[:, cs:cs + cw], tm_[:, 0:cw], sh[:, cs:cs + cw], ALU.add, [tm_, sh], [sh])
                    cs += cw
                r_ = sh[:, 0:1024]; k_ = sh[:, 1024:2048]; v_ = sh[:, 2048:3072]
                if KS_ <= 1:
                    continue
                act(lo16[:, 0:64], sh[:, 3072 + 64 * d:3136 + 64 * d], AF.Tanh, [sh], [lo16])
                cp("dve", lo16[:, 64:128], sh[:, 3200 + 64 * d:3264 + 64 * d], [sh], [lo16])
                tr(psT[:, 0:128], lo16[:, :], identb[:], [lo16, identb], [psT])
                cp("dve", loT[:], psT[:, 0:128], [psT], [loT])
                for (lo_, bias_, dst_) in ((0, w0bc, sg), (64, a0bc, al)):
                    for hf in range(2):
                        W = Wd[wk % 2]; wk += 1
                        mm(W[:], loT[lo_:lo_ + 64, :], WAb[lo_:lo_ + 64, hf * 512:(hf + 1) * 512], True, True, [loT, WAb], [W])
                        tt("dve", t1[:, hf * 512:(hf + 1) * 512], W[:], bias_[:, hf * 512:(hf + 1) * 512], ALU.add, [W, bias_], [t1])
                    act(dst_[:], t1[:], AF.Sigmoid, [t1], [dst_])
                tt("pool", kk[:], k_, kkbc[:], ALU.mult, [sh, kkbc], [kk])
                tt("dve", t1[:], kk[:], kk[:], ALU.mult, [kk], [t1])
                red("dve", ss[:], t1[:].rearrange("p (a b) -> p a b", b=64), [t1], [ss])
                act(ss[:], ss[:], AF.Sqrt, [ss], [ss])
                ts("dve", ss[:], ss[:], 1e-12, None, ALU.max, None, [ss], [ss])
                recip(ss[:], ss[:], [ss], [ss])
                tt("dve", kk[:].rearrange("p (a b) -> p a b", b=64), kk[:].rearrange("p (a b) -> p a b", b=64),
                   ss[:].unsqueeze(2).to_broadcast([128, 16, 64]), ALU.mult, [kk, ss], [kk])
                tt("pool", bb[:], kk[:], al[:], ALU.mult, [kk, al], [bb])
                tt("dve", t1[:], al[:], kabc[:], ALU.mult, [al, kabc], [t1])
                tt("dve", t1[:], t1[:], omka[:], ALU.add, [t1, omka], [t1])
                tt("pool", kd[:], k_, t1[:], ALU.mult, [sh, t1], [kd])
                if own:
                    tt("dve", t1[:], r_, rkbc[:], ALU.mult, [sh, rkbc], [t1])
                    tt("dve", t1[:], t1[:], kd[:], ALU.mult, [t1, kd], [t1])
                    if d == 0:
                        red("dve", bonA[:, i - 2, :], t1[:].rearrange("p (a b) -> p a b", b=64), [t1], [bonA])
                    else:
                        red("dve", bon[:], t1[:].rearrange("p (a b) -> p a b", b=64), [t1], [bon])
                        tt("dve", bon[:], bon[:], bonA[:, i - 2, :], ALU.add, [bon, bonA], [bon])
                if "dbg_sh" in dbg and d == 0 and i == 2:
                    dma("sp", dbg["dbg_sh"][0:128, :], sh[:], reads=[sh])
                    for k_i, t_ in enumerate([sg, al, kk, kd]):
                        dma("sp", dbg["dbg_sh"][128 * (k_i + 1):128 * (k_i + 2), 0:1024], t_[:], reads=[t_])
                for hf in range(2):
                    hs = slice(hf * 512, (hf + 1) * 512)
                    W = Wd[wk % 2]; wk += 1
                    mm(W[:], triI[:], sg[:, hs], True, True, [triI, sg], [W])
                    if own:
                        act(E[0][:, hs], W[:], AF.Exp, [W], [E[0]])
                    act(E[1][:, hs], W[:], AF.Exp, [W], [E[1]], scale=-1.0)
                    stt("dve", t1[:, hs], sg[:, hs], CDEC, W[:], ALU.mult, ALU.add, [sg, W], [t1])
                if own:
                    tt("pool", rt[:], r_, E[0][:], ALU.mult, [sh, E[0]], [rt])
                tt("dve", kt[:], kd[:], E[1][:], ALU.mult, [kd, E[1]], [kt])
                tt("pool", bt[:], bb[:], E[1][:], ALU.mult, [bb, E[1]], [bt])
                act(E[0][:], t1[:], AF.Exp, [t1], [E[0]])
                stt("dve", at[:], kk[:], -1.0, E[0][:], ALU.mult, ALU.mult, [kk, E[0]], [at])
                for hf in range(2):
                    hs = slice(hf * 512, (hf + 1) * 512)
                    W = Wd[wk % 2]; wk += 1
                    mm(W[:], triS[:], sg[:, hs], True, True, [triS, sg], [W])
                    act(E[1][:, hs], W[:], AF.Exp, [W], [E[1]])
                tt("pool", ktp[:], kd[:], E[1][:], ALU.mult, [kd, E[1]], [ktp])
                tt("dve", btp[:], bb[:], E[1][:], ALU.mult, [bb, E[1]], [btp])
                cp("pool", vb[:], v_, [sh], [vb])
                for h in range(16):
                    mm(Wd[0][0:64, h:h + 1], sg[:, h * 64:(h + 1) * 64], negc[:, 0:1], True, True, [sg, negc], [Wd[0]])
                act(WC[:], Wd[0][0:64, 0:16], AF.Exp, [Wd[0]], [WC])
                if KS_ <= 2:
                    continue
                DD = (d == 0 and i in (0, 2))
                if DD:
                    for nm_, t_ in (("bt", bt), ("kt", kt), ("at", at), ("rt", rt), ("btp", btp), ("ktp", ktp), ("vb", vb)):
                        dd("%s_%d" % (nm_, i), t_, t_[:], [128, 1024])
                    dd("WC_%d" % i, WC, WC[:], [64, 16])
                srcs = [(bt, 0), (kt, 192), (at, 320)] + ([(rt, 448)] if own else [])
                for si_, (src, off) in enumerate(srcs):
                    for pr in range(8):
                        tr(psT[:, pr * 128:(pr + 1) * 128], src[:, pr * 128:(pr + 1) * 128], identb[:], [src, identb], [psT])
                    cp("act" if si_ % 2 == 0 else "dve", CM[:, :, off:off + 128], psT[:].rearrange("p (a b) -> p a b", b=128), [psT], [CM])
                if DD:
                    dd("CM_%d" % i, CM, CM[:, 0, :], [128, 576])
                if KS_ <= 3:
                    continue
                for grp in range(4):
                    hs4 = [(g, 4 * grp + g, (4 * grp + g) // 2, 64 * ((4 * grp + g) % 2)) for g in range(4)]
                    for (g, h, pr, pb) in hs4:
                        lb = LB[g]
                        mm(lb[:, 128:448], CM[pb:pb + 64, pr, 320:448], CM[pb:pb + 64, pr, 0:320], True, True, [CM], [lb])
                        mm(lb[:, 0:128], CM[pb:pb + 64, pr, 0:128], CM[pb:pb + 64, pr, 320:448], True, True, [CM], [lb])
                    for (g, h, pr, pb) in hs4:
                        lb = LB[g]
                        tt("dve", tmpT[g][:], lb[:, 0:256], msk[:, 0, :], ALU.mult, [lb, msk], [tmpT[g]])
                        tt("pool", TT[g][:], tmpT[g][:], II[:], ALU.add, [tmpT[g], II], [TT[g]])
                        tt("dve", X0[g][:], lb[:, 0:448], maskx[:], ALU.mult, [lb, maskx], [X0[g]])
                        if DD and h < 2:
                            dd("X0_%d_%d" % (i, h), X0[g], X0[g][:], [128, 448])
                    if KS_ <= 4:
                        continue
                    for lev in range(1, 7):
                        for (g, h, pr, pb) in hs4:
                            lb = LB[g]
                            mm(lb[:, 0:128], X0[g][:, 128:256], TT[g][:, 0:128], True, True, [X0[g], TT[g]], [lb])
                            mm(lb[:, 128:256], X0[g][:, 0:128], TT[g][:, 128:256], True, True, [X0[g], TT[g]], [lb])
                            cp("act", CC[g][:], lb[:, 0:256], [lb], [CC[g]])
                        for (g, h, pr, pb) in hs4:
                            lb = LB[g]
                            mm(lb[:, 256:384], TT[g][:, 128:256], CC[g][:, 0:128], True, True, [TT[g], CC[g]], [lb])
                            mm(lb[:, 384:512], TT[g][:, 0:128], CC[g][:, 128:256], True, True, [TT[g], CC[g]], [lb])
                            tt("dve", tmpT[g][:], lb[:, 256:512], msk[:, lev, :], ALU.mult, [lb, msk], [tmpT[g]])
                            tt("pool", TT[g][:], TT[g][:], tmpT[g][:], ALU.add, [TT[g], tmpT[g]], [TT[g]])
                    for (g, h, pr, pb) in hs4:
                        lb = LB[g]
                        mm(lb[:, 0:192], TT[g][:, 0:128], X0[g][:, 256:448], True, True, [TT[g], X0[g]], [lb])
                        cp("dve" if g % 2 == 0 else "act", Zf[g][:], lb[:, 0:192], [lb], [Zf[g]])
                        if DD and h < 2:
                            dd("Zf_%d_%d" % (i, h), Zf[g], Zf[g][:], [128, 192])
                            dd("TT_%d_%d" % (i, h), TT[g], TT[g][:], [128, 256])
                    if KS_ <= 5:
                        continue
                    if own:
                        for (g, h, pr, pb) in hs4:
                            lb = LB[g]
                            mm(lb[:, 0:128], CM[pb:pb + 64, pr, 0:128], CM[pb:pb + 64, pr, 448:576], True, True, [CM], [lb])
                            mm(lb[:, 128:256], CM[pb:pb + 64, pr, 192:320], CM[pb:pb + 64, pr, 448:576], True, True, [CM], [lb])
                            tt("dve", ARBK[g][:], lb[:, 0:256], masky[:], ALU.mult, [lb, masky], [ARBK[g]])
                    lo_c = 0 if own else 128
                    for (g, h, pr, pb) in hs4:
                        lb = LB[g]; fb = FB[g]; fo = (g % 2) * 256
                        hc = slice(h * 64, (h + 1) * 64)
                        AbT = Zf[g][:, 0:64]; PTt = Zf[g][:, 64:192]
                        if own:
                            mm(fb[0:64, fo:fo + 128], AbT, ARBK[g][:, 0:128], True, False, [Zf[g], ARBK[g]], [FT[g]])
                            mm(fb[0:64, fo:fo + 128], rt[:, hc], identb[:], False, True, [rt, identb], [FT[g]])
                        mm(fb[0:64, fo + 128:fo + 192], AbT, btp[:, hc], True, True, [Zf[g], btp], [FT[g]])
                        cp("act", QG[g][:, lo_c:192], fb[0:64, fo + lo_c:fo + 192], [FT[g]], [QG[g]])
                        if own:
                            mm(lb[:, 256:384], PTt, ARBK[g][:, 0:128], True, False, [Zf[g], ARBK[g]], [lb])
                            mm(lb[:, 256:384], identb[:], ARBK[g][:, 128:256], False, True, [identb, ARBK[g]], [lb])
                        mm(lb[:, 384:448], PTt, btp[:, hc], True, False, [Zf[g], btp], [lb])
                        mm(lb[:, 384:448], identb[:], ktp[:, hc], False, True, [identb, ktp], [lb])
                        cp("dve", MYH[g][:, lo_c:192], lb[:, 256 + lo_c:448], [lb], [MYH[g]])
                        if DD and h < 2:
                            dd("QG_%d_%d" % (i, h), QG[g], QG[g][:], [64, 192])
                            dd("MYH_%d_%d" % (i, h), MYH[g], MYH[g][:], [128, 192])
                            if own:
                                dd("ARBK_%d_%d" % (i, h), ARBK[g], ARBK[g][:], [128, 256])
                    if KS_ <= 6:
                        continue
                    for (g, h, pr, pb) in hs4:
                        fb = FB[g]; fo = (g % 2) * 256
                        hc = slice(h * 64, (h + 1) * 64)
                        STc = ST[h][:, cur[h], :]; STn = ST[h][:, 1 - cur[h], :]
                        if own:
                            yo = B6[:, (h % 8) * 64:(h % 8) * 64 + 64]
                            mm(yo, QG[g][:, 0:128], STb[h][:], True, False, [QG[g], STb[h]], [B6])
                            mm(yo, MYH[g][:, 0:128], vb[:, hc], False, True, [MYH[g], vb], [B6])
                        sreg = fb[0:64, fo + 192:fo + 256]
                        mm(sreg, QG[g][:, 128:192], STb[h][:], True, False, [QG[g], STb[h]], [FT[g]])
                        mm(sreg, MYH[g][:, 128:192], vb[:, hc], False, True, [MYH[g], vb], [FT[g]])
                        act(tmpS[g][:], STc, AF.Copy, [ST[h], WC], [tmpS[g]], scale=WC[:, h:h + 1])
                        act(tmpP[g][:], sreg, AF.Copy, [FT[g]], [tmpP[g]])
                        tt("pool", STn, tmpP[g][:], tmpS[g][:], ALU.add, [tmpP[g], tmpS[g]], [ST[h]])
                        cp("pool", STb[h][:], STn, [ST[h]], [STb[h]])
                        if DD and h < 2:
                            dd("ST_%d_%d" % (i, h), ST[h], STn, [64, 64])
                        cur[h] ^= 1
                    if own and grp in (1, 3):
                        hs = slice((grp // 2) * 512, (grp // 2) * 512 + 512)
                        if d == 0:
                            cp("act", ysb[:, hs], B6[:], [B6], [ysb])
                        else:
                            tt("dve", ysb[:, hs], B6[:], ya[:, hs], ALU.add, [B6, ya], [ysb])
                if not own:
                    continue
                if d == 0:
                    if DD:
                        dd("ysb_%d" % i, ysb, ysb[:], [128, 1024])
                    dma("sp", YA[(i - 2) * 128:(i - 1) * 128, :], ysb[:], reads=[ysb], writes=[YAb])
                    continue
                if "dbg_y" in dbg:
                    dma("sp", dbg["dbg_y"][(i - 2) * 128:(i - 1) * 128, :], ysb[:], reads=[ysb])
                y3 = ysb[:].rearrange("p (a b) -> p a b", b=64)
                red("dve", s1[:], y3, [ysb], [s1])
                tt("pool", t1[:], ysb[:], ysb[:], ALU.mult, [ysb], [t1])
                red("dve", s2[:], t1[:].rearrange("p (a b) -> p a b", b=64), [t1], [s2])
                ts("dve", mean[:], s1[:], 1.0 / 64, None, ALU.mult, None, [s1], [mean])
                tt("dve", m2[:], mean[:], mean[:], ALU.mult, [mean], [m2])
                stt("dve", s2[:], s2[:], 1.0 / 64, m2[:], ALU.mult, ALU.subtract, [s2, m2], [s2])
                act(s2[:], s2[:], AF.Sqrt, [s2], [s2], bias=GN_EPS)
                recip(s2[:], s2[:], [s2], [s2])
                tt("dve", y3, y3, mean[:].unsqueeze(2).to_broadcast([128, 16, 64]), ALU.subtract, [ysb, mean], [ysb])
                tt("dve", y3, y3, s2[:].unsqueeze(2).to_broadcast([128, 16, 64]), ALU.mult, [ysb, s2], [ysb])
                tt("pool", ysb[:], ysb[:], lngbc[:], ALU.mult, [ysb, lngbc], [ysb])
                tt("pool", ysb[:], ysb[:], lnbbc[:], ALU.add, [ysb, lnbbc], [ysb])
                tt("dve", t1[:].rearrange("p (a b) -> p a b", b=64), sh[:, 2048:3072].rearrange("p (a b) -> p a b", b=64),
                   bon[:].unsqueeze(2).to_broadcast([128, 16, 64]), ALU.mult, [sh, bon], [t1])
                tt("dve", ysb[:], ysb[:], t1[:], ALU.add, [ysb, t1], [ysb])
                if "dbg_rw" in dbg:
                    dma("sp", dbg["dbg_rw"][(i - 2) * 128:(i - 1) * 128, :], ysb[:], reads=[ysb])
                act(zr[:], zr[:], AF.Silu, [zr], [zr])
                tt("pool", mixb[:], ysb[:], zr[:], ALU.mult, [ysb, zr], [mixb])
                for cch in range(8):
                    tr(psT[:, cch * 128:(cch + 1) * 128], mixb[:, cch * 128:(cch + 1) * 128], identb[:], [mixb, identb], [psT])
                cp("act", mtT[:], psT[:].rearrange("p (a b) -> p a b", b=128), [psT], [mtT])
                dma("sp", MT[0:8, :, (i - 2) * 128:(i - 1) * 128].rearrange("c p t -> p c t"), mtT[:], reads=[mtT], writes=[MTb])
        S.barrier()

    if stop_after >= 2:
        rwkv_sweep(0)
    if stop_after >= 3:
        rwkv_sweep(1)

    def rope_apply(eng2, t3, rot3, rp, nh, reads_t, T_t, T_rot, T_rp):
        t5 = t3.rearrange("p h (a b c) -> p h a b c", a=2, b=2)
        r5 = rot3.rearrange("p h (a b c) -> p h a b c", a=2, b=2)
        cp("pool", r5[:, :, :, 0, :], t5[:, :, :, 1, :], [T_t], [T_rot])
        cp("pool", r5[:, :, :, 1, :], t5[:, :, :, 0, :], [T_t], [T_rot])
        cosb = rp[:, 0:64].unsqueeze(1).to_broadcast([128, nh, 64])
        sinb = rp[:, 64:128].unsqueeze(1).to_broadcast([128, nh, 64])
        tt("dve", t3, t3, cosb, ALU.mult, [T_t, T_rp], [T_t])
        tt("dve", rot3, rot3, sinb, ALU.mult, [T_rot, T_rp], [T_rot])
        tt("dve", t3, t3, rot3, ALU.add, [T_t, T_rot], [T_t])

    def attention():
        with contextlib.ExitStack() as st:
            kgbc = sb(st, "kgbc", [128, 64]); bcast_load("sp", kgbc, kg[0:1, :])
            qgbc = sb(st, "qgbc", [128, 64]); bcast_load("sp", qgbc, qg[0:1, :])
            ts("dve", qgbc[:], qgbc[:], 0.125, None, ALU.mult, None, [qgbc], [qgbc])
            esk = sb(st, "esk", [128, 16]); bcast_load("sp", esk, sink[0:1, :])
            act(esk[:], esk[:], AF.Exp, [esk], [esk])
            amask = sb(st, "amask", [128, 2, 128], BF16)
            dma("pool", amask[:], c_amask.rearrange("a k q -> k a q"), writes=[amask])
            KT = sb(st, "KT", [64, 19, 4, 128], BF16)
            V1 = sb(st, "V1", [128, 19, 4, 65], BF16)
            memset("pool", V1[:], 1.0, [V1])
            kv = sb(st, "kv", [128, 512]); rot = sb(st, "rot", [128, 1024]); rp = sb(st, "rp", [128, 128])
            sq = sb(st, "sq", [128, 1024]); ss = sb(st, "ssq_a", [128, 16])
            kb = sb(st, "kb", [128, 256], BF16)
            q = sb(st, "q", [128, 1024]); za = sb(st, "za", [128, 1024]); qb = sb(st, "qb", [128, 1024], BF16)
            QT = sb(st, "QT", [64, 16, 128], BF16)
            PTs = [sb(st, "PTs%d" % k, [128, 512], BF16) for k in range(5)]
            att = sb(st, "att", [128, 1024]); den = sb(st, "den", [128, 16])
            mixb = sb(st, "mixb_a", [128, 1024], BF16); mtT = sb(st, "mtT_a", [128, 8, 128], BF16)
            psT = ps(st, "psT_a", [128, 1024], BF16)
            Sps = [ps(st, "Sps%d" % k, [128, 512]) for k in range(2)]
            Og = [ps(st, "Og%d" % k, [128, 512]) for k in range(4)]
            for n in range(19):
                i = n
                dma("sp", kv[:], P[i * 128:(i + 1) * 128, C_KA:C_KA + 512], reads=[Ptile[i]], writes=[kv])
                k3 = kv[:, 0:256].rearrange("p (h c) -> p h c", c=64)
                tt("dve", sq[:, 0:256], kv[:, 0:256], kv[:, 0:256], ALU.mult, [kv], [sq])
                red("dve", ss[:, 0:4], sq[:, 0:256].rearrange("p (h c) -> p h c", c=64), [sq], [ss])
                act(ss[:, 0:4], ss[:, 0:4], AF.Sqrt, [ss], [ss], scale=1.0 / 64, bias=1e-6)
                recip(ss[:, 0:4], ss[:, 0:4], [ss], [ss])
                tt("dve", k3, k3, ss[:, 0:4].unsqueeze(2).to_broadcast([128, 4, 64]), ALU.mult, [kv, ss], [kv])
                tt("dve", k3, k3, kgbc[:].unsqueeze(1).to_broadcast([128, 4, 64]), ALU.mult, [kv, kgbc], [kv])
                if i >= 2:
                    dma("sp", rp[:], rope[(i - 2) * 128:(i - 1) * 128, :], writes=[rp])
                    rope_apply(None, k3, rot[:, 0:256].rearrange("p (h c) -> p h c", c=64), rp, 4, None, kv, rot, rp)
                cp("act", kb[:], kv[:, 0:256], [kv], [kb])
                for g in range(4):
                    tr(psT[0:64, g * 128:(g + 1) * 128], kb[:, g * 64:(g + 1) * 64], identb[:], [kb, identb], [psT])
                cp("dve", KT[:, n, :, :], psT[0:64, 0:512].rearrange("p (g t) -> p g t", t=128), [psT], [KT])
                cp("act", V1[:, n, :, 0:64], kv[:, 256:512].rearrange("p (h c) -> p h c", c=64), [kv], [V1])
            sk = 0
            for i in range(OWN0, OWN1):
                r0 = i * 128
                dma("sp", q[:], P[r0:r0 + 128, C_Q:C_Q + 1024], reads=[Ptile[i]], writes=[q])
                dma("sp", za[:], P[r0:r0 + 128, C_ZA:C_ZA + 1024], reads=[Ptile[i]], writes=[za])
                dma("sp", rp[:], rope[(i - 2) * 128:(i - 1) * 128, :], writes=[rp])
                q3 = q[:].rearrange("p (h c) -> p h c", c=64)
                tt("pool", sq[:], q[:], q[:], ALU.mult, [q], [sq])
                red("dve", ss[:], sq[:].rearrange("p (h c) -> p h c", c=64), [sq], [ss])
                act(ss[:], ss[:], AF.Sqrt, [ss], [ss], scale=1.0 / 64, bias=1e-6)
                recip(ss[:], ss[:], [ss], [ss])
                tt("dve", q3, q3, ss[:].unsqueeze(2).to_broadcast([128, 16, 64]), ALU.mult, [q, ss], [q])
                tt("dve", q3, q3, qgbc[:].unsqueeze(1).to_broadcast([128, 16, 64]), ALU.mult, [q, qgbc], [q])
                rope_apply(None, q3, rot[:].rearrange("p (h c) -> p h c", c=64), rp, 16, None, q, rot, rp)
                cp("act", qb[:], q[:], [q], [qb])
                for half in range(2):
                    for hx in range(8):
                        hd = half * 8 + hx
                        tr(psT[0:64, hx * 128:(hx + 1) * 128], qb[:, hd * 64:(hd + 1) * 64], identb[:], [qb, identb], [psT])
                    cp("dve", QT[:, half * 8:(half + 1) * 8, :], psT[0:64, :].rearrange("p (g t) -> p g t", t=128), [psT], [QT])
                for g in range(4):
                    keyt = ([i - 1] if i > OWN0 else []) + [i, i + 1, 0, 1]
                    for ki, kt_ in enumerate(keyt):
                        sp_ = Sps[sk % 2]; sk += 1
                        pt_ = PTs[ki]
                        mm(sp_[:], KT[:, kt_, g, :], QT[:, 4 * g:4 * g + 4, :].rearrange("p a b -> p (a b)"), True, True, [KT, QT], [sp_])
                        act(pt_[:], sp_[:], AF.Exp, [sp_], [pt_])
                        is_prev = (i > OWN0 and ki == 0); is_next = (ki == (2 if i > OWN0 else 1))
                        if is_prev or is_next:
                            mi = 0 if is_prev else 1
                            for hx in range(4):
                                tt("pool", pt_[:, hx * 128:(hx + 1) * 128], pt_[:, hx * 128:(hx + 1) * 128], amask[:, mi, :], ALU.mult, [pt_, amask], [pt_])
                    for hx in range(4):
                        for ki, kt_ in enumerate(keyt):
                            mm(Og[g][:, hx * 65:(hx + 1) * 65], PTs[ki][:, hx * 128:(hx + 1) * 128], V1[:, kt_, g, :],
                               ki == 0, ki == len(keyt) - 1, [PTs[ki], V1], [Og[g]])
                for g in range(4):
                    o3 = Og[g][:, 0:260].rearrange("p (h c) -> p h c", c=65)
                    tt("dve", den[:, 4 * g:4 * g + 4], o3[:, :, 64], esk[:, 4 * g:4 * g + 4], ALU.add, [Og[g], esk], [den])
                    recip(den[:, 4 * g:4 * g + 4], den[:, 4 * g:4 * g + 4], [den], [den])
                    tt("dve", att[:, g * 256:(g + 1) * 256].rearrange("p (h c) -> p h c", c=64), o3[:, :, 0:64],
                       den[:, 4 * g:4 * g + 4].unsqueeze(2).to_broadcast([128, 4, 64]), ALU.mult, [Og[g], den], [att])
                if "dbg_att" in dbg:
                    dma("sp", dbg["dbg_att"][(i - 2) * 128:(i - 1) * 128, :], att[:], reads=[att])
                act(za[:], za[:], AF.Silu, [za], [za])
                tt("pool", mixb[:], att[:], za[:], ALU.mult, [att, za], [mixb])
                for cch in range(8):
                    tr(psT[:, cch * 128:(cch + 1) * 128], mixb[:, cch * 128:(cch + 1) * 128], identb[:], [mixb, identb], [psT])
                cp("act", mtT[:], psT[:].rearrange("p (a b) -> p a b", b=128), [psT], [mtT])
                dma("sp", MT[8:16, :, (i - 2) * 128:(i - 1) * 128].rearrange("c p t -> p c t"), mtT[:], reads=[mtT], writes=[MTb])
        S.barrier()

    def outproj():
        with contextlib.ExitStack() as st:
            wo = sb(st, "wo", [128, 16, D], BF16)
            wov = w_out.rearrange("(j p) n -> p j n", p=128)
            for k4 in range(4):
                dma("pool", wo[:, k4 * 4:(k4 + 1) * 4, :], wov[:, k4 * 4:(k4 + 1) * 4, :], writes=[wo])
            gate = sb(st, "gate", [128, D]); dma("sp", gate[:], GATE[:, :], reads=[GATEb], writes=[gate])
            mt = [sb(st, "mt%d" % k, [128, 16, 128], BF16) for k in range(2)]
            xo = [sb(st, "xo%d" % k, [128, D]) for k in range(2)]
            ot = [sb(st, "ot%d" % k, [128, D]) for k in range(2)]
            pp = [ps(st, "ppo%d" % k, [128, 512]) for k in range(4)]
            for tl in range(16):
                i = tl + 2
                m_ = mt[tl % 2]; x_ = xo[tl % 2]; o_ = ot[tl % 2]
                dma("sp", m_[:], MT[:, :, tl * 128:(tl + 1) * 128].rearrange("c p t -> p c t"), reads=[MTb], writes=[m_])
                dma("act", x_[:], xin[i * 128:(i + 1) * 128, :], writes=[x_])
                for nb in range(4):
                    ns = slice(nb * 512, (nb + 1) * 512)
                    p_ = pp[nb]
                    for cc_ in range(16):
                        mm(p_[:], m_[:, cc_, :], wo[:, cc_, ns], cc_ == 0, cc_ == 15, [m_, wo], [p_])
                    tt("dve", o_[:, ns], p_[:], gate[:, ns], ALU.mult, [p_, gate], [o_])
                    tt("pool", o_[:, ns], o_[:, ns], x_[:, ns], ALU.add, [o_, x_], [o_])
                dma("sp", out[tl * 128:(tl + 1) * 128, :], o_[:], reads=[o_])

    if stop_after >= 4:
        attention()
    if stop_after >= 5:
        outproj()

    S.emit(final_wait_ops=[i for i, o in enumerate(S.ops) if o.is_dma])
    top.close()
    return nc


def host_consts():
    c = {}
    c["c_ident"] = np.eye(128, dtype=np.float32)
    i = np.arange(128)
    triA_incl = (i[:, None] <= i[None, :]).astype(np.float32)
    triA_suf = (i[:, None] > i[None, :]).astype(np.float32)
    triB_incl = (i[:, None] >= i[None, :]).astype(np.float32)
    triB_suf = (i[:, None] < i[None, :]).astype(np.float32)
    c["c_tri"] = (-CDEC * np.stack([triA_incl, triA_suf, triB_incl, triB_suf])).astype(np.float32)
    mx = np.zeros((2, 128, 448), np.float32); my = np.zeros((2, 128, 256), np.float32)
    for d in range(2):
        prec = (i[:, None] < i[None, :]) if d == 0 else (i[:, None] > i[None, :])
        preceq = prec | np.eye(128, dtype=bool)
        mx[d, :, 0:128] = prec; mx[d, :, 128:256] = prec.T; mx[d, :, 256:320] = 1.0; mx[d, :, 320:448] = prec.T
        my[d, :, 0:128] = preceq; my[d, :, 128:256] = preceq
    c["c_maskx"] = mx; c["c_masky"] = my
    cms = np.zeros((2, 7, 128, 256), np.float32)
    for d in range(2):
        prec = (i[:, None] < i[None, :]) if d == 0 else (i[:, None] > i[None, :])
        for s_ in range(7):
            m = prec & ((i[:, None] >> (s_ + 1)) == (i[None, :] >> (s_ + 1))) & ((i[:, None] >> s_) != (i[None, :] >> s_))
            cms[d, s_, :, 0:128] = m; cms[d, s_, :, 128:256] = m.T
    c["c_ms"] = cms
    sh = np.zeros((128, 128), np.float32)
    sh[i, i] = -1.0; sh[i[:-1], i[:-1] + 1] = 0.5; sh[i[1:], i[1:] - 1] = 0.5
    c["c_sh"] = sh
    e = np.zeros((2, 128), np.float32); e[0, 0] = 0.5; e[1, 127] = 0.5
    c["c_e"] = e
    c["c_i64"] = np.concatenate([np.eye(64, dtype=np.float32)] * 2, axis=0)
    am = np.zeros((2, 128, 128), np.float32)
    am[0] = (i[:, None] >= i[None, :]); am[1] = (i[:, None] <= i[None, :])
    c["c_amask"] = am
    return c


def rope_tables(pos):
    pos = np.asarray(pos)
    row = (pos // 64).astype(np.float32); col = (pos % 64).astype(np.float32)
    half = 16
    inv = (10000.0 ** (-np.arange(half, dtype=np.float32) / half)).astype(np.float32)
    ar = row[:, None] * inv; ac = col[:, None] * inv
    ang = np.concatenate([ar, ar, ac, ac], axis=-1)
    cos = np.cos(ang); sin = np.sin(ang)
    sgn = np.concatenate([-np.ones(16), np.ones(16), -np.ones(16), np.ones(16)]).astype(np.float32)
    return np.concatenate([cos, sin * sgn], axis=-1).astype(np.float32)


def core_inputs(inp, b, h, consts):
    f = np.ascontiguousarray
    x = inp["x"][b]; ctx = inp["ctx"][b]
    if h == 1:
        x = x[::-1]; ctx = ctx[::-1]
    m = {}
    m["xin"] = f(np.concatenate([ctx, x], axis=0))
    colf = lambda v: v.reshape(-1, 128).T
    m["cc"] = f(np.concatenate([colf(inp["c"][b]), colf(inp["c_ctx"])], axis=1))
    m["w_ada"] = f(inp["w_ada"][0])
    m["bcol"] = f(np.repeat(colf(inp["b_ada"][0]), 2, axis=1))
    m["ngcol"] = f(colf(inp["norm_g"][0]))
    dA, dB = (0, 1) if h == 0 else (1, 0)
    lw = lambda d: np.arange(3072 + 64 * d, 3072 + 64 * d + 64)
    la = lambda d: np.arange(3200 + 64 * d, 3200 + 64 * d + 64)
    perm = np.concatenate([np.arange(0, 3072), lw(dA), lw(dB), la(dA), la(dB), np.arange(3328, 4352), np.arange(4352, 5376),
                           np.arange(5888, 6912), np.arange(5376, 5632), np.arange(5632, 5888)])
    m["w_in"] = f(inp["w_in"][0][:, perm])
    m["mu"] = f(inp["mu_shift"][0][perm[:SHC]][None, :])
    sel = [dA, dB]
    m["w0"] = f(inp["w0"][0][sel]); m["w2"] = f(inp["w2"][0][sel]); m["a0"] = f(inp["a0"][0][sel]); m["a2"] = f(inp["a2"][0][sel])
    for k in ("k_k", "k_a"):
        m[k] = f(inp[k][0][None, :])
    m["r_k"] = f(inp["r_k"][0].reshape(1, 1024))
    m["ln_g"] = f(inp["ln_x_g"][0][None, :]); m["ln_b"] = f(inp["ln_x_b"][0][None, :])
    m["qg"] = f(inp["q_norm_g"][0][None, :]); m["kg"] = f(inp["k_norm_g"][0][None, :]); m["sink"] = f(inp["sink"][0][None, :])
    m["w_out"] = f(inp["w_out"][0])
    loc = np.arange(17 * 128)
    pos = loc if h == 0 else 4095 - loc
    m["rope"] = rope_tables(pos)
    m.update(consts)
    return m


def kernel(**inputs):
    inp = {k: np.asarray(v) for k, v in inputs.items()}
    consts = host_consts()
    nc = build()
    in_maps = [core_inputs(inp, c // 2, c % 2, consts) for c in range(8)]
    res = run_bass_kernel_spmd(nc, in_maps, core_ids=list(range(8)))
    out = np.empty((4, 4096, D), np.float32)
    for c in range(8):
        b, h = c // 2, c % 2
        o = res.results[c]["out"]
        if h == 0:
            out[b, :2048] = o
        else:
            out[b, 2048:] = o[::-1]
    return out
```

```python
import contextlib
import numpy as np
import concourse.bass as bass
import concourse.mybir as mybir
from concourse.bass_utils import run_bass_kernel_spmd

F32 = mybir.dt.float32
BF16 = mybir.dt.bfloat16
U8 = mybir.dt.uint8
AF = mybir.ActivationFunctionType
ALU = mybir.AluOpType
AX = mybir.AxisListType

ENGS = ["pe", "act", "dve", "pool", "sp"]
CDEC = 0.6065306597126334

D = 2048
NT = 34
OWN0, OWN1 = 2, 18
TL = NT * 128
PC = 6912
C_R, C_K, C_V, C_LO, C_ZR, C_Q, C_ZA, C_KA, C_VA = 0, 1024, 2048, 3072, 3328, 4352, 5376, 6400, 6656
SHC = 3328
DEBUG = {}


class Buf:
    __slots__ = ("last_w", "readers")

    def __init__(self):
        self.last_w = None
        self.readers = []


class T:
    def __init__(self, t):
        self.t = t
        self.b = Buf()

    def __getitem__(self, k):
        return self.t[k]


class Op:
    __slots__ = ("eng", "fn", "deps", "is_dma", "has_dep", "mile", "dsem", "dval")


class Sched:
    def __init__(self, nc, n_dma_sems=32):
        self.nc = nc
        self.ops = []
        self.n_dma_sems = n_dma_sems
        self.last_on = {e: None for e in ENGS}
        self.dmas_since_barrier = []

    def op(self, eng, fn, reads=(), writes=(), dma=False, extra_deps=()):
        o = Op()
        o.eng = eng; o.fn = fn; o.is_dma = dma; o.has_dep = False; o.mile = None; o.dsem = None; o.dval = None
        o.deps = set(extra_deps)
        oid = len(self.ops)
        for t in reads:
            b = t.b
            if b.last_w is not None:
                o.deps.add(b.last_w)
        for t in writes:
            b = t.b
            if b.last_w is not None:
                o.deps.add(b.last_w)
            o.deps.update(b.readers)
        for t in reads:
            t.b.readers.append(oid)
        for t in writes:
            t.b.last_w = oid
            t.b.readers = []
        o.deps.discard(oid)
        self.ops.append(o)
        self.last_on[eng] = oid
        if dma:
            self.dmas_since_barrier.append(oid)
        return oid

    def barrier(self):
        deps = [v for v in self.last_on.values() if v is not None] + list(self.dmas_since_barrier)
        self.dmas_since_barrier = []
        for e in ENGS:
            self.op(e, None, extra_deps=deps)

    def emit(self, final_wait_ops=()):
        nc = self.nc
        ops = self.ops
        for o in ops:
            nd = set()
            for d in o.deps:
                p = ops[d]
                if p.fn is None:
                    if p.eng == o.eng:
                        continue
                    nd.update(p.deps)
                    continue
                if p.eng == o.eng and not p.is_dma and o.eng == "pe" and not o.is_dma:
                    continue
                nd.add(d)
            o.deps = nd
        for o in ops:
            for d in o.deps:
                ops[d].has_dep = True
        for d in final_wait_ops:
            ops[d].has_dep = True
        cnt = {e: 0 for e in ENGS}
        dma_i = 0
        dma_cnt = [0] * self.n_dma_sems
        dma_prev = [None] * self.n_dma_sems
        for i, o in enumerate(ops):
            if o.fn is None:
                continue
            if o.is_dma:
                s = dma_i % self.n_dma_sems
                dma_i += 1
                if dma_prev[s] is not None:
                    o.deps.add(dma_prev[s])
                dma_prev[s] = i
                dma_cnt[s] += 16
                o.dsem = s
                o.dval = dma_cnt[s]
            elif o.has_dep:
                cnt[o.eng] += 1
                o.mile = cnt[o.eng]
        streams = {e: [] for e in ENGS}
        for i, o in enumerate(ops):
            streams[o.eng].append(i)
        with contextlib.ExitStack() as st:
            esem = {e: st.enter_context(nc.semaphore("s_" + e)) for e in ENGS}
            dsem = [st.enter_context(nc.semaphore("d_%d" % k)) for k in range(self.n_dma_sems)]
            block = st.enter_context(nc.Block())

            def run(eng_name, engine):
                seen = {}
                for i in streams[eng_name]:
                    o = ops[i]
                    need = {}
                    for d in o.deps:
                        p = ops[d]
                        if p.is_dma:
                            key = ("d", p.dsem); val = p.dval
                        else:
                            key = ("e", p.eng); val = p.mile
                        if need.get(key, 0) < val:
                            need[key] = val
                    for key, val in need.items():
                        if seen.get(key, 0) >= val:
                            continue
                        seen[key] = val
                        sem = dsem[key[1]] if key[0] == "d" else esem[key[1]]
                        engine.wait_ge(sem, val)
                    if o.fn is None:
                        continue
                    ins = o.fn(engine)
                    if o.is_dma:
                        ins.then_inc(dsem[o.dsem], 16)
                    elif o.mile is not None:
                        ins.then_inc(esem[o.eng], 1)
                if eng_name == "sp":
                    fin = {}
                    for d in final_wait_ops:
                        p = ops[d]
                        fin[p.dsem] = max(fin.get(p.dsem, 0), p.dval)
                    for k_, v_ in fin.items():
                        engine.wait_ge(dsem[k_], v_)

            block.tensor(lambda e: run("pe", e))
            block.scalar(lambda e: run("act", e))
            block.vector(lambda e: run("dve", e))
            block.gpsimd(lambda e: run("pool", e))
            block.sync(lambda e: run("sp", e))


def build(debug_names=(), stop_after=99):
    nc = bass.Bass("TRN2", target_bir_lowering=False)

    def din(name, shape, dt=F32):
        return nc.dram_tensor(name, list(shape), dt, kind="ExternalInput").ap()

    xin = din("xin", [TL, D])
    cc = din("cc", [128, 32])
    w_ada = din("w_ada", [D, 3 * D])
    bcol = din("bcol", [128, 96])
    ngcol = din("ngcol", [128, 16])
    w_in = din("w_in", [D, PC])
    mu = din("mu", [1, SHC])
    w0 = din("w0", [2, 1024]); w2 = din("w2", [2, 64, 1024]); a0 = din("a0", [2, 1024]); a2 = din("a2", [2, 64, 1024])
    k_k = din("k_k", [1, 1024]); k_a = din("k_a", [1, 1024]); r_k = din("r_k", [1, 1024])
    ln_g = din("ln_g", [1, 1024]); ln_b = din("ln_b", [1, 1024])
    qg = din("qg", [1, 64]); kg = din("kg", [1, 64]); sink = din("sink", [1, 16])
    w_out = din("w_out", [D, D])
    rope = din("rope", [17 * 128, 128])
    c_ident = din("c_ident", [128, 128])
    c_tri = din("c_tri", [4, 128, 128])
    c_maskx = din("c_maskx", [2, 128, 448])
    c_masky = din("c_masky", [2, 128, 256])
    c_ms = din("c_ms", [2, 7, 128, 256])
    c_sh = din("c_sh", [128, 128])
    c_e = din("c_e", [2, 128])
    c_i64 = din("c_i64", [128, 64])
    c_amask = din("c_amask", [2, 128, 128])
    out = nc.dram_tensor("out", [2048, D], F32, kind="ExternalOutput").ap()

    def scratch(name, shape, dt):
        kind = "ExternalOutput" if name in debug_names else None
        if kind:
            return nc.dram_tensor(name, list(shape), dt, kind=kind).ap()
        return nc.dram_tensor(name, list(shape), dt).ap()

    P = scratch("P", [TL, PC], F32)
    YA = scratch("YA", [2048, 1024], F32)
    MT = scratch("MT", [16, 128, 2048], BF16)
    dbg = {n: scratch(n, shp, F32) for n, shp in [("dbg_bc", [5, 128, D]), ("dbg_sh", [TL, SHC]), ("dbg_y", [2048, 1024]),
                                                  ("dbg_rw", [2048, 1024]), ("dbg_att", [2048, 1024])] if n in debug_names}
    Pb = T(None); YAb = T(None); MTb = T(None)
    Ptile = [T(None) for _ in range(NT)]

    S = Sched(nc)
    top = contextlib.ExitStack()

    uid = [0]

    def sb(st, name, shape, dt=F32):
        uid[0] += 1
        return T(st.enter_context(nc.sbuf_tensor("%s_%d" % (name, uid[0]), list(shape), dt)))

    def ps(st, name, shape, dt=F32):
        uid[0] += 1
        return T(st.enter_context(nc.psum_tensor("%s_%d" % (name, uid[0]), list(shape), dt)))

    def dma(eng, out_ap, in_ap, reads=(), writes=()):
        return S.op(eng, lambda e: e.dma_start(out=out_ap, in_=in_ap), reads=reads, writes=writes, dma=True)

    def mm(o, lhsT, rhs, start, stop, reads, writes):
        return S.op("pe", lambda e: e.matmul(o, lhsT=lhsT, rhs=rhs, start=start, stop=stop), reads=reads, writes=writes)

    def tr(o, in_, ident, reads, writes):
        return S.op("pe", lambda e: e.transpose(out=o, in_=in_, identity=ident), reads=reads, writes=writes)

    def act(o, in_, func, reads, writes, scale=1.0, bias=0.0, accum=None, eng="act"):
        if accum is None:
            return S.op(eng, lambda e: e.activation(out=o, in_=in_, func=func, scale=scale, bias=bias), reads=reads, writes=writes)
        return S.op(eng, lambda e: e.activation(out=o, in_=in_, func=func, scale=scale, bias=bias, accum_out=accum),
                    reads=reads, writes=writes)

    def tt(eng, o, a, b, op, reads, writes):
        return S.op(eng, lambda e: e.tensor_tensor(out=o, in0=a, in1=b, op=op), reads=reads, writes=writes)

    def ts(eng, o, a, s1, s2, op0, op1, reads, writes):
        if s2 is None:
            return S.op(eng, lambda e: e.tensor_scalar(out=o, in0=a, scalar1=s1, scalar2=None, op0=op0), reads=reads, writes=writes)
        return S.op(eng, lambda e: e.tensor_scalar(out=o, in0=a, scalar1=s1, scalar2=s2, op0=op0, op1=op1), reads=reads, writes=writes)

    def stt(eng, o, a, s, b, op0, op1, reads, writes):
        return S.op(eng, lambda e: e.scalar_tensor_tensor(out=o, in0=a, scalar=s, in1=b, op0=op0, op1=op1), reads=reads, writes=writes)

    def cp(eng, o, a, reads, writes):
        if eng == "act":
            return S.op("act", lambda e: e.activation(out=o, in_=a, func=AF.Copy), reads=reads, writes=writes)
        return S.op(eng, lambda e: e.tensor_copy(out=o, in_=a), reads=reads, writes=writes)

    def red(eng, o, a, reads, writes, op=ALU.add):
        return S.op(eng, lambda e: e.tensor_reduce(out=o, in_=a, axis=AX.X, op=op), reads=reads, writes=writes)

    def recip(o, a, reads, writes):
        return S.op("dve", lambda e: e.reciprocal(out=o, in_=a), reads=reads, writes=writes)

    def memset(eng, o, val, writes):
        return S.op(eng, lambda e: e.memset(o, val), writes=writes)

    ddn = [0]

    def dd(name, T_, ap, shape):
        if "dd" not in debug_names:
            return
        t = nc.dram_tensor("dd_" + name, list(shape), F32, kind="ExternalOutput").ap()
        dma("pool", t, ap, reads=[T_])

    def bc_row(ap_row, n):
        return ap_row.partition_broadcast(128) if hasattr(ap_row, "partition_broadcast") else ap_row

    identf = sb(top, "identf", [128, 128]); identb = sb(top, "identb", [128, 128], BF16)
    onesf = sb(top, "onesf", [128, 128])
    bonA = sb(top, "bonA", [128, 16, 16])
    st01 = contextlib.ExitStack()
    Abc = [sb(st01, "Abc%d" % v, [128, D]) for v in range(2)]
    Bbc = [sb(st01, "Bbc%d" % v, [128, D]) for v in range(2)]
    gatebc = sb(st01, "gatebc", [128, D])
    GATE = scratch("GATE", [128, D], F32); GATEb = T(None)
    dma("sp", identf[:], c_ident[:, :], writes=[identf])
    cp("dve", identb[:], identf[:], [identf], [identb])
    memset("dve", onesf[:], 1.0, [onesf])

    with contextlib.ExitStack() as st:
        cct = sb(st, "cct", [128, 32]); sc = sb(st, "sc", [128, 32])
        bct = sb(st, "bct", [128, 96]); ngt = sb(st, "ngt", [128, 16])
        modc = sb(st, "modc", [128, 96]); acol = sb(st, "acol", [128, 2, 16])
        wa = [sb(st, "wa%d" % i, [128, 16, 512]) for i in range(2)]
        dg = [sb(st, "dg%d" % i, [128, 512]) for i in range(2)]
        psA = ps(st, "psA", [128, 96])
        psB = [ps(st, "psB%d" % i, [128, 512]) for i in range(2)]
        dma("sp", cct[:], cc[:, :], writes=[cct]); dma("sp", bct[:], bcol[:, :], writes=[bct]); dma("sp", ngt[:], ngcol[:, :], writes=[ngt])
        act(sc[:], cct[:], AF.Silu, [cct], [sc])
        sc3 = sc[:].rearrange("p (v j) -> p v j", j=16)
        wav = w_ada.rearrange("(j p) n -> p j n", p=128)
        for g in range(12):
            w = wa[g % 2]
            dma("sp" if g % 2 == 0 else "pool", w[:], wav[:, :, g * 512:(g + 1) * 512], writes=[w])
            for m4 in range(4):
                m = g * 4 + m4
                for j in range(16):
                    mm(psA[:, 2 * m:2 * m + 2], w[:, j, m4 * 128:(m4 + 1) * 128], sc3[:, :, j], j == 0, j == 15, [w, sc], [psA])
        tt("dve", modc[:], psA[:], bct[:], ALU.add, [psA, bct], [modc])
        mc3 = modc[:].rearrange("p (m v) -> p v m", v=2)
        for v in range(2):
            stt("dve", acol[:, v, :], mc3[:, v, 16:32], 1.0, ngt[:], ALU.add, ALU.mult, [modc, ngt], [acol])
        jobs = [(acol, lambda v, m: acol[:, v, m:m + 1], Abc[0], 0), (acol, lambda v, m: acol[:, v, m:m + 1], Abc[1], 1),
                (modc, lambda v, m: mc3[:, v, m:m + 1], Bbc[0], 0), (modc, lambda v, m: mc3[:, v, m:m + 1], Bbc[1], 1),
                (modc, lambda v, m: mc3[:, v, 32 + m:33 + m], gatebc, 0)]
        k = 0
        for src, colf, dst, v in jobs:
            for m4 in range(4):
                d_ = dg[k % 2]; p_ = psB[k % 2]; k += 1
                for mi in range(4):
                    m = m4 * 4 + mi
                    ts("dve", d_[:, mi * 128:(mi + 1) * 128], identf[:], colf(v, m), None, ALU.mult, ALU.bypass, [identf, src], [d_])
                mm(p_[:], onesf[:], d_[:], True, True, [onesf, d_], [p_])
                cp("act", dst[:, m4 * 512:(m4 + 1) * 512], p_[:], [p_], [dst])
        dma("sp", GATE[:, :], gatebc[:], reads=[gatebc], writes=[GATEb])
        if "dbg_bc" in dbg:
            for i, t_ in enumerate([Abc[0], Abc[1], Bbc[0], Bbc[1], gatebc]):
                dma("sp", dbg["dbg_bc"][i], t_[:], reads=[t_])
    S.barrier()

    blocks = [(0, 512, 'r'), (512, 512, 'r'), (1024, 512, 'k'), (1536, 512, 'k'), (2048, 512, 'v'), (2560, 512, 'v'),
              (3072, 256, 'lo'), (3328, 512, 'zr'), (3840, 512, 'zr'), (4352, 512, 'q'), (4864, 512, 'q'),
              (5376, 512, 'za'), (5888, 512, 'za'), (6400, 512, 'kv')]
    tblocks = [([0, 1], {'k', 'v', 'lo', 'kv'})] + [(list(range(s, s + 4)), None) for s in (2, 6, 10, 14)] + \
              [([18, 19, 20, 21], {'r', 'k', 'v', 'lo', 'kv'})] + [(list(range(s, s + 4)), {'k', 'v', 'lo'}) for s in (22, 26, 30)]
    if stop_after >= 1:
        with contextlib.ExitStack() as st:
            xt = [sb(st, "xt%d" % i, [128, D]) for i in range(2)]
            junk = sb(st, "junk", [128, D], BF16)
            ssq = [sb(st, "ssq%d" % i, [128, 1]) for i in range(2)]
            xs = [sb(st, "xs%d" % i, [128, D]) for i in range(2)]
            xn = [sb(st, "xn%d" % i, [128, D], BF16) for i in range(2)]
            xnT = [sb(st, "xnT%d" % i, [128, 16, 512], BF16) for i in range(2)]
            wb = [sb(st, "wb%d" % i, [128, 16, 512], BF16) for i in range(3)]
            stg = [sb(st, "stg%d" % i, [128, 512]) for i in range(4)]
            pT = [ps(st, "pT%d" % i, [128, 1024], BF16) for i in range(2)]
            pp = [ps(st, "pp%d" % i, [128, 512]) for i in range(4)]
            winv = w_in.rearrange("(j p) n -> p j n", p=128)
            wi = 0; si = 0; ti = 0
            for bi, (tiles, need) in enumerate(tblocks):
                xT = xnT[bi % 2]
                for tl, i in enumerate(tiles):
                    v = 1 if i < 2 else 0
                    x_ = xt[ti % 2]; sq_ = ssq[ti % 2]; xs_ = xs[ti % 2]; xn_ = xn[ti % 2]; ti += 1
                    dma("act", x_[:], xin[i * 128:(i + 1) * 128, :], writes=[x_])
                    memset("dve", sq_[:], 0.0, [sq_])
                    act(junk[:], x_[:], AF.Square, [x_, sq_], [junk, sq_], accum=sq_[:, 0:1])
                    act(sq_[:], sq_[:], AF.Sqrt, [sq_], [sq_], scale=1.0 / D, bias=1e-6)
                    recip(sq_[:], sq_[:], [sq_], [sq_])
                    stt("dve", xs_[:], x_[:], sq_[:, 0:1], Abc[v][:], ALU.mult, ALU.mult, [x_, sq_, Abc[v]], [xs_])
                    tt("pool", xn_[:], xs_[:], Bbc[v][:], ALU.add, [xs_, Bbc[v]], [xn_])
                    for half in range(2):
                        p_ = pT[half]
                        for jj in range(8):
                            j = half * 8 + jj
                            tr(p_[:, jj * 128:(jj + 1) * 128], xn_[:, j * 128:(j + 1) * 128], identb[:], [xn_, identb], [p_])
                        cp("act" if half == 0 else "dve", xT[:, half * 8:(half + 1) * 8, tl * 128:(tl + 1) * 128],
                           p_[:].rearrange("p (j t) -> p j t", t=128), [p_], [xT])
                for (c0, cw, grp) in blocks:
                    if need is not None and grp not in need:
                        continue
                    w = wb[wi % 3]; wi += 1
                    dma("pool", w[:, :, 0:cw], winv[:, :, c0:c0 + cw], writes=[w])
                    for tl, i in enumerate(tiles):
                        p_ = pp[si % 4]; s_ = stg[si % 4]; si += 1
                        for j in range(16):
                            mm(p_[:, 0:cw], xT[:, j, tl * 128:(tl + 1) * 128], w[:, j, 0:cw], j == 0, j == 15, [xT, w], [p_])
                        cp("act" if si % 2 == 0 else "dve", s_[:, 0:cw], p_[:, 0:cw], [p_], [s_])
                        dma("sp", P[i * 128:(i + 1) * 128, c0:c0 + cw], s_[:, 0:cw], reads=[s_], writes=[Ptile[i]])
        S.barrier()

    st01.close()
    GN_EPS = 64e-5

    def bcast_load(eng, dst, src_row):
        return dma(eng, dst[:].rearrange("p (o n) -> p o n", o=1), src_row.partition_broadcast(128), writes=[dst])

    def rwkv_sweep(d):
        tiles = list(range(0, 18)) if d == 0 else [1, 0] + list(range(33, 1, -1))
        import os
        KT_ = int(os.environ.get("KTILES", "99")); KS_ = int(os.environ.get("KSTAGE", "99"))
        tiles = tiles[:KT_]
        with contextlib.ExitStack() as st:
            mubc = sb(st, "mubc", [128, SHC]); bcast_load("sp", mubc, mu[0:1, :])
            w0bc = sb(st, "w0bc", [128, 1024]); bcast_load("sp", w0bc, w0[d:d + 1, :])
            a0bc = sb(st, "a0bc", [128, 1024]); bcast_load("sp", a0bc, a0[d:d + 1, :])
            kkbc = sb(st, "kkbc", [128, 1024]); bcast_load("sp", kkbc, k_k[0:1, :])
            kabc = sb(st, "kabc", [128, 1024]); bcast_load("sp", kabc, k_a[0:1, :])
            rkbc = sb(st, "rkbc", [128, 1024]); bcast_load("sp", rkbc, r_k[0:1, :])
            if d == 1:
                lngbc = sb(st, "lngbc", [128, 1024]); bcast_load("sp", lngbc, ln_g[0:1, :])
                lnbbc = sb(st, "lnbbc", [128, 1024]); bcast_load("sp", lnbbc, ln_b[0:1, :])
            WAb = sb(st, "WAb", [128, 1024], BF16)
            dma("pool", WAb[0:64, :], w2[d], writes=[WAb]); dma("pool", WAb[64:128, :], a2[d], writes=[WAb])
            triI = sb(st, "triI", [128, 128]); dma("sp", triI[:], c_tri[2 * d], writes=[triI])
            triS = sb(st, "triS", [128, 128]); dma("sp", triS[:], c_tri[2 * d + 1], writes=[triS])
            negc = sb(st, "negc", [128, 1]); memset("dve", negc[:], -CDEC, [negc])
            maskx = sb(st, "maskx", [128, 448], BF16); dma("pool", maskx[:], c_maskx[d], writes=[maskx])
            masky = sb(st, "masky", [128, 256], BF16); dma("pool", masky[:], c_masky[d], writes=[masky])
            shm = sb(st, "shm", [128, 128], BF16); dma("pool", shm[:], c_sh[:, :], writes=[shm])
            e2 = sb(st, "e2", [2, 128], BF16); dma("pool", e2[:], c_e[:, :], writes=[e2])
            CM = sb(st, "CM", [128, 8, 576], BF16)
            for pr in range(8):
                dma("pool", CM[:, pr, 128:192], c_i64[:, :], writes=[CM])
            ST = [sb(st, "ST%d" % h, [64, 2, 64]) for h in range(16)]
            for h in range(16):
                memset("pool", ST[h][:], 0.0, [ST[h]])
            cur = [0] * 16
            sh = sb(st, "sh", [128, SHC]); pc16 = sb(st, "pc16", [128, SHC], BF16); nb16 = sb(st, "nb16", [2, SHC], BF16)
            lo16 = sb(st, "lo16", [128, 128], BF16); loT = sb(st, "loT", [128, 128], BF16)
            sg = sb(st, "sg", [128, 1024]); al = sb(st, "al", [128, 1024]); kk = sb(st, "kk", [128, 1024])
            bb = sb(st, "bb", [128, 1024]); kd = sb(st, "kd", [128, 1024]); t1 = sb(st, "t1", [128, 1024])
            E = [sb(st, "E%d" % k, [128, 1024]) for k in range(2)]
            ss = sb(st, "ss", [128, 16]); bon = sb(st, "bon", [128, 16]); WC = sb(st, "WC", [64, 16])
            rt, kt, bt, at, ktp, btp, vb = [sb(st, n, [128, 1024], BF16) for n in ("rt", "kt", "bt", "at", "ktp", "btp", "vb")]
            X0 = [sb(st, "X0%d" % p_, [128, 448], BF16) for p_ in range(8)]
            TT = [sb(st, "TT%d" % p_, [128, 256], BF16) for p_ in range(8)]
            CC = [sb(st, "CC%d" % p_, [128, 256], BF16) for p_ in range(8)]
            tmpT = [sb(st, "tmpT%d" % p_, [128, 256], BF16) for p_ in range(4)]
            Zf = [sb(st, "Zf%d" % p_, [128, 192], BF16) for p_ in range(8)]
            msk = sb(st, "msk", [128, 7, 256], U8); dma("pool", msk[:], c_ms[d].rearrange("s p c -> p s c"), writes=[msk])
            II = sb(st, "II", [128, 256], BF16)
            cp("pool", II[:, 0:128], identb[:], [identb], [II]); cp("pool", II[:, 128:256], identb[:], [identb], [II])
            ARBK = [sb(st, "ARBK%d" % p_, [128, 256], BF16) for p_ in range(4)]
            QG = [sb(st, "QG%d" % p_, [64, 192], BF16) for p_ in range(4)]
            STb = [sb(st, "STb%d" % h, [64, 64], BF16) for h in range(16)]
            tmpS = [sb(st, "tmpS%d" % h, [64, 64]) for h in range(4)]
            tmpP = [sb(st, "tmpP%d" % h, [64, 64]) for h in range(4)]
            for h in range(16):
                memset("pool", STb[h][:], 0.0, [STb[h]])
            MYH = [sb(st, "MYH%d" % p_, [128, 192], BF16) for p_ in range(4)]
            ysb = sb(st, "ysb", [128, 1024])
            if d == 1:
                ya = sb(st, "ya", [128, 1024]); zr = sb(st, "zr", [128, 1024])
                mixb = sb(st, "mixb", [128, 1024], BF16); mtT = sb(st, "mtT", [128, 8, 128], BF16)
                s1 = sb(st, "s1", [128, 16]); s2 = sb(st, "s2", [128, 16]); mean = sb(st, "mean", [128, 16]); m2 = sb(st, "m2", [128, 16])
            Wd = [ps(st, "Wd%d" % k, [128, 512]) for k in range(2)]
            psT = ps(st, "psT", [128, 1024], BF16)
            Xp = [ps(st, "Xp%d" % k, [128, 512]) for k in range(2)]
            LB = [Xp[0], Xp[1], Wd[0], Wd[1]]
            b5 = ps(st, "b5", [128, 512])
            B6 = ps(st, "B6", [128, 512])
            b7 = ps(st, "b7", [128, 512])
            FB = [b5, b5, b7, b7]
            _ft5 = T(b5.t); _ft7 = T(b7.t)
            FT = [_ft5, _ft5, _ft7, _ft7]
            wk = 0
            for i in tiles:
                own = OWN0 <= i < OWN1
                c0 = 0 if own else 1024
                r0 = i * 128
                has_prev = i not in (0, 2); has_next = i not in (1, 33)
                dma("sp", sh[:, c0:SHC], P[r0:r0 + 128, c0:SHC], reads=[Ptile[i]], writes=[sh])
                dma("pool", pc16[:, c0:SHC], P[r0:r0 + 128, c0:SHC], reads=[Ptile[i]], writes=[pc16])
                memset("pool", nb16[:], 0.0, [nb16])
                if has_prev:
                    dma("pool", nb16[0:1, c0:SHC], P[r0 - 1:r0, c0:SHC], reads=[Ptile[i - 1]], writes=[nb16])
                if has_next:
                    dma("pool", nb16[1:2, c0:SHC], P[r0 + 128:r0 + 129, c0:SHC], reads=[Ptile[i + 1]], writes=[nb16])
                if d == 1 and own:
                    dma("sp", ya[:], YA[(i - 2) * 128:(i - 1) * 128, :], reads=[YAb], writes=[ya])
                    dma("sp", zr[:], P[r0:r0 + 128, C_ZR:C_ZR + 1024], reads=[Ptile[i]], writes=[zr])
                cs = c0
                while cs < SHC:
                    cw = min(512, SHC - cs)
                    W = Wd[wk % 2]; tm_ = E[wk % 2]; wk += 1
                    mm(W[:, 0:cw], shm[:], pc16[:, cs:cs + cw], True, False, [shm, pc16], [W])
                    mm(W[:, 0:cw], e2[0:2, :], nb16[0:2, cs:cs + cw], False, True, [e2, nb16], [W])
                    tt("dve", tm_[:, 0:cw], W[:, 0:cw], mubc[:, cs:cs + cw], ALU.mult, [W, mubc], [tm_])
                    tt("pool", sh[:, cs:cs + cw], tm_[:, 0:cw], sh[:, cs:cs + cw], ALU.add, [tm_, sh], [sh])
                    cs += cw
                r_ = sh[:, 0:1024]; k_ = sh[:, 1024:2048]; v_ = sh[:, 2048:3072]
                if KS_ <= 1:
                    continue
                act(lo16[:, 0:64], sh[:, 3072 + 64 * d:3136 + 64 * d], AF.Tanh, [sh], [lo16])
                cp("dve", lo16[:, 64:128], sh[:, 3200 + 64 * d:3264 + 64 * d], [sh], [lo16])
                tr(psT[:, 0:128], lo16[:, :], identb[:], [lo16, identb], [psT])
                cp("dve", loT[:], psT[:, 0:128], [psT], [loT])
                for (lo_, bias_, dst_) in ((0, w0bc, sg), (64, a0bc, al)):
                    for hf in range(2):
                        W = Wd[wk % 2]; wk += 1
                        mm(W[:], loT[lo_:lo_ + 64, :], WAb[lo_:lo_ + 64, hf * 512:(hf + 1) * 512], True, True, [loT, WAb], [W])
                        tt("dve", t1[:, hf * 512:(hf + 1) * 512], W[:], bias_[:, hf * 512:(hf + 1) * 512], ALU.add, [W, bias_], [t1])
                    act(dst_[:], t1[:], AF.Sigmoid, [t1], [dst_])
                tt("pool", kk[:], k_, kkbc[:], ALU.mult, [sh, kkbc], [kk])
                tt("dve", t1[:], kk[:], kk[:], ALU.mult, [kk], [t1])
                red("dve", ss[:], t1[:].rearrange("p (a b) -> p a b", b=64), [t1], [ss])
                act(ss[:], ss[:], AF.Sqrt, [ss], [ss])
                ts("dve", ss[:], ss[:], 1e-12, None, ALU.max, None, [ss], [ss])
                recip(ss[:], ss[:], [ss], [ss])
                tt("dve", kk[:].rearrange("p (a b) -> p a b", b=64), kk[:].rearrange("p (a b) -> p a b", b=64),
                   ss[:].unsqueeze(2).to_broadcast([128, 16, 64]), ALU.mult, [kk, ss], [kk])
                tt("pool", bb[:], kk[:], al[:], ALU.mult, [kk, al], [bb])
                stt("dve", t1[:], al[:], -1.0, kabc[:], ALU.add, ALU.mult, [al, kabc], [t1])
                stt("dve", kd[:], t1[:], 1.0, k_, ALU.add, ALU.mult, [t1, sh], [kd])
                if own:
                    tt("dve", t1[:], r_, rkbc[:], ALU.mult, [sh, rkbc], [t1])
                    tt("dve", t1[:], t1[:], kd[:], ALU.mult, [t1, kd], [t1])
                    if d == 0:
                        red("dve", bonA[:, i - 2, :], t1[:].rearrange("p (a b) -> p a b", b=64), [t1], [bonA])
                    else:
                        red("dve", bon[:], t1[:].rearrange("p (a b) -> p a b", b=64), [t1], [bon])
                        tt("dve", bon[:], bon[:], bonA[:, i - 2, :], ALU.add, [bon, bonA], [bon])
                if "dbg_sh" in dbg and d == 0 and i == 2:
                    dma("sp", dbg["dbg_sh"][0:128, :], sh[:], reads=[sh])
                    for k_i, t_ in enumerate([sg, al, kk, kd]):
                        dma("sp", dbg["dbg_sh"][128 * (k_i + 1):128 * (k_i + 2), 0:1024], t_[:], reads=[t_])
                for hf in range(2):
                    hs = slice(hf * 512, (hf + 1) * 512)
                    W = Wd[wk % 2]; wk += 1
                    mm(W[:], triI[:], sg[:, hs], True, True, [triI, sg], [W])
                    if own:
                        act(E[0][:, hs], W[:], AF.Exp, [W], [E[0]])
                    act(E[1][:, hs], W[:], AF.Exp, [W], [E[1]], scale=-1.0)
                    stt("dve", t1[:, hs], sg[:, hs], CDEC, W[:], ALU.mult, ALU.add, [sg, W], [t1])
                if own:
                    tt("pool", rt[:], r_, E[0][:], ALU.mult, [sh, E[0]], [rt])
                tt("dve", kt[:], kd[:], E[1][:], ALU.mult, [kd, E[1]], [kt])
                tt("pool", bt[:], bb[:], E[1][:], ALU.mult, [bb, E[1]], [bt])
                act(E[0][:], t1[:], AF.Exp, [t1], [E[0]])
                stt("dve", at[:], kk[:], -1.0, E[0][:], ALU.mult, ALU.mult, [kk, E[0]], [at])
                for hf in range(2):
                    hs = slice(hf * 512, (hf + 1) * 512)
                    W = Wd[wk % 2]; wk += 1
                    mm(W[:], triS[:], sg[:, hs], True, True, [triS, sg], [W])
                    act(E[1][:, hs], W[:], AF.Exp, [W], [E[1]])
                tt("pool", ktp[:], kd[:], E[1][:], ALU.mult, [kd, E[1]], [ktp])
                tt("dve", btp[:], bb[:], E[1][:], ALU.mult, [bb, E[1]], [btp])
                cp("pool", vb[:], v_, [sh], [vb])
                for h in range(16):
                    mm(Wd[0][0:64, h:h + 1], sg[:, h * 64:(h + 1) * 64], negc[:, 0:1], True, True, [sg, negc], [Wd[0]])
                act(WC[:], Wd[0][0:64, 0:16], AF.Exp, [Wd[0]], [WC])
                if KS_ <= 2:
                    continue
                DD = (d == 0 and i in (0, 2))
                if DD:
                    for nm_, t_ in (("bt", bt), ("kt", kt), ("at", at), ("rt", rt), ("btp", btp), ("ktp", ktp), ("vb", vb)):
                        dd("%s_%d" % (nm_, i), t_, t_[:], [128, 1024])
                    dd("WC_%d" % i, WC, WC[:], [64, 16])
                srcs = [(bt, 0), (kt, 192), (at, 320)] + ([(rt, 448)] if own else [])
                for si_, (src, off) in enumerate(srcs):
                    for pr in range(8):
                        tr(psT[:, pr * 128:(pr + 1) * 128], src[:, pr * 128:(pr + 1) * 128], identb[:], [src, identb], [psT])
                    cp("act" if si_ % 2 == 0 else "dve", CM[:, :, off:off + 128], psT[:].rearrange("p (a b) -> p a b", b=128), [psT], [CM])
                if DD:
                    dd("CM_%d" % i, CM, CM[:, 0, :], [128, 576])
                if KS_ <= 3:
                    continue
                for grp in range(2):
                    hs4 = [(g, 8 * grp + g, (8 * grp + g) // 2, 64 * ((8 * grp + g) % 2)) for g in range(8)]
                    for (g, h, pr, pb) in hs4:
                        lb = LB[g % 4]
                        mm(lb[:, 128:448], CM[pb:pb + 64, pr, 320:448], CM[pb:pb + 64, pr, 0:320], True, True, [CM], [lb])
                        mm(lb[:, 0:128], CM[pb:pb + 64, pr, 0:128], CM[pb:pb + 64, pr, 320:448], True, True, [CM], [lb])
                        cp("pool", TT[g][:], II[:], [II], [TT[g]])
                        S.op("dve", lambda e, g=g, lb=lb: e.copy_predicated(out=TT[g][:], mask=msk[:, 0, :], data=lb[:, 0:256]),
                             reads=[lb, msk, TT[g]], writes=[TT[g]])
                        tt("dve", X0[g][:], lb[:, 0:448], maskx[:], ALU.mult, [lb, maskx], [X0[g]])
                        if DD and h < 2:
                            dd("X0_%d_%d" % (i, h), X0[g], X0[g][:], [128, 448])
                    if KS_ <= 4:
                        continue
                    for lev in range(1, 7):
                        for (g, h, pr, pb) in hs4:
                            lb = LB[g % 4]
                            mm(lb[:, 0:128], X0[g][:, 128:256], TT[g][:, 0:128], True, True, [X0[g], TT[g]], [lb])
                            mm(lb[:, 128:256], X0[g][:, 0:128], TT[g][:, 128:256], True, True, [X0[g], TT[g]], [lb])
                            cp("act", CC[g][:], lb[:, 0:256], [lb], [CC[g]])
                        for (g, h, pr, pb) in hs4:
                            lb = LB[g % 4]
                            mm(lb[:, 256:384], TT[g][:, 128:256], CC[g][:, 0:128], True, True, [TT[g], CC[g]], [lb])
                            mm(lb[:, 384:512], TT[g][:, 0:128], CC[g][:, 128:256], True, True, [TT[g], CC[g]], [lb])
                            S.op("dve", lambda e, g=g, lb=lb, lev=lev: e.copy_predicated(out=TT[g][:], mask=msk[:, lev, :], data=lb[:, 256:512]),
                                 reads=[lb, msk, TT[g]], writes=[TT[g]])
                    for (g, h, pr, pb) in hs4:
                        lb = LB[g % 4]
                        mm(lb[:, 0:192], TT[g][:, 0:128], X0[g][:, 256:448], True, True, [TT[g], X0[g]], [lb])
                        cp("dve" if g % 2 == 0 else "act", Zf[g][:], lb[:, 0:192], [lb], [Zf[g]])
                        if DD and h < 2:
                            dd("Zf_%d_%d" % (i, h), Zf[g], Zf[g][:], [128, 192])
                            dd("TT_%d_%d" % (i, h), TT[g], TT[g][:], [128, 256])
                    for sub in range(2):
                        hsub = hs4[4 * sub:4 * sub + 4]
                        if own:
                            for (g, h, pr, pb) in hsub:
                                lb = LB[g % 4]
                                mm(lb[:, 0:128], CM[pb:pb + 64, pr, 0:128], CM[pb:pb + 64, pr, 448:576], True, True, [CM], [lb])
                                mm(lb[:, 128:256], CM[pb:pb + 64, pr, 192:320], CM[pb:pb + 64, pr, 448:576], True, True, [CM], [lb])
                                tt("dve", ARBK[g % 4][:], lb[:, 0:256], masky[:], ALU.mult, [lb, masky], [ARBK[g % 4]])
                        lo_c = 0 if own else 128
                        for (g, h, pr, pb) in hsub:
                            lb = LB[g % 4]; fb = FB[g % 4]; fo = ((g % 4) % 2) * 256
                            hc = slice(h * 64, (h + 1) * 64)
                            AbT = Zf[g][:, 0:64]; PTt = Zf[g][:, 64:192]
                            if own:
                                mm(fb[0:64, fo:fo + 128], AbT, ARBK[g % 4][:, 0:128], True, False, [Zf[g], ARBK[g % 4]], [FT[g % 4]])
                                mm(fb[0:64, fo:fo + 128], rt[:, hc], identb[:], False, True, [rt, identb], [FT[g % 4]])
                            mm(fb[0:64, fo + 128:fo + 192], AbT, btp[:, hc], True, True, [Zf[g], btp], [FT[g % 4]])
                            cp("act", QG[g % 4][:, lo_c:192], fb[0:64, fo + lo_c:fo + 192], [FT[g % 4]], [QG[g % 4]])
                            if own:
                                mm(lb[:, 256:384], PTt, ARBK[g % 4][:, 0:128], True, False, [Zf[g], ARBK[g % 4]], [lb])
                                mm(lb[:, 256:384], identb[:], ARBK[g % 4][:, 128:256], False, True, [identb, ARBK[g % 4]], [lb])
                            mm(lb[:, 384:448], PTt, btp[:, hc], True, False, [Zf[g], btp], [lb])
                            mm(lb[:, 384:448], identb[:], ktp[:, hc], False, True, [identb, ktp], [lb])
                            cp("dve", MYH[g % 4][:, lo_c:192], lb[:, 256 + lo_c:448], [lb], [MYH[g % 4]])
                            if DD and h < 2:
                                dd("QG_%d_%d" % (i, h), QG[g % 4], QG[g % 4][:], [64, 192])
                                dd("MYH_%d_%d" % (i, h), MYH[g % 4], MYH[g % 4][:], [128, 192])
                                if own:
                                    dd("ARBK_%d_%d" % (i, h), ARBK[g % 4], ARBK[g % 4][:], [128, 256])
                        for (g, h, pr, pb) in hsub:
                            fb = FB[g % 4]; fo = ((g % 4) % 2) * 256
                            hc = slice(h * 64, (h + 1) * 64)
                            STc = ST[h][:, cur[h], :]; STn = ST[h][:, 1 - cur[h], :]
                            if own:
                                yo = B6[:, (h % 8) * 64:(h % 8) * 64 + 64]
                                mm(yo, QG[g % 4][:, 0:128], STb[h][:], True, False, [QG[g % 4], STb[h]], [B6])
                                mm(yo, MYH[g % 4][:, 0:128], vb[:, hc], False, True, [MYH[g % 4], vb], [B6])
                            sreg = fb[0:64, fo + 192:fo + 256]
                            mm(sreg, QG[g % 4][:, 128:192], STb[h][:], True, False, [QG[g % 4], STb[h]], [FT[g % 4]])
                            mm(sreg, MYH[g % 4][:, 128:192], vb[:, hc], False, True, [MYH[g % 4], vb], [FT[g % 4]])
                            act(tmpS[g % 4][:], STc, AF.Copy, [ST[h], WC], [tmpS[g % 4]], scale=WC[:, h:h + 1])
                            act(tmpP[g % 4][:], sreg, AF.Copy, [FT[g % 4]], [tmpP[g % 4]])
                            tt("pool", STn, tmpP[g % 4][:], tmpS[g % 4][:], ALU.add, [tmpP[g % 4], tmpS[g % 4]], [ST[h]])
                            cp("pool", STb[h][:], STn, [ST[h]], [STb[h]])
                            if DD and h < 2:
                                dd("ST_%d_%d" % (i, h), ST[h], STn, [64, 64])
                            cur[h] ^= 1
                    if own:
                        hs = slice(grp * 512, grp * 512 + 512)
                        if d == 0:
                            cp("act", ysb[:, hs], B6[:], [B6], [ysb])
                        else:
                            tt("dve", ysb[:, hs], B6[:], ya[:, hs], ALU.add, [B6, ya], [ysb])
                if not own:
                    continue
                if d == 0:
                    if DD:
                        dd("ysb_%d" % i, ysb, ysb[:], [128, 1024])
                    dma("sp", YA[(i - 2) * 128:(i - 1) * 128, :], ysb[:], reads=[ysb], writes=[YAb])
                    continue
                if "dbg_y" in dbg:
                    dma("sp", dbg["dbg_y"][(i - 2) * 128:(i - 1) * 128, :], ysb[:], reads=[ysb])
                y3 = ysb[:].rearrange("p (a b) -> p a b", b=64)
                red("dve", s1[:], y3, [ysb], [s1])
                tt("pool", t1[:], ysb[:], ysb[:], ALU.mult, [ysb], [t1])
                red("dve", s2[:], t1[:].rearrange("p (a b) -> p a b", b=64), [t1], [s2])
                ts("dve", mean[:], s1[:], 1.0 / 64, None, ALU.mult, None, [s1], [mean])
                tt("dve", m2[:], mean[:], mean[:], ALU.mult, [mean], [m2])
                stt("dve", s2[:], s2[:], 1.0 / 64, m2[:], ALU.mult, ALU.subtract, [s2, m2], [s2])
                act(s2[:], s2[:], AF.Sqrt, [s2], [s2], bias=GN_EPS)
                recip(s2[:], s2[:], [s2], [s2])
                tt("dve", y3, y3, mean[:].unsqueeze(2).to_broadcast([128, 16, 64]), ALU.subtract, [ysb, mean], [ysb])
                tt("dve", y3, y3, s2[:].unsqueeze(2).to_broadcast([128, 16, 64]), ALU.mult, [ysb, s2], [ysb])
                tt("pool", ysb[:], ysb[:], lngbc[:], ALU.mult, [ysb, lngbc], [ysb])
                tt("pool", ysb[:], ysb[:], lnbbc[:], ALU.add, [ysb, lnbbc], [ysb])
                tt("dve", t1[:].rearrange("p (a b) -> p a b", b=64), sh[:, 2048:3072].rearrange("p (a b) -> p a b", b=64),
                   bon[:].unsqueeze(2).to_broadcast([128, 16, 64]), ALU.mult, [sh, bon], [t1])
                tt("dve", ysb[:], ysb[:], t1[:], ALU.add, [ysb, t1], [ysb])
                if "dbg_rw" in dbg:
                    dma("sp", dbg["dbg_rw"][(i - 2) * 128:(i - 1) * 128, :], ysb[:], reads=[ysb])
                act(zr[:], zr[:], AF.Silu, [zr], [zr])
                tt("pool", mixb[:], ysb[:], zr[:], ALU.mult, [ysb, zr], [mixb])
                for cch in range(8):
                    tr(psT[:, cch * 128:(cch + 1) * 128], mixb[:, cch * 128:(cch + 1) * 128], identb[:], [mixb, identb], [psT])
                cp("act", mtT[:], psT[:].rearrange("p (a b) -> p a b", b=128), [psT], [mtT])
                dma("sp", MT[0:8, :, (i - 2) * 128:(i - 1) * 128].rearrange("c p t -> p c t"), mtT[:], reads=[mtT], writes=[MTb])
        S.barrier()

    if stop_after >= 2:
        rwkv_sweep(0)
    if stop_after >= 3:
        rwkv_sweep(1)

    def rope_apply(eng2, t3, rot3, rp, nh, reads_t, T_t, T_rot, T_rp):
        t5 = t3.rearrange("p h (a b c) -> p h a b c", a=2, b=2)
        r5 = rot3.rearrange("p h (a b c) -> p h a b c", a=2, b=2)
        cp("pool", r5[:, :, :, 0, :], t5[:, :, :, 1, :], [T_t], [T_rot])
        cp("pool", r5[:, :, :, 1, :], t5[:, :, :, 0, :], [T_t], [T_rot])
        cosb = rp[:, 0:64].unsqueeze(1).to_broadcast([128, nh, 64])
        sinb = rp[:, 64:128].unsqueeze(1).to_broadcast([128, nh, 64])
        tt("dve", t3, t3, cosb, ALU.mult, [T_t, T_rp], [T_t])
        tt("dve", rot3, rot3, sinb, ALU.mult, [T_rot, T_rp], [T_rot])
        tt("dve", t3, t3, rot3, ALU.add, [T_t, T_rot], [T_t])

    def attention():
        with contextlib.ExitStack() as st:
            kgbc = sb(st, "kgbc", [128, 64]); bcast_load("sp", kgbc, kg[0:1, :])
            qgbc = sb(st, "qgbc", [128, 64]); bcast_load("sp", qgbc, qg[0:1, :])
            ts("dve", qgbc[:], qgbc[:], 0.125, None, ALU.mult, None, [qgbc], [qgbc])
            esk = sb(st, "esk", [128, 16]); bcast_load("sp", esk, sink[0:1, :])
            act(esk[:], esk[:], AF.Exp, [esk], [esk])
            amask = sb(st, "amask", [128, 2, 128], BF16)
            dma("pool", amask[:], c_amask.rearrange("a k q -> k a q"), writes=[amask])
            KT = sb(st, "KT", [64, 19, 4, 128], BF16)
            V1 = sb(st, "V1", [128, 19, 4, 65], BF16)
            memset("pool", V1[:], 1.0, [V1])
            kv = sb(st, "kv", [128, 512]); rot = sb(st, "rot", [128, 1024]); rp = sb(st, "rp", [128, 128])
            sq = sb(st, "sq", [128, 1024]); ss = sb(st, "ssq_a", [128, 16])
            kb = sb(st, "kb", [128, 256], BF16)
            q = sb(st, "q", [128, 1024]); za = sb(st, "za", [128, 1024]); qb = sb(st, "qb", [128, 1024], BF16)
            QT = sb(st, "QT", [64, 16, 128], BF16)
            PTs = [sb(st, "PTs%d" % k, [128, 512], BF16) for k in range(5)]
            att = sb(st, "att", [128, 1024]); den = sb(st, "den", [128, 16])
            mixb = sb(st, "mixb_a", [128, 1024], BF16); mtT = sb(st, "mtT_a", [128, 8, 128], BF16)
            psT = ps(st, "psT_a", [128, 1024], BF16)
            Sps = [ps(st, "Sps%d" % k, [128, 512]) for k in range(2)]
            Og = [ps(st, "Og%d" % k, [128, 512]) for k in range(4)]
            for n in range(19):
                i = n
                dma("sp", kv[:], P[i * 128:(i + 1) * 128, C_KA:C_KA + 512], reads=[Ptile[i]], writes=[kv])
                k3 = kv[:, 0:256].rearrange("p (h c) -> p h c", c=64)
                tt("dve", sq[:, 0:256], kv[:, 0:256], kv[:, 0:256], ALU.mult, [kv], [sq])
                red("dve", ss[:, 0:4], sq[:, 0:256].rearrange("p (h c) -> p h c", c=64), [sq], [ss])
                act(ss[:, 0:4], ss[:, 0:4], AF.Sqrt, [ss], [ss], scale=1.0 / 64, bias=1e-6)
                recip(ss[:, 0:4], ss[:, 0:4], [ss], [ss])
                tt("dve", k3, k3, ss[:, 0:4].unsqueeze(2).to_broadcast([128, 4, 64]), ALU.mult, [kv, ss], [kv])
                tt("dve", k3, k3, kgbc[:].unsqueeze(1).to_broadcast([128, 4, 64]), ALU.mult, [kv, kgbc], [kv])
                if i >= 2:
                    dma("sp", rp[:], rope[(i - 2) * 128:(i - 1) * 128, :], writes=[rp])
                    rope_apply(None, k3, rot[:, 0:256].rearrange("p (h c) -> p h c", c=64), rp, 4, None, kv, rot, rp)
                cp("act", kb[:], kv[:, 0:256], [kv], [kb])
                for g in range(4):
                    tr(psT[0:64, g * 128:(g + 1) * 128], kb[:, g * 64:(g + 1) * 64], identb[:], [kb, identb], [psT])
                cp("dve", KT[:, n, :, :], psT[0:64, 0:512].rearrange("p (g t) -> p g t", t=128), [psT], [KT])
                cp("act", V1[:, n, :, 0:64], kv[:, 256:512].rearrange("p (h c) -> p h c", c=64), [kv], [V1])
            sk = 0
            for i in range(OWN0, OWN1):
                r0 = i * 128
                dma("sp", q[:], P[r0:r0 + 128, C_Q:C_Q + 1024], reads=[Ptile[i]], writes=[q])
                dma("sp", za[:], P[r0:r0 + 128, C_ZA:C_ZA + 1024], reads=[Ptile[i]], writes=[za])
                dma("sp", rp[:], rope[(i - 2) * 128:(i - 1) * 128, :], writes=[rp])
                q3 = q[:].rearrange("p (h c) -> p h c", c=64)
                tt("pool", sq[:], q[:], q[:], ALU.mult, [q], [sq])
                red("dve", ss[:], sq[:].rearrange("p (h c) -> p h c", c=64), [sq], [ss])
                act(ss[:], ss[:], AF.Sqrt, [ss], [ss], scale=1.0 / 64, bias=1e-6)
                recip(ss[:], ss[:], [ss], [ss])
                tt("dve", q3, q3, ss[:].unsqueeze(2).to_broadcast([128, 16, 64]), ALU.mult, [q, ss], [q])
                tt("dve", q3, q3, qgbc[:].unsqueeze(1).to_broadcast([128, 16, 64]), ALU.mult, [q, qgbc], [q])
                rope_apply(None, q3, rot[:].rearrange("p (h c) -> p h c", c=64), rp, 16, None, q, rot, rp)
                cp("act", qb[:], q[:], [q], [qb])
                for half in range(2):
                    for hx in range(8):
                        hd = half * 8 + hx
                        tr(psT[0:64, hx * 128:(hx + 1) * 128], qb[:, hd * 64:(hd + 1) * 64], identb[:], [qb, identb], [psT])
                    cp("dve", QT[:, half * 8:(half + 1) * 8, :], psT[0:64, :].rearrange("p (g t) -> p g t", t=128), [psT], [QT])
                for g in range(4):
                    keyt = ([i - 1] if i > OWN0 else []) + [i, i + 1, 0, 1]
                    for ki, kt_ in enumerate(keyt):
                        sp_ = Sps[sk % 2]; sk += 1
                        pt_ = PTs[ki]
                        mm(sp_[:], KT[:, kt_, g, :], QT[:, 4 * g:4 * g + 4, :].rearrange("p a b -> p (a b)"), True, True, [KT, QT], [sp_])
                        act(pt_[:], sp_[:], AF.Exp, [sp_], [pt_])
                        is_prev = (i > OWN0 and ki == 0); is_next = (ki == (2 if i > OWN0 else 1))
                        if is_prev or is_next:
                            mi = 0 if is_prev else 1
                            for hx in range(4):
                                tt("pool", pt_[:, hx * 128:(hx + 1) * 128], pt_[:, hx * 128:(hx + 1) * 128], amask[:, mi, :], ALU.mult, [pt_, amask], [pt_])
                    for hx in range(4):
                        for ki, kt_ in enumerate(keyt):
                            mm(Og[g][:, hx * 65:(hx + 1) * 65], PTs[ki][:, hx * 128:(hx + 1) * 128], V1[:, kt_, g, :],
                               ki == 0, ki == len(keyt) - 1, [PTs[ki], V1], [Og[g]])
                for g in range(4):
                    o3 = Og[g][:, 0:260].rearrange("p (h c) -> p h c", c=65)
                    tt("dve", den[:, 4 * g:4 * g + 4], o3[:, :, 64], esk[:, 4 * g:4 * g + 4], ALU.add, [Og[g], esk], [den])
                    recip(den[:, 4 * g:4 * g + 4], den[:, 4 * g:4 * g + 4], [den], [den])
                    tt("dve", att[:, g * 256:(g + 1) * 256].rearrange("p (h c) -> p h c", c=64), o3[:, :, 0:64],
                       den[:, 4 * g:4 * g + 4].unsqueeze(2).to_broadcast([128, 4, 64]), ALU.mult, [Og[g], den], [att])
                if "dbg_att" in dbg:
                    dma("sp", dbg["dbg_att"][(i - 2) * 128:(i - 1) * 128, :], att[:], reads=[att])
                act(za[:], za[:], AF.Silu, [za], [za])
                tt("pool", mixb[:], att[:], za[:], ALU.mult, [att, za], [mixb])
                for cch in range(8):
                    tr(psT[:, cch * 128:(cch + 1) * 128], mixb[:, cch * 128:(cch + 1) * 128], identb[:], [mixb, identb], [psT])
                cp("act", mtT[:], psT[:].rearrange("p (a b) -> p a b", b=128), [psT], [mtT])
                dma("sp", MT[8:16, :, (i - 2) * 128:(i - 1) * 128].rearrange("c p t -> p c t"), mtT[:], reads=[mtT], writes=[MTb])
        S.barrier()

    def outproj():
        with contextlib.ExitStack() as st:
            wo = sb(st, "wo", [128, 16, D], BF16)
            wov = w_out.rearrange("(j p) n -> p j n", p=128)
            for k4 in range(4):
                dma("pool", wo[:, k4 * 4:(k4 + 1) * 4, :], wov[:, k4 * 4:(k4 + 1) * 4, :], writes=[wo])
            gate = sb(st, "gate", [128, D]); dma("sp", gate[:], GATE[:, :], reads=[GATEb], writes=[gate])
            mt = [sb(st, "mt%d" % k, [128, 16, 128], BF16) for k in range(2)]
            xo = [sb(st, "xo%d" % k, [128, D]) for k in range(2)]
            ot = [sb(st, "ot%d" % k, [128, D]) for k in range(2)]
            pp = [ps(st, "ppo%d" % k, [128, 512]) for k in range(4)]
            for tl in range(16):
                i = tl + 2
                m_ = mt[tl % 2]; x_ = xo[tl % 2]; o_ = ot[tl % 2]
                dma("sp", m_[:], MT[:, :, tl * 128:(tl + 1) * 128].rearrange("c p t -> p c t"), reads=[MTb], writes=[m_])
                dma("act", x_[:], xin[i * 128:(i + 1) * 128, :], writes=[x_])
                for nb in range(4):
                    ns = slice(nb * 512, (nb + 1) * 512)
                    p_ = pp[nb]
                    for cc_ in range(16):
                        mm(p_[:], m_[:, cc_, :], wo[:, cc_, ns], cc_ == 0, cc_ == 15, [m_, wo], [p_])
                    tt("dve", o_[:, ns], p_[:], gate[:, ns], ALU.mult, [p_, gate], [o_])
                    tt("pool", o_[:, ns], o_[:, ns], x_[:, ns], ALU.add, [o_, x_], [o_])
                dma("sp", out[tl * 128:(tl + 1) * 128, :], o_[:], reads=[o_])

    if stop_after >= 4:
        attention()
    if stop_after >= 5:
        outproj()

    S.emit(final_wait_ops=[i for i, o in enumerate(S.ops) if o.is_dma])
    top.close()
    return nc


def host_consts():
    c = {}
    c["c_ident"] = np.eye(128, dtype=np.float32)
    i = np.arange(128)
    triA_incl = (i[:, None] <= i[None, :]).astype(np.float32)
    triA_suf = (i[:, None] > i[None, :]).astype(np.float32)
    triB_incl = (i[:, None] >= i[None, :]).astype(np.float32)
    triB_suf = (i[:, None] < i[None, :]).astype(np.float32)
    c["c_tri"] = (-CDEC * np.stack([triA_incl, triA_suf, triB_incl, triB_suf])).astype(np.float32)
    mx = np.zeros((2, 128, 448), np.float32); my = np.zeros((2, 128, 256), np.float32)
    for d in range(2):
        prec = (i[:, None] < i[None, :]) if d == 0 else (i[:, None] > i[None, :])
        preceq = prec | np.eye(128, dtype=bool)
        mx[d, :, 0:128] = prec; mx[d, :, 128:256] = prec.T; mx[d, :, 256:320] = 1.0; mx[d, :, 320:448] = prec.T
        my[d, :, 0:128] = preceq; my[d, :, 128:256] = preceq
    c["c_maskx"] = mx; c["c_masky"] = my
    cms = np.zeros((2, 7, 128, 256), np.float32)
    for d in range(2):
        prec = (i[:, None] < i[None, :]) if d == 0 else (i[:, None] > i[None, :])
        for s_ in range(7):
            m = prec & ((i[:, None] >> (s_ + 1)) == (i[None, :] >> (s_ + 1))) & ((i[:, None] >> s_) != (i[None, :] >> s_))
            cms[d, s_, :, 0:128] = m; cms[d, s_, :, 128:256] = m.T
    c["c_ms"] = cms
    sh = np.zeros((128, 128), np.float32)
    sh[i, i] = -1.0; sh[i[:-1], i[:-1] + 1] = 0.5; sh[i[1:], i[1:] - 1] = 0.5
    c["c_sh"] = sh
    e = np.zeros((2, 128), np.float32); e[0, 0] = 0.5; e[1, 127] = 0.5
    c["c_e"] = e
    c["c_i64"] = np.concatenate([np.eye(64, dtype=np.float32)] * 2, axis=0)
    am = np.zeros((2, 128, 128), np.float32)
    am[0] = (i[:, None] >= i[None, :]); am[1] = (i[:, None] <= i[None, :])
    c["c_amask"] = am
    return c


def rope_tables(pos):
    pos = np.asarray(pos)
    row = (pos // 64).astype(np.float32); col = (pos % 64).astype(np.float32)
    half = 16
    inv = (10000.0 ** (-np.arange(half, dtype=np.float32) / half)).astype(np.float32)
    ar = row[:, None] * inv; ac = col[:, None] * inv
    ang = np.concatenate([ar, ar, ac, ac], axis=-1)
    cos = np.cos(ang); sin = np.sin(ang)
    sgn = np.concatenate([-np.ones(16), np.ones(16), -np.ones(16), np.ones(16)]).astype(np.float32)
    return np.concatenate([cos, sin * sgn], axis=-1).astype(np.float32)


def core_inputs(inp, b, h, consts):
    f = np.ascontiguousarray
    x = inp["x"][b]; ctx = inp["ctx"][b]
    if h == 1:
        x = x[::-1]; ctx = ctx[::-1]
    m = {}
    m["xin"] = f(np.concatenate([ctx, x], axis=0))
    colf = lambda v: v.reshape(-1, 128).T
    m["cc"] = f(np.concatenate([colf(inp["c"][b]), colf(inp["c_ctx"])], axis=1))
    m["w_ada"] = f(inp["w_ada"][0])
    m["bcol"] = f(np.repeat(colf(inp["b_ada"][0]), 2, axis=1))
    m["ngcol"] = f(colf(inp["norm_g"][0]))
    dA, dB = (0, 1) if h == 0 else (1, 0)
    lw = lambda d: np.arange(3072 + 64 * d, 3072 + 64 * d + 64)
    la = lambda d: np.arange(3200 + 64 * d, 3200 + 64 * d + 64)
    perm = np.concatenate([np.arange(0, 3072), lw(dA), lw(dB), la(dA), la(dB), np.arange(3328, 4352), np.arange(4352, 5376),
                           np.arange(5888, 6912), np.arange(5376, 5632), np.arange(5632, 5888)])
    m["w_in"] = f(inp["w_in"][0][:, perm])
    m["mu"] = f(inp["mu_shift"][0][perm[:SHC]][None, :])
    sel = [dA, dB]
    m["w0"] = f(inp["w0"][0][sel]); m["w2"] = f(inp["w2"][0][sel]); m["a0"] = f(inp["a0"][0][sel]); m["a2"] = f(inp["a2"][0][sel])
    for k in ("k_k", "k_a"):
        m[k] = f(inp[k][0][None, :])
    m["r_k"] = f(inp["r_k"][0].reshape(1, 1024))
    m["ln_g"] = f(inp["ln_x_g"][0][None, :]); m["ln_b"] = f(inp["ln_x_b"][0][None, :])
    m["qg"] = f(inp["q_norm_g"][0][None, :]); m["kg"] = f(inp["k_norm_g"][0][None, :]); m["sink"] = f(inp["sink"][0][None, :])
    m["w_out"] = f(inp["w_out"][0])
    loc = np.arange(17 * 128)
    pos = loc if h == 0 else 4095 - loc
    m["rope"] = rope_tables(pos)
    m.update(consts)
    return m


def kernel(**inputs):
    inp = {k: np.asarray(v) for k, v in inputs.items()}
    consts = host_consts()
    nc = build()
    in_maps = [core_inputs(inp, c // 2, c % 2, consts) for c in range(8)]
    res = run_bass_kernel_spmd(nc, in_maps, core_ids=list(range(8)))
    out = np.empty((4, 4096, D), np.float32)
    for c in range(8):
        b, h = c // 2, c % 2
        o = res.results[c]["out"]
        if h == 0:
            out[b, :2048] = o
        else:
            out[b, 2048:] = o[::-1]
    return out
```

```python
import contextlib
import numpy as np
import concourse.bass as bass
import concourse.mybir as mybir
from concourse.bass_utils import run_bass_kernel_spmd

F32 = mybir.dt.float32
BF16 = mybir.dt.bfloat16
U8 = mybir.dt.uint8
AF = mybir.ActivationFunctionType
ALU = mybir.AluOpType
AX = mybir.AxisListType

ENGS = ["pe", "act", "dve", "pool", "sp"]
CDEC = 0.6065306597126334

D = 2048
NT = 34
OWN0, OWN1 = 2, 18
TL = NT * 128
PC = 6912
C_R, C_K, C_V, C_LO, C_ZR, C_Q, C_ZA, C_KA, C_VA = 0, 1024, 2048, 3072, 3328, 4352, 5376, 6400, 6656
SHC = 3328
DEBUG = {}


class Buf:
    __slots__ = ("last_w", "readers")

    def __init__(self):
        self.last_w = None
        self.readers = []


class T:
    def __init__(self, t):
        self.t = t
        self.b = Buf()

    def __getitem__(self, k):
        return self.t[k]


class Op:
    __slots__ = ("eng", "fn", "deps", "is_dma", "has_dep", "mile", "dsem", "dval")


class Sched:
    def __init__(self, nc, n_dma_sems=32):
        self.nc = nc
        self.ops = []
        self.n_dma_sems = n_dma_sems
        self.last_on = {e: None for e in ENGS}
        self.dmas_since_barrier = []

    def op(self, eng, fn, reads=(), writes=(), dma=False, extra_deps=()):
        o = Op()
        o.eng = eng; o.fn = fn; o.is_dma = dma; o.has_dep = False; o.mile = None; o.dsem = None; o.dval = None
        o.deps = set(extra_deps)
        oid = len(self.ops)
        for t in reads:
            b = t.b
            if b.last_w is not None:
                o.deps.add(b.last_w)
        for t in writes:
            b = t.b
            if b.last_w is not None:
                o.deps.add(b.last_w)
            o.deps.update(b.readers)
        for t in reads:
            t.b.readers.append(oid)
        for t in writes:
            t.b.last_w = oid
            t.b.readers = []
        o.deps.discard(oid)
        self.ops.append(o)
        self.last_on[eng] = oid
        if dma:
            self.dmas_since_barrier.append(oid)
        return oid

    def barrier(self):
        deps = [v for v in self.last_on.values() if v is not None] + list(self.dmas_since_barrier)
        self.dmas_since_barrier = []
        for e in ENGS:
            self.op(e, None, extra_deps=deps)

    def emit(self, final_wait_ops=()):
        nc = self.nc
        ops = self.ops
        for o in ops:
            nd = set()
            for d in o.deps:
                p = ops[d]
                if p.fn is None:
                    if p.eng == o.eng:
                        continue
                    nd.update(p.deps)
                    continue
                if p.eng == o.eng and not p.is_dma and o.eng == "pe" and not o.is_dma:
                    continue
                nd.add(d)
            o.deps = nd
        for o in ops:
            for d in o.deps:
                ops[d].has_dep = True
        for d in final_wait_ops:
            ops[d].has_dep = True
        cnt = {e: 0 for e in ENGS}
        dma_i = 0
        dma_cnt = [0] * self.n_dma_sems
        dma_prev = [None] * self.n_dma_sems
        for i, o in enumerate(ops):
            if o.fn is None:
                continue
            if o.is_dma:
                s = dma_i % self.n_dma_sems
                dma_i += 1
                if dma_prev[s] is not None:
                    o.deps.add(dma_prev[s])
                dma_prev[s] = i
                dma_cnt[s] += 16
                o.dsem = s
                o.dval = dma_cnt[s]
            elif o.has_dep:
                cnt[o.eng] += 1
                o.mile = cnt[o.eng]
        streams = {e: [] for e in ENGS}
        for i, o in enumerate(ops):
            streams[o.eng].append(i)
        with contextlib.ExitStack() as st:
            esem = {e: st.enter_context(nc.semaphore("s_" + e)) for e in ENGS}
            dsem = [st.enter_context(nc.semaphore("d_%d" % k)) for k in range(self.n_dma_sems)]
            block = st.enter_context(nc.Block())

            def run(eng_name, engine):
                seen = {}
                for i in streams[eng_name]:
                    o = ops[i]
                    need = {}
                    for d in o.deps:
                        p = ops[d]
                        if p.is_dma:
                            key = ("d", p.dsem); val = p.dval
                        else:
                            key = ("e", p.eng); val = p.mile
                        if need.get(key, 0) < val:
                            need[key] = val
                    for key, val in need.items():
                        if seen.get(key, 0) >= val:
                            continue
                        seen[key] = val
                        sem = dsem[key[1]] if key[0] == "d" else esem[key[1]]
                        engine.wait_ge(sem, val)
                    if o.fn is None:
                        continue
                    ins = o.fn(engine)
                    if o.is_dma:
                        ins.then_inc(dsem[o.dsem], 16)
                    elif o.mile is not None:
                        ins.then_inc(esem[o.eng], 1)
                if eng_name == "sp":
                    fin = {}
                    for d in final_wait_ops:
                        p = ops[d]
                        fin[p.dsem] = max(fin.get(p.dsem, 0), p.dval)
                    for k_, v_ in fin.items():
                        engine.wait_ge(dsem[k_], v_)

            block.tensor(lambda e: run("pe", e))
            block.scalar(lambda e: run("act", e))
            block.vector(lambda e: run("dve", e))
            block.gpsimd(lambda e: run("pool", e))
            block.sync(lambda e: run("sp", e))


def build(debug_names=(), stop_after=99):
    nc = bass.Bass("TRN2", target_bir_lowering=False)

    def din(name, shape, dt=F32):
        return nc.dram_tensor(name, list(shape), dt, kind="ExternalInput").ap()

    xin = din("xin", [TL, D])
    cc = din("cc", [128, 32])
    w_ada = din("w_ada", [D, 3 * D])
    bcol = din("bcol", [128, 96])
    ngcol = din("ngcol", [128, 16])
    w_in = din("w_in", [D, PC])
    mu = din("mu", [1, SHC])
    w0 = din("w0", [2, 1024]); w2 = din("w2", [2, 64, 1024]); a0 = din("a0", [2, 1024]); a2 = din("a2", [2, 64, 1024])
    k_k = din("k_k", [1, 1024]); k_a = din("k_a", [1, 1024]); r_k = din("r_k", [1, 1024])
    ln_g = din("ln_g", [1, 1024]); ln_b = din("ln_b", [1, 1024])
    qg = din("qg", [1, 64]); kg = din("kg", [1, 64]); sink = din("sink", [1, 16])
    w_out = din("w_out", [D, D])
    rope = din("rope", [17 * 128, 128])
    c_ident = din("c_ident", [128, 128])
    c_tri = din("c_tri", [4, 128, 128])
    c_maskx = din("c_maskx", [2, 128, 448])
    c_masky = din("c_masky", [2, 128, 256])
    c_ms = din("c_ms", [2, 7, 128, 256])
    c_sh = din("c_sh", [128, 128])
    c_e = din("c_e", [2, 128])
    c_i64 = din("c_i64", [128, 64])
    c_amask = din("c_amask", [2, 128, 128])
    out = nc.dram_tensor("out", [2048, D], F32, kind="ExternalOutput").ap()

    def scratch(name, shape, dt):
        kind = "ExternalOutput" if name in debug_names else None
        if kind:
            return nc.dram_tensor(name, list(shape), dt, kind=kind).ap()
        return nc.dram_tensor(name, list(shape), dt).ap()

    P = scratch("P", [TL, PC], F32)
    YA = scratch("YA", [2048, 1024], F32)
    MT = scratch("MT", [16, 128, 2048], BF16)
    dbg = {n: scratch(n, shp, F32) for n, shp in [("dbg_bc", [5, 128, D]), ("dbg_sh", [TL, SHC]), ("dbg_y", [2048, 1024]),
                                                  ("dbg_rw", [2048, 1024]), ("dbg_att", [2048, 1024])] if n in debug_names}
    Pb = T(None); YAb = T(None); MTb = T(None)
    Ptile = [T(None) for _ in range(NT)]

    S = Sched(nc)
    top = contextlib.ExitStack()

    uid = [0]

    def sb(st, name, shape, dt=F32):
        uid[0] += 1
        return T(st.enter_context(nc.sbuf_tensor("%s_%d" % (name, uid[0]), list(shape), dt)))

    def ps(st, name, shape, dt=F32):
        uid[0] += 1
        return T(st.enter_context(nc.psum_tensor("%s_%d" % (name, uid[0]), list(shape), dt)))

    def dma(eng, out_ap, in_ap, reads=(), writes=()):
        return S.op(eng, lambda e: e.dma_start(out=out_ap, in_=in_ap), reads=reads, writes=writes, dma=True)

    def mm(o, lhsT, rhs, start, stop, reads, writes):
        return S.op("pe", lambda e: e.matmul(o, lhsT=lhsT, rhs=rhs, start=start, stop=stop), reads=reads, writes=writes)

    def tr(o, in_, ident, reads, writes):
        return S.op("pe", lambda e: e.transpose(out=o, in_=in_, identity=ident), reads=reads, writes=writes)

    def act(o, in_, func, reads, writes, scale=1.0, bias=0.0, accum=None, eng="act"):
        if accum is None:
            return S.op(eng, lambda e: e.activation(out=o, in_=in_, func=func, scale=scale, bias=bias), reads=reads, writes=writes)
        return S.op(eng, lambda e: e.activation(out=o, in_=in_, func=func, scale=scale, bias=bias, accum_out=accum),
                    reads=reads, writes=writes)

    def tt(eng, o, a, b, op, reads, writes):
        return S.op(eng, lambda e: e.tensor_tensor(out=o, in0=a, in1=b, op=op), reads=reads, writes=writes)

    def ts(eng, o, a, s1, s2, op0, op1, reads, writes):
        if s2 is None:
            return S.op(eng, lambda e: e.tensor_scalar(out=o, in0=a, scalar1=s1, scalar2=None, op0=op0), reads=reads, writes=writes)
        return S.op(eng, lambda e: e.tensor_scalar(out=o, in0=a, scalar1=s1, scalar2=s2, op0=op0, op1=op1), reads=reads, writes=writes)

    def stt(eng, o, a, s, b, op0, op1, reads, writes):
        return S.op(eng, lambda e: e.scalar_tensor_tensor(out=o, in0=a, scalar=s, in1=b, op0=op0, op1=op1), reads=reads, writes=writes)

    def cp(eng, o, a, reads, writes):
        if eng == "act":
            return S.op("act", lambda e: e.activation(out=o, in_=a, func=AF.Copy), reads=reads, writes=writes)
        return S.op(eng, lambda e: e.tensor_copy(out=o, in_=a), reads=reads, writes=writes)

    def red(eng, o, a, reads, writes, op=ALU.add):
        return S.op(eng, lambda e: e.tensor_reduce(out=o, in_=a, axis=AX.X, op=op), reads=reads, writes=writes)

    def recip(o, a, reads, writes):
        return S.op("dve", lambda e: e.reciprocal(out=o, in_=a), reads=reads, writes=writes)

    def memset(eng, o, val, writes):
        return S.op(eng, lambda e: e.memset(o, val), writes=writes)

    ddn = [0]

    def dd(name, T_, ap, shape):
        if "dd" not in debug_names:
            return
        t = nc.dram_tensor("dd_" + name, list(shape), F32, kind="ExternalOutput").ap()
        dma("pool", t, ap, reads=[T_])

    def bc_row(ap_row, n):
        return ap_row.partition_broadcast(128) if hasattr(ap_row, "partition_broadcast") else ap_row

    identf = sb(top, "identf", [128, 128]); identb = sb(top, "identb", [128, 128], BF16)
    onesf = sb(top, "onesf", [128, 128])
    bonA = sb(top, "bonA", [128, 16, 16])
    st01 = contextlib.ExitStack()
    Abc = [sb(st01, "Abc%d" % v, [128, D]) for v in range(2)]
    Bbc = [sb(st01, "Bbc%d" % v, [128, D]) for v in range(2)]
    gatebc = sb(st01, "gatebc", [128, D])
    GATE = scratch("GATE", [128, D], F32); GATEb = T(None)
    dma("sp", identf[:], c_ident[:, :], writes=[identf])
    cp("dve", identb[:], identf[:], [identf], [identb])
    memset("dve", onesf[:], 1.0, [onesf])

    with contextlib.ExitStack() as st:
        cct = sb(st, "cct", [128, 32]); sc = sb(st, "sc", [128, 32])
        bct = sb(st, "bct", [128, 96]); ngt = sb(st, "ngt", [128, 16])
        modc = sb(st, "modc", [128, 96]); acol = sb(st, "acol", [128, 2, 16])
        wa = [sb(st, "wa%d" % i, [128, 16, 512]) for i in range(2)]
        dg = [sb(st, "dg%d" % i, [128, 512]) for i in range(2)]
        psA = ps(st, "psA", [128, 96])
        psB = [ps(st, "psB%d" % i, [128, 512]) for i in range(2)]
        dma("sp", cct[:], cc[:, :], writes=[cct]); dma("sp", bct[:], bcol[:, :], writes=[bct]); dma("sp", ngt[:], ngcol[:, :], writes=[ngt])
        act(sc[:], cct[:], AF.Silu, [cct], [sc])
        sc3 = sc[:].rearrange("p (v j) -> p v j", j=16)
        wav = w_ada.rearrange("(j p) n -> p j n", p=128)
        for g in range(12):
            w = wa[g % 2]
            dma("sp" if g % 2 == 0 else "pool", w[:], wav[:, :, g * 512:(g + 1) * 512], writes=[w])
            for m4 in range(4):
                m = g * 4 + m4
                for j in range(16):
                    mm(psA[:, 2 * m:2 * m + 2], w[:, j, m4 * 128:(m4 + 1) * 128], sc3[:, :, j], j == 0, j == 15, [w, sc], [psA])
        tt("dve", modc[:], psA[:], bct[:], ALU.add, [psA, bct], [modc])
        mc3 = modc[:].rearrange("p (m v) -> p v m", v=2)
        for v in range(2):
            stt("dve", acol[:, v, :], mc3[:, v, 16:32], 1.0, ngt[:], ALU.add, ALU.mult, [modc, ngt], [acol])
        jobs = [(acol, lambda v, m: acol[:, v, m:m + 1], Abc[0], 0), (acol, lambda v, m: acol[:, v, m:m + 1], Abc[1], 1),
                (modc, lambda v, m: mc3[:, v, m:m + 1], Bbc[0], 0), (modc, lambda v, m: mc3[:, v, m:m + 1], Bbc[1], 1),
                (modc, lambda v, m: mc3[:, v, 32 + m:33 + m], gatebc, 0)]
        k = 0
        for src, colf, dst, v in jobs:
            for m4 in range(4):
                d_ = dg[k % 2]; p_ = psB[k % 2]; k += 1
                for mi in range(4):
                    m = m4 * 4 + mi
                    ts("dve", d_[:, mi * 128:(mi + 1) * 128], identf[:], colf(v, m), None, ALU.mult, ALU.bypass, [identf, src], [d_])
                mm(p_[:], onesf[:], d_[:], True, True, [onesf, d_], [p_])
                cp("act", dst[:, m4 * 512:(m4 + 1) * 512], p_[:], [p_], [dst])
        dma("sp", GATE[:, :], gatebc[:], reads=[gatebc], writes=[GATEb])
        if "dbg_bc" in dbg:
            for i, t_ in enumerate([Abc[0], Abc[1], Bbc[0], Bbc[1], gatebc]):
                dma("sp", dbg["dbg_bc"][i], t_[:], reads=[t_])
    S.barrier()

    blocks = [(0, 512, 'r'), (512, 512, 'r'), (1024, 512, 'k'), (1536, 512, 'k'), (2048, 512, 'v'), (2560, 512, 'v'),
              (3072, 256, 'lo'), (3328, 512, 'zr'), (3840, 512, 'zr'), (4352, 512, 'q'), (4864, 512, 'q'),
              (5376, 512, 'za'), (5888, 512, 'za'), (6400, 512, 'kv')]
    tblocks = [([0, 1], {'k', 'v', 'lo', 'kv'})] + [(list(range(s, s + 4)), None) for s in (2, 6, 10, 14)] + \
              [([18, 19, 20, 21], {'r', 'k', 'v', 'lo', 'kv'})] + [(list(range(s, s + 4)), {'k', 'v', 'lo'}) for s in (22, 26, 30)]
    if stop_after >= 1:
        with contextlib.ExitStack() as st:
            xt = [sb(st, "xt%d" % i, [128, D]) for i in range(2)]
            junk = sb(st, "junk", [128, D], BF16)
            ssq = [sb(st, "ssq%d" % i, [128, 1]) for i in range(2)]
            xs = [sb(st, "xs%d" % i, [128, D]) for i in range(2)]
            xn = [sb(st, "xn%d" % i, [128, D], BF16) for i in range(2)]
            xnT = [sb(st, "xnT%d" % i, [128, 16, 512], BF16) for i in range(2)]
            wb = [sb(st, "wb%d" % i, [128, 16, 512], BF16) for i in range(3)]
            stg = [sb(st, "stg%d" % i, [128, 512]) for i in range(4)]
            pT = [ps(st, "pT%d" % i, [128, 1024], BF16) for i in range(2)]
            pp = [ps(st, "pp%d" % i, [128, 512]) for i in range(4)]
            winv = w_in.rearrange("(j p) n -> p j n", p=128)
            wi = 0; si = 0; ti = 0
            for bi, (tiles, need) in enumerate(tblocks):
                xT = xnT[bi % 2]
                for tl, i in enumerate(tiles):
                    v = 1 if i < 2 else 0
                    x_ = xt[ti % 2]; sq_ = ssq[ti % 2]; xs_ = xs[ti % 2]; xn_ = xn[ti % 2]; ti += 1
                    dma("act", x_[:], xin[i * 128:(i + 1) * 128, :], writes=[x_])
                    memset("dve", sq_[:], 0.0, [sq_])
                    act(junk[:], x_[:], AF.Square, [x_, sq_], [junk, sq_], accum=sq_[:, 0:1])
                    act(sq_[:], sq_[:], AF.Sqrt, [sq_], [sq_], scale=1.0 / D, bias=1e-6)
                    recip(sq_[:], sq_[:], [sq_], [sq_])
                    stt("dve", xs_[:], x_[:], sq_[:, 0:1], Abc[v][:], ALU.mult, ALU.mult, [x_, sq_, Abc[v]], [xs_])
                    tt("pool", xn_[:], xs_[:], Bbc[v][:], ALU.add, [xs_, Bbc[v]], [xn_])
                    for half in range(2):
                        p_ = pT[half]
                        for jj in range(8):
                            j = half * 8 + jj
                            tr(p_[:, jj * 128:(jj + 1) * 128], xn_[:, j * 128:(j + 1) * 128], identb[:], [xn_, identb], [p_])
                        cp("act" if half == 0 else "dve", xT[:, half * 8:(half + 1) * 8, tl * 128:(tl + 1) * 128],
                           p_[:].rearrange("p (j t) -> p j t", t=128), [p_], [xT])
                for (c0, cw, grp) in blocks:
                    if need is not None and grp not in need:
                        continue
                    w = wb[wi % 3]; wi += 1
                    dma("pool", w[:, :, 0:cw], winv[:, :, c0:c0 + cw], writes=[w])
                    for tl, i in enumerate(tiles):
                        p_ = pp[si % 4]; s_ = stg[si % 4]; si += 1
                        for j in range(16):
                            mm(p_[:, 0:cw], xT[:, j, tl * 128:(tl + 1) * 128], w[:, j, 0:cw], j == 0, j == 15, [xT, w], [p_])
                        cp("act" if si % 2 == 0 else "dve", s_[:, 0:cw], p_[:, 0:cw], [p_], [s_])
                        dma("sp", P[i * 128:(i + 1) * 128, c0:c0 + cw], s_[:, 0:cw], reads=[s_], writes=[Ptile[i]])
        S.barrier()

    st01.close()
    GN_EPS = 64e-5

    def bcast_load(eng, dst, src_row):
        return dma(eng, dst[:].rearrange("p (o n) -> p o n", o=1), src_row.partition_broadcast(128), writes=[dst])

    def rwkv_sweep(d):
        tiles = list(range(0, 18)) if d == 0 else [1, 0] + list(range(33, 1, -1))
        import os
        KT_ = int(os.environ.get("KTILES", "99")); KS_ = int(os.environ.get("KSTAGE", "99"))
        tiles = tiles[:KT_]
        with contextlib.ExitStack() as st:
            mubc = sb(st, "mubc", [128, SHC]); bcast_load("sp", mubc, mu[0:1, :])
            w0bc = sb(st, "w0bc", [128, 1024]); bcast_load("sp", w0bc, w0[d:d + 1, :])
            a0bc = sb(st, "a0bc", [128, 1024]); bcast_load("sp", a0bc, a0[d:d + 1, :])
            kkbc = sb(st, "kkbc", [128, 1024]); bcast_load("sp", kkbc, k_k[0:1, :])
            kabc = sb(st, "kabc", [128, 1024]); bcast_load("sp", kabc, k_a[0:1, :])
            rkbc = sb(st, "rkbc", [128, 1024]); bcast_load("sp", rkbc, r_k[0:1, :])
            if d == 1:
                lngbc = sb(st, "lngbc", [128, 1024]); bcast_load("sp", lngbc, ln_g[0:1, :])
                lnbbc = sb(st, "lnbbc", [128, 1024]); bcast_load("sp", lnbbc, ln_b[0:1, :])
            WAb = sb(st, "WAb", [128, 1024], BF16)
            dma("pool", WAb[0:64, :], w2[d], writes=[WAb]); dma("pool", WAb[64:128, :], a2[d], writes=[WAb])
            triI = sb(st, "triI", [128, 128]); dma("sp", triI[:], c_tri[2 * d], writes=[triI])
            triS = sb(st, "triS", [128, 128]); dma("sp", triS[:], c_tri[2 * d + 1], writes=[triS])
            negc = sb(st, "negc", [128, 1]); memset("dve", negc[:], -CDEC, [negc])
            maskx = sb(st, "maskx", [128, 448], BF16); dma("pool", maskx[:], c_maskx[d], writes=[maskx])
            masky = sb(st, "masky", [128, 256], BF16); dma("pool", masky[:], c_masky[d], writes=[masky])
            shm = sb(st, "shm", [128, 128], BF16); dma("pool", shm[:], c_sh[:, :], writes=[shm])
            e2 = sb(st, "e2", [2, 128], BF16); dma("pool", e2[:], c_e[:, :], writes=[e2])
            CM = sb(st, "CM", [128, 8, 576], BF16)
            for pr in range(8):
                dma("pool", CM[:, pr, 128:192], c_i64[:, :], writes=[CM])
            ST = [sb(st, "ST%d" % h, [64, 2, 64]) for h in range(16)]
            for h in range(16):
                memset("pool", ST[h][:], 0.0, [ST[h]])
            cur = [0] * 16
            sh = sb(st, "sh", [128, SHC]); pc16 = sb(st, "pc16", [128, SHC], BF16); nb16 = sb(st, "nb16", [2, SHC], BF16)
            lo16 = sb(st, "lo16", [128, 128], BF16); loT = sb(st, "loT", [128, 128], BF16)
            sg = sb(st, "sg", [128, 1024]); al = sb(st, "al", [128, 1024]); kk = sb(st, "kk", [128, 1024])
            bb = sb(st, "bb", [128, 1024]); kd = sb(st, "kd", [128, 1024]); t1 = sb(st, "t1", [128, 1024])
            E = [sb(st, "E%d" % k, [128, 1024]) for k in range(2)]
            ss = sb(st, "ss", [128, 16]); bon = sb(st, "bon", [128, 16]); WC = sb(st, "WC", [64, 16])
            rt, kt, bt, at, ktp, btp, vb = [sb(st, n, [128, 1024], BF16) for n in ("rt", "kt", "bt", "at", "ktp", "btp", "vb")]
            X0 = [sb(st, "X0%d" % p_, [128, 448], BF16) for p_ in range(8)]
            TT = [sb(st, "TT%d" % p_, [128, 256], BF16) for p_ in range(8)]
            CC = [sb(st, "CC%d" % p_, [128, 256], BF16) for p_ in range(8)]
            tmpT = [sb(st, "tmpT%d" % p_, [128, 256], BF16) for p_ in range(4)]
            Zf = [sb(st, "Zf%d" % p_, [128, 192], BF16) for p_ in range(8)]
            msk = sb(st, "msk", [128, 7, 256], U8); dma("pool", msk[:], c_ms[d].rearrange("s p c -> p s c"), writes=[msk])
            II = sb(st, "II", [128, 256], BF16)
            cp("pool", II[:, 0:128], identb[:], [identb], [II]); cp("pool", II[:, 128:256], identb[:], [identb], [II])
            ARBK = [sb(st, "ARBK%d" % p_, [128, 256], BF16) for p_ in range(4)]
            QG = [sb(st, "QG%d" % p_, [64, 192], BF16) for p_ in range(4)]
            STb = [sb(st, "STb%d" % h, [64, 64], BF16) for h in range(16)]
            tmpS = [sb(st, "tmpS%d" % h, [64, 64]) for h in range(4)]
            tmpP = [sb(st, "tmpP%d" % h, [64, 64]) for h in range(4)]
            for h in range(16):
                memset("pool", STb[h][:], 0.0, [STb[h]])
            MYH = [sb(st, "MYH%d" % p_, [128, 192], BF16) for p_ in range(4)]
            ysb = sb(st, "ysb", [128, 1024])
            if d == 1:
                ya = sb(st, "ya", [128, 1024]); zr = sb(st, "zr", [128, 1024])
                mixb = sb(st, "mixb", [128, 1024], BF16); mtT = sb(st, "mtT", [128, 8, 128], BF16)
                s1 = sb(st, "s1", [128, 16]); s2 = sb(st, "s2", [128, 16]); mean = sb(st, "mean", [128, 16]); m2 = sb(st, "m2", [128, 16])
            Wd = [ps(st, "Wd%d" % k, [128, 512]) for k in range(2)]
            psT = ps(st, "psT", [128, 1024], BF16)
            Xp = [ps(st, "Xp%d" % k, [128, 512]) for k in range(2)]
            LB = [Xp[0], Xp[1], Wd[0], Wd[1]]
            b5 = ps(st, "b5", [128, 512])
            B6 = ps(st, "B6", [128, 512])
            b7 = ps(st, "b7", [128, 512])
            FB = [b5, b5, b7, b7]
            _ft5 = T(b5.t); _ft7 = T(b7.t)
            FT = [_ft5, _ft5, _ft7, _ft7]
            wk = 0
            def issue_p16(i_):
                own_ = OWN0 <= i_ < OWN1
                c0_ = 0 if own_ else 1024
                r0_ = i_ * 128
                dma("pool", pc16[:, c0_:SHC], P[r0_:r0_ + 128, c0_:SHC], reads=[Ptile[i_]], writes=[pc16])
                memset("pool", nb16[:], 0.0, [nb16])
                if i_ not in (0, 2):
                    dma("pool", nb16[0:1, c0_:SHC], P[r0_ - 1:r0_, c0_:SHC], reads=[Ptile[i_ - 1]], writes=[nb16])
                if i_ not in (1, 33):
                    dma("pool", nb16[1:2, c0_:SHC], P[r0_ + 128:r0_ + 129, c0_:SHC], reads=[Ptile[i_ + 1]], writes=[nb16])

            for ti_, i in enumerate(tiles):
                own = OWN0 <= i < OWN1
                c0 = 0 if own else 1024
                r0 = i * 128
                has_prev = i not in (0, 2); has_next = i not in (1, 33)
                dma("sp", sh[:, c0:SHC], P[r0:r0 + 128, c0:SHC], reads=[Ptile[i]], writes=[sh])
                if ti_ == 0:
                    issue_p16(i)
                if d == 1 and own:
                    dma("sp", ya[:], YA[(i - 2) * 128:(i - 1) * 128, :], reads=[YAb], writes=[ya])
                    dma("sp", zr[:], P[r0:r0 + 128, C_ZR:C_ZR + 1024], reads=[Ptile[i]], writes=[zr])
                cs = c0
                while cs < SHC:
                    cw = min(512, SHC - cs)
                    W = Wd[wk % 2]; tm_ = E[wk % 2]; wk += 1
                    mm(W[:, 0:cw], shm[:], pc16[:, cs:cs + cw], True, False, [shm, pc16], [W])
                    mm(W[:, 0:cw], e2[0:2, :], nb16[0:2, cs:cs + cw], False, True, [e2, nb16], [W])
                    tt("dve", tm_[:, 0:cw], W[:, 0:cw], mubc[:, cs:cs + cw], ALU.mult, [W, mubc], [tm_])
                    tt("pool", sh[:, cs:cs + cw], tm_[:, 0:cw], sh[:, cs:cs + cw], ALU.add, [tm_, sh], [sh])
                    cs += cw
                if ti_ + 1 < len(tiles):
                    issue_p16(tiles[ti_ + 1])
                r_ = sh[:, 0:1024]; k_ = sh[:, 1024:2048]; v_ = sh[:, 2048:3072]
                if KS_ <= 1:
                    continue
                act(lo16[:, 0:64], sh[:, 3072 + 64 * d:3136 + 64 * d], AF.Tanh, [sh], [lo16])
                cp("dve", lo16[:, 64:128], sh[:, 3200 + 64 * d:3264 + 64 * d], [sh], [lo16])
                tr(psT[:, 0:128], lo16[:, :], identb[:], [lo16, identb], [psT])
                cp("dve", loT[:], psT[:, 0:128], [psT], [loT])
                for (lo_, bias_, dst_) in ((0, w0bc, sg), (64, a0bc, al)):
                    for hf in range(2):
                        W = Wd[wk % 2]; wk += 1
                        mm(W[:], loT[lo_:lo_ + 64, :], WAb[lo_:lo_ + 64, hf * 512:(hf + 1) * 512], True, True, [loT, WAb], [W])
                        tt("dve", t1[:, hf * 512:(hf + 1) * 512], W[:], bias_[:, hf * 512:(hf + 1) * 512], ALU.add, [W, bias_], [t1])
                    act(dst_[:], t1[:], AF.Sigmoid, [t1], [dst_])
                tt("pool", kk[:], k_, kkbc[:], ALU.mult, [sh, kkbc], [kk])
                tt("dve", t1[:], kk[:], kk[:], ALU.mult, [kk], [t1])
                red("dve", ss[:], t1[:].rearrange("p (a b) -> p a b", b=64), [t1], [ss])
                act(ss[:], ss[:], AF.Sqrt, [ss], [ss])
                ts("dve", ss[:], ss[:], 1e-12, None, ALU.max, None, [ss], [ss])
                recip(ss[:], ss[:], [ss], [ss])
                tt("dve", kk[:].rearrange("p (a b) -> p a b", b=64), kk[:].rearrange("p (a b) -> p a b", b=64),
                   ss[:].unsqueeze(2).to_broadcast([128, 16, 64]), ALU.mult, [kk, ss], [kk])
                tt("pool", bb[:], kk[:], al[:], ALU.mult, [kk, al], [bb])
                stt("dve", t1[:], al[:], -1.0, kabc[:], ALU.add, ALU.mult, [al, kabc], [t1])
                stt("dve", kd[:], t1[:], 1.0, k_, ALU.add, ALU.mult, [t1, sh], [kd])
                if own:
                    tt("dve", t1[:], r_, rkbc[:], ALU.mult, [sh, rkbc], [t1])
                    tt("dve", t1[:], t1[:], kd[:], ALU.mult, [t1, kd], [t1])
                    if d == 0:
                        red("dve", bonA[:, i - 2, :], t1[:].rearrange("p (a b) -> p a b", b=64), [t1], [bonA])
                    else:
                        red("dve", bon[:], t1[:].rearrange("p (a b) -> p a b", b=64), [t1], [bon])
                        tt("dve", bon[:], bon[:], bonA[:, i - 2, :], ALU.add, [bon, bonA], [bon])
                if "dbg_sh" in dbg and d == 0 and i == 2:
                    dma("sp", dbg["dbg_sh"][0:128, :], sh[:], reads=[sh])
                    for k_i, t_ in enumerate([sg, al, kk, kd]):
                        dma("sp", dbg["dbg_sh"][128 * (k_i + 1):128 * (k_i + 2), 0:1024], t_[:], reads=[t_])
                for hf in range(2):
                    hs = slice(hf * 512, (hf + 1) * 512)
                    W = Wd[wk % 2]; wk += 1
                    mm(W[:], triI[:], sg[:, hs], True, True, [triI, sg], [W])
                    if own:
                        act(E[0][:, hs], W[:], AF.Exp, [W], [E[0]])
                    act(E[1][:, hs], W[:], AF.Exp, [W], [E[1]], scale=-1.0)
                    stt("dve", t1[:, hs], sg[:, hs], CDEC, W[:], ALU.mult, ALU.add, [sg, W], [t1])
                if own:
                    tt("pool", rt[:], r_, E[0][:], ALU.mult, [sh, E[0]], [rt])
                tt("dve", kt[:], kd[:], E[1][:], ALU.mult, [kd, E[1]], [kt])
                tt("dve", bt[:], bb[:], E[1][:], ALU.mult, [bb, E[1]], [bt])
                act(E[0][:], t1[:], AF.Exp, [t1], [E[0]])
                stt("dve", at[:], kk[:], -1.0, E[0][:], ALU.mult, ALU.mult, [kk, E[0]], [at])
                for hf in range(2):
                    hs = slice(hf * 512, (hf + 1) * 512)
                    W = Wd[wk % 2]; wk += 1
                    mm(W[:], triS[:], sg[:, hs], True, True, [triS, sg], [W])
                    act(E[1][:, hs], W[:], AF.Exp, [W], [E[1]])
                tt("pool", ktp[:], kd[:], E[1][:], ALU.mult, [kd, E[1]], [ktp])
                tt("dve", btp[:], bb[:], E[1][:], ALU.mult, [bb, E[1]], [btp])
                cp("act", vb[:], v_, [sh], [vb])
                for h in range(16):
                    mm(Wd[0][0:64, h:h + 1], sg[:, h * 64:(h + 1) * 64], negc[:, 0:1], True, True, [sg, negc], [Wd[0]])
                act(WC[:], Wd[0][0:64, 0:16], AF.Exp, [Wd[0]], [WC])
                if KS_ <= 2:
                    continue
                DD = (d == 0 and i in (0, 2))
                if DD:
                    for nm_, t_ in (("bt", bt), ("kt", kt), ("at", at), ("rt", rt), ("btp", btp), ("ktp", ktp), ("vb", vb)):
                        dd("%s_%d" % (nm_, i), t_, t_[:], [128, 1024])
                    dd("WC_%d" % i, WC, WC[:], [64, 16])
                srcs = [(bt, 0), (kt, 192), (at, 320)] + ([(rt, 448)] if own else [])
                for si_, (src, off) in enumerate(srcs):
                    for pr in range(8):
                        tr(psT[:, pr * 128:(pr + 1) * 128], src[:, pr * 128:(pr + 1) * 128], identb[:], [src, identb], [psT])
                    cp("act" if si_ % 2 == 0 else "dve", CM[:, :, off:off + 128], psT[:].rearrange("p (a b) -> p a b", b=128), [psT], [CM])
                if DD:
                    dd("CM_%d" % i, CM, CM[:, 0, :], [128, 576])
                if KS_ <= 3:
                    continue
                for grp in range(2):
                    hs4 = [(g, 8 * grp + g, (8 * grp + g) // 2, 64 * ((8 * grp + g) % 2)) for g in range(8)]
                    for (g, h, pr, pb) in hs4:
                        lb = LB[g % 4]
                        mm(lb[:, 128:448], CM[pb:pb + 64, pr, 320:448], CM[pb:pb + 64, pr, 0:320], True, True, [CM], [lb])
                        mm(lb[:, 0:128], CM[pb:pb + 64, pr, 0:128], CM[pb:pb + 64, pr, 320:448], True, True, [CM], [lb])
                        cp("pool", TT[g][:], II[:], [II], [TT[g]])
                        S.op("dve", lambda e, g=g, lb=lb: e.copy_predicated(out=TT[g][:], mask=msk[:, 0, :], data=lb[:, 0:256]),
                             reads=[lb, msk, TT[g]], writes=[TT[g]])
                        tt("dve", X0[g][:], lb[:, 0:448], maskx[:], ALU.mult, [lb, maskx], [X0[g]])
                        if DD and h < 2:
                            dd("X0_%d_%d" % (i, h), X0[g], X0[g][:], [128, 448])
                    if KS_ <= 4:
                        continue
                    for lev in range(1, 7):
                        for (g, h, pr, pb) in hs4:
                            lb = LB[g % 4]
                            mm(lb[:, 0:128], X0[g][:, 128:256], TT[g][:, 0:128], True, True, [X0[g], TT[g]], [lb])
                            mm(lb[:, 128:256], X0[g][:, 0:128], TT[g][:, 128:256], True, True, [X0[g], TT[g]], [lb])
                            cp("act", CC[g][:], lb[:, 0:256], [lb], [CC[g]])
                        for (g, h, pr, pb) in hs4:
                            lb = LB[g % 4]
                            mm(lb[:, 256:384], TT[g][:, 128:256], CC[g][:, 0:128], True, True, [TT[g], CC[g]], [lb])
                            mm(lb[:, 384:512], TT[g][:, 0:128], CC[g][:, 128:256], True, True, [TT[g], CC[g]], [lb])
                            S.op("dve", lambda e, g=g, lb=lb, lev=lev: e.copy_predicated(out=TT[g][:], mask=msk[:, lev, :], data=lb[:, 256:512]),
                                 reads=[lb, msk, TT[g]], writes=[TT[g]])
                    for (g, h, pr, pb) in hs4:
                        lb = LB[g % 4]
                        mm(lb[:, 0:192], TT[g][:, 0:128], X0[g][:, 256:448], True, True, [TT[g], X0[g]], [lb])
                        cp("dve" if g % 2 == 0 else "act", Zf[g][:], lb[:, 0:192], [lb], [Zf[g]])
                        if DD and h < 2:
                            dd("Zf_%d_%d" % (i, h), Zf[g], Zf[g][:], [128, 192])
                            dd("TT_%d_%d" % (i, h), TT[g], TT[g][:], [128, 256])
                    for sub in range(2):
                        hsub = hs4[4 * sub:4 * sub + 4]
                        if own:
                            for (g, h, pr, pb) in hsub:
                                lb = LB[g % 4]
                                mm(lb[:, 0:128], CM[pb:pb + 64, pr, 0:128], CM[pb:pb + 64, pr, 448:576], True, True, [CM], [lb])
                                mm(lb[:, 128:256], CM[pb:pb + 64, pr, 192:320], CM[pb:pb + 64, pr, 448:576], True, True, [CM], [lb])
                                tt("dve", ARBK[g % 4][:], lb[:, 0:256], masky[:], ALU.mult, [lb, masky], [ARBK[g % 4]])
                        lo_c = 0 if own else 128
                        for (g, h, pr, pb) in hsub:
                            lb = LB[g % 4]; fb = FB[g % 4]; fo = ((g % 4) % 2) * 256
                            hc = slice(h * 64, (h + 1) * 64)
                            AbT = Zf[g][:, 0:64]; PTt = Zf[g][:, 64:192]
                            if own:
                                mm(fb[0:64, fo:fo + 128], AbT, ARBK[g % 4][:, 0:128], True, False, [Zf[g], ARBK[g % 4]], [FT[g % 4]])
                                mm(fb[0:64, fo:fo + 128], rt[:, hc], identb[:], False, True, [rt, identb], [FT[g % 4]])
                            mm(fb[0:64, fo + 128:fo + 192], AbT, btp[:, hc], True, True, [Zf[g], btp], [FT[g % 4]])
                            cp("act", QG[g % 4][:, lo_c:192], fb[0:64, fo + lo_c:fo + 192], [FT[g % 4]], [QG[g % 4]])
                            if own:
                                mm(lb[:, 256:384], PTt, ARBK[g % 4][:, 0:128], True, False, [Zf[g], ARBK[g % 4]], [lb])
                                mm(lb[:, 256:384], identb[:], ARBK[g % 4][:, 128:256], False, True, [identb, ARBK[g % 4]], [lb])
                            mm(lb[:, 384:448], PTt, btp[:, hc], True, False, [Zf[g], btp], [lb])
                            mm(lb[:, 384:448], identb[:], ktp[:, hc], False, True, [identb, ktp], [lb])
                            cp("dve", MYH[g % 4][:, lo_c:192], lb[:, 256 + lo_c:448], [lb], [MYH[g % 4]])
                            if DD and h < 2:
                                dd("QG_%d_%d" % (i, h), QG[g % 4], QG[g % 4][:], [64, 192])
                                dd("MYH_%d_%d" % (i, h), MYH[g % 4], MYH[g % 4][:], [128, 192])
                                if own:
                                    dd("ARBK_%d_%d" % (i, h), ARBK[g % 4], ARBK[g % 4][:], [128, 256])
                        for (g, h, pr, pb) in hsub:
                            fb = FB[g % 4]; fo = ((g % 4) % 2) * 256
                            hc = slice(h * 64, (h + 1) * 64)
                            STc = ST[h][:, cur[h], :]; STn = ST[h][:, 1 - cur[h], :]
                            if own:
                                yo = B6[:, (h % 8) * 64:(h % 8) * 64 + 64]
                                mm(yo, QG[g % 4][:, 0:128], STb[h][:], True, False, [QG[g % 4], STb[h]], [B6])
                                mm(yo, MYH[g % 4][:, 0:128], vb[:, hc], False, True, [MYH[g % 4], vb], [B6])
                            sreg = fb[0:64, fo + 192:fo + 256]
                            mm(sreg, QG[g % 4][:, 128:192], STb[h][:], True, False, [QG[g % 4], STb[h]], [FT[g % 4]])
                            mm(sreg, MYH[g % 4][:, 128:192], vb[:, hc], False, True, [MYH[g % 4], vb], [FT[g % 4]])
                            act(tmpS[g % 4][:], STc, AF.Copy, [ST[h], WC], [tmpS[g % 4]], scale=WC[:, h:h + 1])
                            act(tmpP[g % 4][:], sreg, AF.Copy, [FT[g % 4]], [tmpP[g % 4]])
                            tt("pool", STn, tmpP[g % 4][:], tmpS[g % 4][:], ALU.add, [tmpP[g % 4], tmpS[g % 4]], [ST[h]])
                            cp("pool", STb[h][:], STn, [ST[h]], [STb[h]])
                            if DD and h < 2:
                                dd("ST_%d_%d" % (i, h), ST[h], STn, [64, 64])
                            cur[h] ^= 1
                    if own:
                        hs = slice(grp * 512, grp * 512 + 512)
                        if d == 0:
                            cp("act", ysb[:, hs], B6[:], [B6], [ysb])
                        else:
                            tt("dve", ysb[:, hs], B6[:], ya[:, hs], ALU.add, [B6, ya], [ysb])
                if not own:
                    continue
                if d == 0:
                    if DD:
                        dd("ysb_%d" % i, ysb, ysb[:], [128, 1024])
                    dma("sp", YA[(i - 2) * 128:(i - 1) * 128, :], ysb[:], reads=[ysb], writes=[YAb])
                    continue
                if "dbg_y" in dbg:
                    dma("sp", dbg["dbg_y"][(i - 2) * 128:(i - 1) * 128, :], ysb[:], reads=[ysb])
                y3 = ysb[:].rearrange("p (a b) -> p a b", b=64)
                red("dve", s1[:], y3, [ysb], [s1])
                tt("pool", t1[:], ysb[:], ysb[:], ALU.mult, [ysb], [t1])
                red("dve", s2[:], t1[:].rearrange("p (a b) -> p a b", b=64), [t1], [s2])
                ts("dve", mean[:], s1[:], 1.0 / 64, None, ALU.mult, None, [s1], [mean])
                tt("dve", m2[:], mean[:], mean[:], ALU.mult, [mean], [m2])
                stt("dve", s2[:], s2[:], 1.0 / 64, m2[:], ALU.mult, ALU.subtract, [s2, m2], [s2])
                act(s2[:], s2[:], AF.Sqrt, [s2], [s2], bias=GN_EPS)
                recip(s2[:], s2[:], [s2], [s2])
                tt("dve", y3, y3, mean[:].unsqueeze(2).to_broadcast([128, 16, 64]), ALU.subtract, [ysb, mean], [ysb])
                tt("dve", y3, y3, s2[:].unsqueeze(2).to_broadcast([128, 16, 64]), ALU.mult, [ysb, s2], [ysb])
                tt("pool", ysb[:], ysb[:], lngbc[:], ALU.mult, [ysb, lngbc], [ysb])
                tt("pool", ysb[:], ysb[:], lnbbc[:], ALU.add, [ysb, lnbbc], [ysb])
                tt("dve", t1[:].rearrange("p (a b) -> p a b", b=64), sh[:, 2048:3072].rearrange("p (a b) -> p a b", b=64),
                   bon[:].unsqueeze(2).to_broadcast([128, 16, 64]), ALU.mult, [sh, bon], [t1])
                tt("dve", ysb[:], ysb[:], t1[:], ALU.add, [ysb, t1], [ysb])
                if "dbg_rw" in dbg:
                    dma("sp", dbg["dbg_rw"][(i - 2) * 128:(i - 1) * 128, :], ysb[:], reads=[ysb])
                act(zr[:], zr[:], AF.Silu, [zr], [zr])
                tt("pool", mixb[:], ysb[:], zr[:], ALU.mult, [ysb, zr], [mixb])
                for cch in range(8):
                    tr(psT[:, cch * 128:(cch + 1) * 128], mixb[:, cch * 128:(cch + 1) * 128], identb[:], [mixb, identb], [psT])
                cp("act", mtT[:], psT[:].rearrange("p (a b) -> p a b", b=128), [psT], [mtT])
                dma("sp", MT[0:8, :, (i - 2) * 128:(i - 1) * 128].rearrange("c p t -> p c t"), mtT[:], reads=[mtT], writes=[MTb])
        S.barrier()

    if stop_after >= 2:
        rwkv_sweep(0)
    if stop_after >= 3:
        rwkv_sweep(1)

    def rope_apply(eng2, t3, rot3, rp, nh, reads_t, T_t, T_rot, T_rp):
        t5 = t3.rearrange("p h (a b c) -> p h a b c", a=2, b=2)
        r5 = rot3.rearrange("p h (a b c) -> p h a b c", a=2, b=2)
        cp("pool", r5[:, :, :, 0, :], t5[:, :, :, 1, :], [T_t], [T_rot])
        cp("pool", r5[:, :, :, 1, :], t5[:, :, :, 0, :], [T_t], [T_rot])
        cosb = rp[:, 0:64].unsqueeze(1).to_broadcast([128, nh, 64])
        sinb = rp[:, 64:128].unsqueeze(1).to_broadcast([128, nh, 64])
        tt("dve", t3, t3, cosb, ALU.mult, [T_t, T_rp], [T_t])
        tt("dve", rot3, rot3, sinb, ALU.mult, [T_rot, T_rp], [T_rot])
        tt("dve", t3, t3, rot3, ALU.add, [T_t, T_rot], [T_t])

    def attention():
        with contextlib.ExitStack() as st:
            kgbc = sb(st, "kgbc", [128, 64]); bcast_load("sp", kgbc, kg[0:1, :])
            qgbc = sb(st, "qgbc", [128, 64]); bcast_load("sp", qgbc, qg[0:1, :])
            ts("dve", qgbc[:], qgbc[:], 0.125, None, ALU.mult, None, [qgbc], [qgbc])
            esk = sb(st, "esk", [128, 16]); bcast_load("sp", esk, sink[0:1, :])
            act(esk[:], esk[:], AF.Exp, [esk], [esk])
            amask = sb(st, "amask", [128, 2, 128], BF16)
            dma("pool", amask[:], c_amask.rearrange("a k q -> k a q"), writes=[amask])
            KT = sb(st, "KT", [64, 19, 4, 128], BF16)
            V1 = sb(st, "V1", [128, 19, 4, 65], BF16)
            memset("pool", V1[:], 1.0, [V1])
            kv = sb(st, "kv", [128, 512]); rot = sb(st, "rot", [128, 1024]); rp = sb(st, "rp", [128, 128])
            sq = sb(st, "sq", [128, 1024]); ss = sb(st, "ssq_a", [128, 16])
            kb = sb(st, "kb", [128, 256], BF16)
            q = sb(st, "q", [128, 1024]); za = sb(st, "za", [128, 1024]); qb = sb(st, "qb", [128, 1024], BF16)
            QT = sb(st, "QT", [64, 16, 128], BF16)
            PTs = [sb(st, "PTs%d" % k, [128, 512], BF16) for k in range(5)]
            att = sb(st, "att", [128, 1024]); den = sb(st, "den", [128, 16])
            mixb = sb(st, "mixb_a", [128, 1024], BF16); mtT = sb(st, "mtT_a", [128, 8, 128], BF16)
            psT = ps(st, "psT_a", [128, 1024], BF16)
            Sps = [ps(st, "Sps%d" % k, [128, 512]) for k in range(2)]
            Og = [ps(st, "Og%d" % k, [128, 512]) for k in range(4)]
            for n in range(19):
                i = n
                dma("sp", kv[:], P[i * 128:(i + 1) * 128, C_KA:C_KA + 512], reads=[Ptile[i]], writes=[kv])
                k3 = kv[:, 0:256].rearrange("p (h c) -> p h c", c=64)
                tt("dve", sq[:, 0:256], kv[:, 0:256], kv[:, 0:256], ALU.mult, [kv], [sq])
                red("dve", ss[:, 0:4], sq[:, 0:256].rearrange("p (h c) -> p h c", c=64), [sq], [ss])
                act(ss[:, 0:4], ss[:, 0:4], AF.Sqrt, [ss], [ss], scale=1.0 / 64, bias=1e-6)
                recip(ss[:, 0:4], ss[:, 0:4], [ss], [ss])
                tt("dve", k3, k3, ss[:, 0:4].unsqueeze(2).to_broadcast([128, 4, 64]), ALU.mult, [kv, ss], [kv])
                tt("dve", k3, k3, kgbc[:].unsqueeze(1).to_broadcast([128, 4, 64]), ALU.mult, [kv, kgbc], [kv])
                if i >= 2:
                    dma("sp", rp[:], rope[(i - 2) * 128:(i - 1) * 128, :], writes=[rp])
                    rope_apply(None, k3, rot[:, 0:256].rearrange("p (h c) -> p h c", c=64), rp, 4, None, kv, rot, rp)
                cp("act", kb[:], kv[:, 0:256], [kv], [kb])
                for g in range(4):
                    tr(psT[0:64, g * 128:(g + 1) * 128], kb[:, g * 64:(g + 1) * 64], identb[:], [kb, identb], [psT])
                cp("dve", KT[:, n, :, :], psT[0:64, 0:512].rearrange("p (g t) -> p g t", t=128), [psT], [KT])
                cp("act", V1[:, n, :, 0:64], kv[:, 256:512].rearrange("p (h c) -> p h c", c=64), [kv], [V1])
            sk = 0
            for i in range(OWN0, OWN1):
                r0 = i * 128
                dma("sp", q[:], P[r0:r0 + 128, C_Q:C_Q + 1024], reads=[Ptile[i]], writes=[q])
                dma("sp", za[:], P[r0:r0 + 128, C_ZA:C_ZA + 1024], reads=[Ptile[i]], writes=[za])
                dma("sp", rp[:], rope[(i - 2) * 128:(i - 1) * 128, :], writes=[rp])
                q3 = q[:].rearrange("p (h c) -> p h c", c=64)
                tt("pool", sq[:], q[:], q[:], ALU.mult, [q], [sq])
                red("dve", ss[:], sq[:].rearrange("p (h c) -> p h c", c=64), [sq], [ss])
                act(ss[:], ss[:], AF.Sqrt, [ss], [ss], scale=1.0 / 64, bias=1e-6)
                recip(ss[:], ss[:], [ss], [ss])
                tt("dve", q3, q3, ss[:].unsqueeze(2).to_broadcast([128, 16, 64]), ALU.mult, [q, ss], [q])
                tt("dve", q3, q3, qgbc[:].unsqueeze(1).to_broadcast([128, 16, 64]), ALU.mult, [q, qgbc], [q])
                rope_apply(None, q3, rot[:].rearrange("p (h c) -> p h c", c=64), rp, 16, None, q, rot, rp)
                cp("act", qb[:], q[:], [q], [qb])
                for half in range(2):
                    for hx in range(8):
                        hd = half * 8 + hx
                        tr(psT[0:64, hx * 128:(hx + 1) * 128], qb[:, hd * 64:(hd + 1) * 64], identb[:], [qb, identb], [psT])
                    cp("dve", QT[:, half * 8:(half + 1) * 8, :], psT[0:64, :].rearrange("p (g t) -> p g t", t=128), [psT], [QT])
                for g in range(4):
                    keyt = ([i - 1] if i > OWN0 else []) + [i, i + 1, 0, 1]
                    for ki, kt_ in enumerate(keyt):
                        sp_ = Sps[sk % 2]; sk += 1
                        pt_ = PTs[ki]
                        mm(sp_[:], KT[:, kt_, g, :], QT[:, 4 * g:4 * g + 4, :].rearrange("p a b -> p (a b)"), True, True, [KT, QT], [sp_])
                        act(pt_[:], sp_[:], AF.Exp, [sp_], [pt_])
                        is_prev = (i > OWN0 and ki == 0); is_next = (ki == (2 if i > OWN0 else 1))
                        if is_prev or is_next:
                            mi = 0 if is_prev else 1
                            for hx in range(4):
                                tt("pool", pt_[:, hx * 128:(hx + 1) * 128], pt_[:, hx * 128:(hx + 1) * 128], amask[:, mi, :], ALU.mult, [pt_, amask], [pt_])
                    for hx in range(4):
                        for ki, kt_ in enumerate(keyt):
                            mm(Og[g][:, hx * 65:(hx + 1) * 65], PTs[ki][:, hx * 128:(hx + 1) * 128], V1[:, kt_, g, :],
                               ki == 0, ki == len(keyt) - 1, [PTs[ki], V1], [Og[g]])
                for g in range(4):
                    o3 = Og[g][:, 0:260].rearrange("p (h c) -> p h c", c=65)
                    tt("dve", den[:, 4 * g:4 * g + 4], o3[:, :, 64], esk[:, 4 * g:4 * g + 4], ALU.add, [Og[g], esk], [den])
                    recip(den[:, 4 * g:4 * g + 4], den[:, 4 * g:4 * g + 4], [den], [den])
                    tt("dve", att[:, g * 256:(g + 1) * 256].rearrange("p (h c) -> p h c", c=64), o3[:, :, 0:64],
                       den[:, 4 * g:4 * g + 4].unsqueeze(2).to_broadcast([128, 4, 64]), ALU.mult, [Og[g], den], [att])
                if "dbg_att" in dbg:
                    dma("sp", dbg["dbg_att"][(i - 2) * 128:(i - 1) * 128, :], att[:], reads=[att])
                act(za[:], za[:], AF.Silu, [za], [za])
                tt("pool", mixb[:], att[:], za[:], ALU.mult, [att, za], [mixb])
                for cch in range(8):
                    tr(psT[:, cch * 128:(cch + 1) * 128], mixb[:, cch * 128:(cch + 1) * 128], identb[:], [mixb, identb], [psT])
                cp("act", mtT[:], psT[:].rearrange("p (a b) -> p a b", b=128), [psT], [mtT])
                dma("sp", MT[8:16, :, (i - 2) * 128:(i - 1) * 128].rearrange("c p t -> p c t"), mtT[:], reads=[mtT], writes=[MTb])
        S.barrier()

    def outproj():
        with contextlib.ExitStack() as st:
            wo = sb(st, "wo", [128, 16, D], BF16)
            wov = w_out.rearrange("(j p) n -> p j n", p=128)
            for k4 in range(4):
                dma("pool", wo[:, k4 * 4:(k4 + 1) * 4, :], wov[:, k4 * 4:(k4 + 1) * 4, :], writes=[wo])
            gate = sb(st, "gate", [128, D]); dma("sp", gate[:], GATE[:, :], reads=[GATEb], writes=[gate])
            mt = [sb(st, "mt%d" % k, [128, 16, 128], BF16) for k in range(2)]
            xo = [sb(st, "xo%d" % k, [128, D]) for k in range(2)]
            ot = [sb(st, "ot%d" % k, [128, D]) for k in range(2)]
            pp = [ps(st, "ppo%d" % k, [128, 512]) for k in range(4)]
            for tl in range(16):
                i = tl + 2
                m_ = mt[tl % 2]; x_ = xo[tl % 2]; o_ = ot[tl % 2]
                dma("sp", m_[:], MT[:, :, tl * 128:(tl + 1) * 128].rearrange("c p t -> p c t"), reads=[MTb], writes=[m_])
                dma("act", x_[:], xin[i * 128:(i + 1) * 128, :], writes=[x_])
                for nb in range(4):
                    ns = slice(nb * 512, (nb + 1) * 512)
                    p_ = pp[nb]
                    for cc_ in range(16):
                        mm(p_[:], m_[:, cc_, :], wo[:, cc_, ns], cc_ == 0, cc_ == 15, [m_, wo], [p_])
                    tt("dve", o_[:, ns], p_[:], gate[:, ns], ALU.mult, [p_, gate], [o_])
                    tt("pool", o_[:, ns], o_[:, ns], x_[:, ns], ALU.add, [o_, x_], [o_])
                dma("sp", out[tl * 128:(tl + 1) * 128, :], o_[:], reads=[o_])

    if stop_after >= 4:
        attention()
    if stop_after >= 5:
        outproj()

    S.emit(final_wait_ops=[i for i, o in enumerate(S.ops) if o.is_dma])
    top.close()
    return nc


def host_consts():
    c = {}
    c["c_ident"] = np.eye(128, dtype=np.float32)
    i = np.arange(128)
    triA_incl = (i[:, None] <= i[None, :]).astype(np.float32)
    triA_suf = (i[:, None] > i[None, :]).astype(np.float32)
    triB_incl = (i[:, None] >= i[None, :]).astype(np.float32)
    triB_suf = (i[:, None] < i[None, :]).astype(np.float32)
    c["c_tri"] = (-CDEC * np.stack([triA_incl, triA_suf, triB_incl, triB_suf])).astype(np.float32)
    mx = np.zeros((2, 128, 448), np.float32); my = np.zeros((2, 128, 256), np.float32)
    for d in range(2):
        prec = (i[:, None] < i[None, :]) if d == 0 else (i[:, None] > i[None, :])
        preceq = prec | np.eye(128, dtype=bool)
        mx[d, :, 0:128] = prec; mx[d, :, 128:256] = prec.T; mx[d, :, 256:320] = 1.0; mx[d, :, 320:448] = prec.T
        my[d, :, 0:128] = preceq; my[d, :, 128:256] = preceq
    c["c_maskx"] = mx; c["c_masky"] = my
    cms = np.zeros((2, 7, 128, 256), np.float32)
    for d in range(2):
        prec = (i[:, None] < i[None, :]) if d == 0 else (i[:, None] > i[None, :])
        for s_ in range(7):
            m = prec & ((i[:, None] >> (s_ + 1)) == (i[None, :] >> (s_ + 1))) & ((i[:, None] >> s_) != (i[None, :] >> s_))
            cms[d, s_, :, 0:128] = m; cms[d, s_, :, 128:256] = m.T
    c["c_ms"] = cms
    sh = np.zeros((128, 128), np.float32)
    sh[i, i] = -1.0; sh[i[:-1], i[:-1] + 1] = 0.5; sh[i[1:], i[1:] - 1] = 0.5
    c["c_sh"] = sh
    e = np.zeros((2, 128), np.float32); e[0, 0] = 0.5; e[1, 127] = 0.5
    c["c_e"] = e
    c["c_i64"] = np.concatenate([np.eye(64, dtype=np.float32)] * 2, axis=0)
    am = np.zeros((2, 128, 128), np.float32)
    am[0] = (i[:, None] >= i[None, :]); am[1] = (i[:, None] <= i[None, :])
    c["c_amask"] = am
    return c


def rope_tables(pos):
    pos = np.asarray(pos)
    row = (pos // 64).astype(np.float32); col = (pos % 64).astype(np.float32)
    half = 16
    inv = (10000.0 ** (-np.arange(half, dtype=np.float32) / half)).astype(np.float32)
    ar = row[:, None] * inv; ac = col[:, None] * inv
    ang = np.concatenate([ar, ar, ac, ac], axis=-1)
    cos = np.cos(ang); sin = np.sin(ang)
    sgn = np.concatenate([-np.ones(16), np.ones(16), -np.ones(16), np.ones(16)]).astype(np.float32)
    return np.concatenate([cos, sin * sgn], axis=-1).astype(np.float32)


def core_inputs(inp, b, h, consts):
    f = np.ascontiguousarray
    x = inp["x"][b]; ctx = inp["ctx"][b]
    if h == 1:
        x = x[::-1]; ctx = ctx[::-1]
    m = {}
    m["xin"] = f(np.concatenate([ctx, x], axis=0))
    colf = lambda v: v.reshape(-1, 128).T
    m["cc"] = f(np.concatenate([colf(inp["c"][b]), colf(inp["c_ctx"])], axis=1))
    m["w_ada"] = f(inp["w_ada"][0])
    m["bcol"] = f(np.repeat(colf(inp["b_ada"][0]), 2, axis=1))
    m["ngcol"] = f(colf(inp["norm_g"][0]))
    dA, dB = (0, 1) if h == 0 else (1, 0)
    lw = lambda d: np.arange(3072 + 64 * d, 3072 + 64 * d + 64)
    la = lambda d: np.arange(3200 + 64 * d, 3200 + 64 * d + 64)
    perm = np.concatenate([np.arange(0, 3072), lw(dA), lw(dB), la(dA), la(dB), np.arange(3328, 4352), np.arange(4352, 5376),
                           np.arange(5888, 6912), np.arange(5376, 5632), np.arange(5632, 5888)])
    m["w_in"] = f(inp["w_in"][0][:, perm])
    m["mu"] = f(inp["mu_shift"][0][perm[:SHC]][None, :])
    sel = [dA, dB]
    m["w0"] = f(inp["w0"][0][sel]); m["w2"] = f(inp["w2"][0][sel]); m["a0"] = f(inp["a0"][0][sel]); m["a2"] = f(inp["a2"][0][sel])
    for k in ("k_k", "k_a"):
        m[k] = f(inp[k][0][None, :])
    m["r_k"] = f(inp["r_k"][0].reshape(1, 1024))
    m["ln_g"] = f(inp["ln_x_g"][0][None, :]); m["ln_b"] = f(inp["ln_x_b"][0][None, :])
    m["qg"] = f(inp["q_norm_g"][0][None, :]); m["kg"] = f(inp["k_norm_g"][0][None, :]); m["sink"] = f(inp["sink"][0][None, :])
    m["w_out"] = f(inp["w_out"][0])
    loc = np.arange(17 * 128)
    pos = loc if h == 0 else 4095 - loc
    m["rope"] = rope_tables(pos)
    m.update(consts)
    return m


def kernel(**inputs):
    inp = {k: np.asarray(v) for k, v in inputs.items()}
    consts = host_consts()
    nc = build()
    in_maps = [core_inputs(inp, c // 2, c % 2, consts) for c in range(8)]
    res = run_bass_kernel_spmd(nc, in_maps, core_ids=list(range(8)))
    out = np.empty((4, 4096, D), np.float32)
    for c in range(8):
        b, h = c // 2, c % 2
        o = res.results[c]["out"]
        if h == 0:
            out[b, :2048] = o
        else:
            out[b, 2048:] = o[::-1]
    return out
```

```python
import contextlib
import numpy as np
import concourse.bass as bass
import concourse.mybir as mybir
from concourse.bass_utils import run_bass_kernel_spmd

F32 = mybir.dt.float32
BF16 = mybir.dt.bfloat16
U8 = mybir.dt.uint8
AF = mybir.ActivationFunctionType
ALU = mybir.AluOpType
AX = mybir.AxisListType

ENGS = ["pe", "act", "dve", "pool", "sp"]
CDEC = 0.6065306597126334

D = 2048
NT = 34
OWN0, OWN1 = 2, 18
TL = NT * 128
PC = 6912
C_R, C_K, C_V, C_LO, C_ZR, C_Q, C_ZA, C_KA, C_VA = 0, 1024, 2048, 3072, 3328, 4352, 5376, 6400, 6656
SHC = 3328
DEBUG = {}


class Buf:
    __slots__ = ("last_w", "readers")

    def __init__(self):
        self.last_w = None
        self.readers = []


class T:
    def __init__(self, t):
        self.t = t
        self.b = Buf()

    def __getitem__(self, k):
        return self.t[k]


class Op:
    __slots__ = ("eng", "fn", "deps", "is_dma", "has_dep", "mile", "dsem", "dval")


class Sched:
    def __init__(self, nc, n_dma_sems=32):
        self.nc = nc
        self.ops = []
        self.n_dma_sems = n_dma_sems
        self.last_on = {e: None for e in ENGS}
        self.dmas_since_barrier = []

    def op(self, eng, fn, reads=(), writes=(), dma=False, extra_deps=()):
        o = Op()
        o.eng = eng; o.fn = fn; o.is_dma = dma; o.has_dep = False; o.mile = None; o.dsem = None; o.dval = None
        o.deps = set(extra_deps)
        oid = len(self.ops)
        for t in reads:
            b = t.b
            if b.last_w is not None:
                o.deps.add(b.last_w)
        for t in writes:
            b = t.b
            if b.last_w is not None:
                o.deps.add(b.last_w)
            o.deps.update(b.readers)
        for t in reads:
            t.b.readers.append(oid)
        for t in writes:
            t.b.last_w = oid
            t.b.readers = []
        o.deps.discard(oid)
        self.ops.append(o)
        self.last_on[eng] = oid
        if dma:
            self.dmas_since_barrier.append(oid)
        return oid

    def barrier(self):
        deps = [v for v in self.last_on.values() if v is not None] + list(self.dmas_since_barrier)
        self.dmas_since_barrier = []
        for e in ENGS:
            self.op(e, None, extra_deps=deps)

    def emit(self, final_wait_ops=()):
        nc = self.nc
        ops = self.ops
        for o in ops:
            nd = set()
            for d in o.deps:
                p = ops[d]
                if p.fn is None:
                    if p.eng == o.eng:
                        continue
                    nd.update(p.deps)
                    continue
                if p.eng == o.eng and not p.is_dma and o.eng == "pe" and not o.is_dma:
                    continue
                nd.add(d)
            o.deps = nd
        for o in ops:
            for d in o.deps:
                ops[d].has_dep = True
        for d in final_wait_ops:
            ops[d].has_dep = True
        cnt = {e: 0 for e in ENGS}
        dma_i = 0
        dma_cnt = [0] * self.n_dma_sems
        dma_prev = [None] * self.n_dma_sems
        for i, o in enumerate(ops):
            if o.fn is None:
                continue
            if o.is_dma:
                s = dma_i % self.n_dma_sems
                dma_i += 1
                if dma_prev[s] is not None:
                    o.deps.add(dma_prev[s])
                dma_prev[s] = i
                dma_cnt[s] += 16
                o.dsem = s
                o.dval = dma_cnt[s]
            elif o.has_dep:
                cnt[o.eng] += 1
                o.mile = cnt[o.eng]
        streams = {e: [] for e in ENGS}
        for i, o in enumerate(ops):
            streams[o.eng].append(i)
        with contextlib.ExitStack() as st:
            esem = {e: st.enter_context(nc.semaphore("s_" + e)) for e in ENGS}
            dsem = [st.enter_context(nc.semaphore("d_%d" % k)) for k in range(self.n_dma_sems)]
            block = st.enter_context(nc.Block())

            def run(eng_name, engine):
                seen = {}
                for i in streams[eng_name]:
                    o = ops[i]
                    need = {}
                    for d in o.deps:
                        p = ops[d]
                        if p.is_dma:
                            key = ("d", p.dsem); val = p.dval
                        else:
                            key = ("e", p.eng); val = p.mile
                        if need.get(key, 0) < val:
                            need[key] = val
                    for key, val in need.items():
                        if seen.get(key, 0) >= val:
                            continue
                        seen[key] = val
                        sem = dsem[key[1]] if key[0] == "d" else esem[key[1]]
                        engine.wait_ge(sem, val)
                    if o.fn is None:
                        continue
                    ins = o.fn(engine)
                    if o.is_dma:
                        ins.then_inc(dsem[o.dsem], 16)
                    elif o.mile is not None:
                        ins.then_inc(esem[o.eng], 1)
                if eng_name == "sp":
                    fin = {}
                    for d in final_wait_ops:
                        p = ops[d]
                        fin[p.dsem] = max(fin.get(p.dsem, 0), p.dval)
                    for k_, v_ in fin.items():
                        engine.wait_ge(dsem[k_], v_)

            block.tensor(lambda e: run("pe", e))
            block.scalar(lambda e: run("act", e))
            block.vector(lambda e: run("dve", e))
            block.gpsimd(lambda e: run("pool", e))
            block.sync(lambda e: run("sp", e))


def build(debug_names=(), stop_after=99):
    nc = bass.Bass("TRN2", target_bir_lowering=False)

    def din(name, shape, dt=F32):
        return nc.dram_tensor(name, list(shape), dt, kind="ExternalInput").ap()

    xin = din("xin", [TL, D])
    cc = din("cc", [128, 32])
    w_ada = din("w_ada", [D, 3 * D])
    bcol = din("bcol", [128, 96])
    ngcol = din("ngcol", [128, 16])
    w_in = din("w_in", [D, PC])
    mu = din("mu", [1, SHC])
    w0 = din("w0", [2, 1024]); w2 = din("w2", [2, 64, 1024]); a0 = din("a0", [2, 1024]); a2 = din("a2", [2, 64, 1024])
    k_k = din("k_k", [1, 1024]); k_a = din("k_a", [1, 1024]); r_k = din("r_k", [1, 1024])
    ln_g = din("ln_g", [1, 1024]); ln_b = din("ln_b", [1, 1024])
    qg = din("qg", [1, 64]); kg = din("kg", [1, 64]); sink = din("sink", [1, 16])
    w_out = din("w_out", [D, D])
    rope = din("rope", [17 * 128, 128])
    c_ident = din("c_ident", [128, 128])
    c_tri = din("c_tri", [4, 128, 128])
    c_maskx = din("c_maskx", [2, 128, 448])
    c_masky = din("c_masky", [2, 128, 256])
    c_ms = din("c_ms", [2, 7, 128, 256])
    c_sh = din("c_sh", [128, 128])
    c_e = din("c_e", [2, 128])
    c_i64 = din("c_i64", [128, 64])
    c_amask = din("c_amask", [2, 128, 128])
    out = nc.dram_tensor("out", [2048, D], F32, kind="ExternalOutput").ap()

    def scratch(name, shape, dt):
        kind = "ExternalOutput" if name in debug_names else None
        if kind:
            return nc.dram_tensor(name, list(shape), dt, kind=kind).ap()
        return nc.dram_tensor(name, list(shape), dt).ap()

    P = scratch("P", [TL, PC], F32)
    WB = scratch("WB", [D, PC], BF16); WBb = T(None)
    YA = scratch("YA", [2048, 1024], F32)
    MT = scratch("MT", [16, 128, 2048], BF16)
    dbg = {n: scratch(n, shp, F32) for n, shp in [("dbg_bc", [5, 128, D]), ("dbg_sh", [TL, SHC]), ("dbg_y", [2048, 1024]),
                                                  ("dbg_rw", [2048, 1024]), ("dbg_att", [2048, 1024])] if n in debug_names}
    Pb = T(None); YAb = T(None); MTb = T(None)
    Ptile = [T(None) for _ in range(NT)]

    S = Sched(nc)
    top = contextlib.ExitStack()

    uid = [0]

    def sb(st, name, shape, dt=F32):
        uid[0] += 1
        return T(st.enter_context(nc.sbuf_tensor("%s_%d" % (name, uid[0]), list(shape), dt)))

    def ps(st, name, shape, dt=F32):
        uid[0] += 1
        return T(st.enter_context(nc.psum_tensor("%s_%d" % (name, uid[0]), list(shape), dt)))

    def dma(eng, out_ap, in_ap, reads=(), writes=()):
        return S.op(eng, lambda e: e.dma_start(out=out_ap, in_=in_ap), reads=reads, writes=writes, dma=True)

    def mm(o, lhsT, rhs, start, stop, reads, writes):
        return S.op("pe", lambda e: e.matmul(o, lhsT=lhsT, rhs=rhs, start=start, stop=stop), reads=reads, writes=writes)

    def tr(o, in_, ident, reads, writes):
        return S.op("pe", lambda e: e.transpose(out=o, in_=in_, identity=ident), reads=reads, writes=writes)

    def act(o, in_, func, reads, writes, scale=1.0, bias=0.0, accum=None, eng="act"):
        if accum is None:
            return S.op(eng, lambda e: e.activation(out=o, in_=in_, func=func, scale=scale, bias=bias), reads=reads, writes=writes)
        return S.op(eng, lambda e: e.activation(out=o, in_=in_, func=func, scale=scale, bias=bias, accum_out=accum),
                    reads=reads, writes=writes)

    def tt(eng, o, a, b, op, reads, writes):
        return S.op(eng, lambda e: e.tensor_tensor(out=o, in0=a, in1=b, op=op), reads=reads, writes=writes)

    def ts(eng, o, a, s1, s2, op0, op1, reads, writes):
        if s2 is None:
            return S.op(eng, lambda e: e.tensor_scalar(out=o, in0=a, scalar1=s1, scalar2=None, op0=op0), reads=reads, writes=writes)
        return S.op(eng, lambda e: e.tensor_scalar(out=o, in0=a, scalar1=s1, scalar2=s2, op0=op0, op1=op1), reads=reads, writes=writes)

    def stt(eng, o, a, s, b, op0, op1, reads, writes):
        return S.op(eng, lambda e: e.scalar_tensor_tensor(out=o, in0=a, scalar=s, in1=b, op0=op0, op1=op1), reads=reads, writes=writes)

    def cp(eng, o, a, reads, writes):
        if eng == "act":
            return S.op("act", lambda e: e.activation(out=o, in_=a, func=AF.Copy), reads=reads, writes=writes)
        return S.op(eng, lambda e: e.tensor_copy(out=o, in_=a), reads=reads, writes=writes)

    def red(eng, o, a, reads, writes, op=ALU.add):
        return S.op(eng, lambda e: e.tensor_reduce(out=o, in_=a, axis=AX.X, op=op), reads=reads, writes=writes)

    def recip(o, a, reads, writes):
        return S.op("dve", lambda e: e.reciprocal(out=o, in_=a), reads=reads, writes=writes)

    def memset(eng, o, val, writes):
        return S.op(eng, lambda e: e.memset(o, val), writes=writes)

    ddn = [0]

    def dd(name, T_, ap, shape):
        if "dd" not in debug_names:
            return
        t = nc.dram_tensor("dd_" + name, list(shape), F32, kind="ExternalOutput").ap()
        dma("pool", t, ap, reads=[T_])

    def bc_row(ap_row, n):
        return ap_row.partition_broadcast(128) if hasattr(ap_row, "partition_broadcast") else ap_row

    identf = sb(top, "identf", [128, 128]); identb = sb(top, "identb", [128, 128], BF16)
    onesf = sb(top, "onesf", [128, 128])
    bonA = sb(top, "bonA", [128, 16, 16])
    st01 = contextlib.ExitStack()
    Abc = [sb(st01, "Abc%d" % v, [128, D]) for v in range(2)]
    Bbc = [sb(st01, "Bbc%d" % v, [128, D]) for v in range(2)]
    gatebc = sb(st01, "gatebc", [128, D])
    GATE = scratch("GATE", [128, D], F32); GATEb = T(None)
    dma("sp", identf[:], c_ident[:, :], writes=[identf])
    for k8 in range(8):
        dma("pool", WB[k8 * 256:(k8 + 1) * 256, :], w_in[k8 * 256:(k8 + 1) * 256, :], writes=[WBb])
    cp("dve", identb[:], identf[:], [identf], [identb])
    memset("dve", onesf[:], 1.0, [onesf])

    with contextlib.ExitStack() as st:
        cct = sb(st, "cct", [128, 32]); sc = sb(st, "sc", [128, 32])
        bct = sb(st, "bct", [128, 96]); ngt = sb(st, "ngt", [128, 16])
        modc = sb(st, "modc", [128, 96]); acol = sb(st, "acol", [128, 2, 16])
        wa = [sb(st, "wa%d" % i, [128, 16, 512]) for i in range(2)]
        dg = [sb(st, "dg%d" % i, [128, 512]) for i in range(2)]
        psA = ps(st, "psA", [128, 96])
        psB = [ps(st, "psB%d" % i, [128, 512]) for i in range(2)]
        dma("sp", cct[:], cc[:, :], writes=[cct]); dma("sp", bct[:], bcol[:, :], writes=[bct]); dma("sp", ngt[:], ngcol[:, :], writes=[ngt])
        act(sc[:], cct[:], AF.Silu, [cct], [sc])
        sc3 = sc[:].rearrange("p (v j) -> p v j", j=16)
        wav = w_ada.rearrange("(j p) n -> p j n", p=128)
        for g in range(12):
            w = wa[g % 2]
            dma("sp" if g % 2 == 0 else "act", w[:], wav[:, :, g * 512:(g + 1) * 512], writes=[w])
            for m4 in range(4):
                m = g * 4 + m4
                for j in range(16):
                    mm(psA[:, 2 * m:2 * m + 2], w[:, j, m4 * 128:(m4 + 1) * 128], sc3[:, :, j], j == 0, j == 15, [w, sc], [psA])
        tt("dve", modc[:], psA[:], bct[:], ALU.add, [psA, bct], [modc])
        mc3 = modc[:].rearrange("p (m v) -> p v m", v=2)
        for v in range(2):
            stt("dve", acol[:, v, :], mc3[:, v, 16:32], 1.0, ngt[:], ALU.add, ALU.mult, [modc, ngt], [acol])
        jobs = [(acol, lambda v, m: acol[:, v, m:m + 1], Abc[0], 0), (acol, lambda v, m: acol[:, v, m:m + 1], Abc[1], 1),
                (modc, lambda v, m: mc3[:, v, m:m + 1], Bbc[0], 0), (modc, lambda v, m: mc3[:, v, m:m + 1], Bbc[1], 1),
                (modc, lambda v, m: mc3[:, v, 32 + m:33 + m], gatebc, 0)]
        k = 0
        for src, colf, dst, v in jobs:
            for m4 in range(4):
                d_ = dg[k % 2]; p_ = psB[k % 2]; k += 1
                for mi in range(4):
                    m = m4 * 4 + mi
                    ts("dve", d_[:, mi * 128:(mi + 1) * 128], identf[:], colf(v, m), None, ALU.mult, ALU.bypass, [identf, src], [d_])
                mm(p_[:], onesf[:], d_[:], True, True, [onesf, d_], [p_])
                cp("act", dst[:, m4 * 512:(m4 + 1) * 512], p_[:], [p_], [dst])
        dma("sp", GATE[:, :], gatebc[:], reads=[gatebc], writes=[GATEb])
        if "dbg_bc" in dbg:
            for i, t_ in enumerate([Abc[0], Abc[1], Bbc[0], Bbc[1], gatebc]):
                dma("sp", dbg["dbg_bc"][i], t_[:], reads=[t_])
    S.barrier()

    blocks = [(0, 512, 'r'), (512, 512, 'r'), (1024, 512, 'k'), (1536, 512, 'k'), (2048, 512, 'v'), (2560, 512, 'v'),
              (3072, 256, 'lo'), (3328, 512, 'zr'), (3840, 512, 'zr'), (4352, 512, 'q'), (4864, 512, 'q'),
              (5376, 512, 'za'), (5888, 512, 'za'), (6400, 512, 'kv')]
    tblocks = [([0, 1], {'k', 'v', 'lo', 'kv'})] + [(list(range(s, s + 4)), None) for s in (2, 6, 10, 14)] + \
              [([18, 19, 20, 21], {'r', 'k', 'v', 'lo', 'kv'})] + [(list(range(s, s + 4)), {'k', 'v', 'lo'}) for s in (22, 26, 30)]
    if stop_after >= 1:
        with contextlib.ExitStack() as st:
            xt = [sb(st, "xt%d" % i, [128, D]) for i in range(2)]
            junk = sb(st, "junk", [128, D], BF16)
            ssq = [sb(st, "ssq%d" % i, [128, 1]) for i in range(2)]
            xs = [sb(st, "xs%d" % i, [128, D]) for i in range(2)]
            xn = [sb(st, "xn%d" % i, [128, D], BF16) for i in range(2)]
            xnT = [sb(st, "xnT%d" % i, [128, 16, 512], BF16) for i in range(2)]
            wb = [sb(st, "wb%d" % i, [128, 16, 512], BF16) for i in range(3)]
            stg = [sb(st, "stg%d" % i, [128, 512]) for i in range(4)]
            pT = [ps(st, "pT%d" % i, [128, 1024], BF16) for i in range(2)]
            pp = [ps(st, "pp%d" % i, [128, 512]) for i in range(4)]
            winv = WB.rearrange("(j p) n -> p j n", p=128)
            wi = 0; si = 0; ti = 0
            for bi, (tiles, need) in enumerate(tblocks):
                xT = xnT[bi % 2]
                for tl, i in enumerate(tiles):
                    v = 1 if i < 2 else 0
                    x_ = xt[ti % 2]; sq_ = ssq[ti % 2]; xs_ = xs[ti % 2]; xn_ = xn[ti % 2]; ti += 1
                    dma("act", x_[:], xin[i * 128:(i + 1) * 128, :], writes=[x_])
                    memset("dve", sq_[:], 0.0, [sq_])
                    act(junk[:], x_[:], AF.Square, [x_, sq_], [junk, sq_], accum=sq_[:, 0:1])
                    act(sq_[:], sq_[:], AF.Sqrt, [sq_], [sq_], scale=1.0 / D, bias=1e-6)
                    recip(sq_[:], sq_[:], [sq_], [sq_])
                    stt("dve", xs_[:], x_[:], sq_[:, 0:1], Abc[v][:], ALU.mult, ALU.mult, [x_, sq_, Abc[v]], [xs_])
                    tt("pool", xn_[:], xs_[:], Bbc[v][:], ALU.add, [xs_, Bbc[v]], [xn_])
                    for half in range(2):
                        p_ = pT[half]
                        for jj in range(8):
                            j = half * 8 + jj
                            tr(p_[:, jj * 128:(jj + 1) * 128], xn_[:, j * 128:(j + 1) * 128], identb[:], [xn_, identb], [p_])
                        cp("act" if half == 0 else "dve", xT[:, half * 8:(half + 1) * 8, tl * 128:(tl + 1) * 128],
                           p_[:].rearrange("p (j t) -> p j t", t=128), [p_], [xT])
                for (c0, cw, grp) in blocks:
                    if need is not None and grp not in need:
                        continue
                    w = wb[wi % 3]; wi += 1
                    dma("sp", w[:, :, 0:cw], winv[:, :, c0:c0 + cw], reads=[WBb], writes=[w])
                    for tl, i in enumerate(tiles):
                        p_ = pp[si % 4]; s_ = stg[si % 4]; si += 1
                        for j in range(16):
                            mm(p_[:, 0:cw], xT[:, j, tl * 128:(tl + 1) * 128], w[:, j, 0:cw], j == 0, j == 15, [xT, w], [p_])
                        cp("act" if si % 2 == 0 else "dve", s_[:, 0:cw], p_[:, 0:cw], [p_], [s_])
                        dma("pool", P[i * 128:(i + 1) * 128, c0:c0 + cw], s_[:, 0:cw], reads=[s_], writes=[Ptile[i]])
        S.barrier()

    st01.close()
    GN_EPS = 64e-5

    def bcast_load(eng, dst, src_row):
        return dma(eng, dst[:].rearrange("p (o n) -> p o n", o=1), src_row.partition_broadcast(128), writes=[dst])

    def rwkv_sweep(d):
        tiles = list(range(0, 18)) if d == 0 else [1, 0] + list(range(33, 1, -1))
        import os
        KT_ = int(os.environ.get("KTILES", "99")); KS_ = int(os.environ.get("KSTAGE", "99"))
        tiles = tiles[:KT_]
        with contextlib.ExitStack() as st:
            mubc = sb(st, "mubc", [128, SHC]); bcast_load("sp", mubc, mu[0:1, :])
            w0bc = sb(st, "w0bc", [128, 1024]); bcast_load("sp", w0bc, w0[d:d + 1, :])
            a0bc = sb(st, "a0bc", [128, 1024]); bcast_load("sp", a0bc, a0[d:d + 1, :])
            kkbc = sb(st, "kkbc", [128, 1024]); bcast_load("sp", kkbc, k_k[0:1, :])
            kabc = sb(st, "kabc", [128, 1024]); bcast_load("sp", kabc, k_a[0:1, :])
            rkbc = sb(st, "rkbc", [128, 1024]); bcast_load("sp", rkbc, r_k[0:1, :])
            if d == 1:
                lngbc = sb(st, "lngbc", [128, 1024]); bcast_load("sp", lngbc, ln_g[0:1, :])
                lnbbc = sb(st, "lnbbc", [128, 1024]); bcast_load("sp", lnbbc, ln_b[0:1, :])
            WAb = sb(st, "WAb", [128, 1024], BF16)
            dma("pool", WAb[0:64, :], w2[d], writes=[WAb]); dma("pool", WAb[64:128, :], a2[d], writes=[WAb])
            triI = sb(st, "triI", [128, 128]); dma("sp", triI[:], c_tri[2 * d], writes=[triI])
            triS = sb(st, "triS", [128, 128]); dma("sp", triS[:], c_tri[2 * d + 1], writes=[triS])
            negc = sb(st, "negc", [128, 1]); memset("dve", negc[:], -CDEC, [negc])
            maskx = sb(st, "maskx", [128, 448], BF16); dma("pool", maskx[:], c_maskx[d], writes=[maskx])
            masky = sb(st, "masky", [128, 256], BF16); dma("pool", masky[:], c_masky[d], writes=[masky])
            shm = sb(st, "shm", [128, 128], BF16); dma("pool", shm[:], c_sh[:, :], writes=[shm])
            e2 = sb(st, "e2", [2, 128], BF16); dma("pool", e2[:], c_e[:, :], writes=[e2])
            CM = sb(st, "CM", [128, 8, 576], BF16)
            for pr in range(8):
                dma("pool", CM[:, pr, 128:192], c_i64[:, :], writes=[CM])
            ST = [sb(st, "ST%d" % h, [64, 2, 64]) for h in range(16)]
            for h in range(16):
                memset("pool", ST[h][:], 0.0, [ST[h]])
            cur = [0] * 16
            sh = sb(st, "sh", [128, SHC]); pc16 = sb(st, "pc16", [128, SHC], BF16); nb16 = sb(st, "nb16", [2, SHC], BF16)
            lo16 = sb(st, "lo16", [128, 128], BF16); loT = sb(st, "loT", [128, 128], BF16)
            sg = sb(st, "sg", [128, 1024]); al = sb(st, "al", [128, 1024]); kk = sb(st, "kk", [128, 1024])
            bb = sb(st, "bb", [128, 1024]); kd = sb(st, "kd", [128, 1024]); t1 = sb(st, "t1", [128, 1024])
            E = [sb(st, "E%d" % k, [128, 1024]) for k in range(2)]
            ss = sb(st, "ss", [128, 16]); bon = sb(st, "bon", [128, 16]); WC = sb(st, "WC", [64, 16])
            rt, kt, bt, at, ktp, btp, vb = [sb(st, n, [128, 1024], BF16) for n in ("rt", "kt", "bt", "at", "ktp", "btp", "vb")]
            X0 = [sb(st, "X0%d" % p_, [128, 448], BF16) for p_ in range(8)]
            TT = [sb(st, "TT%d" % p_, [128, 256], BF16) for p_ in range(8)]
            CC = [sb(st, "CC%d" % p_, [128, 256], BF16) for p_ in range(8)]
            Zf = [sb(st, "Zf%d" % p_, [128, 192], BF16) for p_ in range(8)]
            msk = sb(st, "msk", [128, 7, 256], U8); dma("pool", msk[:], c_ms[d].rearrange("s p c -> p s c"), writes=[msk])
            II = sb(st, "II", [128, 256], BF16)
            cp("pool", II[:, 0:128], identb[:], [identb], [II]); cp("pool", II[:, 128:256], identb[:], [identb], [II])
            ARBK = [sb(st, "ARBK%d" % p_, [128, 256], BF16) for p_ in range(4)]
            QG = [sb(st, "QG%d" % p_, [64, 192], BF16) for p_ in range(4)]
            STb = [sb(st, "STb%d" % h, [64, 64], BF16) for h in range(16)]
            tmpS = [sb(st, "tmpS%d" % h, [64, 64]) for h in range(4)]
            tmpP = [sb(st, "tmpP%d" % h, [64, 64]) for h in range(4)]
            for h in range(16):
                memset("pool", STb[h][:], 0.0, [STb[h]])
            MYH = [sb(st, "MYH%d" % p_, [128, 192], BF16) for p_ in range(4)]
            ysb = sb(st, "ysb", [128, 1024])
            if d == 1:
                ya = sb(st, "ya", [128, 1024]); zr = sb(st, "zr", [128, 1024])
                mixb = sb(st, "mixb", [128, 1024], BF16); mtT = sb(st, "mtT", [128, 8, 128], BF16)
                s1 = sb(st, "s1", [128, 16]); s2 = sb(st, "s2", [128, 16]); mean = sb(st, "mean", [128, 16]); m2 = sb(st, "m2", [128, 16])
            Wd = [ps(st, "Wd%d" % k, [128, 512]) for k in range(2)]
            psT = ps(st, "psT", [128, 1024], BF16)
            Xp = [ps(st, "Xp%d" % k, [128, 512]) for k in range(2)]
            LB = [Xp[0], Xp[1], Wd[0], Wd[1]]
            b5 = ps(st, "b5", [128, 512])
            B6 = ps(st, "B6", [128, 512])
            b7 = ps(st, "b7", [128, 512])
            FB = [b5, b5, b7, b7]
            _ft5 = T(b5.t); _ft7 = T(b7.t)
            FT = [_ft5, _ft5, _ft7, _ft7]
            wk = 0
            def issue_p16(i_):
                own_ = OWN0 <= i_ < OWN1
                c0_ = 0 if own_ else 1024
                r0_ = i_ * 128
                dma("pool", pc16[:, c0_:SHC], P[r0_:r0_ + 128, c0_:SHC], reads=[Ptile[i_]], writes=[pc16])
                memset("pool", nb16[:], 0.0, [nb16])
                if i_ not in (0, 2):
                    dma("pool", nb16[0:1, c0_:SHC], P[r0_ - 1:r0_, c0_:SHC], reads=[Ptile[i_ - 1]], writes=[nb16])
                if i_ not in (1, 33):
                    dma("pool", nb16[1:2, c0_:SHC], P[r0_ + 128:r0_ + 129, c0_:SHC], reads=[Ptile[i_ + 1]], writes=[nb16])

            for ti_, i in enumerate(tiles):
                own = OWN0 <= i < OWN1
                c0 = 0 if own else 1024
                r0 = i * 128
                has_prev = i not in (0, 2); has_next = i not in (1, 33)
                dma("sp", sh[:, c0:SHC], P[r0:r0 + 128, c0:SHC], reads=[Ptile[i]], writes=[sh])
                if ti_ == 0:
                    issue_p16(i)
                if d == 1 and own:
                    dma("sp", ya[:], YA[(i - 2) * 128:(i - 1) * 128, :], reads=[YAb], writes=[ya])
                    dma("sp", zr[:], P[r0:r0 + 128, C_ZR:C_ZR + 1024], reads=[Ptile[i]], writes=[zr])
                cs = c0
                while cs < SHC:
                    cw = min(512, SHC - cs)
                    W = Wd[wk % 2]; tm_ = E[wk % 2]; wk += 1
                    mm(W[:, 0:cw], shm[:], pc16[:, cs:cs + cw], True, False, [shm, pc16], [W])
                    mm(W[:, 0:cw], e2[0:2, :], nb16[0:2, cs:cs + cw], False, True, [e2, nb16], [W])
                    tt("dve", tm_[:, 0:cw], W[:, 0:cw], mubc[:, cs:cs + cw], ALU.mult, [W, mubc], [tm_])
                    tt("dve", sh[:, cs:cs + cw], tm_[:, 0:cw], sh[:, cs:cs + cw], ALU.add, [tm_, sh], [sh])
                    cs += cw
                if ti_ + 1 < len(tiles):
                    issue_p16(tiles[ti_ + 1])
                r_ = sh[:, 0:1024]; k_ = sh[:, 1024:2048]; v_ = sh[:, 2048:3072]
                if KS_ <= 1:
                    continue
                act(lo16[:, 0:64], sh[:, 3072 + 64 * d:3136 + 64 * d], AF.Tanh, [sh], [lo16])
                cp("dve", lo16[:, 64:128], sh[:, 3200 + 64 * d:3264 + 64 * d], [sh], [lo16])
                tr(psT[:, 0:128], lo16[:, :], identb[:], [lo16, identb], [psT])
                cp("dve", loT[:], psT[:, 0:128], [psT], [loT])
                for (lo_, bias_, dst_) in ((0, w0bc, sg), (64, a0bc, al)):
                    for hf in range(2):
                        W = Wd[wk % 2]; wk += 1
                        mm(W[:], loT[lo_:lo_ + 64, :], WAb[lo_:lo_ + 64, hf * 512:(hf + 1) * 512], True, True, [loT, WAb], [W])
                        tt("dve", t1[:, hf * 512:(hf + 1) * 512], W[:], bias_[:, hf * 512:(hf + 1) * 512], ALU.add, [W, bias_], [t1])
                    act(dst_[:], t1[:], AF.Sigmoid, [t1], [dst_])
                tt("dve", kk[:], k_, kkbc[:], ALU.mult, [sh, kkbc], [kk])
                tt("dve", t1[:], kk[:], kk[:], ALU.mult, [kk], [t1])
                red("dve", ss[:], t1[:].rearrange("p (a b) -> p a b", b=64), [t1], [ss])
                act(ss[:], ss[:], AF.Sqrt, [ss], [ss])
                ts("dve", ss[:], ss[:], 1e-12, None, ALU.max, None, [ss], [ss])
                recip(ss[:], ss[:], [ss], [ss])
                tt("dve", kk[:].rearrange("p (a b) -> p a b", b=64), kk[:].rearrange("p (a b) -> p a b", b=64),
                   ss[:].unsqueeze(2).to_broadcast([128, 16, 64]), ALU.mult, [kk, ss], [kk])
                tt("dve", bb[:], kk[:], al[:], ALU.mult, [kk, al], [bb])
                stt("dve", t1[:], al[:], -1.0, kabc[:], ALU.add, ALU.mult, [al, kabc], [t1])
                stt("dve", kd[:], t1[:], 1.0, k_, ALU.add, ALU.mult, [t1, sh], [kd])
                if own:
                    tt("dve", t1[:], r_, rkbc[:], ALU.mult, [sh, rkbc], [t1])
                    tt("dve", t1[:], t1[:], kd[:], ALU.mult, [t1, kd], [t1])
                    if d == 0:
                        red("dve", bonA[:, i - 2, :], t1[:].rearrange("p (a b) -> p a b", b=64), [t1], [bonA])
                    else:
                        red("dve", bon[:], t1[:].rearrange("p (a b) -> p a b", b=64), [t1], [bon])
                        tt("dve", bon[:], bon[:], bonA[:, i - 2, :], ALU.add, [bon, bonA], [bon])
                if "dbg_sh" in dbg and d == 0 and i == 2:
                    dma("sp", dbg["dbg_sh"][0:128, :], sh[:], reads=[sh])
                    for k_i, t_ in enumerate([sg, al, kk, kd]):
                        dma("sp", dbg["dbg_sh"][128 * (k_i + 1):128 * (k_i + 2), 0:1024], t_[:], reads=[t_])
                for hf in range(2):
                    hs = slice(hf * 512, (hf + 1) * 512)
                    W = Wd[wk % 2]; wk += 1
                    mm(W[:], triI[:], sg[:, hs], True, True, [triI, sg], [W])
                    if own:
                        act(E[0][:, hs], W[:], AF.Exp, [W], [E[0]])
                    act(E[1][:, hs], W[:], AF.Exp, [W], [E[1]], scale=-1.0)
                    stt("dve", t1[:, hs], sg[:, hs], CDEC, W[:], ALU.mult, ALU.add, [sg, W], [t1])
                if own:
                    tt("dve", rt[:], r_, E[0][:], ALU.mult, [sh, E[0]], [rt])
                tt("dve", kt[:], kd[:], E[1][:], ALU.mult, [kd, E[1]], [kt])
                tt("dve", bt[:], bb[:], E[1][:], ALU.mult, [bb, E[1]], [bt])
                act(E[0][:], t1[:], AF.Exp, [t1], [E[0]])
                stt("dve", at[:], kk[:], -1.0, E[0][:], ALU.mult, ALU.mult, [kk, E[0]], [at])
                for hf in range(2):
                    hs = slice(hf * 512, (hf + 1) * 512)
                    W = Wd[wk % 2]; wk += 1
                    mm(W[:], triS[:], sg[:, hs], True, True, [triS, sg], [W])
                    act(E[1][:, hs], W[:], AF.Exp, [W], [E[1]])
                tt("dve", ktp[:], kd[:], E[1][:], ALU.mult, [kd, E[1]], [ktp])
                tt("dve", btp[:], bb[:], E[1][:], ALU.mult, [bb, E[1]], [btp])
                cp("act", vb[:], v_, [sh], [vb])
                for h in range(16):
                    mm(Wd[0][0:64, h:h + 1], sg[:, h * 64:(h + 1) * 64], negc[:, 0:1], True, True, [sg, negc], [Wd[0]])
                act(WC[:], Wd[0][0:64, 0:16], AF.Exp, [Wd[0]], [WC])
                if KS_ <= 2:
                    continue
                DD = (d == 0 and i in (0, 2))
                if DD:
                    for nm_, t_ in (("bt", bt), ("kt", kt), ("at", at), ("rt", rt), ("btp", btp), ("ktp", ktp), ("vb", vb)):
                        dd("%s_%d" % (nm_, i), t_, t_[:], [128, 1024])
                    dd("WC_%d" % i, WC, WC[:], [64, 16])
                srcs = [(bt, 0), (kt, 192), (at, 320)] + ([(rt, 448)] if own else [])
                for si_, (src, off) in enumerate(srcs):
                    for pr in range(8):
                        tr(psT[:, pr * 128:(pr + 1) * 128], src[:, pr * 128:(pr + 1) * 128], identb[:], [src, identb], [psT])
                    cp("act" if si_ % 2 == 0 else "dve", CM[:, :, off:off + 128], psT[:].rearrange("p (a b) -> p a b", b=128), [psT], [CM])
                if DD:
                    dd("CM_%d" % i, CM, CM[:, 0, :], [128, 576])
                if KS_ <= 3:
                    continue
                for grp in range(2):
                    hs4 = [(g, 8 * grp + g, (8 * grp + g) // 2, 64 * ((8 * grp + g) % 2)) for g in range(8)]
                    for (g, h, pr, pb) in hs4:
                        lb = LB[g % 4]
                        mm(lb[:, 128:448], CM[pb:pb + 64, pr, 320:448], CM[pb:pb + 64, pr, 0:320], True, True, [CM], [lb])
                        mm(lb[:, 0:128], CM[pb:pb + 64, pr, 0:128], CM[pb:pb + 64, pr, 320:448], True, True, [CM], [lb])
                        cp("pool", TT[g][:], II[:], [II], [TT[g]])
                        S.op("dve", lambda e, g=g, lb=lb: e.copy_predicated(out=TT[g][:], mask=msk[:, 0, :], data=lb[:, 0:256]),
                             reads=[lb, msk, TT[g]], writes=[TT[g]])
                        tt("dve", X0[g][:], lb[:, 0:448], maskx[:], ALU.mult, [lb, maskx], [X0[g]])
                        if DD and h < 2:
                            dd("X0_%d_%d" % (i, h), X0[g], X0[g][:], [128, 448])
                    if KS_ <= 4:
                        continue
                    for lev in range(1, 7):
                        for (g, h, pr, pb) in hs4:
                            lb = LB[g % 4]
                            mm(lb[:, 0:128], X0[g][:, 128:256], TT[g][:, 0:128], True, True, [X0[g], TT[g]], [lb])
                            mm(lb[:, 128:256], X0[g][:, 0:128], TT[g][:, 128:256], True, True, [X0[g], TT[g]], [lb])
                            cp("act", CC[g][:], lb[:, 0:256], [lb], [CC[g]])
                        for (g, h, pr, pb) in hs4:
                            lb = LB[g % 4]
                            mm(lb[:, 256:384], TT[g][:, 128:256], CC[g][:, 0:128], True, True, [TT[g], CC[g]], [lb])
                            mm(lb[:, 384:512], TT[g][:, 0:128], CC[g][:, 128:256], True, True, [TT[g], CC[g]], [lb])
                            S.op("dve", lambda e, g=g, lb=lb, lev=lev: e.copy_predicated(out=TT[g][:], mask=msk[:, lev, :], data=lb[:, 256:512]),
                                 reads=[lb, msk, TT[g]], writes=[TT[g]])
                    for (g, h, pr, pb) in hs4:
                        lb = LB[g % 4]
                        mm(lb[:, 0:192], TT[g][:, 0:128], X0[g][:, 256:448], True, True, [TT[g], X0[g]], [lb])
                        cp("dve" if g % 2 == 0 else "act", Zf[g][:], lb[:, 0:192], [lb], [Zf[g]])
                        if DD and h < 2:
                            dd("Zf_%d_%d" % (i, h), Zf[g], Zf[g][:], [128, 192])
                            dd("TT_%d_%d" % (i, h), TT[g], TT[g][:], [128, 256])
                    for sub in range(2):
                        hsub = hs4[4 * sub:4 * sub + 4]
                        if own:
                            for (g, h, pr, pb) in hsub:
                                lb = LB[g % 4]
                                mm(lb[:, 0:128], CM[pb:pb + 64, pr, 0:128], CM[pb:pb + 64, pr, 448:576], True, True, [CM], [lb])
                                mm(lb[:, 128:256], CM[pb:pb + 64, pr, 192:320], CM[pb:pb + 64, pr, 448:576], True, True, [CM], [lb])
                                tt("dve", ARBK[g % 4][:], lb[:, 0:256], masky[:], ALU.mult, [lb, masky], [ARBK[g % 4]])
                        lo_c = 0 if own else 128
                        for (g, h, pr, pb) in hsub:
                            lb = LB[g % 4]; fb = FB[g % 4]; fo = ((g % 4) % 2) * 256
                            hc = slice(h * 64, (h + 1) * 64)
                            AbT = Zf[g][:, 0:64]; PTt = Zf[g][:, 64:192]
                            if own:
                                mm(fb[0:64, fo:fo + 128], AbT, ARBK[g % 4][:, 0:128], True, False, [Zf[g], ARBK[g % 4]], [FT[g % 4]])
                                mm(fb[0:64, fo:fo + 128], rt[:, hc], identb[:], False, True, [rt, identb], [FT[g % 4]])
                            mm(fb[0:64, fo + 128:fo + 192], AbT, btp[:, hc], True, True, [Zf[g], btp], [FT[g % 4]])
                            cp("act", QG[g % 4][:, lo_c:192], fb[0:64, fo + lo_c:fo + 192], [FT[g % 4]], [QG[g % 4]])
                            if own:
                                mm(lb[:, 256:384], PTt, ARBK[g % 4][:, 0:128], True, False, [Zf[g], ARBK[g % 4]], [lb])
                                mm(lb[:, 256:384], identb[:], ARBK[g % 4][:, 128:256], False, True, [identb, ARBK[g % 4]], [lb])
                            mm(lb[:, 384:448], PTt, btp[:, hc], True, False, [Zf[g], btp], [lb])
                            mm(lb[:, 384:448], identb[:], ktp[:, hc], False, True, [identb, ktp], [lb])
                            cp("dve", MYH[g % 4][:, lo_c:192], lb[:, 256 + lo_c:448], [lb], [MYH[g % 4]])
                            if DD and h < 2:
                                dd("QG_%d_%d" % (i, h), QG[g % 4], QG[g % 4][:], [64, 192])
                                dd("MYH_%d_%d" % (i, h), MYH[g % 4], MYH[g % 4][:], [128, 192])
                                if own:
                                    dd("ARBK_%d_%d" % (i, h), ARBK[g % 4], ARBK[g % 4][:], [128, 256])
                        for (g, h, pr, pb) in hsub:
                            fb = FB[g % 4]; fo = ((g % 4) % 2) * 256
                            hc = slice(h * 64, (h + 1) * 64)
                            STc = ST[h][:, cur[h], :]; STn = ST[h][:, 1 - cur[h], :]
                            if own:
                                yo = B6[:, (h % 8) * 64:(h % 8) * 64 + 64]
                                mm(yo, QG[g % 4][:, 0:128], STb[h][:], True, False, [QG[g % 4], STb[h]], [B6])
                                mm(yo, MYH[g % 4][:, 0:128], vb[:, hc], False, True, [MYH[g % 4], vb], [B6])
                            sreg = fb[0:64, fo + 192:fo + 256]
                            mm(sreg, QG[g % 4][:, 128:192], STb[h][:], True, False, [QG[g % 4], STb[h]], [FT[g % 4]])
                            mm(sreg, MYH[g % 4][:, 128:192], vb[:, hc], False, True, [MYH[g % 4], vb], [FT[g % 4]])
                            act(tmpS[g % 4][:], STc, AF.Copy, [ST[h], WC], [tmpS[g % 4]], scale=WC[:, h:h + 1])
                            act(tmpP[g % 4][:], sreg, AF.Copy, [FT[g % 4]], [tmpP[g % 4]])
                            tt("pool", STn, tmpP[g % 4][:], tmpS[g % 4][:], ALU.add, [tmpP[g % 4], tmpS[g % 4]], [ST[h]])
                            cp("pool", STb[h][:], STn, [ST[h]], [STb[h]])
                            if DD and h < 2:
                                dd("ST_%d_%d" % (i, h), ST[h], STn, [64, 64])
                            cur[h] ^= 1
                    if own:
                        hs = slice(grp * 512, grp * 512 + 512)
                        if d == 0:
                            cp("act", ysb[:, hs], B6[:], [B6], [ysb])
                        else:
                            tt("dve", ysb[:, hs], B6[:], ya[:, hs], ALU.add, [B6, ya], [ysb])
                if not own:
                    continue
                if d == 0:
                    if DD:
                        dd("ysb_%d" % i, ysb, ysb[:], [128, 1024])
                    dma("act", YA[(i - 2) * 128:(i - 1) * 128, :], ysb[:], reads=[ysb], writes=[YAb])
                    continue
                if "dbg_y" in dbg:
                    dma("sp", dbg["dbg_y"][(i - 2) * 128:(i - 1) * 128, :], ysb[:], reads=[ysb])
                y3 = ysb[:].rearrange("p (a b) -> p a b", b=64)
                red("dve", s1[:], y3, [ysb], [s1])
                tt("dve", t1[:], ysb[:], ysb[:], ALU.mult, [ysb], [t1])
                red("dve", s2[:], t1[:].rearrange("p (a b) -> p a b", b=64), [t1], [s2])
                ts("dve", mean[:], s1[:], 1.0 / 64, None, ALU.mult, None, [s1], [mean])
                tt("dve", m2[:], mean[:], mean[:], ALU.mult, [mean], [m2])
                stt("dve", s2[:], s2[:], 1.0 / 64, m2[:], ALU.mult, ALU.subtract, [s2, m2], [s2])
                act(s2[:], s2[:], AF.Sqrt, [s2], [s2], bias=GN_EPS)
                recip(s2[:], s2[:], [s2], [s2])
                tt("dve", y3, y3, mean[:].unsqueeze(2).to_broadcast([128, 16, 64]), ALU.subtract, [ysb, mean], [ysb])
                tt("dve", y3, y3, s2[:].unsqueeze(2).to_broadcast([128, 16, 64]), ALU.mult, [ysb, s2], [ysb])
                tt("dve", ysb[:], ysb[:], lngbc[:], ALU.mult, [ysb, lngbc], [ysb])
                tt("dve", ysb[:], ysb[:], lnbbc[:], ALU.add, [ysb, lnbbc], [ysb])
                tt("dve", t1[:].rearrange("p (a b) -> p a b", b=64), vb[:].rearrange("p (a b) -> p a b", b=64),
                   bon[:].unsqueeze(2).to_broadcast([128, 16, 64]), ALU.mult, [vb, bon], [t1])
                tt("dve", ysb[:], ysb[:], t1[:], ALU.add, [ysb, t1], [ysb])
                if "dbg_rw" in dbg:
                    dma("sp", dbg["dbg_rw"][(i - 2) * 128:(i - 1) * 128, :], ysb[:], reads=[ysb])
                act(zr[:], zr[:], AF.Silu, [zr], [zr])
                tt("dve", mixb[:], ysb[:], zr[:], ALU.mult, [ysb, zr], [mixb])
                for cch in range(8):
                    tr(psT[:, cch * 128:(cch + 1) * 128], mixb[:, cch * 128:(cch + 1) * 128], identb[:], [mixb, identb], [psT])
                cp("act", mtT[:], psT[:].rearrange("p (a b) -> p a b", b=128), [psT], [mtT])
                dma("act", MT[0:8, :, (i - 2) * 128:(i - 1) * 128].rearrange("c p t -> p c t"), mtT[:], reads=[mtT], writes=[MTb])
        S.barrier()

    if stop_after >= 2:
        rwkv_sweep(0)
    if stop_after >= 3:
        rwkv_sweep(1)

    def rope_apply(eng2, t3, rot3, rp, nh, reads_t, T_t, T_rot, T_rp):
        t5 = t3.rearrange("p h (a b c) -> p h a b c", a=2, b=2)
        r5 = rot3.rearrange("p h (a b c) -> p h a b c", a=2, b=2)
        cp("pool", r5[:, :, :, 0, :], t5[:, :, :, 1, :], [T_t], [T_rot])
        cp("pool", r5[:, :, :, 1, :], t5[:, :, :, 0, :], [T_t], [T_rot])
        cosb = rp[:, 0:64].unsqueeze(1).to_broadcast([128, nh, 64])
        sinb = rp[:, 64:128].unsqueeze(1).to_broadcast([128, nh, 64])
        tt("dve", t3, t3, cosb, ALU.mult, [T_t, T_rp], [T_t])
        tt("dve", rot3, rot3, sinb, ALU.mult, [T_rot, T_rp], [T_rot])
        tt("dve", t3, t3, rot3, ALU.add, [T_t, T_rot], [T_t])

    def attention():
        with contextlib.ExitStack() as st:
            kgbc = sb(st, "kgbc", [128, 64]); bcast_load("sp", kgbc, kg[0:1, :])
            qgbc = sb(st, "qgbc", [128, 64]); bcast_load("sp", qgbc, qg[0:1, :])
            ts("dve", qgbc[:], qgbc[:], 0.125, None, ALU.mult, None, [qgbc], [qgbc])
            esk = sb(st, "esk", [128, 16]); bcast_load("sp", esk, sink[0:1, :])
            act(esk[:], esk[:], AF.Exp, [esk], [esk])
            amask = sb(st, "amask", [128, 2, 128], BF16)
            dma("pool", amask[:], c_amask.rearrange("a k q -> k a q"), writes=[amask])
            KT = sb(st, "KT", [64, 19, 4, 128], BF16)
            V1 = sb(st, "V1", [128, 19, 4, 65], BF16)
            memset("pool", V1[:], 1.0, [V1])
            kv = sb(st, "kv", [128, 512]); rot = sb(st, "rot", [128, 1024]); rp = sb(st, "rp", [128, 128])
            sq = sb(st, "sq", [128, 1024]); ss = sb(st, "ssq_a", [128, 16])
            kb = sb(st, "kb", [128, 256], BF16)
            q = sb(st, "q", [128, 1024]); za = sb(st, "za", [128, 1024]); qb = sb(st, "qb", [128, 1024], BF16)
            QT = sb(st, "QT", [64, 16, 128], BF16)
            PTs = [sb(st, "PTs%d" % k, [128, 512], BF16) for k in range(5)]
            att = sb(st, "att", [128, 1024]); den = sb(st, "den", [128, 16])
            mixb = sb(st, "mixb_a", [128, 1024], BF16); mtT = sb(st, "mtT_a", [128, 8, 128], BF16)
            psT = ps(st, "psT_a", [128, 1024], BF16)
            Sps = [ps(st, "Sps%d" % k, [128, 512]) for k in range(2)]
            Og = [ps(st, "Og%d" % k, [128, 512]) for k in range(4)]
            for n in range(19):
                i = n
                dma("sp", kv[:], P[i * 128:(i + 1) * 128, C_KA:C_KA + 512], reads=[Ptile[i]], writes=[kv])
                k3 = kv[:, 0:256].rearrange("p (h c) -> p h c", c=64)
                tt("dve", sq[:, 0:256], kv[:, 0:256], kv[:, 0:256], ALU.mult, [kv], [sq])
                red("dve", ss[:, 0:4], sq[:, 0:256].rearrange("p (h c) -> p h c", c=64), [sq], [ss])
                act(ss[:, 0:4], ss[:, 0:4], AF.Sqrt, [ss], [ss], scale=1.0 / 64, bias=1e-6)
                recip(ss[:, 0:4], ss[:, 0:4], [ss], [ss])
                tt("dve", k3, k3, ss[:, 0:4].unsqueeze(2).to_broadcast([128, 4, 64]), ALU.mult, [kv, ss], [kv])
                tt("dve", k3, k3, kgbc[:].unsqueeze(1).to_broadcast([128, 4, 64]), ALU.mult, [kv, kgbc], [kv])
                if i >= 2:
                    dma("sp", rp[:], rope[(i - 2) * 128:(i - 1) * 128, :], writes=[rp])
                    rope_apply(None, k3, rot[:, 0:256].rearrange("p (h c) -> p h c", c=64), rp, 4, None, kv, rot, rp)
                cp("act", kb[:], kv[:, 0:256], [kv], [kb])
                for g in range(4):
                    tr(psT[0:64, g * 128:(g + 1) * 128], kb[:, g * 64:(g + 1) * 64], identb[:], [kb, identb], [psT])
                cp("dve", KT[:, n, :, :], psT[0:64, 0:512].rearrange("p (g t) -> p g t", t=128), [psT], [KT])
                cp("act", V1[:, n, :, 0:64], kv[:, 256:512].rearrange("p (h c) -> p h c", c=64), [kv], [V1])
            sk = 0
            for i in range(OWN0, OWN1):
                r0 = i * 128
                dma("sp", q[:], P[r0:r0 + 128, C_Q:C_Q + 1024], reads=[Ptile[i]], writes=[q])
                dma("sp", za[:], P[r0:r0 + 128, C_ZA:C_ZA + 1024], reads=[Ptile[i]], writes=[za])
                dma("sp", rp[:], rope[(i - 2) * 128:(i - 1) * 128, :], writes=[rp])
                q3 = q[:].rearrange("p (h c) -> p h c", c=64)
                tt("dve", sq[:], q[:], q[:], ALU.mult, [q], [sq])
                red("dve", ss[:], sq[:].rearrange("p (h c) -> p h c", c=64), [sq], [ss])
                act(ss[:], ss[:], AF.Sqrt, [ss], [ss], scale=1.0 / 64, bias=1e-6)
                recip(ss[:], ss[:], [ss], [ss])
                tt("dve", q3, q3, ss[:].unsqueeze(2).to_broadcast([128, 16, 64]), ALU.mult, [q, ss], [q])
                tt("dve", q3, q3, qgbc[:].unsqueeze(1).to_broadcast([128, 16, 64]), ALU.mult, [q, qgbc], [q])
                rope_apply(None, q3, rot[:].rearrange("p (h c) -> p h c", c=64), rp, 16, None, q, rot, rp)
                cp("act", qb[:], q[:], [q], [qb])
                for half in range(2):
                    for hx in range(8):
                        hd = half * 8 + hx
                        tr(psT[0:64, hx * 128:(hx + 1) * 128], qb[:, hd * 64:(hd + 1) * 64], identb[:], [qb, identb], [psT])
                    cp("dve", QT[:, half * 8:(half + 1) * 8, :], psT[0:64, :].rearrange("p (g t) -> p g t", t=128), [psT], [QT])
                for g in range(4):
                    keyt = ([i - 1] if i > OWN0 else []) + [i, i + 1, 0, 1]
                    for ki, kt_ in enumerate(keyt):
                        sp_ = Sps[sk % 2]; sk += 1
                        pt_ = PTs[ki]
                        mm(sp_[:], KT[:, kt_, g, :], QT[:, 4 * g:4 * g + 4, :].rearrange("p a b -> p (a b)"), True, True, [KT, QT], [sp_])
                        act(pt_[:], sp_[:], AF.Exp, [sp_], [pt_])
                        is_prev = (i > OWN0 and ki == 0); is_next = (ki == (2 if i > OWN0 else 1))
                        if is_prev or is_next:
                            mi = 0 if is_prev else 1
                            p3 = pt_[:].rearrange("p (a b) -> p a b", b=128)
                            tt("dve", p3, p3, amask[:, mi, :].unsqueeze(1).to_broadcast([128, 4, 128]), ALU.mult, [pt_, amask], [pt_])
                    for hx in range(4):
                        for ki, kt_ in enumerate(keyt):
                            mm(Og[g][:, hx * 65:(hx + 1) * 65], PTs[ki][:, hx * 128:(hx + 1) * 128], V1[:, kt_, g, :],
                               ki == 0, ki == len(keyt) - 1, [PTs[ki], V1], [Og[g]])
                for g in range(4):
                    o3 = Og[g][:, 0:260].rearrange("p (h c) -> p h c", c=65)
                    tt("dve", den[:, 4 * g:4 * g + 4], o3[:, :, 64], esk[:, 4 * g:4 * g + 4], ALU.add, [Og[g], esk], [den])
                    recip(den[:, 4 * g:4 * g + 4], den[:, 4 * g:4 * g + 4], [den], [den])
                    tt("dve", att[:, g * 256:(g + 1) * 256].rearrange("p (h c) -> p h c", c=64), o3[:, :, 0:64],
                       den[:, 4 * g:4 * g + 4].unsqueeze(2).to_broadcast([128, 4, 64]), ALU.mult, [Og[g], den], [att])
                if "dbg_att" in dbg:
                    dma("sp", dbg["dbg_att"][(i - 2) * 128:(i - 1) * 128, :], att[:], reads=[att])
                act(za[:], za[:], AF.Silu, [za], [za])
                tt("dve", mixb[:], att[:], za[:], ALU.mult, [att, za], [mixb])
                for cch in range(8):
                    tr(psT[:, cch * 128:(cch + 1) * 128], mixb[:, cch * 128:(cch + 1) * 128], identb[:], [mixb, identb], [psT])
                cp("act", mtT[:], psT[:].rearrange("p (a b) -> p a b", b=128), [psT], [mtT])
                dma("sp", MT[8:16, :, (i - 2) * 128:(i - 1) * 128].rearrange("c p t -> p c t"), mtT[:], reads=[mtT], writes=[MTb])
        S.barrier()

    def outproj():
        with contextlib.ExitStack() as st:
            wo = sb(st, "wo", [128, 16, D], BF16)
            wov = w_out.rearrange("(j p) n -> p j n", p=128)
            for k4 in range(4):
                dma("pool", wo[:, k4 * 4:(k4 + 1) * 4, :], wov[:, k4 * 4:(k4 + 1) * 4, :], writes=[wo])
            gate = sb(st, "gate", [128, D]); dma("sp", gate[:], GATE[:, :], reads=[GATEb], writes=[gate])
            mt = [sb(st, "mt%d" % k, [128, 16, 128], BF16) for k in range(2)]
            xo = [sb(st, "xo%d" % k, [128, D]) for k in range(2)]
            ot = [sb(st, "ot%d" % k, [128, D]) for k in range(2)]
            pp = [ps(st, "ppo%d" % k, [128, 512]) for k in range(4)]
            for tl in range(16):
                i = tl + 2
                m_ = mt[tl % 2]; x_ = xo[tl % 2]; o_ = ot[tl % 2]
                dma("sp", m_[:], MT[:, :, tl * 128:(tl + 1) * 128].rearrange("c p t -> p c t"), reads=[MTb], writes=[m_])
                dma("act", x_[:], xin[i * 128:(i + 1) * 128, :], writes=[x_])
                for nb in range(4):
                    ns = slice(nb * 512, (nb + 1) * 512)
                    p_ = pp[nb]
                    for cc_ in range(16):
                        mm(p_[:], m_[:, cc_, :], wo[:, cc_, ns], cc_ == 0, cc_ == 15, [m_, wo], [p_])
                    tt("dve", o_[:, ns], p_[:], gate[:, ns], ALU.mult, [p_, gate], [o_])
                    tt("pool", o_[:, ns], o_[:, ns], x_[:, ns], ALU.add, [o_, x_], [o_])
                dma("sp", out[tl * 128:(tl + 1) * 128, :], o_[:], reads=[o_])

    if stop_after >= 4:
        attention()
    if stop_after >= 5:
        outproj()

    S.emit(final_wait_ops=[i for i, o in enumerate(S.ops) if o.is_dma])
    top.close()
    return nc


def host_consts():
    c = {}
    c["c_ident"] = np.eye(128, dtype=np.float32)
    i = np.arange(128)
    triA_incl = (i[:, None] <= i[None, :]).astype(np.float32)
    triA_suf = (i[:, None] > i[None, :]).astype(np.float32)
    triB_incl = (i[:, None] >= i[None, :]).astype(np.float32)
    triB_suf = (i[:, None] < i[None, :]).astype(np.float32)
    c["c_tri"] = (-CDEC * np.stack([triA_incl, triA_suf, triB_incl, triB_suf])).astype(np.float32)
    mx = np.zeros((2, 128, 448), np.float32); my = np.zeros((2, 128, 256), np.float32)
    for d in range(2):
        prec = (i[:, None] < i[None, :]) if d == 0 else (i[:, None] > i[None, :])
        preceq = prec | np.eye(128, dtype=bool)
        mx[d, :, 0:128] = prec; mx[d, :, 128:256] = prec.T; mx[d, :, 256:320] = 1.0; mx[d, :, 320:448] = prec.T
        my[d, :, 0:128] = preceq; my[d, :, 128:256] = preceq
    c["c_maskx"] = mx; c["c_masky"] = my
    cms = np.zeros((2, 7, 128, 256), np.float32)
    for d in range(2):
        prec = (i[:, None] < i[None, :]) if d == 0 else (i[:, None] > i[None, :])
        for s_ in range(7):
            m = prec & ((i[:, None] >> (s_ + 1)) == (i[None, :] >> (s_ + 1))) & ((i[:, None] >> s_) != (i[None, :] >> s_))
            cms[d, s_, :, 0:128] = m; cms[d, s_, :, 128:256] = m.T
    c["c_ms"] = cms
    sh = np.zeros((128, 128), np.float32)
    sh[i, i] = -1.0; sh[i[:-1], i[:-1] + 1] = 0.5; sh[i[1:], i[1:] - 1] = 0.5
    c["c_sh"] = sh
    e = np.zeros((2, 128), np.float32); e[0, 0] = 0.5; e[1, 127] = 0.5
    c["c_e"] = e
    c["c_i64"] = np.concatenate([np.eye(64, dtype=np.float32)] * 2, axis=0)
    am = np.zeros((2, 128, 128), np.float32)
    am[0] = (i[:, None] >= i[None, :]); am[1] = (i[:, None] <= i[None, :])
    c["c_amask"] = am
    return c


def rope_tables(pos):
    pos = np.asarray(pos)
    row = (pos // 64).astype(np.float32); col = (pos % 64).astype(np.float32)
    half = 16
    inv = (10000.0 ** (-np.arange(half, dtype=np.float32) / half)).astype(np.float32)
    ar = row[:, None] * inv; ac = col[:, None] * inv
    ang = np.concatenate([ar, ar, ac, ac], axis=-1)
    cos = np.cos(ang); sin = np.sin(ang)
    sgn = np.concatenate([-np.ones(16), np.ones(16), -np.ones(16), np.ones(16)]).astype(np.float32)
    return np.concatenate([cos, sin * sgn], axis=-1).astype(np.float32)


def core_inputs(inp, b, h, consts):
    f = np.ascontiguousarray
    x = inp["x"][b]; ctx = inp["ctx"][b]
    if h == 1:
        x = x[::-1]; ctx = ctx[::-1]
    m = {}
    m["xin"] = f(np.concatenate([ctx, x], axis=0))
    colf = lambda v: v.reshape(-1, 128).T
    m["cc"] = f(np.concatenate([colf(inp["c"][b]), colf(inp["c_ctx"])], axis=1))
    m["w_ada"] = f(inp["w_ada"][0])
    m["bcol"] = f(np.repeat(colf(inp["b_ada"][0]), 2, axis=1))
    m["ngcol"] = f(colf(inp["norm_g"][0]))
    dA, dB = (0, 1) if h == 0 else (1, 0)
    lw = lambda d: np.arange(3072 + 64 * d, 3072 + 64 * d + 64)
    la = lambda d: np.arange(3200 + 64 * d, 3200 + 64 * d + 64)
    perm = np.concatenate([np.arange(0, 3072), lw(dA), lw(dB), la(dA), la(dB), np.arange(3328, 4352), np.arange(4352, 5376),
                           np.arange(5888, 6912), np.arange(5376, 5632), np.arange(5632, 5888)])
    m["w_in"] = f(inp["w_in"][0][:, perm])
    m["mu"] = f(inp["mu_shift"][0][perm[:SHC]][None, :])
    sel = [dA, dB]
    m["w0"] = f(inp["w0"][0][sel]); m["w2"] = f(inp["w2"][0][sel]); m["a0"] = f(inp["a0"][0][sel]); m["a2"] = f(inp["a2"][0][sel])
    for k in ("k_k", "k_a"):
        m[k] = f(inp[k][0][None, :])
    m["r_k"] = f(inp["r_k"][0].reshape(1, 1024))
    m["ln_g"] = f(inp["ln_x_g"][0][None, :]); m["ln_b"] = f(inp["ln_x_b"][0][None, :])
    m["qg"] = f(inp["q_norm_g"][0][None, :]); m["kg"] = f(inp["k_norm_g"][0][None, :]); m["sink"] = f(inp["sink"][0][None, :])
    m["w_out"] = f(inp["w_out"][0])
    loc = np.arange(17 * 128)
    pos = loc if h == 0 else 4095 - loc
    m["rope"] = rope_tables(pos)
    m.update(consts)
    return m


def kernel(**inputs):
    inp = {k: np.asarray(v) for k, v in inputs.items()}
    consts = host_consts()
    nc = build()
    in_maps = [core_inputs(inp, c // 2, c % 2, consts) for c in range(8)]
    res = run_bass_kernel_spmd(nc, in_maps, core_ids=list(range(8)))
    out = np.empty((4, 4096, D), np.float32)
    for c in range(8):
        b, h = c // 2, c % 2
        o = res.results[c]["out"]
        if h == 0:
            out[b, :2048] = o
        else:
            out[b, 2048:] = o[::-1]
    return out
```

```python
import contextlib
import numpy as np
import concourse.bass as bass
import concourse.mybir as mybir
from concourse.bass_utils import run_bass_kernel_spmd

F32 = mybir.dt.float32
BF16 = mybir.dt.bfloat16
U8 = mybir.dt.uint8
AF = mybir.ActivationFunctionType
ALU = mybir.AluOpType
AX = mybir.AxisListType

ENGS = ["pe", "act", "dve", "pool", "sp"]
CDEC = 0.6065306597126334

D = 2048
NT = 34
OWN0, OWN1 = 2, 18
TL = NT * 128
PC = 6912
C_R, C_K, C_V, C_LO, C_ZR, C_Q, C_ZA, C_KA, C_VA = 0, 1024, 2048, 3072, 3328, 4352, 5376, 6400, 6656
SHC = 3328
DEBUG = {}


class Buf:
    __slots__ = ("last_w", "readers")

    def __init__(self):
        self.last_w = None
        self.readers = []


class T:
    def __init__(self, t):
        self.t = t
        self.b = Buf()

    def __getitem__(self, k):
        return self.t[k]


class Op:
    __slots__ = ("eng", "fn", "deps", "is_dma", "has_dep", "mile", "dsem", "dval")


class Sched:
    def __init__(self, nc, n_dma_sems=32):
        self.nc = nc
        self.ops = []
        self.n_dma_sems = n_dma_sems
        self.last_on = {e: None for e in ENGS}
        self.dmas_since_barrier = []

    def op(self, eng, fn, reads=(), writes=(), dma=False, extra_deps=()):
        o = Op()
        o.eng = eng; o.fn = fn; o.is_dma = dma; o.has_dep = False; o.mile = None; o.dsem = None; o.dval = None
        o.deps = set(extra_deps)
        oid = len(self.ops)
        for t in reads:
            b = t.b
            if b.last_w is not None:
                o.deps.add(b.last_w)
        for t in writes:
            b = t.b
            if b.last_w is not None:
                o.deps.add(b.last_w)
            o.deps.update(b.readers)
        for t in reads:
            t.b.readers.append(oid)
        for t in writes:
            t.b.last_w = oid
            t.b.readers = []
        o.deps.discard(oid)
        self.ops.append(o)
        self.last_on[eng] = oid
        if dma:
            self.dmas_since_barrier.append(oid)
        return oid

    def barrier(self):
        deps = [v for v in self.last_on.values() if v is not None] + list(self.dmas_since_barrier)
        self.dmas_since_barrier = []
        for e in ENGS:
            self.op(e, None, extra_deps=deps)

    def emit(self, final_wait_ops=()):
        nc = self.nc
        ops = self.ops
        for o in ops:
            nd = set()
            for d in o.deps:
                p = ops[d]
                if p.fn is None:
                    if p.eng == o.eng:
                        continue
                    nd.update(p.deps)
                    continue
                if p.eng == o.eng and not p.is_dma and o.eng == "pe" and not o.is_dma:
                    continue
                nd.add(d)
            o.deps = nd
        for o in ops:
            for d in o.deps:
                ops[d].has_dep = True
        for d in final_wait_ops:
            ops[d].has_dep = True
        cnt = {e: 0 for e in ENGS}
        dma_i = 0
        dma_cnt = [0] * self.n_dma_sems
        dma_prev = [None] * self.n_dma_sems
        for i, o in enumerate(ops):
            if o.fn is None:
                continue
            if o.is_dma:
                s = dma_i % self.n_dma_sems
                dma_i += 1
                if dma_prev[s] is not None:
                    o.deps.add(dma_prev[s])
                dma_prev[s] = i
                dma_cnt[s] += 16
                o.dsem = s
                o.dval = dma_cnt[s]
            elif o.has_dep:
                cnt[o.eng] += 1
                o.mile = cnt[o.eng]
        streams = {e: [] for e in ENGS}
        for i, o in enumerate(ops):
            streams[o.eng].append(i)
        with contextlib.ExitStack() as st:
            esem = {e: st.enter_context(nc.semaphore("s_" + e)) for e in ENGS}
            dsem = [st.enter_context(nc.semaphore("d_%d" % k)) for k in range(self.n_dma_sems)]
            block = st.enter_context(nc.Block())

            def run(eng_name, engine):
                seen = {}
                for i in streams[eng_name]:
                    o = ops[i]
                    need = {}
                    for d in o.deps:
                        p = ops[d]
                        if p.is_dma:
                            key = ("d", p.dsem); val = p.dval
                        else:
                            key = ("e", p.eng); val = p.mile
                        if need.get(key, 0) < val:
                            need[key] = val
                    todo = []
                    for key, val in need.items():
                        if seen.get(key, 0) >= val:
                            continue
                        seen[key] = val
                        todo.append((dsem[key[1]] if key[0] == "d" else esem[key[1]], val))
                    embed = None
                    if todo and o.fn is not None and not o.is_dma:
                        embed = todo.pop()
                    for sem, val in todo:
                        engine.wait_ge(sem, val)
                    if o.fn is None:
                        continue
                    ins = o.fn(engine)
                    if embed is not None:
                        ins._wait_ge(embed[0], embed[1])
                    if o.is_dma:
                        ins.then_inc(dsem[o.dsem], 16)
                    elif o.mile is not None:
                        ins.then_inc(esem[o.eng], 1)
                if eng_name == "sp":
                    fin = {}
                    for d in final_wait_ops:
                        p = ops[d]
                        fin[p.dsem] = max(fin.get(p.dsem, 0), p.dval)
                    for k_, v_ in fin.items():
                        engine.wait_ge(dsem[k_], v_)

            block.tensor(lambda e: run("pe", e))
            block.scalar(lambda e: run("act", e))
            block.vector(lambda e: run("dve", e))
            block.gpsimd(lambda e: run("pool", e))
            block.sync(lambda e: run("sp", e))


def build(debug_names=(), stop_after=99):
    nc = bass.Bass("TRN2", target_bir_lowering=False)

    def din(name, shape, dt=F32):
        return nc.dram_tensor(name, list(shape), dt, kind="ExternalInput").ap()

    xin = din("xin", [TL, D])
    cc = din("cc", [128, 32])
    w_ada = din("w_ada", [D, 3 * D])
    bcol = din("bcol", [128, 96])
    ngcol = din("ngcol", [128, 16])
    w_in = din("w_in", [D, PC])
    mu = din("mu", [1, SHC])
    w0 = din("w0", [2, 1024]); w2 = din("w2", [2, 64, 1024]); a0 = din("a0", [2, 1024]); a2 = din("a2", [2, 64, 1024])
    k_k = din("k_k", [1, 1024]); k_a = din("k_a", [1, 1024]); r_k = din("r_k", [1, 1024])
    ln_g = din("ln_g", [1, 1024]); ln_b = din("ln_b", [1, 1024])
    qg = din("qg", [1, 64]); kg = din("kg", [1, 64]); sink = din("sink", [1, 16])
    w_out = din("w_out", [D, D])
    rope = din("rope", [17 * 128, 128])
    c_ident = din("c_ident", [128, 128])
    c_tri = din("c_tri", [4, 128, 128])
    c_maskx = din("c_maskx", [2, 128, 448])
    c_masky = din("c_masky", [2, 128, 256])
    c_ms = din("c_ms", [2, 7, 128, 256])
    c_sh = din("c_sh", [128, 128])
    c_e = din("c_e", [2, 128])
    c_i64 = din("c_i64", [128, 64])
    c_amask = din("c_amask", [2, 128, 128])
    out = nc.dram_tensor("out", [2048, D], F32, kind="ExternalOutput").ap()

    def scratch(name, shape, dt):
        kind = "ExternalOutput" if name in debug_names else None
        if kind:
            return nc.dram_tensor(name, list(shape), dt, kind=kind).ap()
        return nc.dram_tensor(name, list(shape), dt).ap()

    P = scratch("P", [TL, PC], F32)
    WB = scratch("WB", [D, PC], BF16); WBb = T(None)
    YA = scratch("YA", [2048, 1024], F32)
    MT = scratch("MT", [16, 128, 2048], BF16)
    dbg = {n: scratch(n, shp, F32) for n, shp in [("dbg_bc", [5, 128, D]), ("dbg_sh", [TL, SHC]), ("dbg_y", [2048, 1024]),
                                                  ("dbg_rw", [2048, 1024]), ("dbg_att", [2048, 1024])] if n in debug_names}
    Pb = T(None); YAb = T(None); MTb = T(None)
    Ptile = [T(None) for _ in range(NT)]

    S = Sched(nc)
    top = contextlib.ExitStack()

    uid = [0]

    def sb(st, name, shape, dt=F32):
        uid[0] += 1
        return T(st.enter_context(nc.sbuf_tensor("%s_%d" % (name, uid[0]), list(shape), dt)))

    def ps(st, name, shape, dt=F32):
        uid[0] += 1
        return T(st.enter_context(nc.psum_tensor("%s_%d" % (name, uid[0]), list(shape), dt)))

    def dma(eng, out_ap, in_ap, reads=(), writes=()):
        return S.op(eng, lambda e: e.dma_start(out=out_ap, in_=in_ap), reads=reads, writes=writes, dma=True)

    def mm(o, lhsT, rhs, start, stop, reads, writes):
        return S.op("pe", lambda e: e.matmul(o, lhsT=lhsT, rhs=rhs, start=start, stop=stop), reads=reads, writes=writes)

    def tr(o, in_, ident, reads, writes):
        return S.op("pe", lambda e: e.transpose(out=o, in_=in_, identity=ident), reads=reads, writes=writes)

    def act(o, in_, func, reads, writes, scale=1.0, bias=0.0, accum=None, eng="act"):
        if accum is None:
            return S.op(eng, lambda e: e.activation(out=o, in_=in_, func=func, scale=scale, bias=bias), reads=reads, writes=writes)
        return S.op(eng, lambda e: e.activation(out=o, in_=in_, func=func, scale=scale, bias=bias, accum_out=accum),
                    reads=reads, writes=writes)

    def tt(eng, o, a, b, op, reads, writes):
        return S.op(eng, lambda e: e.tensor_tensor(out=o, in0=a, in1=b, op=op), reads=reads, writes=writes)

    def ts(eng, o, a, s1, s2, op0, op1, reads, writes):
        if s2 is None:
            return S.op(eng, lambda e: e.tensor_scalar(out=o, in0=a, scalar1=s1, scalar2=None, op0=op0), reads=reads, writes=writes)
        return S.op(eng, lambda e: e.tensor_scalar(out=o, in0=a, scalar1=s1, scalar2=s2, op0=op0, op1=op1), reads=reads, writes=writes)

    def stt(eng, o, a, s, b, op0, op1, reads, writes):
        return S.op(eng, lambda e: e.scalar_tensor_tensor(out=o, in0=a, scalar=s, in1=b, op0=op0, op1=op1), reads=reads, writes=writes)

    def cp(eng, o, a, reads, writes):
        if eng == "act":
            return S.op("act", lambda e: e.activation(out=o, in_=a, func=AF.Copy), reads=reads, writes=writes)
        return S.op(eng, lambda e: e.tensor_copy(out=o, in_=a), reads=reads, writes=writes)

    def red(eng, o, a, reads, writes, op=ALU.add):
        return S.op(eng, lambda e: e.tensor_reduce(out=o, in_=a, axis=AX.X, op=op), reads=reads, writes=writes)

    def recip(o, a, reads, writes):
        return S.op("dve", lambda e: e.reciprocal(out=o, in_=a), reads=reads, writes=writes)

    def memset(eng, o, val, writes):
        return S.op(eng, lambda e: e.memset(o, val), writes=writes)

    ddn = [0]

    def dd(name, T_, ap, shape):
        if "dd" not in debug_names:
            return
        t = nc.dram_tensor("dd_" + name, list(shape), F32, kind="ExternalOutput").ap()
        dma("pool", t, ap, reads=[T_])

    def bc_row(ap_row, n):
        return ap_row.partition_broadcast(128) if hasattr(ap_row, "partition_broadcast") else ap_row

    identf = sb(top, "identf", [128, 128]); identb = sb(top, "identb", [128, 128], BF16)
    onesf = sb(top, "onesf", [128, 128])
    bonA = sb(top, "bonA", [128, 16, 16])
    st01 = contextlib.ExitStack()
    Abc = [sb(st01, "Abc%d" % v, [128, D]) for v in range(2)]
    Bbc = [sb(st01, "Bbc%d" % v, [128, D]) for v in range(2)]
    gatebc = sb(st01, "gatebc", [128, D])
    GATE = scratch("GATE", [128, D], F32); GATEb = T(None)
    dma("sp", identf[:], c_ident[:, :], writes=[identf])
    for k8 in range(8):
        dma("pool", WB[k8 * 256:(k8 + 1) * 256, :], w_in[k8 * 256:(k8 + 1) * 256, :], writes=[WBb])
    cp("dve", identb[:], identf[:], [identf], [identb])
    memset("dve", onesf[:], 1.0, [onesf])

    with contextlib.ExitStack() as st:
        cct = sb(st, "cct", [128, 32]); sc = sb(st, "sc", [128, 32])
        bct = sb(st, "bct", [128, 96]); ngt = sb(st, "ngt", [128, 16])
        modc = sb(st, "modc", [128, 96]); acol = sb(st, "acol", [128, 2, 16])
        wa = [sb(st, "wa%d" % i, [128, 16, 512]) for i in range(2)]
        dg = [sb(st, "dg%d" % i, [128, 512]) for i in range(2)]
        psA = ps(st, "psA", [128, 96])
        psB = [ps(st, "psB%d" % i, [128, 512]) for i in range(2)]
        dma("sp", cct[:], cc[:, :], writes=[cct]); dma("sp", bct[:], bcol[:, :], writes=[bct]); dma("sp", ngt[:], ngcol[:, :], writes=[ngt])
        act(sc[:], cct[:], AF.Silu, [cct], [sc])
        sc3 = sc[:].rearrange("p (v j) -> p v j", j=16)
        wav = w_ada.rearrange("(j p) n -> p j n", p=128)
        for g in range(12):
            w = wa[g % 2]
            dma("sp" if g % 2 == 0 else "act", w[:], wav[:, :, g * 512:(g + 1) * 512], writes=[w])
            for m4 in range(4):
                m = g * 4 + m4
                for j in range(16):
                    mm(psA[:, 2 * m:2 * m + 2], w[:, j, m4 * 128:(m4 + 1) * 128], sc3[:, :, j], j == 0, j == 15, [w, sc], [psA])
        tt("dve", modc[:], psA[:], bct[:], ALU.add, [psA, bct], [modc])
        mc3 = modc[:].rearrange("p (m v) -> p v m", v=2)
        for v in range(2):
            stt("dve", acol[:, v, :], mc3[:, v, 16:32], 1.0, ngt[:], ALU.add, ALU.mult, [modc, ngt], [acol])
        jobs = [(acol, lambda v, m: acol[:, v, m:m + 1], Abc[0], 0), (acol, lambda v, m: acol[:, v, m:m + 1], Abc[1], 1),
                (modc, lambda v, m: mc3[:, v, m:m + 1], Bbc[0], 0), (modc, lambda v, m: mc3[:, v, m:m + 1], Bbc[1], 1),
                (modc, lambda v, m: mc3[:, v, 32 + m:33 + m], gatebc, 0)]
        k = 0
        for src, colf, dst, v in jobs:
            for m4 in range(4):
                d_ = dg[k % 2]; p_ = psB[k % 2]; k += 1
                for mi in range(4):
                    m = m4 * 4 + mi
                    ts("dve", d_[:, mi * 128:(mi + 1) * 128], identf[:], colf(v, m), None, ALU.mult, ALU.bypass, [identf, src], [d_])
                mm(p_[:], onesf[:], d_[:], True, True, [onesf, d_], [p_])
                cp("act", dst[:, m4 * 512:(m4 + 1) * 512], p_[:], [p_], [dst])
        dma("sp", GATE[:, :], gatebc[:], reads=[gatebc], writes=[GATEb])
        if "dbg_bc" in dbg:
            for i, t_ in enumerate([Abc[0], Abc[1], Bbc[0], Bbc[1], gatebc]):
                dma("sp", dbg["dbg_bc"][i], t_[:], reads=[t_])
    S.barrier()

    blocks = [(0, 512, 'r'), (512, 512, 'r'), (1024, 512, 'k'), (1536, 512, 'k'), (2048, 512, 'v'), (2560, 512, 'v'),
              (3072, 256, 'lo'), (3328, 512, 'zr'), (3840, 512, 'zr'), (4352, 512, 'q'), (4864, 512, 'q'),
              (5376, 512, 'za'), (5888, 512, 'za'), (6400, 512, 'kv')]
    tblocks = [([0, 1], {'k', 'v', 'lo', 'kv'})] + [(list(range(s, s + 4)), None) for s in (2, 6, 10, 14)] + \
              [([18, 19, 20, 21], {'r', 'k', 'v', 'lo', 'kv'})] + [(list(range(s, s + 4)), {'k', 'v', 'lo'}) for s in (22, 26, 30)]
    if stop_after >= 1:
        with contextlib.ExitStack() as st:
            xt = [sb(st, "xt%d" % i, [128, D]) for i in range(2)]
            junk = sb(st, "junk", [128, D], BF16)
            ssq = [sb(st, "ssq%d" % i, [128, 1]) for i in range(2)]
            xs = [sb(st, "xs%d" % i, [128, D]) for i in range(2)]
            xn = [sb(st, "xn%d" % i, [128, D], BF16) for i in range(2)]
            xnT = [sb(st, "xnT%d" % i, [128, 16, 512], BF16) for i in range(2)]
            wb = [sb(st, "wb%d" % i, [128, 16, 512], BF16) for i in range(3)]
            stg = [sb(st, "stg%d" % i, [128, 512]) for i in range(4)]
            pT = [ps(st, "pT%d" % i, [128, 1024], BF16) for i in range(2)]
            pp = [ps(st, "pp%d" % i, [128, 512]) for i in range(4)]
            winv = WB.rearrange("(j p) n -> p j n", p=128)
            wi = 0; si = 0; ti = 0
            for bi, (tiles, need) in enumerate(tblocks):
                xT = xnT[bi % 2]
                for tl, i in enumerate(tiles):
                    v = 1 if i < 2 else 0
                    x_ = xt[ti % 2]; sq_ = ssq[ti % 2]; xs_ = xs[ti % 2]; xn_ = xn[ti % 2]; ti += 1
                    dma("act", x_[:], xin[i * 128:(i + 1) * 128, :], writes=[x_])
                    memset("dve", sq_[:], 0.0, [sq_])
                    act(junk[:], x_[:], AF.Square, [x_, sq_], [junk, sq_], accum=sq_[:, 0:1])
                    act(sq_[:], sq_[:], AF.Sqrt, [sq_], [sq_], scale=1.0 / D, bias=1e-6)
                    recip(sq_[:], sq_[:], [sq_], [sq_])
                    stt("dve", xs_[:], x_[:], sq_[:, 0:1], Abc[v][:], ALU.mult, ALU.mult, [x_, sq_, Abc[v]], [xs_])
                    tt("pool", xn_[:], xs_[:], Bbc[v][:], ALU.add, [xs_, Bbc[v]], [xn_])
                    for half in range(2):
                        p_ = pT[half]
                        for jj in range(8):
                            j = half * 8 + jj
                            tr(p_[:, jj * 128:(jj + 1) * 128], xn_[:, j * 128:(j + 1) * 128], identb[:], [xn_, identb], [p_])
                        cp("act" if half == 0 else "dve", xT[:, half * 8:(half + 1) * 8, tl * 128:(tl + 1) * 128],
                           p_[:].rearrange("p (j t) -> p j t", t=128), [p_], [xT])
                for (c0, cw, grp) in blocks:
                    if need is not None and grp not in need:
                        continue
                    w = wb[wi % 3]; wi += 1
                    dma("sp", w[:, :, 0:cw], winv[:, :, c0:c0 + cw], reads=[WBb], writes=[w])
                    for tl, i in enumerate(tiles):
                        p_ = pp[si % 4]; s_ = stg[si % 4]; si += 1
                        for j in range(16):
                            mm(p_[:, 0:cw], xT[:, j, tl * 128:(tl + 1) * 128], w[:, j, 0:cw], j == 0, j == 15, [xT, w], [p_])
                        cp("act" if si % 2 == 0 else "dve", s_[:, 0:cw], p_[:, 0:cw], [p_], [s_])
                        dma("pool", P[i * 128:(i + 1) * 128, c0:c0 + cw], s_[:, 0:cw], reads=[s_], writes=[Ptile[i]])
        S.barrier()

    st01.close()
    GN_EPS = 64e-5

    def bcast_load(eng, dst, src_row):
        return dma(eng, dst[:].rearrange("p (o n) -> p o n", o=1), src_row.partition_broadcast(128), writes=[dst])

    def rwkv_sweep(d):
        tiles = list(range(0, 18)) if d == 0 else [1, 0] + list(range(33, 1, -1))
        import os
        KT_ = int(os.environ.get("KTILES", "99")); KS_ = int(os.environ.get("KSTAGE", "99"))
        tiles = tiles[:KT_]
        with contextlib.ExitStack() as st:
            mubc = sb(st, "mubc", [128, SHC]); bcast_load("sp", mubc, mu[0:1, :])
            w0bc = sb(st, "w0bc", [128, 1024]); bcast_load("sp", w0bc, w0[d:d + 1, :])
            a0bc = sb(st, "a0bc", [128, 1024]); bcast_load("sp", a0bc, a0[d:d + 1, :])
            kkbc = sb(st, "kkbc", [128, 1024]); bcast_load("sp", kkbc, k_k[0:1, :])
            kabc = sb(st, "kabc", [128, 1024]); bcast_load("sp", kabc, k_a[0:1, :])
            rkbc = sb(st, "rkbc", [128, 1024]); bcast_load("sp", rkbc, r_k[0:1, :])
            if d == 1:
                lngbc = sb(st, "lngbc", [128, 1024]); bcast_load("sp", lngbc, ln_g[0:1, :])
                lnbbc = sb(st, "lnbbc", [128, 1024]); bcast_load("sp", lnbbc, ln_b[0:1, :])
            WAb = sb(st, "WAb", [128, 1024], BF16)
            dma("pool", WAb[0:64, :], w2[d], writes=[WAb]); dma("pool", WAb[64:128, :], a2[d], writes=[WAb])
            triI = sb(st, "triI", [128, 128]); dma("sp", triI[:], c_tri[2 * d], writes=[triI])
            triS = sb(st, "triS", [128, 128]); dma("sp", triS[:], c_tri[2 * d + 1], writes=[triS])
            negc = sb(st, "negc", [128, 1]); memset("dve", negc[:], -CDEC, [negc])
            maskx = sb(st, "maskx", [128, 448], BF16); dma("pool", maskx[:], c_maskx[d], writes=[maskx])
            masky = sb(st, "masky", [128, 256], BF16); dma("pool", masky[:], c_masky[d], writes=[masky])
            shm = sb(st, "shm", [128, 128], BF16); dma("pool", shm[:], c_sh[:, :], writes=[shm])
            e2 = sb(st, "e2", [2, 128], BF16); dma("pool", e2[:], c_e[:, :], writes=[e2])
            CM = sb(st, "CM", [128, 8, 576], BF16)
            for pr in range(8):
                dma("pool", CM[:, pr, 128:192], c_i64[:, :], writes=[CM])
            ST = [sb(st, "ST%d" % h, [64, 2, 64]) for h in range(16)]
            for h in range(16):
                memset("pool", ST[h][:], 0.0, [ST[h]])
            cur = [0] * 16
            sh = sb(st, "sh", [128, SHC]); pc16 = sb(st, "pc16", [128, SHC], BF16); nb16 = sb(st, "nb16", [2, SHC], BF16)
            lo16 = sb(st, "lo16", [128, 128], BF16); loT = sb(st, "loT", [128, 128], BF16)
            sg = sb(st, "sg", [128, 1024]); al = sb(st, "al", [128, 1024]); kk = sb(st, "kk", [128, 1024])
            bb = sb(st, "bb", [128, 1024]); kd = sb(st, "kd", [128, 1024]); t1 = sb(st, "t1", [128, 1024])
            E = [sb(st, "E%d" % k, [128, 1024]) for k in range(2)]
            ss = sb(st, "ss", [128, 16]); bon = sb(st, "bon", [128, 16]); WC = sb(st, "WC", [64, 16])
            rt, kt, bt, at, ktp, btp, vb = [sb(st, n, [128, 1024], BF16) for n in ("rt", "kt", "bt", "at", "ktp", "btp", "vb")]
            X0 = [sb(st, "X0%d" % p_, [128, 448], BF16) for p_ in range(8)]
            TT = [sb(st, "TT%d" % p_, [128, 256], BF16) for p_ in range(8)]
            CC = [sb(st, "CC%d" % p_, [128, 256], BF16) for p_ in range(8)]
            Zf = [sb(st, "Zf%d" % p_, [128, 192], BF16) for p_ in range(8)]
            msk = sb(st, "msk", [128, 7, 256], U8); dma("pool", msk[:], c_ms[d].rearrange("s p c -> p s c"), writes=[msk])
            II = sb(st, "II", [128, 256], BF16)
            cp("pool", II[:, 0:128], identb[:], [identb], [II]); cp("pool", II[:, 128:256], identb[:], [identb], [II])
            ARBK = [sb(st, "ARBK%d" % p_, [128, 256], BF16) for p_ in range(4)]
            QG = [sb(st, "QG%d" % p_, [64, 192], BF16) for p_ in range(4)]
            STb = [sb(st, "STb%d" % h, [64, 64], BF16) for h in range(16)]
            tmpS = [sb(st, "tmpS%d" % h, [64, 64]) for h in range(4)]
            tmpP = [sb(st, "tmpP%d" % h, [64, 64]) for h in range(4)]
            for h in range(16):
                memset("pool", STb[h][:], 0.0, [STb[h]])
            MYH = [sb(st, "MYH%d" % p_, [128, 192], BF16) for p_ in range(4)]
            ysb = sb(st, "ysb", [128, 1024])
            if d == 1:
                ya = sb(st, "ya", [128, 1024]); zr = sb(st, "zr", [128, 1024])
                mixb = sb(st, "mixb", [128, 1024], BF16); mtT = sb(st, "mtT", [128, 8, 128], BF16)
                s1 = sb(st, "s1", [128, 16]); s2 = sb(st, "s2", [128, 16]); mean = sb(st, "mean", [128, 16]); m2 = sb(st, "m2", [128, 16])
            Wd = [ps(st, "Wd%d" % k, [128, 512]) for k in range(2)]
            psT = ps(st, "psT", [128, 1024], BF16)
            Xp = [ps(st, "Xp%d" % k, [128, 512]) for k in range(2)]
            LB = [Xp[0], Xp[1], Wd[0], Wd[1]]
            b5 = ps(st, "b5", [128, 512])
            B6 = ps(st, "B6", [128, 512])
            b7 = ps(st, "b7", [128, 512])
            FB = [b5, b5, b7, b7]
            _ft5 = T(b5.t); _ft7 = T(b7.t)
            FT = [_ft5, _ft5, _ft7, _ft7]
            wk = 0
            def issue_p16(i_):
                own_ = OWN0 <= i_ < OWN1
                c0_ = 0 if own_ else 1024
                r0_ = i_ * 128
                dma("pool", pc16[:, c0_:SHC], P[r0_:r0_ + 128, c0_:SHC], reads=[Ptile[i_]], writes=[pc16])
                memset("pool", nb16[:], 0.0, [nb16])
                if i_ not in (0, 2):
                    dma("pool", nb16[0:1, c0_:SHC], P[r0_ - 1:r0_, c0_:SHC], reads=[Ptile[i_ - 1]], writes=[nb16])
                if i_ not in (1, 33):
                    dma("pool", nb16[1:2, c0_:SHC], P[r0_ + 128:r0_ + 129, c0_:SHC], reads=[Ptile[i_ + 1]], writes=[nb16])

            for ti_, i in enumerate(tiles):
                own = OWN0 <= i < OWN1
                c0 = 0 if own else 1024
                r0 = i * 128
                has_prev = i not in (0, 2); has_next = i not in (1, 33)
                dma("sp", sh[:, c0:SHC], P[r0:r0 + 128, c0:SHC], reads=[Ptile[i]], writes=[sh])
                if ti_ == 0:
                    issue_p16(i)
                if d == 1 and own:
                    dma("sp", ya[:], YA[(i - 2) * 128:(i - 1) * 128, :], reads=[YAb], writes=[ya])
                    dma("sp", zr[:], P[r0:r0 + 128, C_ZR:C_ZR + 1024], reads=[Ptile[i]], writes=[zr])
                cs = c0
                while cs < SHC:
                    cw = min(512, SHC - cs)
                    W = Wd[wk % 2]; tm_ = E[wk % 2]; wk += 1
                    mm(W[:, 0:cw], shm[:], pc16[:, cs:cs + cw], True, False, [shm, pc16], [W])
                    mm(W[:, 0:cw], e2[0:2, :], nb16[0:2, cs:cs + cw], False, True, [e2, nb16], [W])
                    tt("dve", tm_[:, 0:cw], W[:, 0:cw], mubc[:, cs:cs + cw], ALU.mult, [W, mubc], [tm_])
                    tt("dve", sh[:, cs:cs + cw], tm_[:, 0:cw], sh[:, cs:cs + cw], ALU.add, [tm_, sh], [sh])
                    cs += cw
                if ti_ + 1 < len(tiles):
                    issue_p16(tiles[ti_ + 1])
                r_ = sh[:, 0:1024]; k_ = sh[:, 1024:2048]; v_ = sh[:, 2048:3072]
                if KS_ <= 1:
                    continue
                act(lo16[:, 0:64], sh[:, 3072 + 64 * d:3136 + 64 * d], AF.Tanh, [sh], [lo16])
                cp("dve", lo16[:, 64:128], sh[:, 3200 + 64 * d:3264 + 64 * d], [sh], [lo16])
                tr(psT[:, 0:128], lo16[:, :], identb[:], [lo16, identb], [psT])
                cp("dve", loT[:], psT[:, 0:128], [psT], [loT])
                for (lo_, bias_, dst_) in ((0, w0bc, sg), (64, a0bc, al)):
                    for hf in range(2):
                        W = Wd[wk % 2]; wk += 1
                        mm(W[:], loT[lo_:lo_ + 64, :], WAb[lo_:lo_ + 64, hf * 512:(hf + 1) * 512], True, True, [loT, WAb], [W])
                        tt("dve", t1[:, hf * 512:(hf + 1) * 512], W[:], bias_[:, hf * 512:(hf + 1) * 512], ALU.add, [W, bias_], [t1])
                    act(dst_[:], t1[:], AF.Sigmoid, [t1], [dst_])
                tt("dve", kk[:], k_, kkbc[:], ALU.mult, [sh, kkbc], [kk])
                tt("dve", t1[:], kk[:], kk[:], ALU.mult, [kk], [t1])
                red("dve", ss[:], t1[:].rearrange("p (a b) -> p a b", b=64), [t1], [ss])
                act(ss[:], ss[:], AF.Sqrt, [ss], [ss])
                ts("dve", ss[:], ss[:], 1e-12, None, ALU.max, None, [ss], [ss])
                recip(ss[:], ss[:], [ss], [ss])
                tt("dve", kk[:].rearrange("p (a b) -> p a b", b=64), kk[:].rearrange("p (a b) -> p a b", b=64),
                   ss[:].unsqueeze(2).to_broadcast([128, 16, 64]), ALU.mult, [kk, ss], [kk])
                tt("dve", bb[:], kk[:], al[:], ALU.mult, [kk, al], [bb])
                stt("dve", t1[:], al[:], -1.0, kabc[:], ALU.add, ALU.mult, [al, kabc], [t1])
                stt("dve", kd[:], t1[:], 1.0, k_, ALU.add, ALU.mult, [t1, sh], [kd])
                if own:
                    tt("dve", t1[:], r_, rkbc[:], ALU.mult, [sh, rkbc], [t1])
                    tt("dve", t1[:], t1[:], kd[:], ALU.mult, [t1, kd], [t1])
                    if d == 0:
                        red("dve", bonA[:, i - 2, :], t1[:].rearrange("p (a b) -> p a b", b=64), [t1], [bonA])
                    else:
                        red("dve", bon[:], t1[:].rearrange("p (a b) -> p a b", b=64), [t1], [bon])
                        tt("dve", bon[:], bon[:], bonA[:, i - 2, :], ALU.add, [bon, bonA], [bon])
                if "dbg_sh" in dbg and d == 0 and i == 2:
                    dma("sp", dbg["dbg_sh"][0:128, :], sh[:], reads=[sh])
                    for k_i, t_ in enumerate([sg, al, kk, kd]):
                        dma("sp", dbg["dbg_sh"][128 * (k_i + 1):128 * (k_i + 2), 0:1024], t_[:], reads=[t_])
                for hf in range(2):
                    hs = slice(hf * 512, (hf + 1) * 512)
                    W = Wd[wk % 2]; wk += 1
                    mm(W[:], triI[:], sg[:, hs], True, True, [triI, sg], [W])
                    if own:
                        act(E[0][:, hs], W[:], AF.Exp, [W], [E[0]])
                    act(E[1][:, hs], W[:], AF.Exp, [W], [E[1]], scale=-1.0)
                    stt("dve", t1[:, hs], sg[:, hs], CDEC, W[:], ALU.mult, ALU.add, [sg, W], [t1])
                if own:
                    tt("dve", rt[:], r_, E[0][:], ALU.mult, [sh, E[0]], [rt])
                tt("dve", kt[:], kd[:], E[1][:], ALU.mult, [kd, E[1]], [kt])
                tt("dve", bt[:], bb[:], E[1][:], ALU.mult, [bb, E[1]], [bt])
                act(E[0][:], t1[:], AF.Exp, [t1], [E[0]])
                stt("dve", at[:], kk[:], -1.0, E[0][:], ALU.mult, ALU.mult, [kk, E[0]], [at])
                for hf in range(2):
                    hs = slice(hf * 512, (hf + 1) * 512)
                    W = Wd[wk % 2]; wk += 1
                    mm(W[:], triS[:], sg[:, hs], True, True, [triS, sg], [W])
                    act(E[1][:, hs], W[:], AF.Exp, [W], [E[1]])
                tt("dve", ktp[:], kd[:], E[1][:], ALU.mult, [kd, E[1]], [ktp])
                tt("dve", btp[:], bb[:], E[1][:], ALU.mult, [bb, E[1]], [btp])
                cp("act", vb[:], v_, [sh], [vb])
                for h in range(16):
                    mm(Wd[0][0:64, h:h + 1], sg[:, h * 64:(h + 1) * 64], negc[:, 0:1], True, True, [sg, negc], [Wd[0]])
                act(WC[:], Wd[0][0:64, 0:16], AF.Exp, [Wd[0]], [WC])
                if KS_ <= 2:
                    continue
                DD = (d == 0 and i in (0, 2))
                if DD:
                    for nm_, t_ in (("bt", bt), ("kt", kt), ("at", at), ("rt", rt), ("btp", btp), ("ktp", ktp), ("vb", vb)):
                        dd("%s_%d" % (nm_, i), t_, t_[:], [128, 1024])
                    dd("WC_%d" % i, WC, WC[:], [64, 16])
                srcs = [(bt, 0), (kt, 192), (at, 320)] + ([(rt, 448)] if own else [])
                for si_, (src, off) in enumerate(srcs):
                    for pr in range(8):
                        tr(psT[:, pr * 128:(pr + 1) * 128], src[:, pr * 128:(pr + 1) * 128], identb[:], [src, identb], [psT])
                    cp("act" if si_ % 2 == 0 else "dve", CM[:, :, off:off + 128], psT[:].rearrange("p (a b) -> p a b", b=128), [psT], [CM])
                if DD:
                    dd("CM_%d" % i, CM, CM[:, 0, :], [128, 576])
                if KS_ <= 3:
                    continue
                for grp in range(2):
                    hs4 = [(g, 8 * grp + g, (8 * grp + g) // 2, 64 * ((8 * grp + g) % 2)) for g in range(8)]
                    for (g, h, pr, pb) in hs4:
                        lb = LB[g % 4]
                        mm(lb[:, 128:448], CM[pb:pb + 64, pr, 320:448], CM[pb:pb + 64, pr, 0:320], True, True, [CM], [lb])
                        mm(lb[:, 0:128], CM[pb:pb + 64, pr, 0:128], CM[pb:pb + 64, pr, 320:448], True, True, [CM], [lb])
                        cp("pool", TT[g][:], II[:], [II], [TT[g]])
                        S.op("dve", lambda e, g=g, lb=lb: e.copy_predicated(out=TT[g][:], mask=msk[:, 0, :], data=lb[:, 0:256]),
                             reads=[lb, msk, TT[g]], writes=[TT[g]])
                        tt("dve", X0[g][:], lb[:, 0:448], maskx[:], ALU.mult, [lb, maskx], [X0[g]])
                        if DD and h < 2:
                            dd("X0_%d_%d" % (i, h), X0[g], X0[g][:], [128, 448])
                    if KS_ <= 4:
                        continue
                    for lev in range(1, 7):
                        for (g, h, pr, pb) in hs4:
                            lb = LB[g % 4]
                            mm(lb[:, 0:128], X0[g][:, 128:256], TT[g][:, 0:128], True, True, [X0[g], TT[g]], [lb])
                            mm(lb[:, 128:256], X0[g][:, 0:128], TT[g][:, 128:256], True, True, [X0[g], TT[g]], [lb])
                            cp("act", CC[g][:], lb[:, 0:256], [lb], [CC[g]])
                        for (g, h, pr, pb) in hs4:
                            lb = LB[g % 4]
                            mm(lb[:, 256:384], TT[g][:, 128:256], CC[g][:, 0:128], True, True, [TT[g], CC[g]], [lb])
                            mm(lb[:, 384:512], TT[g][:, 0:128], CC[g][:, 128:256], True, True, [TT[g], CC[g]], [lb])
                            S.op("dve", lambda e, g=g, lb=lb, lev=lev: e.copy_predicated(out=TT[g][:], mask=msk[:, lev, :], data=lb[:, 256:512]),
                                 reads=[lb, msk, TT[g]], writes=[TT[g]])
                    for (g, h, pr, pb) in hs4:
                        lb = LB[g % 4]
                        mm(lb[:, 0:192], TT[g][:, 0:128], X0[g][:, 256:448], True, True, [TT[g], X0[g]], [lb])
                        cp("dve" if g % 2 == 0 else "act", Zf[g][:], lb[:, 0:192], [lb], [Zf[g]])
                        if DD and h < 2:
                            dd("Zf_%d_%d" % (i, h), Zf[g], Zf[g][:], [128, 192])
                            dd("TT_%d_%d" % (i, h), TT[g], TT[g][:], [128, 256])
                    for sub in range(2):
                        hsub = hs4[4 * sub:4 * sub + 4]
                        if own:
                            for (g, h, pr, pb) in hsub:
                                lb = LB[g % 4]
                                mm(lb[:, 0:128], CM[pb:pb + 64, pr, 0:128], CM[pb:pb + 64, pr, 448:576], True, True, [CM], [lb])
                                mm(lb[:, 128:256], CM[pb:pb + 64, pr, 192:320], CM[pb:pb + 64, pr, 448:576], True, True, [CM], [lb])
                                tt("dve", ARBK[g % 4][:], lb[:, 0:256], masky[:], ALU.mult, [lb, masky], [ARBK[g % 4]])
                        lo_c = 0 if own else 128
                        for (g, h, pr, pb) in hsub:
                            lb = LB[g % 4]; fb = FB[g % 4]; fo = ((g % 4) % 2) * 256
                            hc = slice(h * 64, (h + 1) * 64)
                            AbT = Zf[g][:, 0:64]; PTt = Zf[g][:, 64:192]
                            if own:
                                mm(fb[0:64, fo:fo + 128], AbT, ARBK[g % 4][:, 0:128], True, False, [Zf[g], ARBK[g % 4]], [FT[g % 4]])
                                mm(fb[0:64, fo:fo + 128], rt[:, hc], identb[:], False, True, [rt, identb], [FT[g % 4]])
                            mm(fb[0:64, fo + 128:fo + 192], AbT, btp[:, hc], True, True, [Zf[g], btp], [FT[g % 4]])
                            cp("act", QG[g % 4][:, lo_c:192], fb[0:64, fo + lo_c:fo + 192], [FT[g % 4]], [QG[g % 4]])
                            if own:
                                mm(lb[:, 256:384], PTt, ARBK[g % 4][:, 0:128], True, False, [Zf[g], ARBK[g % 4]], [lb])
                                mm(lb[:, 256:384], identb[:], ARBK[g % 4][:, 128:256], False, True, [identb, ARBK[g % 4]], [lb])
                            mm(lb[:, 384:448], PTt, btp[:, hc], True, False, [Zf[g], btp], [lb])
                            mm(lb[:, 384:448], identb[:], ktp[:, hc], False, True, [identb, ktp], [lb])
                            cp("dve", MYH[g % 4][:, lo_c:192], lb[:, 256 + lo_c:448], [lb], [MYH[g % 4]])
                            if DD and h < 2:
                                dd("QG_%d_%d" % (i, h), QG[g % 4], QG[g % 4][:], [64, 192])
                                dd("MYH_%d_%d" % (i, h), MYH[g % 4], MYH[g % 4][:], [128, 192])
                                if own:
                                    dd("ARBK_%d_%d" % (i, h), ARBK[g % 4], ARBK[g % 4][:], [128, 256])
                        for (g, h, pr, pb) in hsub:
                            fb = FB[g % 4]; fo = ((g % 4) % 2) * 256
                            hc = slice(h * 64, (h + 1) * 64)
                            STc = ST[h][:, cur[h], :]; STn = ST[h][:, 1 - cur[h], :]
                            if own:
                                yo = B6[:, (h % 8) * 64:(h % 8) * 64 + 64]
                                mm(yo, QG[g % 4][:, 0:128], STb[h][:], True, False, [QG[g % 4], STb[h]], [B6])
                                mm(yo, MYH[g % 4][:, 0:128], vb[:, hc], False, True, [MYH[g % 4], vb], [B6])
                            sreg = fb[0:64, fo + 192:fo + 256]
                            mm(sreg, QG[g % 4][:, 128:192], STb[h][:], True, False, [QG[g % 4], STb[h]], [FT[g % 4]])
                            mm(sreg, MYH[g % 4][:, 128:192], vb[:, hc], False, True, [MYH[g % 4], vb], [FT[g % 4]])
                            act(tmpS[g % 4][:], STc, AF.Copy, [ST[h], WC], [tmpS[g % 4]], scale=WC[:, h:h + 1])
                            act(tmpP[g % 4][:], sreg, AF.Copy, [FT[g % 4]], [tmpP[g % 4]])
                            tt("pool", STn, tmpP[g % 4][:], tmpS[g % 4][:], ALU.add, [tmpP[g % 4], tmpS[g % 4]], [ST[h]])
                            cp("pool", STb[h][:], STn, [ST[h]], [STb[h]])
                            if DD and h < 2:
                                dd("ST_%d_%d" % (i, h), ST[h], STn, [64, 64])
                            cur[h] ^= 1
                    if own:
                        hs = slice(grp * 512, grp * 512 + 512)
                        if d == 0:
                            cp("act", ysb[:, hs], B6[:], [B6], [ysb])
                        else:
                            tt("dve", ysb[:, hs], B6[:], ya[:, hs], ALU.add, [B6, ya], [ysb])
                if not own:
                    continue
                if d == 0:
                    if DD:
                        dd("ysb_%d" % i, ysb, ysb[:], [128, 1024])
                    dma("act", YA[(i - 2) * 128:(i - 1) * 128, :], ysb[:], reads=[ysb], writes=[YAb])
                    continue
                if "dbg_y" in dbg:
                    dma("sp", dbg["dbg_y"][(i - 2) * 128:(i - 1) * 128, :], ysb[:], reads=[ysb])
                y3 = ysb[:].rearrange("p (a b) -> p a b", b=64)
                red("dve", s1[:], y3, [ysb], [s1])
                tt("dve", t1[:], ysb[:], ysb[:], ALU.mult, [ysb], [t1])
                red("dve", s2[:], t1[:].rearrange("p (a b) -> p a b", b=64), [t1], [s2])
                ts("dve", mean[:], s1[:], 1.0 / 64, None, ALU.mult, None, [s1], [mean])
                tt("dve", m2[:], mean[:], mean[:], ALU.mult, [mean], [m2])
                stt("dve", s2[:], s2[:], 1.0 / 64, m2[:], ALU.mult, ALU.subtract, [s2, m2], [s2])
                act(s2[:], s2[:], AF.Sqrt, [s2], [s2], bias=GN_EPS)
                recip(s2[:], s2[:], [s2], [s2])
                tt("dve", y3, y3, mean[:].unsqueeze(2).to_broadcast([128, 16, 64]), ALU.subtract, [ysb, mean], [ysb])
                tt("dve", y3, y3, s2[:].unsqueeze(2).to_broadcast([128, 16, 64]), ALU.mult, [ysb, s2], [ysb])
                tt("dve", ysb[:], ysb[:], lngbc[:], ALU.mult, [ysb, lngbc], [ysb])
                tt("dve", ysb[:], ysb[:], lnbbc[:], ALU.add, [ysb, lnbbc], [ysb])
                tt("dve", t1[:].rearrange("p (a b) -> p a b", b=64), vb[:].rearrange("p (a b) -> p a b", b=64),
                   bon[:].unsqueeze(2).to_broadcast([128, 16, 64]), ALU.mult, [vb, bon], [t1])
                tt("dve", ysb[:], ysb[:], t1[:], ALU.add, [ysb, t1], [ysb])
                if "dbg_rw" in dbg:
                    dma("sp", dbg["dbg_rw"][(i - 2) * 128:(i - 1) * 128, :], ysb[:], reads=[ysb])
                act(zr[:], zr[:], AF.Silu, [zr], [zr])
                tt("dve", mixb[:], ysb[:], zr[:], ALU.mult, [ysb, zr], [mixb])
                for cch in range(8):
                    tr(psT[:, cch * 128:(cch + 1) * 128], mixb[:, cch * 128:(cch + 1) * 128], identb[:], [mixb, identb], [psT])
                cp("act", mtT[:], psT[:].rearrange("p (a b) -> p a b", b=128), [psT], [mtT])
                dma("act", MT[0:8, :, (i - 2) * 128:(i - 1) * 128].rearrange("c p t -> p c t"), mtT[:], reads=[mtT], writes=[MTb])
        S.barrier()

    if stop_after >= 2:
        rwkv_sweep(0)
    if stop_after >= 3:
        rwkv_sweep(1)

    def rope_apply(eng2, t3, rot3, rp, nh, reads_t, T_t, T_rot, T_rp):
        t5 = t3.rearrange("p h (a b c) -> p h a b c", a=2, b=2)
        r5 = rot3.rearrange("p h (a b c) -> p h a b c", a=2, b=2)
        cp("pool", r5[:, :, :, 0, :], t5[:, :, :, 1, :], [T_t], [T_rot])
        cp("pool", r5[:, :, :, 1, :], t5[:, :, :, 0, :], [T_t], [T_rot])
        cosb = rp[:, 0:64].unsqueeze(1).to_broadcast([128, nh, 64])
        sinb = rp[:, 64:128].unsqueeze(1).to_broadcast([128, nh, 64])
        tt("dve", t3, t3, cosb, ALU.mult, [T_t, T_rp], [T_t])
        tt("dve", rot3, rot3, sinb, ALU.mult, [T_rot, T_rp], [T_rot])
        tt("dve", t3, t3, rot3, ALU.add, [T_t, T_rot], [T_t])

    def attention():
        with contextlib.ExitStack() as st:
            kgbc = sb(st, "kgbc", [128, 64]); bcast_load("sp", kgbc, kg[0:1, :])
            qgbc = sb(st, "qgbc", [128, 64]); bcast_load("sp", qgbc, qg[0:1, :])
            ts("dve", qgbc[:], qgbc[:], 0.125, None, ALU.mult, None, [qgbc], [qgbc])
            esk = sb(st, "esk", [128, 16]); bcast_load("sp", esk, sink[0:1, :])
            act(esk[:], esk[:], AF.Exp, [esk], [esk])
            amask = sb(st, "amask", [128, 2, 128], BF16)
            dma("pool", amask[:], c_amask.rearrange("a k q -> k a q"), writes=[amask])
            KT = sb(st, "KT", [64, 19, 4, 128], BF16)
            V1 = sb(st, "V1", [128, 19, 4, 65], BF16)
            memset("pool", V1[:], 1.0, [V1])
            kv = sb(st, "kv", [128, 512]); rot = sb(st, "rot", [128, 1024]); rp = sb(st, "rp", [128, 128])
            sq = sb(st, "sq", [128, 1024]); ss = sb(st, "ssq_a", [128, 16])
            kb = sb(st, "kb", [128, 256], BF16)
            q = sb(st, "q", [128, 1024]); za = sb(st, "za", [128, 1024]); qb = sb(st, "qb", [128, 1024], BF16)
            QT = sb(st, "QT", [64, 16, 128], BF16)
            PTs = [sb(st, "PTs%d" % k, [128, 512], BF16) for k in range(5)]
            att = sb(st, "att", [128, 1024]); den = sb(st, "den", [128, 16])
            mixb = sb(st, "mixb_a", [128, 1024], BF16); mtT = sb(st, "mtT_a", [128, 8, 128], BF16)
            psT = ps(st, "psT_a", [128, 1024], BF16)
            Sps = [ps(st, "Sps%d" % k, [128, 512]) for k in range(2)]
            Og = [ps(st, "Og%d" % k, [128, 512]) for k in range(4)]
            for n in range(19):
                i = n
                dma("sp", kv[:], P[i * 128:(i + 1) * 128, C_KA:C_KA + 512], reads=[Ptile[i]], writes=[kv])
                k3 = kv[:, 0:256].rearrange("p (h c) -> p h c", c=64)
                tt("dve", sq[:, 0:256], kv[:, 0:256], kv[:, 0:256], ALU.mult, [kv], [sq])
                red("dve", ss[:, 0:4], sq[:, 0:256].rearrange("p (h c) -> p h c", c=64), [sq], [ss])
                act(ss[:, 0:4], ss[:, 0:4], AF.Sqrt, [ss], [ss], scale=1.0 / 64, bias=1e-6)
                recip(ss[:, 0:4], ss[:, 0:4], [ss], [ss])
                tt("dve", k3, k3, ss[:, 0:4].unsqueeze(2).to_broadcast([128, 4, 64]), ALU.mult, [kv, ss], [kv])
                tt("dve", k3, k3, kgbc[:].unsqueeze(1).to_broadcast([128, 4, 64]), ALU.mult, [kv, kgbc], [kv])
                if i >= 2:
                    dma("sp", rp[:], rope[(i - 2) * 128:(i - 1) * 128, :], writes=[rp])
                    rope_apply(None, k3, rot[:, 0:256].rearrange("p (h c) -> p h c", c=64), rp, 4, None, kv, rot, rp)
                cp("act", kb[:], kv[:, 0:256], [kv], [kb])
                for g in range(4):
                    tr(psT[0:64, g * 128:(g + 1) * 128], kb[:, g * 64:(g + 1) * 64], identb[:], [kb, identb], [psT])
                cp("dve", KT[:, n, :, :], psT[0:64, 0:512].rearrange("p (g t) -> p g t", t=128), [psT], [KT])
                cp("act", V1[:, n, :, 0:64], kv[:, 256:512].rearrange("p (h c) -> p h c", c=64), [kv], [V1])
            sk = 0
            for i in range(OWN0, OWN1):
                r0 = i * 128
                dma("sp", q[:], P[r0:r0 + 128, C_Q:C_Q + 1024], reads=[Ptile[i]], writes=[q])
                dma("sp", za[:], P[r0:r0 + 128, C_ZA:C_ZA + 1024], reads=[Ptile[i]], writes=[za])
                dma("sp", rp[:], rope[(i - 2) * 128:(i - 1) * 128, :], writes=[rp])
                q3 = q[:].rearrange("p (h c) -> p h c", c=64)
                tt("dve", sq[:], q[:], q[:], ALU.mult, [q], [sq])
                red("dve", ss[:], sq[:].rearrange("p (h c) -> p h c", c=64), [sq], [ss])
                act(ss[:], ss[:], AF.Sqrt, [ss], [ss], scale=1.0 / 64, bias=1e-6)
                recip(ss[:], ss[:], [ss], [ss])
                tt("dve", q3, q3, ss[:].unsqueeze(2).to_broadcast([128, 16, 64]), ALU.mult, [q, ss], [q])
                tt("dve", q3, q3, qgbc[:].unsqueeze(1).to_broadcast([128, 16, 64]), ALU.mult, [q, qgbc], [q])
                rope_apply(None, q3, rot[:].rearrange("p (h c) -> p h c", c=64), rp, 16, None, q, rot, rp)
                cp("act", qb[:], q[:], [q], [qb])
                for half in range(2):
                    for hx in range(8):
                        hd = half * 8 + hx
                        tr(psT[0:64, hx * 128:(hx + 1) * 128], qb[:, hd * 64:(hd + 1) * 64], identb[:], [qb, identb], [psT])
                    cp("dve", QT[:, half * 8:(half + 1) * 8, :], psT[0:64, :].rearrange("p (g t) -> p g t", t=128), [psT], [QT])
                for g in range(4):
                    keyt = ([i - 1] if i > OWN0 else []) + [i, i + 1, 0, 1]
                    for ki, kt_ in enumerate(keyt):
                        sp_ = Sps[sk % 2]; sk += 1
                        pt_ = PTs[ki]
                        mm(sp_[:], KT[:, kt_, g, :], QT[:, 4 * g:4 * g + 4, :].rearrange("p a b -> p (a b)"), True, True, [KT, QT], [sp_])
                        act(pt_[:], sp_[:], AF.Exp, [sp_], [pt_])
                        is_prev = (i > OWN0 and ki == 0); is_next = (ki == (2 if i > OWN0 else 1))
                        if is_prev or is_next:
                            mi = 0 if is_prev else 1
                            p3 = pt_[:].rearrange("p (a b) -> p a b", b=128)
                            tt("dve", p3, p3, amask[:, mi, :].unsqueeze(1).to_broadcast([128, 4, 128]), ALU.mult, [pt_, amask], [pt_])
                    for hx in range(4):
                        for ki, kt_ in enumerate(keyt):
                            mm(Og[g][:, hx * 65:(hx + 1) * 65], PTs[ki][:, hx * 128:(hx + 1) * 128], V1[:, kt_, g, :],
                               ki == 0, ki == len(keyt) - 1, [PTs[ki], V1], [Og[g]])
                for g in range(4):
                    o3 = Og[g][:, 0:260].rearrange("p (h c) -> p h c", c=65)
                    tt("dve", den[:, 4 * g:4 * g + 4], o3[:, :, 64], esk[:, 4 * g:4 * g + 4], ALU.add, [Og[g], esk], [den])
                    recip(den[:, 4 * g:4 * g + 4], den[:, 4 * g:4 * g + 4], [den], [den])
                    tt("dve", att[:, g * 256:(g + 1) * 256].rearrange("p (h c) -> p h c", c=64), o3[:, :, 0:64],
                       den[:, 4 * g:4 * g + 4].unsqueeze(2).to_broadcast([128, 4, 64]), ALU.mult, [Og[g], den], [att])
                if "dbg_att" in dbg:
                    dma("sp", dbg["dbg_att"][(i - 2) * 128:(i - 1) * 128, :], att[:], reads=[att])
                act(za[:], za[:], AF.Silu, [za], [za])
                tt("dve", mixb[:], att[:], za[:], ALU.mult, [att, za], [mixb])
                for cch in range(8):
                    tr(psT[:, cch * 128:(cch + 1) * 128], mixb[:, cch * 128:(cch + 1) * 128], identb[:], [mixb, identb], [psT])
                cp("act", mtT[:], psT[:].rearrange("p (a b) -> p a b", b=128), [psT], [mtT])
                dma("sp", MT[8:16, :, (i - 2) * 128:(i - 1) * 128].rearrange("c p t -> p c t"), mtT[:], reads=[mtT], writes=[MTb])
        S.barrier()

    def outproj():
        with contextlib.ExitStack() as st:
            wo = sb(st, "wo", [128, 16, D], BF16)
            wov = w_out.rearrange("(j p) n -> p j n", p=128)
            for k4 in range(4):
                dma("pool", wo[:, k4 * 4:(k4 + 1) * 4, :], wov[:, k4 * 4:(k4 + 1) * 4, :], writes=[wo])
            gate = sb(st, "gate", [128, D]); dma("sp", gate[:], GATE[:, :], reads=[GATEb], writes=[gate])
            mt = [sb(st, "mt%d" % k, [128, 16, 128], BF16) for k in range(2)]
            xo = [sb(st, "xo%d" % k, [128, D]) for k in range(2)]
            ot = [sb(st, "ot%d" % k, [128, D]) for k in range(2)]
            pp = [ps(st, "ppo%d" % k, [128, 512]) for k in range(4)]
            for tl in range(16):
                i = tl + 2
                m_ = mt[tl % 2]; x_ = xo[tl % 2]; o_ = ot[tl % 2]
                dma("sp", m_[:], MT[:, :, tl * 128:(tl + 1) * 128].rearrange("c p t -> p c t"), reads=[MTb], writes=[m_])
                dma("act", x_[:], xin[i * 128:(i + 1) * 128, :], writes=[x_])
                for nb in range(4):
                    ns = slice(nb * 512, (nb + 1) * 512)
                    p_ = pp[nb]
                    for cc_ in range(16):
                        mm(p_[:], m_[:, cc_, :], wo[:, cc_, ns], cc_ == 0, cc_ == 15, [m_, wo], [p_])
                    tt("dve", o_[:, ns], p_[:], gate[:, ns], ALU.mult, [p_, gate], [o_])
                    tt("pool", o_[:, ns], o_[:, ns], x_[:, ns], ALU.add, [o_, x_], [o_])
                dma("sp", out[tl * 128:(tl + 1) * 128, :], o_[:], reads=[o_])

    if stop_after >= 4:
        attention()
    if stop_after >= 5:
        outproj()

    S.emit(final_wait_ops=[i for i, o in enumerate(S.ops) if o.is_dma])
    top.close()
    return nc


def host_consts():
    c = {}
    c["c_ident"] = np.eye(128, dtype=np.float32)
    i = np.arange(128)
    triA_incl = (i[:, None] <= i[None, :]).astype(np.float32)
    triA_suf = (i[:, None] > i[None, :]).astype(np.float32)
    triB_incl = (i[:, None] >= i[None, :]).astype(np.float32)
    triB_suf = (i[:, None] < i[None, :]).astype(np.float32)
    c["c_tri"] = (-CDEC * np.stack([triA_incl, triA_suf, triB_incl, triB_suf])).astype(np.float32)
    mx = np.zeros((2, 128, 448), np.float32); my = np.zeros((2, 128, 256), np.float32)
    for d in range(2):
        prec = (i[:, None] < i[None, :]) if d == 0 else (i[:, None] > i[None, :])
        preceq = prec | np.eye(128, dtype=bool)
        mx[d, :, 0:128] = prec; mx[d, :, 128:256] = prec.T; mx[d, :, 256:320] = 1.0; mx[d, :, 320:448] = prec.T
        my[d, :, 0:128] = preceq; my[d, :, 128:256] = preceq
    c["c_maskx"] = mx; c["c_masky"] = my
    cms = np.zeros((2, 7, 128, 256), np.float32)
    for d in range(2):
        prec = (i[:, None] < i[None, :]) if d == 0 else (i[:, None] > i[None, :])
        for s_ in range(7):
            m = prec & ((i[:, None] >> (s_ + 1)) == (i[None, :] >> (s_ + 1))) & ((i[:, None] >> s_) != (i[None, :] >> s_))
            cms[d, s_, :, 0:128] = m; cms[d, s_, :, 128:256] = m.T
    c["c_ms"] = cms
    sh = np.zeros((128, 128), np.float32)
    sh[i, i] = -1.0; sh[i[:-1], i[:-1] + 1] = 0.5; sh[i[1:], i[1:] - 1] = 0.5
    c["c_sh"] = sh
    e = np.zeros((2, 128), np.float32); e[0, 0] = 0.5; e[1, 127] = 0.5
    c["c_e"] = e
    c["c_i64"] = np.concatenate([np.eye(64, dtype=np.float32)] * 2, axis=0)
    am = np.zeros((2, 128, 128), np.float32)
    am[0] = (i[:, None] >= i[None, :]); am[1] = (i[:, None] <= i[None, :])
    c["c_amask"] = am
    return c


def rope_tables(pos):
    pos = np.asarray(pos)
    row = (pos // 64).astype(np.float32); col = (pos % 64).astype(np.float32)
    half = 16
    inv = (10000.0 ** (-np.arange(half, dtype=np.float32) / half)).astype(np.float32)
    ar = row[:, None] * inv; ac = col[:, None] * inv
    ang = np.concatenate([ar, ar, ac, ac], axis=-1)
    cos = np.cos(ang); sin = np.sin(ang)
    sgn = np.concatenate([-np.ones(16), np.ones(16), -np.ones(16), np.ones(16)]).astype(np.float32)
    return np.concatenate([cos, sin * sgn], axis=-1).astype(np.float32)


def core_inputs(inp, b, h, consts):
    f = np.ascontiguousarray
    x = inp["x"][b]; ctx = inp["ctx"][b]
    if h == 1:
        x = x[::-1]; ctx = ctx[::-1]
    m = {}
    m["xin"] = f(np.concatenate([ctx, x], axis=0))
    colf = lambda v: v.reshape(-1, 128).T
    m["cc"] = f(np.concatenate([colf(inp["c"][b]), colf(inp["c_ctx"])], axis=1))
    m["w_ada"] = f(inp["w_ada"][0])
    m["bcol"] = f(np.repeat(colf(inp["b_ada"][0]), 2, axis=1))
    m["ngcol"] = f(colf(inp["norm_g"][0]))
    dA, dB = (0, 1) if h == 0 else (1, 0)
    lw = lambda d: np.arange(3072 + 64 * d, 3072 + 64 * d + 64)
    la = lambda d: np.arange(3200 + 64 * d, 3200 + 64 * d + 64)
    perm = np.concatenate([np.arange(0, 3072), lw(dA), lw(dB), la(dA), la(dB), np.arange(3328, 4352), np.arange(4352, 5376),
                           np.arange(5888, 6912), np.arange(5376, 5632), np.arange(5632, 5888)])
    m["w_in"] = f(inp["w_in"][0][:, perm])
    m["mu"] = f(inp["mu_shift"][0][perm[:SHC]][None, :])
    sel = [dA, dB]
    m["w0"] = f(inp["w0"][0][sel]); m["w2"] = f(inp["w2"][0][sel]); m["a0"] = f(inp["a0"][0][sel]); m["a2"] = f(inp["a2"][0][sel])
    for k in ("k_k", "k_a"):
        m[k] = f(inp[k][0][None, :])
    m["r_k"] = f(inp["r_k"][0].reshape(1, 1024))
    m["ln_g"] = f(inp["ln_x_g"][0][None, :]); m["ln_b"] = f(inp["ln_x_b"][0][None, :])
    m["qg"] = f(inp["q_norm_g"][0][None, :]); m["kg"] = f(inp["k_norm_g"][0][None, :]); m["sink"] = f(inp["sink"][0][None, :])
    m["w_out"] = f(inp["w_out"][0])
    loc = np.arange(17 * 128)
    pos = loc if h == 0 else 4095 - loc
    m["rope"] = rope_tables(pos)
    m.update(consts)
    return m


def kernel(**inputs):
    inp = {k: np.asarray(v) for k, v in inputs.items()}
    consts = host_consts()
    nc = build()
    in_maps = [core_inputs(inp, c // 2, c % 2, consts) for c in range(8)]
    res = run_bass_kernel_spmd(nc, in_maps, core_ids=list(range(8)))
    out = np.empty((4, 4096, D), np.float32)
    for c in range(8):
        b, h = c // 2, c % 2
        o = res.results[c]["out"]
        if h == 0:
            out[b, :2048] = o
        else:
            out[b, 2048:] = o[::-1]
    return out
```

```python
import contextlib
import numpy as np
import concourse.bass as bass
import concourse.mybir as mybir
from concourse.bass_utils import run_bass_kernel_spmd

F32 = mybir.dt.float32
BF16 = mybir.dt.bfloat16
U8 = mybir.dt.uint8
AF = mybir.ActivationFunctionType
ALU = mybir.AluOpType
AX = mybir.AxisListType

ENGS = ["pe", "act", "dve", "pool", "sp"]
CDEC = 0.6065306597126334

D = 2048
NT = 34
OWN0, OWN1 = 2, 18
TL = NT * 128
PC = 6912
C_R, C_K, C_V, C_LO, C_ZR, C_Q, C_ZA, C_KA, C_VA = 0, 1024, 2048, 3072, 3328, 4352, 5376, 6400, 6656
SHC = 3328
DEBUG = {}


class Buf:
    __slots__ = ("last_w", "readers")

    def __init__(self):
        self.last_w = None
        self.readers = []


class T:
    def __init__(self, t):
        self.t = t
        self.b = Buf()

    def __getitem__(self, k):
        return self.t[k]


class Op:
    __slots__ = ("eng", "fn", "deps", "is_dma", "has_dep", "mile", "dsem", "dval", "waits")


class Sched:
    def __init__(self, nc, n_dma_sems=32):
        self.nc = nc
        self.ops = []
        self.n_dma_sems = n_dma_sems
        self.last_on = {e: None for e in ENGS}
        self.dmas_since_barrier = []

    def op(self, eng, fn, reads=(), writes=(), dma=False, extra_deps=()):
        o = Op()
        o.eng = eng; o.fn = fn; o.is_dma = dma; o.has_dep = False; o.mile = None; o.dsem = None; o.dval = None
        o.deps = set(extra_deps)
        oid = len(self.ops)
        for t in reads:
            b = t.b
            if b.last_w is not None:
                o.deps.add(b.last_w)
        for t in writes:
            b = t.b
            if b.last_w is not None:
                o.deps.add(b.last_w)
            o.deps.update(b.readers)
        for t in reads:
            t.b.readers.append(oid)
        for t in writes:
            t.b.last_w = oid
            t.b.readers = []
        o.deps.discard(oid)
        self.ops.append(o)
        self.last_on[eng] = oid
        if dma:
            self.dmas_since_barrier.append(oid)
        return oid

    def barrier(self):
        deps = [v for v in self.last_on.values() if v is not None] + list(self.dmas_since_barrier)
        self.dmas_since_barrier = []
        for e in ENGS:
            self.op(e, None, extra_deps=deps)

    def emit(self, final_wait_ops=()):
        nc = self.nc
        ops = self.ops
        for o in ops:
            nd = set()
            for d in o.deps:
                p = ops[d]
                if p.fn is None:
                    if p.eng == o.eng:
                        continue
                    nd.update(p.deps)
                    continue
                if p.eng == o.eng and not p.is_dma and o.eng == "pe" and not o.is_dma:
                    continue
                nd.add(d)
            o.deps = nd
        for o in ops:
            for d in o.deps:
                ops[d].has_dep = True
        for d in final_wait_ops:
            ops[d].has_dep = True
        cnt = {e: 0 for e in ENGS}
        dma_i = 0
        dma_cnt = [0] * self.n_dma_sems
        dma_prev = [None] * self.n_dma_sems
        for i, o in enumerate(ops):
            if o.fn is None:
                continue
            if o.is_dma:
                s = dma_i % self.n_dma_sems
                dma_i += 1
                if dma_prev[s] is not None:
                    o.deps.add(dma_prev[s])
                dma_prev[s] = i
                dma_cnt[s] += 16
                o.dsem = s
                o.dval = dma_cnt[s]
            elif o.has_dep:
                cnt[o.eng] += 1
                o.mile = cnt[o.eng]
        eng_clock = {e: {} for e in ENGS}
        clock_at = {}

        def keyval(p):
            return (("d", p.dsem), p.dval) if p.is_dma else (("e", p.eng), p.mile)

        for i, o in enumerate(ops):
            cur = eng_clock[o.eng]
            waits = []
            for d in sorted(o.deps, reverse=True):
                key, val = keyval(ops[d])
                if cur.get(key, 0) >= val:
                    continue
                waits.append((key, val))
                for k2, v2 in clock_at[d].items():
                    if cur.get(k2, 0) < v2:
                        cur[k2] = v2
            o.waits = waits
            if o.fn is not None and (o.is_dma or o.mile is not None):
                c = dict(cur)
                key, val = keyval(o)
                c[key] = val
                clock_at[i] = c
        streams = {e: [] for e in ENGS}
        for i, o in enumerate(ops):
            streams[o.eng].append(i)
        with contextlib.ExitStack() as st:
            esem = {e: st.enter_context(nc.semaphore("s_" + e)) for e in ENGS}
            dsem = [st.enter_context(nc.semaphore("d_%d" % k)) for k in range(self.n_dma_sems)]
            block = st.enter_context(nc.Block())

            def run(eng_name, engine):
                seen = {}
                for i in streams[eng_name]:
                    o = ops[i]
                    todo = [(dsem[k_[1]] if k_[0] == "d" else esem[k_[1]], v_) for (k_, v_) in o.waits]
                    embed = None
                    if todo and o.fn is not None and not o.is_dma:
                        embed = todo.pop()
                    for sem, val in todo:
                        engine.wait_ge(sem, val)
                    if o.fn is None:
                        continue
                    ins = o.fn(engine)
                    if embed is not None:
                        ins._wait_ge(embed[0], embed[1])
                    if o.is_dma:
                        ins.then_inc(dsem[o.dsem], 16)
                    elif o.mile is not None:
                        ins.then_inc(esem[o.eng], 1)
                if eng_name == "sp":
                    fin = {}
                    for d in final_wait_ops:
                        p = ops[d]
                        fin[p.dsem] = max(fin.get(p.dsem, 0), p.dval)
                    for k_, v_ in fin.items():
                        engine.wait_ge(dsem[k_], v_)

            block.tensor(lambda e: run("pe", e))
            block.scalar(lambda e: run("act", e))
            block.vector(lambda e: run("dve", e))
            block.gpsimd(lambda e: run("pool", e))
            block.sync(lambda e: run("sp", e))


def build(debug_names=(), stop_after=99):
    nc = bass.Bass("TRN2", target_bir_lowering=False)

    def din(name, shape, dt=F32):
        return nc.dram_tensor(name, list(shape), dt, kind="ExternalInput").ap()

    xin = din("xin", [TL, D])
    cc = din("cc", [128, 32])
    w_ada = din("w_ada", [D, 3 * D])
    bcol = din("bcol", [128, 96])
    ngcol = din("ngcol", [128, 16])
    w_in = din("w_in", [D, PC])
    mu = din("mu", [1, SHC])
    w0 = din("w0", [2, 1024]); w2 = din("w2", [2, 64, 1024]); a0 = din("a0", [2, 1024]); a2 = din("a2", [2, 64, 1024])
    k_k = din("k_k", [1, 1024]); k_a = din("k_a", [1, 1024]); r_k = din("r_k", [1, 1024])
    ln_g = din("ln_g", [1, 1024]); ln_b = din("ln_b", [1, 1024])
    qg = din("qg", [1, 64]); kg = din("kg", [1, 64]); sink = din("sink", [1, 16])
    w_out = din("w_out", [D, D])
    rope = din("rope", [17 * 128, 128])
    c_ident = din("c_ident", [128, 128])
    c_tri = din("c_tri", [4, 128, 128])
    c_maskx = din("c_maskx", [2, 128, 448])
    c_masky = din("c_masky", [2, 128, 256])
    c_ms = din("c_ms", [2, 7, 128, 256])
    c_sh = din("c_sh", [128, 128])
    c_e = din("c_e", [2, 128])
    c_i64 = din("c_i64", [128, 64])
    c_amask = din("c_amask", [2, 128, 128])
    out = nc.dram_tensor("out", [2048, D], F32, kind="ExternalOutput").ap()

    def scratch(name, shape, dt):
        kind = "ExternalOutput" if name in debug_names else None
        if kind:
            return nc.dram_tensor(name, list(shape), dt, kind=kind).ap()
        return nc.dram_tensor(name, list(shape), dt).ap()

    P = scratch("P", [TL, PC], F32)
    WB = scratch("WB", [D, PC], BF16); WBb = T(None)
    YA = scratch("YA", [2048, 1024], F32)
    MT = scratch("MT", [16, 128, 2048], BF16)
    dbg = {n: scratch(n, shp, F32) for n, shp in [("dbg_bc", [5, 128, D]), ("dbg_sh", [TL, SHC]), ("dbg_y", [2048, 1024]),
                                                  ("dbg_rw", [2048, 1024]), ("dbg_att", [2048, 1024])] if n in debug_names}
    Pb = T(None); YAb = T(None); MTb = T(None)
    Ptile = [T(None) for _ in range(NT)]

    S = Sched(nc)
    top = contextlib.ExitStack()

    uid = [0]

    def sb(st, name, shape, dt=F32):
        uid[0] += 1
        return T(st.enter_context(nc.sbuf_tensor("%s_%d" % (name, uid[0]), list(shape), dt)))

    def ps(st, name, shape, dt=F32):
        uid[0] += 1
        return T(st.enter_context(nc.psum_tensor("%s_%d" % (name, uid[0]), list(shape), dt)))

    def dma(eng, out_ap, in_ap, reads=(), writes=()):
        return S.op(eng, lambda e: e.dma_start(out=out_ap, in_=in_ap), reads=reads, writes=writes, dma=True)

    def mm(o, lhsT, rhs, start, stop, reads, writes):
        return S.op("pe", lambda e: e.matmul(o, lhsT=lhsT, rhs=rhs, start=start, stop=stop), reads=reads, writes=writes)

    def tr(o, in_, ident, reads, writes):
        return S.op("pe", lambda e: e.transpose(out=o, in_=in_, identity=ident), reads=reads, writes=writes)

    def act(o, in_, func, reads, writes, scale=1.0, bias=0.0, accum=None, eng="act"):
        if accum is None:
            return S.op(eng, lambda e: e.activation(out=o, in_=in_, func=func, scale=scale, bias=bias), reads=reads, writes=writes)
        return S.op(eng, lambda e: e.activation(out=o, in_=in_, func=func, scale=scale, bias=bias, accum_out=accum),
                    reads=reads, writes=writes)

    def tt(eng, o, a, b, op, reads, writes):
        return S.op(eng, lambda e: e.tensor_tensor(out=o, in0=a, in1=b, op=op), reads=reads, writes=writes)

    def ts(eng, o, a, s1, s2, op0, op1, reads, writes):
        if s2 is None:
            return S.op(eng, lambda e: e.tensor_scalar(out=o, in0=a, scalar1=s1, scalar2=None, op0=op0), reads=reads, writes=writes)
        return S.op(eng, lambda e: e.tensor_scalar(out=o, in0=a, scalar1=s1, scalar2=s2, op0=op0, op1=op1), reads=reads, writes=writes)

    def stt(eng, o, a, s, b, op0, op1, reads, writes):
        return S.op(eng, lambda e: e.scalar_tensor_tensor(out=o, in0=a, scalar=s, in1=b, op0=op0, op1=op1), reads=reads, writes=writes)

    def cp(eng, o, a, reads, writes):
        if eng == "act":
            return S.op("act", lambda e: e.activation(out=o, in_=a, func=AF.Copy), reads=reads, writes=writes)
        return S.op(eng, lambda e: e.tensor_copy(out=o, in_=a), reads=reads, writes=writes)

    def red(eng, o, a, reads, writes, op=ALU.add):
        return S.op(eng, lambda e: e.tensor_reduce(out=o, in_=a, axis=AX.X, op=op), reads=reads, writes=writes)

    def recip(o, a, reads, writes):
        return S.op("dve", lambda e: e.reciprocal(out=o, in_=a), reads=reads, writes=writes)

    def memset(eng, o, val, writes):
        return S.op(eng, lambda e: e.memset(o, val), writes=writes)

    ddn = [0]

    def dd(name, T_, ap, shape):
        if "dd" not in debug_names:
            return
        t = nc.dram_tensor("dd_" + name, list(shape), F32, kind="ExternalOutput").ap()
        dma("pool", t, ap, reads=[T_])

    def bc_row(ap_row, n):
        return ap_row.partition_broadcast(128) if hasattr(ap_row, "partition_broadcast") else ap_row

    identf = sb(top, "identf", [128, 128]); identb = sb(top, "identb", [128, 128], BF16)
    onesf = sb(top, "onesf", [128, 128])
    bonA = sb(top, "bonA", [128, 16, 16])
    st01 = contextlib.ExitStack()
    Abc = [sb(st01, "Abc%d" % v, [128, D]) for v in range(2)]
    Bbc = [sb(st01, "Bbc%d" % v, [128, D]) for v in range(2)]
    gatebc = sb(st01, "gatebc", [128, D])
    GATE = scratch("GATE", [128, D], F32); GATEb = T(None)
    dma("sp", identf[:], c_ident[:, :], writes=[identf])
    for k8 in range(8):
        dma("pool", WB[k8 * 256:(k8 + 1) * 256, :], w_in[k8 * 256:(k8 + 1) * 256, :], writes=[WBb])
    cp("dve", identb[:], identf[:], [identf], [identb])
    memset("dve", onesf[:], 1.0, [onesf])

    with contextlib.ExitStack() as st:
        cct = sb(st, "cct", [128, 32]); sc = sb(st, "sc", [128, 32])
        bct = sb(st, "bct", [128, 96]); ngt = sb(st, "ngt", [128, 16])
        modc = sb(st, "modc", [128, 96]); acol = sb(st, "acol", [128, 2, 16])
        wa = [sb(st, "wa%d" % i, [128, 16, 512]) for i in range(2)]
        dg = [sb(st, "dg%d" % i, [128, 512]) for i in range(2)]
        psA = ps(st, "psA", [128, 96])
        psB = [ps(st, "psB%d" % i, [128, 512]) for i in range(2)]
        dma("sp", cct[:], cc[:, :], writes=[cct]); dma("sp", bct[:], bcol[:, :], writes=[bct]); dma("sp", ngt[:], ngcol[:, :], writes=[ngt])
        act(sc[:], cct[:], AF.Silu, [cct], [sc])
        sc3 = sc[:].rearrange("p (v j) -> p v j", j=16)
        wav = w_ada.rearrange("(j p) n -> p j n", p=128)
        for g in range(12):
            w = wa[g % 2]
            dma("sp" if g % 2 == 0 else "act", w[:], wav[:, :, g * 512:(g + 1) * 512], writes=[w])
            for m4 in range(4):
                m = g * 4 + m4
                for j in range(16):
                    mm(psA[:, 2 * m:2 * m + 2], w[:, j, m4 * 128:(m4 + 1) * 128], sc3[:, :, j], j == 0, j == 15, [w, sc], [psA])
        tt("dve", modc[:], psA[:], bct[:], ALU.add, [psA, bct], [modc])
        mc3 = modc[:].rearrange("p (m v) -> p v m", v=2)
        for v in range(2):
            stt("dve", acol[:, v, :], mc3[:, v, 16:32], 1.0, ngt[:], ALU.add, ALU.mult, [modc, ngt], [acol])
        jobs = [(acol, lambda v, m: acol[:, v, m:m + 1], Abc[0], 0), (acol, lambda v, m: acol[:, v, m:m + 1], Abc[1], 1),
                (modc, lambda v, m: mc3[:, v, m:m + 1], Bbc[0], 0), (modc, lambda v, m: mc3[:, v, m:m + 1], Bbc[1], 1),
                (modc, lambda v, m: mc3[:, v, 32 + m:33 + m], gatebc, 0)]
        k = 0
        for src, colf, dst, v in jobs:
            for m4 in range(4):
                d_ = dg[k % 2]; p_ = psB[k % 2]; k += 1
                for mi in range(4):
                    m = m4 * 4 + mi
                    ts("dve", d_[:, mi * 128:(mi + 1) * 128], identf[:], colf(v, m), None, ALU.mult, ALU.bypass, [identf, src], [d_])
                mm(p_[:], onesf[:], d_[:], True, True, [onesf, d_], [p_])
                cp("act", dst[:, m4 * 512:(m4 + 1) * 512], p_[:], [p_], [dst])
        dma("sp", GATE[:, :], gatebc[:], reads=[gatebc], writes=[GATEb])
        if "dbg_bc" in dbg:
            for i, t_ in enumerate([Abc[0], Abc[1], Bbc[0], Bbc[1], gatebc]):
                dma("sp", dbg["dbg_bc"][i], t_[:], reads=[t_])
    S.barrier()

    blocks = [(0, 512, 'r'), (512, 512, 'r'), (1024, 512, 'k'), (1536, 512, 'k'), (2048, 512, 'v'), (2560, 512, 'v'),
              (3072, 256, 'lo'), (3328, 512, 'zr'), (3840, 512, 'zr'), (4352, 512, 'q'), (4864, 512, 'q'),
              (5376, 512, 'za'), (5888, 512, 'za'), (6400, 512, 'kv')]
    tblocks = [([0, 1], {'k', 'v', 'lo', 'kv'})] + [(list(range(s, s + 4)), None) for s in (2, 6, 10, 14)] + \
              [([18, 19, 20, 21], {'r', 'k', 'v', 'lo', 'kv'})] + [(list(range(s, s + 4)), {'k', 'v', 'lo'}) for s in (22, 26, 30)]
    if stop_after >= 1:
        with contextlib.ExitStack() as st:
            xt = [sb(st, "xt%d" % i, [128, D]) for i in range(2)]
            junk = sb(st, "junk", [128, D], BF16)
            ssq = [sb(st, "ssq%d" % i, [128, 1]) for i in range(2)]
            xs = [sb(st, "xs%d" % i, [128, D]) for i in range(2)]
            xn = [sb(st, "xn%d" % i, [128, D], BF16) for i in range(2)]
            xnT = [sb(st, "xnT%d" % i, [128, 16, 512], BF16) for i in range(2)]
            wb = [sb(st, "wb%d" % i, [128, 16, 512], BF16) for i in range(3)]
            stg = [sb(st, "stg%d" % i, [128, 512]) for i in range(4)]
            pT = [ps(st, "pT%d" % i, [128, 1024], BF16) for i in range(2)]
            pp = [ps(st, "pp%d" % i, [128, 512]) for i in range(4)]
            winv = WB.rearrange("(j p) n -> p j n", p=128)
            wi = 0; si = 0; ti = 0
            for bi, (tiles, need) in enumerate(tblocks):
                xT = xnT[bi % 2]
                for tl, i in enumerate(tiles):
                    v = 1 if i < 2 else 0
                    x_ = xt[ti % 2]; sq_ = ssq[ti % 2]; xs_ = xs[ti % 2]; xn_ = xn[ti % 2]; ti += 1
                    dma("act", x_[:], xin[i * 128:(i + 1) * 128, :], writes=[x_])
                    memset("dve", sq_[:], 0.0, [sq_])
                    act(junk[:], x_[:], AF.Square, [x_, sq_], [junk, sq_], accum=sq_[:, 0:1])
                    act(sq_[:], sq_[:], AF.Sqrt, [sq_], [sq_], scale=1.0 / D, bias=1e-6)
                    recip(sq_[:], sq_[:], [sq_], [sq_])
                    stt("dve", xs_[:], x_[:], sq_[:, 0:1], Abc[v][:], ALU.mult, ALU.mult, [x_, sq_, Abc[v]], [xs_])
                    tt("pool", xn_[:], xs_[:], Bbc[v][:], ALU.add, [xs_, Bbc[v]], [xn_])
                    for half in range(2):
                        p_ = pT[half]
                        for jj in range(8):
                            j = half * 8 + jj
                            tr(p_[:, jj * 128:(jj + 1) * 128], xn_[:, j * 128:(j + 1) * 128], identb[:], [xn_, identb], [p_])
                        cp("act" if half == 0 else "dve", xT[:, half * 8:(half + 1) * 8, tl * 128:(tl + 1) * 128],
                           p_[:].rearrange("p (j t) -> p j t", t=128), [p_], [xT])
                for (c0, cw, grp) in blocks:
                    if need is not None and grp not in need:
                        continue
                    w = wb[wi % 3]; wi += 1
                    dma("sp", w[:, :, 0:cw], winv[:, :, c0:c0 + cw], reads=[WBb], writes=[w])
                    for tl, i in enumerate(tiles):
                        p_ = pp[si % 4]; s_ = stg[si % 4]; si += 1
                        for j in range(16):
                            mm(p_[:, 0:cw], xT[:, j, tl * 128:(tl + 1) * 128], w[:, j, 0:cw], j == 0, j == 15, [xT, w], [p_])
                        cp("act" if si % 2 == 0 else "dve", s_[:, 0:cw], p_[:, 0:cw], [p_], [s_])
                        dma("pool", P[i * 128:(i + 1) * 128, c0:c0 + cw], s_[:, 0:cw], reads=[s_], writes=[Ptile[i]])
        S.barrier()

    st01.close()
    GN_EPS = 64e-5

    def bcast_load(eng, dst, src_row):
        return dma(eng, dst[:].rearrange("p (o n) -> p o n", o=1), src_row.partition_broadcast(128), writes=[dst])

    def rwkv_sweep(d):
        tiles = list(range(0, 18)) if d == 0 else [1, 0] + list(range(33, 1, -1))
        import os
        KT_ = int(os.environ.get("KTILES", "99")); KS_ = int(os.environ.get("KSTAGE", "99"))
        tiles = tiles[:KT_]
        with contextlib.ExitStack() as st:
            mubc = sb(st, "mubc", [128, SHC]); bcast_load("sp", mubc, mu[0:1, :])
            w0bc = sb(st, "w0bc", [128, 1024]); bcast_load("sp", w0bc, w0[d:d + 1, :])
            a0bc = sb(st, "a0bc", [128, 1024]); bcast_load("sp", a0bc, a0[d:d + 1, :])
            kkbc = sb(st, "kkbc", [128, 1024]); bcast_load("sp", kkbc, k_k[0:1, :])
            kabc = sb(st, "kabc", [128, 1024]); bcast_load("sp", kabc, k_a[0:1, :])
            rkbc = sb(st, "rkbc", [128, 1024]); bcast_load("sp", rkbc, r_k[0:1, :])
            if d == 1:
                lngbc = sb(st, "lngbc", [128, 1024]); bcast_load("sp", lngbc, ln_g[0:1, :])
                lnbbc = sb(st, "lnbbc", [128, 1024]); bcast_load("sp", lnbbc, ln_b[0:1, :])
            WAb = sb(st, "WAb", [128, 1024], BF16)
            dma("pool", WAb[0:64, :], w2[d], writes=[WAb]); dma("pool", WAb[64:128, :], a2[d], writes=[WAb])
            triI = sb(st, "triI", [128, 128]); dma("sp", triI[:], c_tri[2 * d], writes=[triI])
            triS = sb(st, "triS", [128, 128]); dma("sp", triS[:], c_tri[2 * d + 1], writes=[triS])
            negc = sb(st, "negc", [128, 1]); memset("dve", negc[:], -CDEC, [negc])
            maskx = sb(st, "maskx", [128, 448], BF16); dma("pool", maskx[:], c_maskx[d], writes=[maskx])
            masky = sb(st, "masky", [128, 256], BF16); dma("pool", masky[:], c_masky[d], writes=[masky])
            shm = sb(st, "shm", [128, 128], BF16); dma("pool", shm[:], c_sh[:, :], writes=[shm])
            e2 = sb(st, "e2", [2, 128], BF16); dma("pool", e2[:], c_e[:, :], writes=[e2])
            CM = sb(st, "CM", [128, 8, 576], BF16)
            for pr in range(8):
                dma("pool", CM[:, pr, 128:192], c_i64[:, :], writes=[CM])
            ST = [sb(st, "ST%d" % h, [64, 2, 64]) for h in range(16)]
            for h in range(16):
                memset("pool", ST[h][:], 0.0, [ST[h]])
            cur = [0] * 16
            sh = sb(st, "sh", [128, SHC]); pc16 = sb(st, "pc16", [128, SHC], BF16); nb16 = sb(st, "nb16", [2, SHC], BF16)
            lo16 = sb(st, "lo16", [128, 128], BF16); loT = sb(st, "loT", [128, 128], BF16)
            sg = sb(st, "sg", [128, 1024]); al = sb(st, "al", [128, 1024]); kk = sb(st, "kk", [128, 1024])
            bb = sb(st, "bb", [128, 1024]); kd = sb(st, "kd", [128, 1024]); t1 = sb(st, "t1", [128, 1024])
            E = [sb(st, "E%d" % k, [128, 1024]) for k in range(2)]
            ss = sb(st, "ss", [128, 16]); bon = sb(st, "bon", [128, 16]); WC = sb(st, "WC", [64, 16])
            rt, kt, bt, at, ktp, btp, vb = [sb(st, n, [128, 1024], BF16) for n in ("rt", "kt", "bt", "at", "ktp", "btp", "vb")]
            X0 = [sb(st, "X0%d" % p_, [128, 448], BF16) for p_ in range(8)]
            TT = [sb(st, "TT%d" % p_, [128, 256], BF16) for p_ in range(8)]
            CC = [sb(st, "CC%d" % p_, [128, 256], BF16) for p_ in range(8)]
            Zf = [sb(st, "Zf%d" % p_, [128, 192], BF16) for p_ in range(8)]
            msk = sb(st, "msk", [128, 7, 256], U8); dma("pool", msk[:], c_ms[d].rearrange("s p c -> p s c"), writes=[msk])
            II = sb(st, "II", [128, 256], BF16)
            cp("pool", II[:, 0:128], identb[:], [identb], [II]); cp("pool", II[:, 128:256], identb[:], [identb], [II])
            ARBK = [sb(st, "ARBK%d" % p_, [128, 256], BF16) for p_ in range(4)]
            QG = [sb(st, "QG%d" % p_, [64, 192], BF16) for p_ in range(4)]
            STb = [sb(st, "STb%d" % h, [64, 64], BF16) for h in range(16)]
            tmpS = [sb(st, "tmpS%d" % h, [64, 64]) for h in range(4)]
            tmpP = [sb(st, "tmpP%d" % h, [64, 64]) for h in range(4)]
            for h in range(16):
                memset("pool", STb[h][:], 0.0, [STb[h]])
            MYH = [sb(st, "MYH%d" % p_, [128, 192], BF16) for p_ in range(4)]
            ysb = sb(st, "ysb", [128, 1024])
            if d == 1:
                ya = sb(st, "ya", [128, 1024]); zr = sb(st, "zr", [128, 1024])
                mixb = sb(st, "mixb", [128, 1024], BF16); mtT = sb(st, "mtT", [128, 8, 128], BF16)
                s1 = sb(st, "s1", [128, 16]); s2 = sb(st, "s2", [128, 16]); mean = sb(st, "mean", [128, 16]); m2 = sb(st, "m2", [128, 16])
            Wd = [ps(st, "Wd%d" % k, [128, 512]) for k in range(2)]
            psT = ps(st, "psT", [128, 1024], BF16)
            Xp = [ps(st, "Xp%d" % k, [128, 512]) for k in range(2)]
            LB = [Xp[0], Xp[1], Wd[0], Wd[1]]
            b5 = ps(st, "b5", [128, 512])
            B6 = ps(st, "B6", [128, 512])
            b7 = ps(st, "b7", [128, 512])
            FB = [b5, b5, b7, b7]
            _ft5 = T(b5.t); _ft7 = T(b7.t)
            FT = [_ft5, _ft5, _ft7, _ft7]
            wk = 0
            def issue_p16(i_):
                own_ = OWN0 <= i_ < OWN1
                c0_ = 0 if own_ else 1024
                r0_ = i_ * 128
                dma("pool", pc16[:, c0_:SHC], P[r0_:r0_ + 128, c0_:SHC], reads=[Ptile[i_]], writes=[pc16])
                memset("pool", nb16[:], 0.0, [nb16])
                if i_ not in (0, 2):
                    dma("pool", nb16[0:1, c0_:SHC], P[r0_ - 1:r0_, c0_:SHC], reads=[Ptile[i_ - 1]], writes=[nb16])
                if i_ not in (1, 33):
                    dma("pool", nb16[1:2, c0_:SHC], P[r0_ + 128:r0_ + 129, c0_:SHC], reads=[Ptile[i_ + 1]], writes=[nb16])

            for ti_, i in enumerate(tiles):
                own = OWN0 <= i < OWN1
                c0 = 0 if own else 1024
                r0 = i * 128
                has_prev = i not in (0, 2); has_next = i not in (1, 33)
                dma("sp", sh[:, c0:SHC], P[r0:r0 + 128, c0:SHC], reads=[Ptile[i]], writes=[sh])
                if ti_ == 0:
                    issue_p16(i)
                if d == 1 and own:
                    dma("sp", ya[:], YA[(i - 2) * 128:(i - 1) * 128, :], reads=[YAb], writes=[ya])
                    dma("sp", zr[:], P[r0:r0 + 128, C_ZR:C_ZR + 1024], reads=[Ptile[i]], writes=[zr])
                cs = c0
                while cs < SHC:
                    cw = min(512, SHC - cs)
                    W = Wd[wk % 2]; tm_ = E[wk % 2]; wk += 1
                    mm(W[:, 0:cw], shm[:], pc16[:, cs:cs + cw], True, False, [shm, pc16], [W])
                    mm(W[:, 0:cw], e2[0:2, :], nb16[0:2, cs:cs + cw], False, True, [e2, nb16], [W])
                    tt("dve", tm_[:, 0:cw], W[:, 0:cw], mubc[:, cs:cs + cw], ALU.mult, [W, mubc], [tm_])
                    tt("dve", sh[:, cs:cs + cw], tm_[:, 0:cw], sh[:, cs:cs + cw], ALU.add, [tm_, sh], [sh])
                    cs += cw
                if ti_ + 1 < len(tiles):
                    issue_p16(tiles[ti_ + 1])
                r_ = sh[:, 0:1024]; k_ = sh[:, 1024:2048]; v_ = sh[:, 2048:3072]
                if KS_ <= 1:
                    continue
                act(lo16[:, 0:64], sh[:, 3072 + 64 * d:3136 + 64 * d], AF.Tanh, [sh], [lo16])
                cp("dve", lo16[:, 64:128], sh[:, 3200 + 64 * d:3264 + 64 * d], [sh], [lo16])
                tr(psT[:, 0:128], lo16[:, :], identb[:], [lo16, identb], [psT])
                cp("dve", loT[:], psT[:, 0:128], [psT], [loT])
                for (lo_, bias_, dst_) in ((0, w0bc, sg), (64, a0bc, al)):
                    for hf in range(2):
                        W = Wd[wk % 2]; wk += 1
                        mm(W[:], loT[lo_:lo_ + 64, :], WAb[lo_:lo_ + 64, hf * 512:(hf + 1) * 512], True, True, [loT, WAb], [W])
                        tt("dve", t1[:, hf * 512:(hf + 1) * 512], W[:], bias_[:, hf * 512:(hf + 1) * 512], ALU.add, [W, bias_], [t1])
                    act(dst_[:], t1[:], AF.Sigmoid, [t1], [dst_])
                tt("dve", kk[:], k_, kkbc[:], ALU.mult, [sh, kkbc], [kk])
                tt("dve", t1[:], kk[:], kk[:], ALU.mult, [kk], [t1])
                red("dve", ss[:], t1[:].rearrange("p (a b) -> p a b", b=64), [t1], [ss])
                act(ss[:], ss[:], AF.Sqrt, [ss], [ss])
                ts("dve", ss[:], ss[:], 1e-12, None, ALU.max, None, [ss], [ss])
                recip(ss[:], ss[:], [ss], [ss])
                tt("dve", kk[:].rearrange("p (a b) -> p a b", b=64), kk[:].rearrange("p (a b) -> p a b", b=64),
                   ss[:].unsqueeze(2).to_broadcast([128, 16, 64]), ALU.mult, [kk, ss], [kk])
                tt("dve", bb[:], kk[:], al[:], ALU.mult, [kk, al], [bb])
                stt("dve", t1[:], al[:], -1.0, kabc[:], ALU.add, ALU.mult, [al, kabc], [t1])
                stt("dve", kd[:], t1[:], 1.0, k_, ALU.add, ALU.mult, [t1, sh], [kd])
                if own:
                    tt("dve", t1[:], r_, rkbc[:], ALU.mult, [sh, rkbc], [t1])
                    tt("dve", t1[:], t1[:], kd[:], ALU.mult, [t1, kd], [t1])
                    if d == 0:
                        red("dve", bonA[:, i - 2, :], t1[:].rearrange("p (a b) -> p a b", b=64), [t1], [bonA])
                    else:
                        red("dve", bon[:], t1[:].rearrange("p (a b) -> p a b", b=64), [t1], [bon])
                        tt("dve", bon[:], bon[:], bonA[:, i - 2, :], ALU.add, [bon, bonA], [bon])
                if "dbg_sh" in dbg and d == 0 and i == 2:
                    dma("sp", dbg["dbg_sh"][0:128, :], sh[:], reads=[sh])
                    for k_i, t_ in enumerate([sg, al, kk, kd]):
                        dma("sp", dbg["dbg_sh"][128 * (k_i + 1):128 * (k_i + 2), 0:1024], t_[:], reads=[t_])
                for hf in range(2):
                    hs = slice(hf * 512, (hf + 1) * 512)
                    W = Wd[wk % 2]; wk += 1
                    mm(W[:], triI[:], sg[:, hs], True, True, [triI, sg], [W])
                    if own:
                        act(E[0][:, hs], W[:], AF.Exp, [W], [E[0]])
                    act(E[1][:, hs], W[:], AF.Exp, [W], [E[1]], scale=-1.0)
                    stt("dve", t1[:, hs], sg[:, hs], CDEC, W[:], ALU.mult, ALU.add, [sg, W], [t1])
                if own:
                    tt("dve", rt[:], r_, E[0][:], ALU.mult, [sh, E[0]], [rt])
                tt("dve", kt[:], kd[:], E[1][:], ALU.mult, [kd, E[1]], [kt])
                tt("dve", bt[:], bb[:], E[1][:], ALU.mult, [bb, E[1]], [bt])
                act(E[0][:], t1[:], AF.Exp, [t1], [E[0]])
                stt("dve", at[:], kk[:], -1.0, E[0][:], ALU.mult, ALU.mult, [kk, E[0]], [at])
                for hf in range(2):
                    hs = slice(hf * 512, (hf + 1) * 512)
                    W = Wd[wk % 2]; wk += 1
                    mm(W[:], triS[:], sg[:, hs], True, True, [triS, sg], [W])
                    act(E[1][:, hs], W[:], AF.Exp, [W], [E[1]])
                tt("dve", ktp[:], kd[:], E[1][:], ALU.mult, [kd, E[1]], [ktp])
                tt("dve", btp[:], bb[:], E[1][:], ALU.mult, [bb, E[1]], [btp])
                cp("act", vb[:], v_, [sh], [vb])
                for h in range(16):
                    mm(Wd[0][0:64, h:h + 1], sg[:, h * 64:(h + 1) * 64], negc[:, 0:1], True, True, [sg, negc], [Wd[0]])
                act(WC[:], Wd[0][0:64, 0:16], AF.Exp, [Wd[0]], [WC])
                if KS_ <= 2:
                    continue
                DD = (d == 0 and i in (0, 2))
                if DD:
                    for nm_, t_ in (("bt", bt), ("kt", kt), ("at", at), ("rt", rt), ("btp", btp), ("ktp", ktp), ("vb", vb)):
                        dd("%s_%d" % (nm_, i), t_, t_[:], [128, 1024])
                    dd("WC_%d" % i, WC, WC[:], [64, 16])
                srcs = [(bt, 0), (kt, 192), (at, 320)] + ([(rt, 448)] if own else [])
                for si_, (src, off) in enumerate(srcs):
                    for pr in range(8):
                        tr(psT[:, pr * 128:(pr + 1) * 128], src[:, pr * 128:(pr + 1) * 128], identb[:], [src, identb], [psT])
                    cp("act" if si_ % 2 == 0 else "dve", CM[:, :, off:off + 128], psT[:].rearrange("p (a b) -> p a b", b=128), [psT], [CM])
                if DD:
                    dd("CM_%d" % i, CM, CM[:, 0, :], [128, 576])
                if KS_ <= 3:
                    continue
                for grp in range(2):
                    hs4 = [(g, 8 * grp + g, (8 * grp + g) // 2, 64 * ((8 * grp + g) % 2)) for g in range(8)]
                    for (g, h, pr, pb) in hs4:
                        lb = LB[g % 4]
                        mm(lb[:, 128:448], CM[pb:pb + 64, pr, 320:448], CM[pb:pb + 64, pr, 0:320], True, True, [CM], [lb])
                        mm(lb[:, 0:128], CM[pb:pb + 64, pr, 0:128], CM[pb:pb + 64, pr, 320:448], True, True, [CM], [lb])
                        cp("pool", TT[g][:], II[:], [II], [TT[g]])
                        S.op("dve", lambda e, g=g, lb=lb: e.copy_predicated(out=TT[g][:], mask=msk[:, 0, :], data=lb[:, 0:256]),
                             reads=[lb, msk, TT[g]], writes=[TT[g]])
                        tt("dve", X0[g][:], lb[:, 0:448], maskx[:], ALU.mult, [lb, maskx], [X0[g]])
                        if DD and h < 2:
                            dd("X0_%d_%d" % (i, h), X0[g], X0[g][:], [128, 448])
                    if KS_ <= 4:
                        continue
                    for lev in range(1, 7):
                        for (g, h, pr, pb) in hs4:
                            lb = LB[g % 4]
                            mm(lb[:, 0:128], X0[g][:, 128:256], TT[g][:, 0:128], True, True, [X0[g], TT[g]], [lb])
                            mm(lb[:, 128:256], X0[g][:, 0:128], TT[g][:, 128:256], True, True, [X0[g], TT[g]], [lb])
                            cp("act", CC[g][:], lb[:, 0:256], [lb], [CC[g]])
                        for (g, h, pr, pb) in hs4:
                            lb = LB[g % 4]
                            mm(lb[:, 256:384], TT[g][:, 128:256], CC[g][:, 0:128], True, True, [TT[g], CC[g]], [lb])
                            mm(lb[:, 384:512], TT[g][:, 0:128], CC[g][:, 128:256], True, True, [TT[g], CC[g]], [lb])
                            S.op("dve", lambda e, g=g, lb=lb, lev=lev: e.copy_predicated(out=TT[g][:], mask=msk[:, lev, :], data=lb[:, 256:512]),
                                 reads=[lb, msk, TT[g]], writes=[TT[g]])
                    for (g, h, pr, pb) in hs4:
                        lb = LB[g % 4]
                        mm(lb[:, 0:192], TT[g][:, 0:128], X0[g][:, 256:448], True, True, [TT[g], X0[g]], [lb])
                        cp("dve" if g % 2 == 0 else "act", Zf[g][:], lb[:, 0:192], [lb], [Zf[g]])
                        if DD and h < 2:
                            dd("Zf_%d_%d" % (i, h), Zf[g], Zf[g][:], [128, 192])
                            dd("TT_%d_%d" % (i, h), TT[g], TT[g][:], [128, 256])
                    for sub in range(2):
                        hsub = hs4[4 * sub:4 * sub + 4]
                        if own:
                            for (g, h, pr, pb) in hsub:
                                lb = LB[g % 4]
                                mm(lb[:, 0:128], CM[pb:pb + 64, pr, 0:128], CM[pb:pb + 64, pr, 448:576], True, True, [CM], [lb])
                                mm(lb[:, 128:256], CM[pb:pb + 64, pr, 192:320], CM[pb:pb + 64, pr, 448:576], True, True, [CM], [lb])
                                tt("dve", ARBK[g % 4][:], lb[:, 0:256], masky[:], ALU.mult, [lb, masky], [ARBK[g % 4]])
                        lo_c = 0 if own else 128
                        for (g, h, pr, pb) in hsub:
                            lb = LB[g % 4]; fb = FB[g % 4]; fo = ((g % 4) % 2) * 256
                            hc = slice(h * 64, (h + 1) * 64)
                            AbT = Zf[g][:, 0:64]; PTt = Zf[g][:, 64:192]
                            if own:
                                mm(fb[0:64, fo:fo + 128], AbT, ARBK[g % 4][:, 0:128], True, False, [Zf[g], ARBK[g % 4]], [FT[g % 4]])
                                mm(fb[0:64, fo:fo + 128], rt[:, hc], identb[:], False, True, [rt, identb], [FT[g % 4]])
                            mm(fb[0:64, fo + 128:fo + 192], AbT, btp[:, hc], True, True, [Zf[g], btp], [FT[g % 4]])
                            cp("act", QG[g % 4][:, lo_c:192], fb[0:64, fo + lo_c:fo + 192], [FT[g % 4]], [QG[g % 4]])
                            if own:
                                mm(lb[:, 256:384], PTt, ARBK[g % 4][:, 0:128], True, False, [Zf[g], ARBK[g % 4]], [lb])
                                mm(lb[:, 256:384], identb[:], ARBK[g % 4][:, 128:256], False, True, [identb, ARBK[g % 4]], [lb])
                            mm(lb[:, 384:448], PTt, btp[:, hc], True, False, [Zf[g], btp], [lb])
                            mm(lb[:, 384:448], identb[:], ktp[:, hc], False, True, [identb, ktp], [lb])
                            cp("dve", MYH[g % 4][:, lo_c:192], lb[:, 256 + lo_c:448], [lb], [MYH[g % 4]])
                            if DD and h < 2:
                                dd("QG_%d_%d" % (i, h), QG[g % 4], QG[g % 4][:], [64, 192])
                                dd("MYH_%d_%d" % (i, h), MYH[g % 4], MYH[g % 4][:], [128, 192])
                                if own:
                                    dd("ARBK_%d_%d" % (i, h), ARBK[g % 4], ARBK[g % 4][:], [128, 256])
                        for (g, h, pr, pb) in hsub:
                            fb = FB[g % 4]; fo = ((g % 4) % 2) * 256
                            hc = slice(h * 64, (h + 1) * 64)
                            STc = ST[h][:, cur[h], :]; STn = ST[h][:, 1 - cur[h], :]
                            if own:
                                yo = B6[:, (h % 8) * 64:(h % 8) * 64 + 64]
                                mm(yo, QG[g % 4][:, 0:128], STb[h][:], True, False, [QG[g % 4], STb[h]], [B6])
                                mm(yo, MYH[g % 4][:, 0:128], vb[:, hc], False, True, [MYH[g % 4], vb], [B6])
                            sreg = fb[0:64, fo + 192:fo + 256]
                            mm(sreg, QG[g % 4][:, 128:192], STb[h][:], True, False, [QG[g % 4], STb[h]], [FT[g % 4]])
                            mm(sreg, MYH[g % 4][:, 128:192], vb[:, hc], False, True, [MYH[g % 4], vb], [FT[g % 4]])
                            act(tmpS[g % 4][:], STc, AF.Copy, [ST[h], WC], [tmpS[g % 4]], scale=WC[:, h:h + 1])
                            act(tmpP[g % 4][:], sreg, AF.Copy, [FT[g % 4]], [tmpP[g % 4]])
                            tt("pool", STn, tmpP[g % 4][:], tmpS[g % 4][:], ALU.add, [tmpP[g % 4], tmpS[g % 4]], [ST[h]])
                            cp("pool", STb[h][:], STn, [ST[h]], [STb[h]])
                            if DD and h < 2:
                                dd("ST_%d_%d" % (i, h), ST[h], STn, [64, 64])
                            cur[h] ^= 1
                    if own:
                        hs = slice(grp * 512, grp * 512 + 512)
                        if d == 0:
                            cp("act", ysb[:, hs], B6[:], [B6], [ysb])
                        else:
                            tt("dve", ysb[:, hs], B6[:], ya[:, hs], ALU.add, [B6, ya], [ysb])
                if not own:
                    continue
                if d == 0:
                    if DD:
                        dd("ysb_%d" % i, ysb, ysb[:], [128, 1024])
                    dma("act", YA[(i - 2) * 128:(i - 1) * 128, :], ysb[:], reads=[ysb], writes=[YAb])
                    continue
                if "dbg_y" in dbg:
                    dma("sp", dbg["dbg_y"][(i - 2) * 128:(i - 1) * 128, :], ysb[:], reads=[ysb])
                y3 = ysb[:].rearrange("p (a b) -> p a b", b=64)
                red("dve", s1[:], y3, [ysb], [s1])
                tt("dve", t1[:], ysb[:], ysb[:], ALU.mult, [ysb], [t1])
                red("dve", s2[:], t1[:].rearrange("p (a b) -> p a b", b=64), [t1], [s2])
                ts("dve", mean[:], s1[:], 1.0 / 64, None, ALU.mult, None, [s1], [mean])
                tt("dve", m2[:], mean[:], mean[:], ALU.mult, [mean], [m2])
                stt("dve", s2[:], s2[:], 1.0 / 64, m2[:], ALU.mult, ALU.subtract, [s2, m2], [s2])
                act(s2[:], s2[:], AF.Sqrt, [s2], [s2], bias=GN_EPS)
                recip(s2[:], s2[:], [s2], [s2])
                tt("dve", y3, y3, mean[:].unsqueeze(2).to_broadcast([128, 16, 64]), ALU.subtract, [ysb, mean], [ysb])
                tt("dve", y3, y3, s2[:].unsqueeze(2).to_broadcast([128, 16, 64]), ALU.mult, [ysb, s2], [ysb])
                tt("dve", ysb[:], ysb[:], lngbc[:], ALU.mult, [ysb, lngbc], [ysb])
                tt("dve", ysb[:], ysb[:], lnbbc[:], ALU.add, [ysb, lnbbc], [ysb])
                tt("dve", t1[:].rearrange("p (a b) -> p a b", b=64), vb[:].rearrange("p (a b) -> p a b", b=64),
                   bon[:].unsqueeze(2).to_broadcast([128, 16, 64]), ALU.mult, [vb, bon], [t1])
                tt("dve", ysb[:], ysb[:], t1[:], ALU.add, [ysb, t1], [ysb])
                if "dbg_rw" in dbg:
                    dma("sp", dbg["dbg_rw"][(i - 2) * 128:(i - 1) * 128, :], ysb[:], reads=[ysb])
                act(zr[:], zr[:], AF.Silu, [zr], [zr])
                tt("dve", mixb[:], ysb[:], zr[:], ALU.mult, [ysb, zr], [mixb])
                for cch in range(8):
                    tr(psT[:, cch * 128:(cch + 1) * 128], mixb[:, cch * 128:(cch + 1) * 128], identb[:], [mixb, identb], [psT])
                cp("act", mtT[:], psT[:].rearrange("p (a b) -> p a b", b=128), [psT], [mtT])
                dma("act", MT[0:8, :, (i - 2) * 128:(i - 1) * 128].rearrange("c p t -> p c t"), mtT[:], reads=[mtT], writes=[MTb])
        S.barrier()

    if stop_after >= 2:
        rwkv_sweep(0)
    if stop_after >= 3:
        rwkv_sweep(1)

    def rope_apply(eng2, t3, rot3, rp, nh, reads_t, T_t, T_rot, T_rp):
        t5 = t3.rearrange("p h (a b c) -> p h a b c", a=2, b=2)
        r5 = rot3.rearrange("p h (a b c) -> p h a b c", a=2, b=2)
        cp("pool", r5[:, :, :, 0, :], t5[:, :, :, 1, :], [T_t], [T_rot])
        cp("pool", r5[:, :, :, 1, :], t5[:, :, :, 0, :], [T_t], [T_rot])
        cosb = rp[:, 0:64].unsqueeze(1).to_broadcast([128, nh, 64])
        sinb = rp[:, 64:128].unsqueeze(1).to_broadcast([128, nh, 64])
        tt("dve", t3, t3, cosb, ALU.mult, [T_t, T_rp], [T_t])
        tt("dve", rot3, rot3, sinb, ALU.mult, [T_rot, T_rp], [T_rot])
        tt("dve", t3, t3, rot3, ALU.add, [T_t, T_rot], [T_t])

    def attention():
        with contextlib.ExitStack() as st:
            kgbc = sb(st, "kgbc", [128, 64]); bcast_load("sp", kgbc, kg[0:1, :])
            qgbc = sb(st, "qgbc", [128, 64]); bcast_load("sp", qgbc, qg[0:1, :])
            ts("dve", qgbc[:], qgbc[:], 0.125, None, ALU.mult, None, [qgbc], [qgbc])
            esk = sb(st, "esk", [128, 16]); bcast_load("sp", esk, sink[0:1, :])
            act(esk[:], esk[:], AF.Exp, [esk], [esk])
            amask = sb(st, "amask", [128, 2, 128], BF16)
            dma("pool", amask[:], c_amask.rearrange("a k q -> k a q"), writes=[amask])
            KT = sb(st, "KT", [64, 19, 4, 128], BF16)
            V1 = sb(st, "V1", [128, 19, 4, 65], BF16)
            memset("pool", V1[:], 1.0, [V1])
            kv = sb(st, "kv", [128, 512]); rot = sb(st, "rot", [128, 1024]); rp = sb(st, "rp", [128, 128])
            sq = sb(st, "sq", [128, 1024]); ss = sb(st, "ssq_a", [128, 16])
            kb = sb(st, "kb", [128, 256], BF16)
            q = sb(st, "q", [128, 1024]); za = sb(st, "za", [128, 1024]); qb = sb(st, "qb", [128, 1024], BF16)
            QT = sb(st, "QT", [64, 16, 128], BF16)
            PTs = [sb(st, "PTs%d" % k, [128, 512], BF16) for k in range(5)]
            att = sb(st, "att", [128, 1024]); den = sb(st, "den", [128, 16])
            mixb = sb(st, "mixb_a", [128, 1024], BF16); mtT = sb(st, "mtT_a", [128, 8, 128], BF16)
            psT = ps(st, "psT_a", [128, 1024], BF16)
            Sps = [ps(st, "Sps%d" % k, [128, 512]) for k in range(2)]
            Og = [ps(st, "Og%d" % k, [128, 512]) for k in range(4)]
            for n in range(19):
                i = n
                dma("sp", kv[:], P[i * 128:(i + 1) * 128, C_KA:C_KA + 512], reads=[Ptile[i]], writes=[kv])
                k3 = kv[:, 0:256].rearrange("p (h c) -> p h c", c=64)
                tt("dve", sq[:, 0:256], kv[:, 0:256], kv[:, 0:256], ALU.mult, [kv], [sq])
                red("dve", ss[:, 0:4], sq[:, 0:256].rearrange("p (h c) -> p h c", c=64), [sq], [ss])
                act(ss[:, 0:4], ss[:, 0:4], AF.Sqrt, [ss], [ss], scale=1.0 / 64, bias=1e-6)
                recip(ss[:, 0:4], ss[:, 0:4], [ss], [ss])
                tt("dve", k3, k3, ss[:, 0:4].unsqueeze(2).to_broadcast([128, 4, 64]), ALU.mult, [kv, ss], [kv])
                tt("dve", k3, k3, kgbc[:].unsqueeze(1).to_broadcast([128, 4, 64]), ALU.mult, [kv, kgbc], [kv])
                if i >= 2:
                    dma("sp", rp[:], rope[(i - 2) * 128:(i - 1) * 128, :], writes=[rp])
                    rope_apply(None, k3, rot[:, 0:256].rearrange("p (h c) -> p h c", c=64), rp, 4, None, kv, rot, rp)
                cp("act", kb[:], kv[:, 0:256], [kv], [kb])
                for g in range(4):
                    tr(psT[0:64, g * 128:(g + 1) * 128], kb[:, g * 64:(g + 1) * 64], identb[:], [kb, identb], [psT])
                cp("dve", KT[:, n, :, :], psT[0:64, 0:512].rearrange("p (g t) -> p g t", t=128), [psT], [KT])
                cp("act", V1[:, n, :, 0:64], kv[:, 256:512].rearrange("p (h c) -> p h c", c=64), [kv], [V1])
            sk = 0
            for i in range(OWN0, OWN1):
                r0 = i * 128
                dma("sp", q[:], P[r0:r0 + 128, C_Q:C_Q + 1024], reads=[Ptile[i]], writes=[q])
                dma("sp", za[:], P[r0:r0 + 128, C_ZA:C_ZA + 1024], reads=[Ptile[i]], writes=[za])
                dma("sp", rp[:], rope[(i - 2) * 128:(i - 1) * 128, :], writes=[rp])
                q3 = q[:].rearrange("p (h c) -> p h c", c=64)
                tt("dve", sq[:], q[:], q[:], ALU.mult, [q], [sq])
                red("dve", ss[:], sq[:].rearrange("p (h c) -> p h c", c=64), [sq], [ss])
                act(ss[:], ss[:], AF.Sqrt, [ss], [ss], scale=1.0 / 64, bias=1e-6)
                recip(ss[:], ss[:], [ss], [ss])
                tt("dve", q3, q3, ss[:].unsqueeze(2).to_broadcast([128, 16, 64]), ALU.mult, [q, ss], [q])
                tt("dve", q3, q3, qgbc[:].unsqueeze(1).to_broadcast([128, 16, 64]), ALU.mult, [q, qgbc], [q])
                rope_apply(None, q3, rot[:].rearrange("p (h c) -> p h c", c=64), rp, 16, None, q, rot, rp)
                cp("act", qb[:], q[:], [q], [qb])
                for half in range(2):
                    for hx in range(8):
                        hd = half * 8 + hx
                        tr(psT[0:64, hx * 128:(hx + 1) * 128], qb[:, hd * 64:(hd + 1) * 64], identb[:], [qb, identb], [psT])
                    cp("dve", QT[:, half * 8:(half + 1) * 8, :], psT[0:64, :].rearrange("p (g t) -> p g t", t=128), [psT], [QT])
                for g in range(4):
                    keyt = ([i - 1] if i > OWN0 else []) + [i, i + 1, 0, 1]
                    for ki, kt_ in enumerate(keyt):
                        sp_ = Sps[sk % 2]; sk += 1
                        pt_ = PTs[ki]
                        mm(sp_[:], KT[:, kt_, g, :], QT[:, 4 * g:4 * g + 4, :].rearrange("p a b -> p (a b)"), True, True, [KT, QT], [sp_])
                        act(pt_[:], sp_[:], AF.Exp, [sp_], [pt_])
                        is_prev = (i > OWN0 and ki == 0); is_next = (ki == (2 if i > OWN0 else 1))
                        if is_prev or is_next:
                            mi = 0 if is_prev else 1
                            p3 = pt_[:].rearrange("p (a b) -> p a b", b=128)
                            tt("dve", p3, p3, amask[:, mi, :].unsqueeze(1).to_broadcast([128, 4, 128]), ALU.mult, [pt_, amask], [pt_])
                    for hx in range(4):
                        for ki, kt_ in enumerate(keyt):
                            mm(Og[g][:, hx * 65:(hx + 1) * 65], PTs[ki][:, hx * 128:(hx + 1) * 128], V1[:, kt_, g, :],
                               ki == 0, ki == len(keyt) - 1, [PTs[ki], V1], [Og[g]])
                for g in range(4):
                    o3 = Og[g][:, 0:260].rearrange("p (h c) -> p h c", c=65)
                    tt("dve", den[:, 4 * g:4 * g + 4], o3[:, :, 64], esk[:, 4 * g:4 * g + 4], ALU.add, [Og[g], esk], [den])
                    recip(den[:, 4 * g:4 * g + 4], den[:, 4 * g:4 * g + 4], [den], [den])
                    tt("dve", att[:, g * 256:(g + 1) * 256].rearrange("p (h c) -> p h c", c=64), o3[:, :, 0:64],
                       den[:, 4 * g:4 * g + 4].unsqueeze(2).to_broadcast([128, 4, 64]), ALU.mult, [Og[g], den], [att])
                if "dbg_att" in dbg:
                    dma("sp", dbg["dbg_att"][(i - 2) * 128:(i - 1) * 128, :], att[:], reads=[att])
                act(za[:], za[:], AF.Silu, [za], [za])
                tt("dve", mixb[:], att[:], za[:], ALU.mult, [att, za], [mixb])
                for cch in range(8):
                    tr(psT[:, cch * 128:(cch + 1) * 128], mixb[:, cch * 128:(cch + 1) * 128], identb[:], [mixb, identb], [psT])
                cp("act", mtT[:], psT[:].rearrange("p (a b) -> p a b", b=128), [psT], [mtT])
                dma("sp", MT[8:16, :, (i - 2) * 128:(i - 1) * 128].rearrange("c p t -> p c t"), mtT[:], reads=[mtT], writes=[MTb])
        S.barrier()

    def outproj():
        with contextlib.ExitStack() as st:
            wo = sb(st, "wo", [128, 16, D], BF16)
            wov = w_out.rearrange("(j p) n -> p j n", p=128)
            for k4 in range(4):
                dma("pool", wo[:, k4 * 4:(k4 + 1) * 4, :], wov[:, k4 * 4:(k4 + 1) * 4, :], writes=[wo])
            gate = sb(st, "gate", [128, D]); dma("sp", gate[:], GATE[:, :], reads=[GATEb], writes=[gate])
            mt = [sb(st, "mt%d" % k, [128, 16, 128], BF16) for k in range(2)]
            xo = [sb(st, "xo%d" % k, [128, D]) for k in range(2)]
            ot = [sb(st, "ot%d" % k, [128, D]) for k in range(2)]
            pp = [ps(st, "ppo%d" % k, [128, 512]) for k in range(4)]
            for tl in range(16):
                i = tl + 2
                m_ = mt[tl % 2]; x_ = xo[tl % 2]; o_ = ot[tl % 2]
                dma("sp", m_[:], MT[:, :, tl * 128:(tl + 1) * 128].rearrange("c p t -> p c t"), reads=[MTb], writes=[m_])
                dma("act", x_[:], xin[i * 128:(i + 1) * 128, :], writes=[x_])
                for nb in range(4):
                    ns = slice(nb * 512, (nb + 1) * 512)
                    p_ = pp[nb]
                    for cc_ in range(16):
                        mm(p_[:], m_[:, cc_, :], wo[:, cc_, ns], cc_ == 0, cc_ == 15, [m_, wo], [p_])
                    tt("dve", o_[:, ns], p_[:], gate[:, ns], ALU.mult, [p_, gate], [o_])
                    tt("pool", o_[:, ns], o_[:, ns], x_[:, ns], ALU.add, [o_, x_], [o_])
                dma("sp", out[tl * 128:(tl + 1) * 128, :], o_[:], reads=[o_])

    if stop_after >= 4:
        attention()
    if stop_after >= 5:
        outproj()

    S.emit(final_wait_ops=[i for i, o in enumerate(S.ops) if o.is_dma])
    top.close()
    return nc


def host_consts():
    c = {}
    c["c_ident"] = np.eye(128, dtype=np.float32)
    i = np.arange(128)
    triA_incl = (i[:, None] <= i[None, :]).astype(np.float32)
    triA_suf = (i[:, None] > i[None, :]).astype(np.float32)
    triB_incl = (i[:, None] >= i[None, :]).astype(np.float32)
    triB_suf = (i[:, None] < i[None, :]).astype(np.float32)
    c["c_tri"] = (-CDEC * np.stack([triA_incl, triA_suf, triB_incl, triB_suf])).astype(np.float32)
    mx = np.zeros((2, 128, 448), np.float32); my = np.zeros((2, 128, 256), np.float32)
    for d in range(2):
        prec = (i[:, None] < i[None, :]) if d == 0 else (i[:, None] > i[None, :])
        preceq = prec | np.eye(128, dtype=bool)
        mx[d, :, 0:128] = prec; mx[d, :, 128:256] = prec.T; mx[d, :, 256:320] = 1.0; mx[d, :, 320:448] = prec.T
        my[d, :, 0:128] = preceq; my[d, :, 128:256] = preceq
    c["c_maskx"] = mx; c["c_masky"] = my
    cms = np.zeros((2, 7, 128, 256), np.float32)
    for d in range(2):
        prec = (i[:, None] < i[None, :]) if d == 0 else (i[:, None] > i[None, :])
        for s_ in range(7):
            m = prec & ((i[:, None] >> (s_ + 1)) == (i[None, :] >> (s_ + 1))) & ((i[:, None] >> s_) != (i[None, :] >> s_))
            cms[d, s_, :, 0:128] = m; cms[d, s_, :, 128:256] = m.T
    c["c_ms"] = cms
    sh = np.zeros((128, 128), np.float32)
    sh[i, i] = -1.0; sh[i[:-1], i[:-1] + 1] = 0.5; sh[i[1:], i[1:] - 1] = 0.5
    c["c_sh"] = sh
    e = np.zeros((2, 128), np.float32); e[0, 0] = 0.5; e[1, 127] = 0.5
    c["c_e"] = e
    c["c_i64"] = np.concatenate([np.eye(64, dtype=np.float32)] * 2, axis=0)
    am = np.zeros((2, 128, 128), np.float32)
    am[0] = (i[:, None] >= i[None, :]); am[1] = (i[:, None] <= i[None, :])
    c["c_amask"] = am
    return c


def rope_tables(pos):
    pos = np.asarray(pos)
    row = (pos // 64).astype(np.float32); col = (pos % 64).astype(np.float32)
    half = 16
    inv = (10000.0 ** (-np.arange(half, dtype=np.float32) / half)).astype(np.float32)
    ar = row[:, None] * inv; ac = col[:, None] * inv
    ang = np.concatenate([ar, ar, ac, ac], axis=-1)
    cos = np.cos(ang); sin = np.sin(ang)
    sgn = np.concatenate([-np.ones(16), np.ones(16), -np.ones(16), np.ones(16)]).astype(np.float32)
    return np.concatenate([cos, sin * sgn], axis=-1).astype(np.float32)


def core_inputs(inp, b, h, consts):
    f = np.ascontiguousarray
    x = inp["x"][b]; ctx = inp["ctx"][b]
    if h == 1:
        x = x[::-1]; ctx = ctx[::-1]
    m = {}
    m["xin"] = f(np.concatenate([ctx, x], axis=0))
    colf = lambda v: v.reshape(-1, 128).T
    m["cc"] = f(np.concatenate([colf(inp["c"][b]), colf(inp["c_ctx"])], axis=1))
    m["w_ada"] = f(inp["w_ada"][0])
    m["bcol"] = f(np.repeat(colf(inp["b_ada"][0]), 2, axis=1))
    m["ngcol"] = f(colf(inp["norm_g"][0]))
    dA, dB = (0, 1) if h == 0 else (1, 0)
    lw = lambda d: np.arange(3072 + 64 * d, 3072 + 64 * d + 64)
    la = lambda d: np.arange(3200 + 64 * d, 3200 + 64 * d + 64)
    perm = np.concatenate([np.arange(0, 3072), lw(dA), lw(dB), la(dA), la(dB), np.arange(3328, 4352), np.arange(4352, 5376),
                           np.arange(5888, 6912), np.arange(5376, 5632), np.arange(5632, 5888)])
    m["w_in"] = f(inp["w_in"][0][:, perm])
    m["mu"] = f(inp["mu_shift"][0][perm[:SHC]][None, :])
    sel = [dA, dB]
    m["w0"] = f(inp["w0"][0][sel]); m["w2"] = f(inp["w2"][0][sel]); m["a0"] = f(inp["a0"][0][sel]); m["a2"] = f(inp["a2"][0][sel])
    for k in ("k_k", "k_a"):
        m[k] = f(inp[k][0][None, :])
    m["r_k"] = f(inp["r_k"][0].reshape(1, 1024))
    m["ln_g"] = f(inp["ln_x_g"][0][None, :]); m["ln_b"] = f(inp["ln_x_b"][0][None, :])
    m["qg"] = f(inp["q_norm_g"][0][None, :]); m["kg"] = f(inp["k_norm_g"][0][None, :]); m["sink"] = f(inp["sink"][0][None, :])
    m["w_out"] = f(inp["w_out"][0])
    loc = np.arange(17 * 128)
    pos = loc if h == 0 else 4095 - loc
    m["rope"] = rope_tables(pos)
    m.update(consts)
    return m


def kernel(**inputs):
    inp = {k: np.asarray(v) for k, v in inputs.items()}
    consts = host_consts()
    nc = build()
    in_maps = [core_inputs(inp, c // 2, c % 2, consts) for c in range(8)]
    res = run_bass_kernel_spmd(nc, in_maps, core_ids=list(range(8)))
    out = np.empty((4, 4096, D), np.float32)
    for c in range(8):
        b, h = c // 2, c % 2
        o = res.results[c]["out"]
        if h == 0:
            out[b, :2048] = o
        else:
            out[b, 2048:] = o[::-1]
    return out
```
